# Optimizing a Trainium2 kernel written in Bass

```python
import jax
import jax.numpy as jnp
from jax import lax
import numpy as np

D_MODEL = 1024
BATCH = 4
SEQ = 8192
DEPTH = 4

N_MIXERS = 4
MEM_LEN = 256
NORM_EPS = 1e-6
ROPE_THETA = 500000.0
NEG_INF = -1e30
POS_OFFSET_MAX = 4096

A_HEAD_DIM = 64
A_HEADS_PER_GROUP = 8
A_GROUPS = ((128, 1), (512, 4), (2048, 16))
A_BLOCK = 128
A_ROT_DIM = A_HEAD_DIM // 4

B_HEADS = 4
B_KEY_DIM = D_MODEL // 2
B_VAL_DIM = D_MODEL
B_DK = B_KEY_DIM // B_HEADS
B_DV = B_VAL_DIM // B_HEADS
B_GATE_RANK = 16
B_GATE_NORMALIZER = 16.0
B_CHUNK = 64

C_HEADS = 16
C_Q_RANK = 384
C_KV_RANK = 256
C_NOPE = 64
C_ROPE = 32
C_VDIM = 64
C_QBLOCK = 128

D_HEAD = 64
D_HEADS = D_MODEL // D_HEAD
D_DECAY_RANK = 64
D_ICL_RANK = 64
D_GATE_RANK = 128
D_GN_EPS = 64e-5

M_HEADS = 4
M_HEAD_DIM = 128

D_FF = 2816
CONV_WIDTH = 3

kernel_name = 'hybrid_interleaved_dilated_gla_mla_rwkv7_trunk'


def _n_occ(m):
    return len(range(m, DEPTH, N_MIXERS))


def rms_norm(x, g, eps=NORM_EPS):
    xf = x.astype(jnp.float32)
    y = xf * lax.rsqrt(jnp.mean(xf * xf, axis=-1, keepdims=True) + eps)
    return (y * g.astype(jnp.float32)).astype(x.dtype)


def apply_rope(x, pos, rot_dim):
    half = rot_dim // 2
    inv_freq = ROPE_THETA ** (-(jnp.arange(half, dtype=jnp.float32) * (2.0 / rot_dim)))
    ang = pos.astype(jnp.float32)[:, :, None] * inv_freq
    cos = jnp.cos(ang)[:, :, None, :]
    sin = jnp.sin(ang)[:, :, None, :]
    x1 = x[..., :half].astype(jnp.float32)
    x2 = x[..., half:rot_dim].astype(jnp.float32)
    rot = jnp.concatenate([x1 * cos - x2 * sin, x2 * cos + x1 * sin], axis=-1).astype(x.dtype)
    return jnp.concatenate([rot, x[..., rot_dim:]], axis=-1)


def _to_blocks(t, dil):
    B, S, H, d = t.shape
    L = S // dil
    nb = -(-L // A_BLOCK)
    t = t.reshape(B, L, dil, H, d).transpose(0, 2, 1, 3, 4)
    t = jnp.pad(t, ((0, 0), (0, 0), (0, nb * A_BLOCK - L), (0, 0), (0, 0)))
    return t.reshape(B, dil, nb, A_BLOCK, H, d)


def _from_blocks(t, S):
    B, dil = t.shape[:2]
    tail = t.shape[4:]
    t = t.reshape((B, dil, -1) + tail)[:, :, : S // dil]
    return jnp.moveaxis(t, 1, 2).reshape((B, S) + tail)


def dilated_window_attention(q, k, v, dil, steps):
    S = q.shape[1]
    qb, kb, vb = _to_blocks(q, dil), _to_blocks(k, dil), _to_blocks(v, dil)
    nb = qb.shape[2]
    pad_prev = ((0, 0), (0, 0), (1, 0), (0, 0), (0, 0), (0, 0))
    kk = jnp.concatenate([jnp.pad(kb, pad_prev)[:, :, :-1], kb], axis=3)
    vv = jnp.concatenate([jnp.pad(vb, pad_prev)[:, :, :-1], vb], axis=3)
    s = jnp.einsum('brnqhd,brnkhd->brnhqk', qb, kk).astype(jnp.float32)
    qi = np.arange(A_BLOCK)[:, None]
    kj = np.arange(2 * A_BLOCK)[None, :]
    dist = A_BLOCK + qi - kj
    band = (dist >= 0) & (dist <= steps)
    mask = np.broadcast_to(band, (nb, A_BLOCK, 2 * A_BLOCK)).copy()
    mask[0] &= kj >= A_BLOCK
    s = jnp.where(jnp.asarray(mask)[None, None, :, None], s, NEG_INF)
    m = jnp.max(s, axis=-1, keepdims=True)
    p = jnp.exp(s - m)
    l = jnp.sum(p, axis=-1, keepdims=True)
    o = jnp.einsum('brnhqk,brnkhd->brnhqd', p, vv.astype(jnp.float32)) / l
    lse = (m + jnp.log(l))[..., 0]
    o = _from_blocks(o.transpose(0, 1, 2, 4, 3, 5), S)
    lse = _from_blocks(lse.transpose(0, 1, 2, 4, 3), S)
    return o, lse


def mixer_dilated(h, pos, w_qkv, w_o):
    B, S, _ = h.shape
    G = len(A_GROUPS)
    qkv = (h @ w_qkv).reshape(B, S, 3, G, A_HEADS_PER_GROUP, A_HEAD_DIM)
    outs, lses = [], []
    for g, (window, dil) in enumerate(A_GROUPS):
        q = apply_rope(qkv[:, :, 0, g], pos, A_ROT_DIM) * (A_HEAD_DIM ** -0.5)
        k = apply_rope(qkv[:, :, 1, g], pos, A_ROT_DIM)
        o, lse = dilated_window_attention(q, k, qkv[:, :, 2, g], dil, window // dil)
        outs.append(o)
        lses.append(lse)
    alpha = jax.nn.softmax(jnp.stack(lses), axis=0)[..., None]
    o = jnp.sum(alpha * jnp.stack(outs), axis=0).astype(h.dtype)
    return o.reshape(B, S, A_HEADS_PER_GROUP * A_HEAD_DIM) @ w_o


def mixer_gla(h, w_in, w_gate2, b_gate, o_norm, w_o):
    B, S, _ = h.shape
    H, dk, dv, C = B_HEADS, B_DK, B_DV, B_CHUNK
    n = S // C
    f32 = jnp.float32
    cuts = np.cumsum([B_KEY_DIM, B_KEY_DIM, B_VAL_DIM, B_VAL_DIM]).tolist()
    q, k, v, gate_out, gate_lr = jnp.split(h @ w_in, cuts, axis=-1)
    log_a = jax.nn.log_sigmoid((gate_lr @ w_gate2 + b_gate).astype(f32)) / B_GATE_NORMALIZER

    def chunks(t, d):
        return t.reshape(B, n, C, H, d)

    qc = chunks(q, dk).astype(f32) * (dk ** -0.5)
    kc = chunks(k, dk).astype(f32)
    vc = chunks(v, dv).astype(f32)
    G = jnp.cumsum(chunks(log_a, dk), axis=2)
    g_end = G[:, :, -1]
    q_dec = qc * jnp.exp(G)
    k_inv = kc * jnp.exp(-G)
    k_end = kc * jnp.exp(g_end[:, :, None] - G)
    causal = jnp.asarray(np.tril(np.ones((C, C), dtype=bool)))
    att = jnp.where(causal, jnp.einsum('bnihd,bnjhd->bnhij', q_dec, k_inv), 0.0)
    o_intra = jnp.einsum('bnhij,bnjhe->bnihe', att, vc)

    def step(state, inp):
        q_t, k_t, v_t, ge_t = inp
        o_t = jnp.einsum('bihd,bhde->bihe', q_t, state)
        state = state * jnp.exp(ge_t)[..., None] + jnp.einsum('bjhd,bjhe->bhde', k_t, v_t)
        return state, o_t

    xs = tuple(jnp.moveaxis(t, 1, 0) for t in (q_dec, k_end, vc, g_end))
    _, o_inter = lax.scan(step, jnp.zeros((B, H, dk, dv), f32), xs)
    o = (o_intra + jnp.moveaxis(o_inter, 0, 1)).reshape(B, S, H, dv)
    o = rms_norm(o, o_norm) * jax.nn.silu(gate_out.astype(f32).reshape(B, S, H, dv))
    return o.reshape(B, S, B_VAL_DIM).astype(h.dtype) @ w_o


def mixer_mla(h, pos, w_in, q_norm, w_uq, kv_norm, w_ukv, w_o):
    B, S, _ = h.shape
    H = C_HEADS
    cq, ckv, k_pe = jnp.split(h @ w_in, [C_Q_RANK, C_Q_RANK + C_KV_RANK], axis=-1)
    q = (rms_norm(cq, q_norm) @ w_uq).reshape(B, S, H, C_NOPE + C_ROPE)
    q_nope = q[..., :C_NOPE]
    q_pe = apply_rope(q[..., C_NOPE:], pos, C_ROPE)
    kv = (rms_norm(ckv, kv_norm) @ w_ukv).reshape(B, S, H, C_NOPE + C_VDIM)
    k_nope, v = kv[..., :C_NOPE], kv[..., C_NOPE:]
    k_pe = apply_rope(k_pe[:, :, None, :], pos, C_ROPE)[:, :, 0]
    scale = (C_NOPE + C_ROPE) ** -0.5
    nb = S // C_QBLOCK
    key_pos = jnp.arange(S)

    def blockify(t):
        return jnp.moveaxis(t.reshape((B, nb, C_QBLOCK) + t.shape[2:]), 1, 0)

    def attend(args):
        qn, qp, b = args
        s = (jnp.einsum('bqhd,bkhd->bhqk', qn, k_nope)
             + jnp.einsum('bqhr,bkr->bhqk', qp, k_pe)).astype(jnp.float32) * scale
        q_pos = b * C_QBLOCK + jnp.arange(C_QBLOCK)
        s = jnp.where(key_pos[None, :] <= q_pos[:, None], s, NEG_INF)
        p = jax.nn.softmax(s, axis=-1).astype(v.dtype)
        return jnp.einsum('bhqk,bkhd->bqhd', p, v)

    o = lax.map(attend, (blockify(q_nope), blockify(q_pe), jnp.arange(nb)))
    o = jnp.moveaxis(o, 0, 1).reshape(B, S, H * C_VDIM)
    return o @ w_o


def mixer_rwkv7(h, mix, w_rkv, w0, w1, w2, a0, a1, a2, g1, g2, k_k, k_a, r_k, lnx_w, lnx_b, w_o):
    B, S, D = h.shape
    H, N = D_HEADS, D_HEAD
    f32 = jnp.float32
    xx = jnp.pad(h, ((0, 0), (1, 0), (0, 0)))[:, :-1] - h
    xr, xw, xk, xv, xa, xg = (h + xx * mix[i] for i in range(6))
    r = xr @ w_rkv[0]
    k = xk @ w_rkv[1]
    v = xv @ w_rkv[2]
    w = -jax.nn.softplus(-(w0 + jnp.tanh(xw @ w1) @ w2).astype(f32)) - 0.5
    a = jax.nn.sigmoid((a0 + (xa @ a1) @ a2).astype(f32))
    g = jax.nn.sigmoid(xg @ g1) @ g2
    kk = (k * k_k).astype(f32).reshape(B, S, H, N)
    kk = kk / jnp.maximum(jnp.linalg.norm(kk, axis=-1, keepdims=True), 1e-12)
    k = k.astype(f32) * (1.0 + (a - 1.0) * k_a)

    def heads(t):
        return t.astype(f32).reshape(B, S, H, N)

    r_h, k_h, v_h, a_h = heads(r), heads(k), heads(v), heads(a)
    decay = heads(jnp.exp(-jnp.exp(w)))

    def step(state, inp):
        r_t, w_t, k_t, v_t, kk_t, a_t = inp
        sa = jnp.einsum('bhvk,bhk->bhv', state, -kk_t)
        state = (state * w_t[:, :, None, :]
                 + sa[..., None] * (kk_t * a_t)[:, :, None, :]
                 + v_t[..., None] * k_t[:, :, None, :])
        return state, jnp.einsum('bhvk,bhk->bhv', state, r_t)

    xs = tuple(jnp.moveaxis(t, 1, 0) for t in (r_h, decay, k_h, v_h, kk, a_h))
    _, y = lax.scan(step, jnp.zeros((B, H, N, N), f32), xs)
    y = jnp.moveaxis(y, 0, 1)
    mu = jnp.mean(y, axis=-1, keepdims=True)
    var = jnp.mean(jnp.square(y - mu), axis=-1, keepdims=True)
    y = ((y - mu) * lax.rsqrt(var + D_GN_EPS)).reshape(B, S, D) * lnx_w + lnx_b
    bonus = jnp.sum(r_h * k_h * r_k, axis=-1, keepdims=True) * v_h
    y = y + bonus.reshape(B, S, D)
    return (y * g).astype(h.dtype) @ w_o


def memory_attention(h, mem_k, mem_v, w_q, w_o):
    B, S, _ = h.shape
    q = (h @ w_q).reshape(B, S, M_HEADS, M_HEAD_DIM)
    s = jnp.einsum('bshd,bmhd->bhsm', q, mem_k).astype(jnp.float32) * (M_HEAD_DIM ** -0.5)
    p = jax.nn.softmax(s, axis=-1).astype(mem_v.dtype)
    o = jnp.einsum('bhsm,bmhd->bshd', p, mem_v)
    return o.reshape(B, S, M_HEADS * M_HEAD_DIM) @ w_o


def conv_ffn(h, w_in, conv_w, conv_b, w_out):
    S = h.shape[1]
    u = h @ w_in
    up = jnp.pad(u, ((0, 0), (CONV_WIDTH - 1, 0), (0, 0)))
    c = conv_b + up[:, :S] * conv_w[0]
    for j in range(1, CONV_WIDTH):
        c = c + up[:, j:j + S] * conv_w[j]
    gate, val = jnp.split(c, 2, axis=-1)
    return (jax.nn.silu(gate) * val) @ w_out


def setup_inputs(seed: int = 0) -> dict:
    key = jax.random.key(seed)
    keys = iter(jax.random.split(key, 64))
    f32 = jnp.float32
    D = D_MODEL
    nA, nB, nC, nD = (_n_occ(m) for m in range(N_MIXERS))

    def dense(shape, fan_in, scale=1.0):
        return jax.random.normal(next(keys), shape, f32) * (scale * fan_in ** -0.5)

    def gain(shape):
        return 1.0 + 0.05 * jax.random.normal(next(keys), shape, f32)

    def small(shape, scale=0.02):
        return scale * jax.random.normal(next(keys), shape, f32)

    def uniform(shape, lo, hi):
        return jax.random.uniform(next(keys), shape, f32, lo, hi)

    a_width = 3 * len(A_GROUPS) * A_HEADS_PER_GROUP * A_HEAD_DIM
    b_width = 2 * B_KEY_DIM + 2 * B_VAL_DIM + B_GATE_RANK
    c_width = C_Q_RANK + C_KV_RANK + C_ROPE
    return {
        'x': jax.random.normal(next(keys), (BATCH, SEQ, D), f32),
        'mem': jax.random.normal(next(keys), (BATCH, MEM_LEN, D), f32),
        'positions': (jax.random.randint(next(keys), (BATCH, 1), 0, POS_OFFSET_MAX, jnp.int32)
                      + jnp.arange(SEQ, dtype=jnp.int32)[None, :]),
        'ln_gains': gain((DEPTH, 6, D)),
        'mem_norm': gain((D,)),
        'mem_w_kv': dense((D, 2 * M_HEADS * M_HEAD_DIM), D),
        'mem_w_q': dense((DEPTH, D, M_HEADS * M_HEAD_DIM), D),
        'mem_w_o': dense((DEPTH, M_HEADS * M_HEAD_DIM, D), M_HEADS * M_HEAD_DIM),
        'ffn_w_in': dense((DEPTH, D, 2 * D_FF), D),
        'ffn_conv_w': dense((DEPTH, CONV_WIDTH, 2 * D_FF), CONV_WIDTH),
        'ffn_conv_b': small((DEPTH, 2 * D_FF)),
        'ffn_w_out': dense((DEPTH, D_FF, D), D_FF),
        'a_w_qkv': dense((nA, D, a_width), D),
        'a_w_o': dense((nA, A_HEADS_PER_GROUP * A_HEAD_DIM, D), A_HEADS_PER_GROUP * A_HEAD_DIM),
        'b_w_in': dense((nB, D, b_width), D),
        'b_w_gate2': dense((nB, B_GATE_RANK, B_KEY_DIM), B_GATE_RANK),
        'b_gate_bias': small((nB, B_KEY_DIM), 0.1),
        'b_o_norm': gain((nB, B_DV)),
        'b_w_o': dense((nB, B_VAL_DIM, D), B_VAL_DIM),
        'c_w_in': dense((nC, D, c_width), D),
        'c_q_norm': gain((nC, C_Q_RANK)),
        'c_w_uq': dense((nC, C_Q_RANK, C_HEADS * (C_NOPE + C_ROPE)), C_Q_RANK),
        'c_kv_norm': gain((nC, C_KV_RANK)),
        'c_w_ukv': dense((nC, C_KV_RANK, C_HEADS * (C_NOPE + C_VDIM)), C_KV_RANK),
        'c_w_o': dense((nC, C_HEADS * C_VDIM, D), C_HEADS * C_VDIM),
        'd_mix': uniform((nD, 6, D), 0.0, 1.0),
        'd_w_rkv': dense((nD, 3, D, D), D),
        'd_w0': uniform((nD, D), -6.0, -1.0),
        'd_w1': dense((nD, D, D_DECAY_RANK), D),
        'd_w2': dense((nD, D_DECAY_RANK, D), D_DECAY_RANK, 0.1),
        'd_a0': small((nD, D), 0.1),
        'd_a1': dense((nD, D, D_ICL_RANK), D),
        'd_a2': dense((nD, D_ICL_RANK, D), D_ICL_RANK),
        'd_g1': dense((nD, D, D_GATE_RANK), D),
        'd_g2': dense((nD, D_GATE_RANK, D), D_GATE_RANK),
        'd_k_k': 0.85 + small((nD, D), 0.05),
        'd_k_a': gain((nD, D)),
        'd_r_k': small((nD, D_HEADS, D_HEAD), 0.1),
        'd_lnx_w': gain((nD, D)),
        'd_lnx_b': small((nD, D)),
        'd_w_o': dense((nD, D, D), D),
    }


def reference(x, mem, positions, ln_gains, mem_norm, mem_w_kv, mem_w_q, mem_w_o,
              ffn_w_in, ffn_conv_w, ffn_conv_b, ffn_w_out,
              a_w_qkv, a_w_o,
              b_w_in, b_w_gate2, b_gate_bias, b_o_norm, b_w_o,
              c_w_in, c_q_norm, c_w_uq, c_kv_norm, c_w_ukv, c_w_o,
              d_mix, d_w_rkv, d_w0, d_w1, d_w2, d_a0, d_a1, d_a2, d_g1, d_g2,
              d_k_k, d_k_a, d_r_k, d_lnx_w, d_lnx_b, d_w_o):
    B, L_mem, _ = mem.shape
    mkv = (rms_norm(mem, mem_norm) @ mem_w_kv).reshape(B, L_mem, 2, M_HEADS, M_HEAD_DIM)
    mem_k, mem_v = mkv[:, :, 0], mkv[:, :, 1]
    for i in range(DEPTH):
        m, j = i % N_MIXERS, i // N_MIXERS
        hn = rms_norm(x, ln_gains[i, 0])
        if m == 0:
            y = mixer_dilated(hn, positions, a_w_qkv[j], a_w_o[j])
        elif m == 1:
            y = mixer_gla(hn, b_w_in[j], b_w_gate2[j], b_gate_bias[j], b_o_norm[j], b_w_o[j])
        elif m == 2:
            y = mixer_mla(hn, positions, c_w_in[j], c_q_norm[j], c_w_uq[j],
                          c_kv_norm[j], c_w_ukv[j], c_w_o[j])
        else:
            y = mixer_rwkv7(hn, d_mix[j], d_w_rkv[j], d_w0[j], d_w1[j], d_w2[j],
                            d_a0[j], d_a1[j], d_a2[j], d_g1[j], d_g2[j],
                            d_k_k[j], d_k_a[j], d_r_k[j], d_lnx_w[j], d_lnx_b[j], d_w_o[j])
        x = x + rms_norm(y, ln_gains[i, 1])
        y = memory_attention(rms_norm(x, ln_gains[i, 2]), mem_k, mem_v, mem_w_q[i], mem_w_o[i])
        x = x + rms_norm(y, ln_gains[i, 3])
        y = conv_ffn(rms_norm(x, ln_gains[i, 4]), ffn_w_in[i], ffn_conv_w[i], ffn_conv_b[i], ffn_w_out[i])
        x = x + rms_norm(y, ln_gains[i, 5])
    return x
```

```python
import numpy as np
from contextlib import ExitStack
import concourse.bass as bass
import concourse.mybir as mybir
from concourse.bass_utils import run_bass_kernel_spmd

F32 = mybir.dt.float32
BF16 = mybir.dt.bfloat16
I32 = mybir.dt.int32
AF = mybir.ActivationFunctionType
ALU = mybir.AluOpType
AX = mybir.AxisListType

D = 1024
S = 8192
NT = S // 128
EPS = 1e-6
MEM = 256
DFF = 2816


class Buf:
    __slots__ = ("t", "w", "r", "excl")

    def __init__(self, t, excl=False):
        self.t = t
        self.w = None
        self.r = {}
        self.excl = excl

    def __getitem__(self, idx):
        return V([self], self.t[idx])

    def v(self, ap):
        return V([self], ap)


class V:
    __slots__ = ("bufs", "ap")

    def __init__(self, bufs, ap):
        self.bufs = bufs
        self.ap = ap

    def __getitem__(self, idx):
        return V(self.bufs, self.ap[idx])


class DramT:
    def __init__(self, ap, blk=128, tracked=True):
        self.ap = ap
        self.blk = blk
        n = (ap.shape[0] + blk - 1) // blk
        self.blocks = [Buf(None) for _ in range(n)] if tracked else None

    def rows(self, r0, r1, cols=None):
        ap = self.ap[r0:r1] if cols is None else self.ap[r0:r1, cols[0]:cols[1]]
        if self.blocks is None:
            return V([], ap)
        return V(self.blocks[r0 // self.blk:(r1 - 1) // self.blk + 1], ap)

    def view(self, ap, r0, r1):
        if self.blocks is None:
            return V([], ap)
        return V(self.blocks[r0 // self.blk:(r1 - 1) // self.blk + 1], ap)


def RO(ap):
    return V([], ap)


COMPUTE = ("pe", "act", "dve", "pool")


class FW:
    def __init__(self, nc, es, n_dma=32):
        self.nc = nc
        self.engs = {"pe": nc.tensor, "act": nc.scalar, "dve": nc.vector, "pool": nc.gpsimd, "sp": nc.sync}
        self.sem = {k: es.enter_context(nc.semaphore("s_" + k)) for k in COMPUTE}
        self.cnt = {k: 0 for k in COMPUTE}
        self.waited = {k: {} for k in self.engs}
        self.dsem = [es.enter_context(nc.semaphore("d%d" % i)) for i in range(n_dma)]
        self.dcnt = [0] * n_dma
        self.dnext = 0
        self.nd = n_dma
        self.ninst = 0

    def _wait(self, eng, key, val):
        if eng == "pe" and key == "pe":
            return
        w = self.waited[eng]
        if w.get(key, 0) >= val:
            return
        sem = self.sem[key] if isinstance(key, str) else self.dsem[key]
        self.engs[eng].wait_ge(sem, val)
        w[key] = val
        self.ninst += 1

    def _deps(self, eng, reads, writes):
        for v in reads:
            for b in v.bufs:
                if b.w is not None:
                    self._wait(eng, b.w[0], b.w[1])
                if b.excl:
                    for k, val in b.r.items():
                        if k != eng:
                            self._wait(eng, k, val)
        for v in writes:
            for b in v.bufs:
                if b.w is not None:
                    self._wait(eng, b.w[0], b.w[1])
                for k, val in b.r.items():
                    self._wait(eng, k, val)

    def _done(self, key, val, reads, writes):
        for v in reads:
            for b in v.bufs:
                b.r[key] = val
        for v in writes:
            for b in v.bufs:
                b.w = (key, val)
                b.r = {}

    def op(self, eng, fn, reads, writes):
        self._deps(eng, reads, writes)
        ins = fn()
        self.cnt[eng] += 1
        ins.then_inc(self.sem[eng], 1)
        self._done(eng, self.cnt[eng], reads, writes)
        self.ninst += 1

    def dma(self, q, out, in_, **kw):
        s = self.dnext
        self.dnext = (s + 1) % self.nd
        if self.dcnt[s]:
            self._wait(q, s, self.dcnt[s])
        self._deps(q, [in_], [out])
        ins = self.engs[q].dma_start(out=out.ap, in_=in_.ap, **kw)
        self.dcnt[s] += 16
        ins.then_inc(self.dsem[s], 16)
        self._done(s, self.dcnt[s], [in_], [out])
        self.ninst += 1

    def barrier(self):
        for eng in self.engs:
            for k in COMPUTE:
                if self.cnt[k]:
                    self._wait(eng, k, self.cnt[k])
            for s in range(self.nd):
                if self.dcnt[s]:
                    self._wait(eng, s, self.dcnt[s])

    def mm(self, out, lhsT, rhs, start=True, stop=True):
        self.op("pe", lambda: self.nc.tensor.matmul(out.ap, lhsT.ap, rhs.ap, start=start, stop=stop),
                [lhsT, rhs], [out])

    def tr(self, out, in_, ident):
        self.op("pe", lambda: self.nc.tensor.transpose(out.ap, in_.ap, ident.ap), [in_, ident], [out])

    def act(self, out, in_, func, bias=None, scale=None, accum=None):
        kw = {}
        reads = [in_]
        writes = [out]
        if bias is not None:
            if isinstance(bias, V):
                kw["bias"] = bias.ap
                reads.append(bias)
            else:
                kw["bias"] = bias
        if scale is not None:
            if isinstance(scale, V):
                kw["scale"] = scale.ap
                reads.append(scale)
            else:
                kw["scale"] = scale
        if accum is not None:
            kw["accum_out"] = accum.ap
            writes.append(accum)
        self.op("act", lambda: self.nc.scalar.activation(out=out.ap, in_=in_.ap, func=func, **kw), reads, writes)

    def _e(self, eng):
        return self.engs[eng]

    def tt(self, eng, out, a, b, op):
        self.op(eng, lambda: self._e(eng).tensor_tensor(out.ap, a.ap, b.ap, op), [a, b], [out])

    def ts(self, eng, out, a, s1, s2=None, op0=ALU.mult, op1=None, accum=None):
        reads = [a]
        writes = [out]
        a1 = s1
        a2 = s2
        if isinstance(s1, V):
            reads.append(s1)
            a1 = s1.ap
        if isinstance(s2, V):
            reads.append(s2)
            a2 = s2.ap
        kw = {}
        if accum is not None:
            kw["accum_out"] = accum.ap
            writes.append(accum)
        if op1 is None:
            self.op(eng, lambda: self._e(eng).tensor_scalar(out.ap, a.ap, a1, None, op0, **kw), reads, writes)
        else:
            self.op(eng, lambda: self._e(eng).tensor_scalar(out.ap, a.ap, a1, a2, op0, op1, **kw), reads, writes)

    def stt(self, eng, out, a, s, b, op0, op1):
        reads = [a, b]
        sa = s
        if isinstance(s, V):
            reads.append(s)
            sa = s.ap
        self.op(eng, lambda: self._e(eng).scalar_tensor_tensor(out.ap, a.ap, sa, b.ap, op0, op1), reads, [out])

    def cp(self, eng, out, in_):
        if eng == "act":
            self.act(out, in_, AF.Copy)
        else:
            self.op(eng, lambda: self._e(eng).tensor_copy(out.ap, in_.ap), [in_], [out])

    def recip(self, out, in_):
        self.op("dve", lambda: self.nc.vector.reciprocal(out.ap, in_.ap), [in_], [out])

    def red(self, eng, out, in_, op, axis=AX.X):
        self.op(eng, lambda: self._e(eng).tensor_reduce(out.ap, in_.ap, axis, op), [in_], [out])

    def memset(self, eng, out, val):
        self.op(eng, lambda: self._e(eng).memset(out.ap, val), [], [out])

    def asel(self, out, in_, pattern, cmp, fill, base, cm):
        self.op("pool", lambda: self.nc.gpsimd.affine_select(out=out.ap, in_=in_.ap, pattern=pattern,
                                                             compare_op=cmp, fill=fill, base=base,
                                                             channel_multiplier=cm), [in_], [out])


class Ctx:
    pass


def sb(c, es, name, shape, dt=F32):
    c.uid += 1
    return Buf(es.enter_context(c.nc.sbuf_tensor("%s_%d" % (name, c.uid), shape, dt)))


class Rot:
    def __init__(self, bufs):
        self.bufs = bufs
        self.i = 0

    def next(self):
        b = self.bufs[self.i]
        self.i = (self.i + 1) % len(self.bufs)
        return b


def rot(c, es, name, n, shape, dt=F32):
    return Rot([sb(c, es, name, shape, dt) for _ in range(n)])


def load_w(c, es_tmp, dst, src_ap, K, N, gcol=None, n0=0, q="sp"):
    fw = c.fw
    CH = 1408 if N % 1408 == 0 else (1024 if N % 1024 == 0 else N)
    if CH > 2048:
        CH = N // ((N + 2047) // 2048)
        assert N % CH == 0
    stg = rot(c, es_tmp, "wstg", 3, [128, CH], F32)
    i = 0
    for kc in range(K // 128):
        for n1 in range(0, N, CH):
            st = stg.next()
            fw.dma(q, st[:, :], RO(src_ap[kc * 128:(kc + 1) * 128, n1:n1 + CH]))
            eng = ("dve", "pool", "act")[i % 3]
            i += 1
            o = dst[:, kc, n0 + n1:n0 + n1 + CH]
            if gcol is None:
                fw.cp(eng, o, st[:, :])
            elif eng == "act":
                fw.act(o, st[:, :], AF.Copy, scale=gcol[:, kc:kc + 1])
            else:
                fw.ts(eng, o, st[:, :], gcol[:, kc:kc + 1])


def load_cols(c, es_tmp, dst_v, src2d_ap, R):
    fw = c.fw
    st = sb(c, es_tmp, "lc", [128, 128], F32)
    fw.dma("sp", st[0:R, :], RO(src2d_ap))
    ps = c.ps.next()
    fw.tr(ps[:, 0:R], st[0:R, :], c.identf[0:R, 0:R])
    fw.cp("dve", dst_v, ps[:, 0:R])


def bcast(c, dst, vec_ap):
    c.fw.dma("sp", dst, RO(vec_ap.partition_broadcast(128)))


def rms_rstd(c, ss_v, n, out_v, tmp):
    fw = c.fw
    k = ss_v.ap.shape[1]
    fw.ts("dve", tmp[:, 0:k], ss_v, 1.0 / n, EPS, ALU.mult, ALU.add)
    fw.act(tmp[:, k:2 * k], tmp[:, 0:k], AF.Sqrt)
    fw.recip(out_v, tmp[:, k:2 * k])


def norm_tile(c, xt, hn, st):
    fw = c.fw
    fw.memset("pool", st[:, 0:1], 0.0)
    fw.act(c.junk[:, :], xt[:, :], AF.Square, accum=st[:, 0:1])
    rms_rstd(c, st[:, 0:1], D, st[:, 3:4], c.mk_tmp(st))
    fw.act(hn[:, :], xt[:, :], AF.Copy, scale=st[:, 3:4])


def transpose_to(c, hn, dstT, t0, nchunk=8):
    fw = c.fw
    ps = c.ps.next()
    psb = ps.v(ps.t[:, :].bitcast(BF16))
    for ch in range(nchunk):
        fw.tr(V([ps], psb.ap[:, ch * 128:(ch + 1) * 128]), hn[:, ch * 128:(ch + 1) * 128], c.identb[:, :])
    eng = c.evac_eng()
    fw.cp(eng, dstT[:, 0:nchunk, t0:t0 + 128],
          V([ps], psb.ap[:, 0:nchunk * 128].rearrange("p (c t) -> p c t", c=nchunk)))


def postnorm_store(c, psA, psB, xt, G, xo, st, dst_rows):
    fw = c.fw
    fw.memset("pool", st[:, 0:2], 0.0)
    fw.act(c.junk[:, 0:512], psA[:, :], AF.Square, accum=st[:, 0:1])
    fw.act(c.junk[:, 512:1024], psB[:, :], AF.Square, accum=st[:, 1:2])
    fw.tt("dve", st[:, 2:3], st[:, 0:1], st[:, 1:2], ALU.add)
    rms_rstd(c, st[:, 2:3], D, st[:, 3:4], c.mk_tmp(st))
    tmp = c.pn_tmp.next()
    fw.stt("dve", tmp[:, 0:512], psA[:, :], st[:, 3:4], G[:, 0:512], ALU.mult, ALU.mult)
    fw.stt("dve", tmp[:, 512:1024], psB[:, :], st[:, 3:4], G[:, 512:1024], ALU.mult, ALU.mult)
    fw.tt("pool", xo[:, :], xt[:, :], tmp[:, :], ALU.add)
    fw.dma("pool", dst_rows, xo[:, :])


def phase_consts(c, es):
    fw = c.fw
    c.identf = sb(c, es, "identf", [128, 128], F32)
    c.identb = sb(c, es, "identb", [128, 128], BF16)
    fw.memset("pool", c.identf[:, :], 1.0)
    fw.asel(c.identf[:, :], c.identf[:, :], [[-1, 128]], ALU.is_equal, 0.0, 0, 1)
    fw.cp("pool", c.identb[:, :], c.identf[:, :])
    c.junk = sb(c, es, "junk", [128, 1024], BF16)
    c.gT = sb(c, es, "gT", [128, 192], F32)
    with ExitStack() as tmp:
        g2 = c.d["ln_gains"].rearrange("l s (c p) -> (l s c) p", p=128)
        load_cols(c, tmp, c.gT[:, 0:96], g2[0:96, :], 96)
        load_cols(c, tmp, c.gT[:, 96:192], g2[96:192, :], 96)
        fw.barrier()


def gcol(c, li, si):
    o = (li * 6 + si) * 8
    return c.gT[:, o:o + 8]


def phase_memkv(c, es):
    fw = c.fw
    c.mem_kT = sb(c, es, "memkT", [128, 4, 256], BF16)
    c.mem_v = sb(c, es, "memv", [128, 2, 512], BF16)
    with ExitStack() as tmp:
        w = sb(c, tmp, "wkv", [128, 8, 1024], BF16)
        gm = sb(c, tmp, "gm", [128, 8], F32)
        load_cols(c, tmp, gm[:, :], c.d["mem_norm"].rearrange("(c p) -> c p", p=128), 8)
        with ExitStack() as t2:
            load_w(c, t2, w, c.d["mem_w_kv"], D, 1024, gcol=gm)
            fw.barrier()
        hnT = sb(c, tmp, "mhnT", [128, 8, 256], BF16)
        for ti in range(2):
            xt = sb(c, tmp, "mx", [128, 1024], F32)
            hn = sb(c, tmp, "mhn", [128, 1024], BF16)
            st = sb(c, tmp, "mst", [128, 16], F32)
            fw.dma("sp", xt[:, :], RO(c.d["mem"][ti * 128:(ti + 1) * 128, :]))
            norm_tile(c, xt, hn, st)
            transpose_to(c, hn, hnT, ti * 128)
        for h in range(4):
            ps = c.ps.next()
            for kc in range(8):
                fw.mm(ps[:, 0:256], w[:, kc, h * 128:(h + 1) * 128], hnT[:, kc, :], kc == 0, kc == 7)
            fw.cp("act", c.mem_kT[:, h, :], ps[:, 0:256])
        for mc in range(2):
            ps = c.ps.next()
            for kc in range(8):
                fw.mm(ps[:, :], hnT[:, kc, mc * 128:(mc + 1) * 128], w[:, kc, 512:1024], kc == 0, kc == 7)
            fw.cp("act", c.mem_v[:, mc, :], ps[:, :])
        fw.barrier()


def phase_mem(c, li, xin, xout):
    fw = c.fw
    TB = 512
    with ExitStack() as es:
        wq = sb(c, es, "wq", [128, 8, 512], BF16)
        wo = sb(c, es, "wo", [128, 4, 1024], BF16)
        G = sb(c, es, "G", [128, 1024], F32)
        bcast(c, G[:, :], c.d["ln_gains"][li, 3])
        with ExitStack() as t2:
            load_w(c, t2, wq, c.d["mem_w_q"][li], D, 512, gcol=gcol(c, li, 2))
            load_w(c, t2, wo, c.d["mem_w_o"][li], 512, 1024)
            fw.barrier()
        xts = rot(c, es, "xt", 6, [128, 1024], F32)
        hns = rot(c, es, "hn", 2, [128, 1024], BF16)
        sts = rot(c, es, "st", 4, [128, 16], F32)
        hnTs = rot(c, es, "hnT", 2, [128, 8, TB], BF16)
        qTs = rot(c, es, "qT", 2, [128, 4, TB], BF16)
        ps_ = rot(c, es, "p", 2, [128, 4, 256], F32)
        pns = rot(c, es, "pn", 2, [128, 4, 256], BF16)
        pTs = rot(c, es, "pT", 2, [128, 8, 128], BF16)
        oTs = rot(c, es, "oT", 2, [128, 4, 128], BF16)
        xos = rot(c, es, "xo", 2, [128, 1024], F32)
        c.pn_tmp = rot(c, es, "pnt", 2, [128, 1024], F32)
        sms = rot(c, es, "sm", 4, [128, 16], F32)
        scale = 128.0 ** -0.5
        for blk in range(S // TB):
            hnT = hnTs.next()
            xl = []
            for ti in range(TB // 128):
                r0 = blk * TB + ti * 128
                xt = xts.next()
                xl.append(xt)
                fw.dma("sp", xt[:, :], xin.rows(r0, r0 + 128))
                hn = hns.next()
                norm_tile(c, xt, hn, sts.next())
                transpose_to(c, hn, hnT, ti * 128)
            qT = qTs.next()
            for h in range(4):
                ps = c.ps.next()
                for kc in range(8):
                    fw.mm(ps[:, :], wq[:, kc, h * 128:(h + 1) * 128], hnT[:, kc, :], kc == 0, kc == 7)
                fw.act(qT[:, h, :], ps[:, :], AF.Copy, scale=scale)
            for ti in range(TB // 128):
                r0 = blk * TB + ti * 128
                tsl = slice(ti * 128, (ti + 1) * 128)
                sm = sms.next()
                pss = [c.ps.next(), c.ps.next()]
                for h in range(4):
                    fw.mm(pss[h // 2][:, (h % 2) * 256:(h % 2 + 1) * 256], qT[:, h, tsl], c.mem_kT[:, h, :])
                for j in range(2):
                    fw.red("dve", sm[:, 2 * j:2 * j + 2],
                           V([pss[j]], pss[j].t[:, :].rearrange("p (h m) -> p h m", h=2)), ALU.max)
                fw.ts("dve", sm[:, 4:8], sm[:, 0:4], -1.0)
                fw.memset("pool", sm[:, 8:12], 0.0)
                p = ps_.next()
                for h in range(4):
                    fw.act(p[:, h, :], pss[h // 2][:, (h % 2) * 256:(h % 2 + 1) * 256], AF.Exp,
                           bias=sm[:, 4 + h:5 + h], accum=sm[:, 8 + h:9 + h])
                fw.recip(sm[:, 12:16], sm[:, 8:12])
                pn = pns.next()
                for h in range(4):
                    fw.ts(("dve", "pool")[h % 2], pn[:, h, :], p[:, h, :], sm[:, 12 + h:13 + h])
                psT = c.ps.next()
                psTb = psT.t[:, :].bitcast(BF16)
                for h in range(4):
                    for mc in range(2):
                        j = h * 2 + mc
                        fw.tr(V([psT], psTb[:, j * 128:(j + 1) * 128]), pn[:, h, mc * 128:(mc + 1) * 128],
                              c.identb[:, :])
                pT = pTs.next()
                fw.cp("act", pT[:, :, :], V([psT], psTb.rearrange("p (j t) -> p j t", j=8)))
                psO = c.ps.next()
                for h in range(4):
                    for mc in range(2):
                        fw.mm(psO[:, h * 128:(h + 1) * 128], c.mem_v[:, mc, h * 128:(h + 1) * 128],
                              pT[:, h * 2 + mc, :], mc == 0, mc == 1)
                oT = oTs.next()
                fw.cp("act", oT[:, :, :], V([psO], psO.t[:, :].rearrange("p (h t) -> p h t", h=4)))
                psY = [c.ps.next(), c.ps.next()]
                for nh in range(2):
                    for h in range(4):
                        fw.mm(psY[nh][:, :], oT[:, h, :], wo[:, h, nh * 512:(nh + 1) * 512], h == 0, h == 3)
                postnorm_store(c, psY[0], psY[1], xl[ti], G, xos.next(), sts.next(), xout.rows(r0, r0 + 128))
        fw.barrier()


def phase_ffn(c, li, xin, xout):
    fw = c.fw
    TB = 256
    NTI = TB // 128
    with ExitStack() as es:
        w1 = sb(c, es, "w1", [128, 8, 2 * DFF], BF16)
        w2 = sb(c, es, "w2", [128, 22, 1024], BF16)
        G = sb(c, es, "G", [128, 1024], F32)
        cw = sb(c, es, "cw", [128, 3, 44], F32)
        cb = sb(c, es, "cb", [128, 44], F32)
        tail = sb(c, es, "tail", [128, 44, 2], F32)
        bcast(c, G[:, :], c.d["ln_gains"][li, 5])
        fw.memset("pool", tail[:, :, :], 0.0)
        with ExitStack() as t2:
            for j in range(3):
                load_cols(c, t2, cw[:, j, :], c.d["ffn_conv_w"][li, j].rearrange("(c p) -> c p", p=128), 44)
            load_cols(c, t2, cb[:, :], c.d["ffn_conv_b"][li].rearrange("(c p) -> c p", p=128), 44)
            load_w(c, t2, w1, c.d["ffn_w_in"][li], D, 2 * DFF, gcol=gcol(c, li, 4))
            load_w(c, t2, w2, c.d["ffn_w_out"][li], DFF, 1024)
            fw.barrier()
        xts = rot(c, es, "xt", 4, [128, 1024], F32)
        hns = rot(c, es, "hn", 2, [128, 1024], BF16)
        sts = rot(c, es, "st", 4, [128, 16], F32)
        hnTs = rot(c, es, "hnT", 1, [128, 8, TB], BF16)
        ubs = rot(c, es, "ub", 4, [128, TB + 2], F32)
        tbs = rot(c, es, "tb", 4, [128, TB], F32)
        sgs = rot(c, es, "sg", 2, [128, TB], F32)
        aTs = rot(c, es, "aT", 1, [128, 22, TB], BF16)
        xos = rot(c, es, "xo", 2, [128, 1024], F32)
        c.pn_tmp = rot(c, es, "pnt", 1, [128, 1024], F32)
        for blk in range(S // TB):
            hnT = hnTs.next()
            xl = []
            for ti in range(NTI):
                r0 = blk * TB + ti * 128
                xt = xts.next()
                xl.append(xt)
                fw.dma("sp", xt[:, :], xin.rows(r0, r0 + 128))
                hn = hns.next()
                norm_tile(c, xt, hn, sts.next())
                transpose_to(c, hn, hnT, ti * 128)
            aT = aTs.next()
            for j in range(22):
                tv = []
                for half, ch in ((0, j), (1, 22 + j)):
                    ps = c.ps.next()
                    for kc in range(8):
                        fw.mm(ps[:, 0:TB], w1[:, kc, ch * 128:(ch + 1) * 128], hnT[:, kc, :], kc == 0, kc == 7)
                    ub = ubs.next()
                    fw.cp("pool", ub[:, 0:2], tail[:, ch, :])
                    fw.cp("act", ub[:, 2:TB + 2], ps[:, 0:TB])
                    fw.cp("pool", tail[:, ch, :], ub[:, TB:TB + 2])
                    t = tbs.next()
                    fw.act(t[:, :], ub[:, 2:TB + 2], AF.Identity, bias=cb[:, ch:ch + 1], scale=cw[:, 2, ch:ch + 1])
                    fw.stt("dve", t[:, :], ub[:, 0:TB], cw[:, 0, ch:ch + 1], t[:, :], ALU.mult, ALU.add)
                    fw.stt("dve", t[:, :], ub[:, 1:TB + 1], cw[:, 1, ch:ch + 1], t[:, :], ALU.mult, ALU.add)
                    tv.append(t)
                sg = sgs.next()
                fw.act(sg[:, :], tv[0][:, :], AF.Silu)
                fw.tt("pool", aT[:, j, :], sg[:, :], tv[1][:, :], ALU.mult)
            for ti in range(NTI):
                r0 = blk * TB + ti * 128
                psY = [c.ps.next(), c.ps.next()]
                for nh in range(2):
                    for kc in range(22):
                        fw.mm(psY[nh][:, :], aT[:, kc, ti * 128:(ti + 1) * 128], w2[:, kc, nh * 512:(nh + 1) * 512],
                              kc == 0, kc == 21)
                postnorm_store(c, psY[0], psY[1], xl[ti], G, xos.next(), sts.next(), xout.rows(r0, r0 + 128))
        fw.barrier()


IN_SPECS = [
    ("x", [S, D], F32), ("mem", [MEM, D], F32), ("positions", [S], I32),
    ("ln_gains", [4, 6, D], F32), ("mem_norm", [D], F32), ("mem_w_kv", [D, 1024], F32),
    ("mem_w_q", [4, D, 512], F32), ("mem_w_o", [4, 512, D], F32),
    ("ffn_w_in", [4, D, 2 * DFF], F32), ("ffn_conv_w", [4, 3, 2 * DFF], F32), ("ffn_conv_b", [4, 2 * DFF], F32),
    ("ffn_w_out", [4, DFF, D], F32),
    ("a_w_qkv", [D, 4608], F32), ("a_w_o", [512, D], F32),
    ("b_w_in", [D, 3088], F32), ("b_w_gate2", [16, 512], F32), ("b_gate_bias", [512], F32),
    ("b_o_norm", [256], F32), ("b_w_o", [D, D], F32),
    ("c_w_in", [D, 672], F32), ("c_q_norm", [384], F32), ("c_w_uq", [384, 1536], F32),
    ("c_kv_norm", [256], F32), ("c_w_ukv", [256, 2048], F32), ("c_w_o", [D, D], F32),
    ("d_mix", [6, D], F32), ("d_w_rkv", [3, D, D], F32), ("d_w0", [D], F32), ("d_w1", [D, 64], F32),
    ("d_w2", [64, D], F32), ("d_a0", [D], F32), ("d_a1", [D, 64], F32), ("d_a2", [64, D], F32),
    ("d_g1", [D, 128], F32), ("d_g2", [128, D], F32), ("d_k_k", [D], F32), ("d_k_a", [D], F32),
    ("d_r_k", [16, 64], F32), ("d_lnx_w", [D], F32), ("d_lnx_b", [D], F32), ("d_w_o", [D, D], F32),
    ("k_invf_c", [128, 1], F32),
]


def host_consts():
    invf = np.zeros((128, 1), np.float32)
    for i in range(32):
        invf[64 + i, 0] = INVF_C[i % 16]
    return {"k_invf_c": invf}

ALL_SUBS = [(k, li) for li in range(4) for k in ("mix", "mem", "ffn")]


def build(subs=None):
    if subs is None:
        subs = ALL_SUBS
    nc = bass.Bass("TRN2", target_bir_lowering=False)
    c = Ctx()
    c.nc = nc
    c.uid = 0
    c.d = {}
    for name, shape, dt in IN_SPECS:
        c.d[name] = nc.dram_tensor(name, shape, dt, kind="ExternalInput").ap()
    out = nc.dram_tensor("out", [S, D], F32, kind="ExternalOutput").ap()
    scr = [nc.dram_tensor("xs%d" % i, [S, D], F32, kind="Internal").ap() for i in range(2)]
    with ExitStack() as es:
        fw = FW(nc, es)
        c.fw = fw
        banks = [Buf(es.enter_context(nc.psum_tensor("ps%d" % i, [128, 512], F32)), excl=True) for i in range(8)]
        c.ps = Rot(banks[0:6])
        c.psx = Rot(banks[6:8])
        c._ev = 0

        def evac_eng():
            c._ev += 1
            return ("act", "dve")[c._ev % 2]
        c.evac_eng = evac_eng
        c.mk_tmp = _Tmp
        phase_consts(c, es)
        phase_memkv(c, es)
        cur = DramT(c.d["x"], tracked=False)
        for i, (kind, li) in enumerate(subs):
            last = i == len(subs) - 1
            dst = DramT(out) if last else DramT(scr[i % 2])
            if kind == "mem":
                phase_mem(c, li, cur, dst)
            elif kind == "ffn":
                phase_ffn(c, li, cur, dst)
            else:
                MIXERS[li](c, li, cur, dst)
            cur = dst
        fw.barrier()
    c.ninst = fw.ninst
    return nc, c


class _Tmp:
    def __init__(self, st):
        self.st = st

    def __getitem__(self, idx):
        p, f = idx
        return self.st[p, slice(8 + f.start, 8 + f.stop)]


INVF_A = [float(np.float32(500000.0) ** np.float32(-(np.float32(i) * np.float32(2.0 / 16)))) for i in range(8)]
INVF_C = [float(np.float32(500000.0) ** np.float32(-(np.float32(i) * np.float32(2.0 / 32)))) for i in range(16)]
TWO_PI = float(2 * np.pi)


def rope_tables(c, COS, SIN, es_tmp, pf, npair, invf):
    fw = c.fw
    nf = len(invf)
    n = npair * nf
    ang = sb(c, es_tmp, "ang", [128, npair, nf], F32)
    u = sb(c, es_tmp, "u", [128, n], F32)
    ni = sb(c, es_tmp, "ni", [128, n], I32)
    nfl = sb(c, es_tmp, "nfl", [128, n], F32)
    r = sb(c, es_tmp, "r", [128, n], F32)
    for f in range(nf):
        fw.ts("dve", ang[:, :, f], pf[:, :], invf[f])
    angf = ang.v(ang.t[:, :, :].rearrange("p a b -> p (a b)"))
    for off, dst in ((0.0, SIN), (0.25, COS)):
        fw.ts("dve", u[:, :], angf, 1.0 / TWO_PI, off, ALU.mult, ALU.add)
        fw.cp("dve", ni[:, :], u[:, :])
        fw.cp("dve", nfl[:, :], ni[:, :])
        if off:
            fw.ts("dve", u[:, :], angf, float(np.pi / 2), None, ALU.add)
            fw.stt("dve", r[:, :], nfl[:, :], -TWO_PI, u[:, :], ALU.mult, ALU.add)
        else:
            fw.stt("dve", r[:, :], nfl[:, :], -TWO_PI, angf, ALU.mult, ALU.add)
        fw.ts("dve", r[:, :], r[:, :], float(np.pi), float(-np.pi), ALU.min, ALU.max)
        fw.act(dst.v(dst.t[:, :, :].rearrange("p a b -> p (a b)")), r[:, :], AF.Sin)
    return COS, SIN


def rope_apply(c, xs, o, cosv, sinv, tmps, nh, hd, half):
    fw = c.fw
    x3 = V(xs.bufs, xs.ap.rearrange("p (h d) -> p h d", h=nh))
    o3 = V(o.bufs, o.ap.rearrange("p (h d) -> p h d", h=nh))
    cb = V(cosv.bufs, cosv.ap.unsqueeze(1).to_broadcast([128, nh, half]))
    sbv = V(sinv.bufs, sinv.ap.unsqueeze(1).to_broadcast([128, nh, half]))
    t = [tmps.next() for _ in range(4)]
    tv = [V([b], b.t[:, 0:nh * half].rearrange("p (h d) -> p h d", h=nh)) for b in t]
    x1 = x3[:, :, 0:half]
    x2 = x3[:, :, half:2 * half]
    fw.tt("dve", tv[0], x1, cb, ALU.mult)
    fw.tt("pool", tv[1], x2, sbv, ALU.mult)
    fw.tt("dve", o3[:, :, 0:half], tv[0], tv[1], ALU.subtract)
    fw.tt("dve", tv[2], x2, cb, ALU.mult)
    fw.tt("pool", tv[3], x1, sbv, ALU.mult)
    fw.tt("pool", o3[:, :, half:2 * half], tv[2], tv[3], ALU.add)
    if hd > 2 * half:
        fw.cp("act", o3[:, :, 2 * half:hd], x3[:, :, 2 * half:hd])


def load_pos(c, dst_i, src_ap_iJ, nJ):
    fw = c.fw
    step = 8
    for j0 in range(0, nJ, step):
        fw.dma("sp", dst_i[:, j0:j0 + step], RO(src_ap_iJ[:, j0:j0 + step]), allow_slow_non_contiguous=True)


def mixer_a(c, li, xin, xout):
    import os
    stage = int(os.environ.get('A_STAGE', '9'))
    fw = c.fw
    nc = c.nc
    U = 2048
    NU = S // U
    GROUPS = [(0, 1), (1, 4), (2, 16)]
    nds = [DramT(nc.dram_tensor("a_nd%d" % g, [S, 520], F32, kind="Internal").ap()) for g in range(3)]
    for g, d in GROUPS:
        nblk = 16 // d
        nJ = 64 // d
        with ExitStack() as es:
            w = sb(c, es, "wa", [128, 8, 1536], BF16)
            with ExitStack() as t2:
                for s3 in range(3):
                    col0 = (s3 * 3 + g) * 512
                    load_w(c, t2, w, c.d["a_w_qkv"][:, col0:col0 + 512], D, 512, gcol=gcol(c, li, 0), n0=s3 * 512)
                fw.barrier()
            COS = sb(c, es, "cos", [128, 64, 8], F32)
            SIN = sb(c, es, "sin", [128, 64, 8], F32)
            with ExitStack() as t2:
                pi = sb(c, t2, "pi", [128, 64], I32)
                pf = sb(c, t2, "pf", [128, 64], F32)
                if d == 1:
                    load_pos(c, pi, c.d["positions"].rearrange("(J i) -> i J", i=128), 64)
                else:
                    src = c.d["positions"].rearrange("(J i r) -> i J r", i=128, r=d)
                    step = max(1, 8 // d * 1)
                    piv = pi.t[:, :].rearrange("i (J r) -> i J r", r=d)
                    for j0 in range(0, nJ, 2):
                        fw.dma("sp", V([pi], piv[:, j0:j0 + 2, :]), RO(src[:, j0:j0 + 2, :]),
                               allow_slow_non_contiguous=True)
                fw.cp("dve", pf[:, :], pi[:, :])
                rope_tables(c, COS, SIN, t2, pf, 64, INVF_A)
                fw.barrier()
            hnT = sb(c, es, "hnTu", [128, 8, U], BF16)
            xts = rot(c, es, "xt", 2, [128, 1024], F32)
            hns = rot(c, es, "hn", 2, [128, 1024], BF16)
            sts = rot(c, es, "st", 4, [128, 16], F32)
            xss = rot(c, es, "xs", 3, [128, 512], F32)
            qrs = rot(c, es, "qr", 3, [128, 512], BF16)
            qTs = rot(c, es, "qT", 2, [128, 4, 128], BF16)
            tms = rot(c, es, "tm", 8, [128, 64], F32)
            pes = rot(c, es, "pe", 3, [128, 512], BF16)
            pms = rot(c, es, "pm", 3, [128, 512], BF16)
            stg = rot(c, es, "stg", 2, [128, 520], F32)
            nbuf = d + 3
            kfree = [sb(c, es, "kT", [128, 8, 128], BF16) for _ in range(nbuf)]
            for kb in kfree:
                fw.memset("pool", kb[:, :, :], 0.0)
            vfree = [sb(c, es, "v65", [128, 8, 80], BF16) for _ in range(nbuf)]
            for vb in vfree:
                fw.memset("pool", vb[:, :, 64:65], 1.0)
            kz = sb(c, es, "kz", [128, 8, 128], BF16)
            vz = sb(c, es, "vz", [128, 8, 80], BF16)
            fw.memset("pool", kz[:, :, :], 0.0)
            fw.memset("pool", vz[:, :, :], 0.0)
            mN = sb(c, es, "mN", [128, 512], BF16)
            mF = sb(c, es, "mF", [128, 512], BF16)
            with ExitStack() as t2:
                mt = sb(c, t2, "mt", [128, 512], F32)
                fw.memset("pool", mt[:, :], 1.0)
                for hh in range(2):
                    fw.asel(mt[:, hh * 256:hh * 256 + 128], mt[:, hh * 256:hh * 256 + 128], [[-1, 128]],
                            ALU.is_ge, 0.0, 0, 1)
                    fw.asel(mt[:, hh * 256 + 128:hh * 256 + 256], mt[:, hh * 256 + 128:hh * 256 + 256], [[1, 128]],
                            ALU.is_ge, 0.0, 0, -1)
                fw.cp("pool", mN[:, :], mt[:, :])
                for hh in range(2):
                    fw.memset("pool", mt[:, hh * 256:hh * 256 + 128], 0.0)
                fw.cp("pool", mF[:, :], mt[:, :])
                fw.barrier()
            carry = {}
            ndv = nds[g].ap.rearrange("(J i r) c -> r J i c", i=128, r=d)
            for u in range(NU if stage >= 2 else 0):
                for ti in range(U // 128):
                    r0 = u * U + ti * 128
                    xt = xts.next()
                    fw.dma("sp", xt[:, :], xin.rows(r0, r0 + 128))
                    hn = hns.next()
                    norm_tile(c, xt, hn, sts.next())
                    transpose_to(c, hn, hnT, ti * 128)
                for r in range(d if stage >= 3 else 0):
                    for jl in range(nblk):
                        J = u * nblk + jl
                        pidx = J * d + r
                        banks = []
                        for s3 in range(3):
                            ps = c.ps.next()
                            banks.append(ps)
                            for kc in range(8):
                                hv = hnT.t[:, kc, :].rearrange("p (j i r) -> p r j i", i=128, r=d)[:, r, jl, :]
                                fw.mm(ps[:, :], V([hnT], hv), w[:, kc, s3 * 512:(s3 + 1) * 512], kc == 0, kc == 7)
                        sub = int(os.environ.get('A_SUB', '9'))
                        if sub < 1:
                            continue
                        psT = c.ps.next()
                        psTb = psT.t[:, :].bitcast(BF16)
                        for s3 in range(2):
                            xs = xss.next()
                            fw.cp("act", xs[:, :], banks[s3][:, :])
                            qr = qrs.next()
                            if sub >= 2:
                                rope_apply(c, xs[:, :], qr[:, :], COS[:, pidx, :], SIN[:, pidx, :], tms, 8, 64, 8)
                            else:
                                fw.cp("dve", qr[:, :], xs[:, :])
                            for ch in range(4 if sub != 11 else 0):
                                jj = s3 * 4 + ch
                                fw.tr(V([psT], psTb[:, jj * 128:(jj + 1) * 128]), qr[:, ch * 128:(ch + 1) * 128],
                                      c.identb[:, :])
                        qT = qTs.next()
                        curK = kfree.pop()
                        curV = vfree.pop()
                        if sub not in (11, 12):
                            fw.cp("act", qT[:, :, :], V([psT], psTb[:, 0:512].rearrange("p (c t) -> p c t", c=4)))
                        if sub not in (11, 12, 13):
                            kv3 = psTb[:, 512:1024].rearrange("p (c t) -> p c t", c=4)
                            k4 = curK.t[:, :, :].rearrange("p (c e) t -> p c e t", e=2)
                            fw.cp("dve", V([curK], k4[0:64, :, 0, :]), V([psT], kv3[0:64]))
                            fw.cp("act", V([curK], k4[64:128, :, 1, :]), V([psT], kv3[64:128]))
                        if sub not in (11, 12, 13, 14):
                            fw.cp("dve", curV[:, :, 0:64], V([banks[2]], banks[2].t[:, :].rearrange("p (h e) -> p h e", h=8)))
                        if J == 0:
                            prevK, prevV, mask = kz, vz, mF
                        else:
                            prevK, prevV = carry[r]
                            mask = mN
                        pO = [c.psx.next(), c.psx.next()]
                        for hp in range(4 if stage >= 4 else 0):
                            ps = c.ps.next()
                            for hh in range(2):
                                fw.mm(ps[:, hh * 256:hh * 256 + 128], prevK[:, hp * 2 + hh, :], qT[:, hp, :])
                                fw.mm(ps[:, hh * 256 + 128:hh * 256 + 256], curK[:, hp * 2 + hh, :], qT[:, hp, :])
                            if sub == 41:
                                continue
                            pe = pes.next()
                            fw.act(pe[:, :], ps[:, :], AF.Exp, scale=0.125)
                            if sub == 42:
                                continue
                            pm = pms.next()
                            fw.tt(("dve", "pool")[hp % 2], pm[:, :], pe[:, :], mask[:, :], ALU.mult)
                            if sub == 43:
                                continue
                            for hh in range(2):
                                h = hp * 2 + hh
                                ov = pO[h // 4][:, (h % 4) * 128:(h % 4) * 128 + 65]
                                fw.mm(ov, pm[:, hh * 256:hh * 256 + 128], prevV[:, h, 0:65], True, False)
                                fw.mm(ov, pm[:, hh * 256 + 128:hh * 256 + 256], curV[:, h, 0:65], False, True)
                        st = stg.next()
                        if stage < 4 or sub in (41, 42, 43, 44):
                            carry[r] = (curK, curV)
                            if J > 0:
                                kfree.append(prevK)
                                vfree.append(prevV)
                            continue
                        fw.cp("act", V([st], st.t[:, 0:260].rearrange("p (h e) -> p h e", h=4)),
                              V([pO[0]], pO[0].t[:, :].rearrange("p (h e) -> p h e", h=4)[:, :, 0:65]))
                        fw.cp("dve", V([st], st.t[:, 260:520].rearrange("p (h e) -> p h e", h=4)),
                              V([pO[1]], pO[1].t[:, :].rearrange("p (h e) -> p h e", h=4)[:, :, 0:65]))
                        if sub != 45:
                            fw.dma("pool", nds[g].view(ndv[r, J], 0, S), st[:, :])
                        if J > 0:
                            kfree.append(prevK)
                            vfree.append(prevV)
                        carry[r] = (curK, curV)
            fw.barrier()
    with ExitStack() as es:
        wo = sb(c, es, "woa", [128, 4, 1024], BF16)
        G = sb(c, es, "G", [128, 1024], F32)
        bcast(c, G[:, :], c.d["ln_gains"][li, 1])
        with ExitStack() as t2:
            load_w(c, t2, wo, c.d["a_w_o"], 512, 1024)
            fw.barrier()
        xts = rot(c, es, "xt", 3, [128, 1024], F32)
        n0s = rot(c, es, "n0", 2, [128, 520], F32)
        n1s = rot(c, es, "n1", 2, [128, 520], F32)
        n2s = rot(c, es, "n2", 2, [128, 520], F32)
        obs = rot(c, es, "ob", 2, [128, 512], BF16)
        oTs = rot(c, es, "oT", 2, [128, 4, 128], BF16)
        sts = rot(c, es, "st", 4, [128, 16], F32)
        xos = rot(c, es, "xo", 2, [128, 1024], F32)
        c.pn_tmp = rot(c, es, "pnt", 2, [128, 1024], F32)
        for ti in range(NT):
            r0 = ti * 128
            xt = xts.next()
            fw.dma("sp", xt[:, :], xin.rows(r0, r0 + 128))
            n0, n1, n2 = n0s.next(), n1s.next(), n2s.next()
            fw.dma("sp", n0[:, :], nds[0].rows(r0, r0 + 128))
            fw.dma("sp", n1[:, :], nds[1].rows(r0, r0 + 128))
            fw.dma("sp", n2[:, :], nds[2].rows(r0, r0 + 128))
            fw.tt("pool", n0[:, :], n0[:, :], n1[:, :], ALU.add)
            fw.tt("dve", n0[:, :], n0[:, :], n2[:, :], ALU.add)
            st = sts.next()
            n3 = V([n0], n0.t[:, :].rearrange("p (h e) -> p h e", h=8))
            fw.recip(st[:, 0:8], n3[:, :, 64])
            ob = obs.next()
            fw.tt("dve", V([ob], ob.t[:, :].rearrange("p (h e) -> p h e", h=8)), n3[:, :, 0:64],
                  V([st], st.t[:, 0:8].unsqueeze(2).to_broadcast([128, 8, 64])), ALU.mult)
            oT = oTs.next()
            transpose_to(c, ob, oT, 0, nchunk=4)
            psY = [c.ps.next(), c.ps.next()]
            for nh in range(2):
                for kc in range(4):
                    fw.mm(psY[nh][:, :], oT[:, kc, :], wo[:, kc, nh * 512:(nh + 1) * 512], kc == 0, kc == 3)
            postnorm_store(c, psY[0], psY[1], xt, G, xos.next(), sts.next(), xout.rows(r0, r0 + 128))
        fw.barrier()


def range_reduce_sin(c, dst, ang, off, u, ni, nfl, r):
    fw = c.fw
    fw.ts("dve", u, ang, 1.0 / TWO_PI, off, ALU.mult, ALU.add)
    fw.cp("dve", ni, u)
    fw.cp("dve", nfl, ni)
    if off:
        fw.ts("dve", u, ang, float(off * TWO_PI), None, ALU.add)
        fw.stt("dve", r, nfl, -TWO_PI, u, ALU.mult, ALU.add)
    else:
        fw.stt("dve", r, nfl, -TWO_PI, ang, ALU.mult, ALU.add)
    fw.ts("dve", r, r, float(np.pi), float(-np.pi), ALU.min, ALU.max)
    fw.act(dst, r, AF.Sin)


def mixer_c(c, li, xin, xout):
    fw = c.fw
    nc = c.nc
    TB = 512
    NB = S // TB
    H = 16
    QTd = nc.dram_tensor("c_qt", [H, 96, S], BF16, kind="Internal").ap()
    KTd = nc.dram_tensor("c_kt", [H, 96, S], BF16, kind="Internal").ap()
    Vd = nc.dram_tensor("c_v", [S, 1024], BF16, kind="Internal").ap()
    OTd = nc.dram_tensor("c_ot", [8, 128, S], BF16, kind="Internal").ap()
    QT = DramT(QTd.rearrange("h r s -> (h r) s"), blk=96)
    KT = DramT(KTd.rearrange("h r s -> (h r) s"), blk=96)
    VD = DramT(Vd)
    OT = DramT(OTd.rearrange("h r s -> (h r) s"), blk=128)
    with ExitStack() as es:
        win = sb(c, es, "c_win", [128, 8, 672], BF16)
        wkp = sb(c, es, "c_wkp", [128, 8, 96], BF16)
        wkp2 = sb(c, es, "c_wkp2", [128, 8, 96], BF16)
        wq = sb(c, es, "c_wq", [128, 3, 1536], BF16)
        wq2 = sb(c, es, "c_wq2", [128, 3, 1536], BF16)
        wkv = sb(c, es, "c_wkv", [128, 2, 2048], BF16)
        invf = sb(c, es, "c_invf", [128, 1], F32)
        gq = sb(c, es, "c_gq", [128, 3], F32)
        gkv = sb(c, es, "c_gkv", [128, 2], F32)
        fw.dma("sp", invf[:, :], RO(c.d["k_invf_c"]))
        with ExitStack() as t2:
            load_cols(c, t2, gq[:, :], c.d["c_q_norm"].rearrange("(c p) -> c p", p=128), 3)
            load_cols(c, t2, gkv[:, :], c.d["c_kv_norm"].rearrange("(c p) -> c p", p=128), 2)
            load_w(c, t2, win, c.d["c_w_in"], D, 672, gcol=gcol(c, li, 0))
            load_w(c, t2, wq, c.d["c_w_uq"], 384, 1536, gcol=gq)
            load_w(c, t2, wkv, c.d["c_w_ukv"], 256, 2048, gcol=gkv)
            fw.memset("pool", wkp[:, :, :], 0.0)
            fw.memset("pool", wkp2[:, :, :], 0.0)
            fw.memset("pool", wq2[:, :, :], 0.0)
            fw.cp("pool", wkp[:, :, 64:96], win[:, :, 640:672])
            fw.cp("pool", wkp2[:, :, 80:96], win[:, :, 640:656])
            fw.ts("pool", wkp2[:, :, 64:80], win[:, :, 656:672], -1.0)
            wq4 = wq.t[:, :, :].rearrange("p k (h e) -> p k h e", h=H)
            wq24 = wq2.t[:, :, :].rearrange("p k (h e) -> p k h e", h=H)
            for kc in range(3):
                fw.cp("pool", V([wq2], wq24[:, kc, :, 80:96]), V([wq], wq4[:, kc, :, 64:80]))
                fw.ts("pool", V([wq2], wq24[:, kc, :, 64:80]), V([wq], wq4[:, kc, :, 80:96]), -1.0)
            fw.barrier()
        xts = rot(c, es, "xt", 2, [128, 1024], F32)
        hns = rot(c, es, "hn", 2, [128, 1024], BF16)
        sts = rot(c, es, "st", 4, [128, 16], F32)
        hnTs = rot(c, es, "hnT", 2, [128, 8, TB], BF16)
        cqTs = rot(c, es, "cqT", 2, [128, 5, TB], BF16)
        cns = rot(c, es, "cn", 2, [128, 640], BF16)
        pis = rot(c, es, "pi", 1, [128, TB], I32)
        pfs = rot(c, es, "pf", 1, [128, TB], F32)
        angs = rot(c, es, "ang", 1, [128, TB], F32)
        CTs = rot(c, es, "CT", 2, [128, TB], F32)
        STs = rot(c, es, "ST", 2, [128, TB], F32)
        us = rot(c, es, "u", 1, [128, TB], F32)
        nis = rot(c, es, "ni", 1, [128, TB], I32)
        nfs = rot(c, es, "nf", 1, [128, TB], F32)
        rs = rot(c, es, "r", 1, [128, TB], F32)
        kpes = rot(c, es, "kpe", 2, [128, TB], BF16)
        t1s = rot(c, es, "t1", 3, [128, TB], F32)
        t2s = rot(c, es, "t2", 3, [128, TB], F32)
        qst = rot(c, es, "qst", 4, [128, TB], BF16)
        kst = rot(c, es, "kst", 4, [128, TB], BF16)
        vst = rot(c, es, "vst", 3, [128, 1024], BF16)
        wkv4 = wkv.t[:, :, :].rearrange("p k (h e) -> p k h e", h=H)
        for blk in range(NB):
            t0 = blk * TB
            hnT = hnTs.next()
            cqT = cqTs.next()
            pi, pf, ang = pis.next(), pfs.next(), angs.next()
            fw.dma("sp", pi[:, :], RO(c.d["positions"][t0:t0 + TB].partition_broadcast(128)))
            fw.cp("pool", pf[:, :], pi[:, :])
            fw.ts("dve", ang[:, :], pf[:, :], invf[:, 0:1])
            CT, ST = CTs.next(), STs.next()
            u, ni, nfl, r = us.next(), nis.next(), nfs.next(), rs.next()
            range_reduce_sin(c, ST[:, :], ang[:, :], 0.0, u[:, :], ni[:, :], nfl[:, :], r[:, :])
            range_reduce_sin(c, CT[:, :], ang[:, :], 0.25, u[:, :], ni[:, :], nfl[:, :], r[:, :])
            for ti in range(TB // 128):
                r0 = t0 + ti * 128
                xt = xts.next()
                fw.dma("sp", xt[:, :], xin.rows(r0, r0 + 128))
                hn = hns.next()
                st = sts.next()
                norm_tile(c, xt, hn, st)
                transpose_to(c, hn, hnT, ti * 128)
                psA, psB = c.ps.next(), c.ps.next()
                for kc in range(8):
                    fw.mm(psA[:, 0:384], hnT[:, kc, ti * 128:(ti + 1) * 128], win[:, kc, 0:384], kc == 0, kc == 7)
                for kc in range(8):
                    fw.mm(psB[:, 0:256], hnT[:, kc, ti * 128:(ti + 1) * 128], win[:, kc, 384:640], kc == 0, kc == 7)
                st2 = sts.next()
                fw.memset("pool", st2[:, 0:2], 0.0)
                fw.act(c.junk[:, 0:384], psA[:, 0:384], AF.Square, accum=st2[:, 0:1])
                fw.act(c.junk[:, 384:640], psB[:, 0:256], AF.Square, accum=st2[:, 1:2])
                fw.ts("dve", st2[:, 2:3], st2[:, 0:1], 1.0 / 384, EPS, ALU.mult, ALU.add)
                fw.ts("dve", st2[:, 3:4], st2[:, 1:2], 1.0 / 256, EPS, ALU.mult, ALU.add)
                fw.act(st2[:, 4:6], st2[:, 2:4], AF.Sqrt)
                fw.recip(st2[:, 6:8], st2[:, 4:6])
                cn = cns.next()
                fw.act(cn[:, 0:384], psA[:, 0:384], AF.Copy, scale=st2[:, 6:7])
                fw.act(cn[:, 384:640], psB[:, 0:256], AF.Copy, scale=st2[:, 7:8])
                transpose_to(c, cn, cqT, ti * 128, nchunk=5)
            psK, psK2 = c.ps.next(), c.ps.next()
            for kc in range(8):
                fw.mm(psK[0:96, :], wkp[:, kc, :], hnT[:, kc, :], kc == 0, kc == 7)
            for kc in range(8):
                fw.mm(psK2[0:96, :], wkp2[:, kc, :], hnT[:, kc, :], kc == 0, kc == 7)
            t1, t2 = t1s.next(), t2s.next()
            kpe = kpes.next()
            fw.tt("dve", t1[0:96, :], psK[0:96, :], CT[0:96, :], ALU.mult)
            fw.tt("dve", t2[0:96, :], psK2[0:96, :], ST[0:96, :], ALU.mult)
            fw.tt("pool", kpe[0:96, :], t1[0:96, :], t2[0:96, :], ALU.add)
            for h in range(H):
                psQ, psQ2, psKn = c.ps.next(), c.ps.next(), c.ps.next()
                for kc in range(3):
                    fw.mm(psQ[0:96, :], wq[:, kc, h * 96:(h + 1) * 96], cqT[:, kc, :], kc == 0, kc == 2)
                for kc in range(3):
                    fw.mm(psQ2[0:96, :], wq2[:, kc, h * 96:(h + 1) * 96], cqT[:, kc, :], kc == 0, kc == 2)
                for kc in range(2):
                    fw.mm(psKn[0:64, :], V([wkv], wkv4[:, kc, h, 0:64]), cqT[:, 3 + kc, :], kc == 0, kc == 1)
                t1, t2 = t1s.next(), t2s.next()
                qs = qst.next()
                fw.tt("dve", t1[0:96, :], psQ[0:96, :], CT[0:96, :], ALU.mult)
                fw.tt("dve", t2[0:96, :], psQ2[0:96, :], ST[0:96, :], ALU.mult)
                fw.tt("pool", qs[0:96, :], t1[0:96, :], t2[0:96, :], ALU.add)
                fw.dma("pool", QT.view(QTd[h, :, t0:t0 + TB], h * 96, (h + 1) * 96), qs[0:96, :])
                ks = kst.next()
                fw.cp("act", ks[0:64, :], psKn[0:64, :])
                fw.cp("pool", ks[64:96, :], kpe[64:96, :])
                fw.dma("pool", KT.view(KTd[h, :, t0:t0 + TB], h * 96, (h + 1) * 96), ks[0:96, :])
            for ti in range(TB // 128):
                r0 = t0 + ti * 128
                vs = vst.next()
                for hf in range(2):
                    ps = c.ps.next()
                    for kc in range(2):
                        fw.mm(V([ps], ps.t[:, :].rearrange("p (h e) -> p h e", h=8)),
                              cqT[:, 3 + kc, ti * 128:(ti + 1) * 128],
                              V([wkv], wkv4[:, kc, hf * 8:(hf + 1) * 8, 64:128]), kc == 0, kc == 1)
                    fw.cp(("act", "dve")[hf], vs[:, hf * 512:(hf + 1) * 512], ps[:, :])
                fw.dma("pool", VD.rows(r0, r0 + 128), vs[:, :])
        fw.barrier()
    with ExitStack() as es:
        qbufs = rot(c, es, "qh", 2, [128, S], BF16)
        kbufs = rot(c, es, "kh", 2, [128, S], BF16)
        vbufs = rot(c, es, "vh", 2, [128, NT, 80], BF16)
        for vb in vbufs.bufs:
            fw.memset("pool", vb[:, :, 64:65], 1.0)
        masks = sb(c, es, "cmask", [128, 4, TB], BF16)
        sel = sb(c, es, "csel", [128, 64], BF16)
        with ExitStack() as t2:
            mt = sb(c, t2, "mt", [128, TB], F32)
            for j in range(4):
                fw.memset("pool", mt[:, :], 1.0)
                fw.asel(mt[:, :], mt[:, :], [[1, TB]], ALU.is_ge, 0.0, -128 * j, -1)
                fw.cp("pool", masks[:, j, :], mt[:, :])
            fw.memset("pool", mt[:, 0:64], 0.0)
            fw.memset("pool", mt[64:96, 0:64], 1.0)
            fw.cp("pool", sel[:, :], mt[:, 0:64])
            fw.barrier()
        pes = rot(c, es, "pe", 4, [128, TB], BF16)
        pms = rot(c, es, "pm", 2, [128, TB], BF16)
        oas = rot(c, es, "oa", 2, [128, TB], BF16)
        ofs = rot(c, es, "of", 2, [128, TB], F32)
        rds = rot(c, es, "rd", 2, [128, TB], F32)
        ons = rot(c, es, "on", 3, [128, TB], BF16)
        scale = 96.0 ** -0.5
        for h in range(H):
            qh, kh, vh = qbufs.next(), kbufs.next(), vbufs.next()
            for q4 in range(4):
                cs = slice(q4 * 2048, (q4 + 1) * 2048)
                fw.dma("sp", qh[0:96, cs], QT.view(QTd[h, :, cs], h * 96, (h + 1) * 96))
                fw.dma("sp", kh[0:96, cs], KT.view(KTd[h, :, cs], h * 96, (h + 1) * 96))
            for q4 in range(4):
                fw.dma("sp", vh[:, q4 * 16:(q4 + 1) * 16, 0:64],
                       VD.view(Vd[q4 * 2048:(q4 + 1) * 2048, h * 64:(h + 1) * 64].rearrange("(t p) e -> p t e", p=128),
                               q4 * 2048, (q4 + 1) * 2048))
            for qb in range(NB):
                acc = c.psx.next()
                nk = 4 * qb + 4
                for kt in range(nk):
                    ps = c.ps.next()
                    fw.mm(ps[:, :], kh[0:96, kt * 128:(kt + 1) * 128], qh[0:96, qb * TB:(qb + 1) * TB])
                    pe = pes.next()
                    fw.act(pe[:, :], ps[:, :], AF.Exp, scale=scale)
                    if kt >= 4 * qb:
                        pm = pms.next()
                        fw.tt(("dve", "pool")[kt % 2], pm[:, :], pe[:, :], masks[:, kt - 4 * qb, :], ALU.mult)
                        pe = pm
                    fw.mm(acc[0:65, :], vh[:, kt, 0:65], pe[:, :], kt == 0, kt == nk - 1)
                oa = oas.next()
                of = ofs.next()
                fw.cp("act", oa[0:65, :], acc[0:65, :])
                fw.cp("dve", of[0:64, :], acc[0:64, :])
                psd = c.ps.next()
                fw.mm(psd[0:64, :], sel[0:65, :], oa[0:65, :])
                rd = rds.next()
                fw.recip(rd[0:64, :], psd[0:64, :])
                on = ons.next()
                fw.tt("pool", on[0:64, :], of[0:64, :], rd[0:64, :], ALU.mult)
                hp, hh = h // 2, h % 2
                fw.dma("pool", OT.view(OTd[hp, hh * 64:(hh + 1) * 64, qb * TB:(qb + 1) * TB], hp * 128, (hp + 1) * 128),
                       on[0:64, :])
        fw.barrier()
    with ExitStack() as es:
        wo = sb(c, es, "c_wo", [128, 8, 1024], BF16)
        G = sb(c, es, "G", [128, 1024], F32)
        bcast(c, G[:, :], c.d["ln_gains"][li, 1])
        with ExitStack() as t2:
            load_w(c, t2, wo, c.d["c_w_o"], 1024, 1024)
            fw.barrier()
        xts = rot(c, es, "xt", 3, [128, 1024], F32)
        oTs = rot(c, es, "oT", 3, [128, 8, 128], BF16)
        sts = rot(c, es, "st", 4, [128, 16], F32)
        xos = rot(c, es, "xo", 2, [128, 1024], F32)
        c.pn_tmp = rot(c, es, "pnt", 2, [128, 1024], F32)
        for ti in range(NT):
            r0 = ti * 128
            xt = xts.next()
            fw.dma("sp", xt[:, :], xin.rows(r0, r0 + 128))
            oT = oTs.next()
            fw.dma("sp", oT[:, :, :], OT.view(OTd[:, :, r0:r0 + 128].rearrange("h r s -> r h s"), 0, 1024))
            psY = [c.ps.next(), c.ps.next()]
            for nh in range(2):
                for kc in range(8):
                    fw.mm(psY[nh][:, :], oT[:, kc, :], wo[:, kc, nh * 512:(nh + 1) * 512], kc == 0, kc == 7)
            postnorm_store(c, psY[0], psY[1], xt, G, xos.next(), sts.next(), xout.rows(r0, r0 + 128))
        fw.barrier()


def mixer_b(c, li, xin, xout):
    fw = c.fw
    TB = 512
    NB = S // TB
    with ExitStack() as es:
        w = sb(c, es, "b_w", [128, 8, 3088], BF16)
        wo = sb(c, es, "b_wo", [128, 8, 1024], BF16)
        wg2 = sb(c, es, "b_wg2", [16, 512], F32)
        bb = sb(c, es, "b_bias", [128, 512], F32)
        onb = sb(c, es, "b_onb", [128, 256], F32)
        G = sb(c, es, "G", [128, 1024], F32)
        triU = sb(c, es, "b_triU", [128, 128], F32)
        triS = sb(c, es, "b_triS", [128, 128], F32)
        maskU = sb(c, es, "b_maskU", [128, 4, 128], F32)
        S32 = sb(c, es, "b_S32", [128, 4, 256], F32)
        Sbf = sb(c, es, "b_Sbf", [128, 4, 256], BF16)
        bcast(c, G[:, :], c.d["ln_gains"][li, 1])
        bcast(c, bb[:, :], c.d["b_gate_bias"])
        bcast(c, onb[:, :], c.d["b_o_norm"])
        fw.dma("sp", wg2[:, :], RO(c.d["b_w_gate2"]))
        fw.memset("pool", S32[:, :, :], 0.0)
        fw.memset("pool", Sbf[:, :, :], 0.0)
        fw.memset("pool", triU[:, :], -1.0 / 16)
        fw.asel(triU[:, :], triU[:, :], [[1, 128]], ALU.is_ge, 0.0, 0, -1)
        fw.memset("pool", triS[:, :], -1.0 / 16)
        fw.asel(triS[:, :], triS[:, :], [[-1, 128]], ALU.is_gt, 0.0, 0, 1)
        fw.memset("pool", maskU[:, :, :], 1.0)
        for h in range(4):
            fw.asel(maskU[:, h, :], maskU[:, h, :], [[1, 128]], ALU.is_ge, 0.0, 0, -1)
        with ExitStack() as t2:
            load_w(c, t2, w, c.d["b_w_in"], D, 3088, gcol=gcol(c, li, 0))
            load_w(c, t2, wo, c.d["b_w_o"], 1024, 1024)
            fw.barrier()
        xts = rot(c, es, "xt", 5, [128, 1024], F32)
        hns = rot(c, es, "hn", 2, [128, 1024], BF16)
        sts = rot(c, es, "st", 6, [128, 16], F32)
        hnTs = rot(c, es, "hnT", 1, [128, 8, TB], BF16)
        qkTs = rot(c, es, "qkT", 2, [128, 8, TB], BF16)
        glTs = rot(c, es, "glT", 2, [16, TB], F32)
        zs = rot(c, es, "z", 2, [128, 512], F32)
        Ls = rot(c, es, "L", 2, [128, 512], F32)
        EGs = rot(c, es, "EG", 2, [128, 4, 128], F32)
        EnGs = rot(c, es, "EnG", 2, [128, 4, 128], F32)
        EGcs = rot(c, es, "EGc", 2, [128, 512], F32)
        qds = rot(c, es, "qd", 2, [128, 4, 128], BF16)
        kis = rot(c, es, "ki", 2, [128, 4, 128], BF16)
        kes = rot(c, es, "ke", 2, [128, 512], BF16)
        vs_ = rot(c, es, "v", 2, [128, 1024], BF16)
        ats = rot(c, es, "at", 2, [128, 4, 128], BF16)
        gss = rot(c, es, "gs", 1, [128, 1024], F32)
        ons = rot(c, es, "on", 1, [128, 1024], F32)
        obs = rot(c, es, "ob", 2, [128, 1024], BF16)
        obTs = rot(c, es, "obT", 2, [128, 8, 128], BF16)
        xos = rot(c, es, "xo", 2, [128, 1024], F32)
        c.pn_tmp = rot(c, es, "pnt", 1, [128, 1024], F32)
        for blk in range(NB):
            t0 = blk * TB
            hnT = hnTs.next()
            xl = []
            for ti in range(4):
                r0 = t0 + ti * 128
                xt = xts.next()
                xl.append(xt)
                fw.dma("sp", xt[:, :], xin.rows(r0, r0 + 128))
                hn = hns.next()
                norm_tile(c, xt, hn, sts.next())
                transpose_to(c, hn, hnT, ti * 128)
            qkT = qkTs.next()
            for j in range(8):
                ps = c.ps.next()
                for kc in range(8):
                    fw.mm(ps[:, :], w[:, kc, j * 128:(j + 1) * 128], hnT[:, kc, :], kc == 0, kc == 7)
                fw.cp(("act", "dve")[j % 2], qkT[:, j, :], ps[:, :])
            glT = glTs.next()
            ps = c.ps.next()
            for kc in range(8):
                fw.mm(ps[0:16, :], w[:, kc, 3072:3088], hnT[:, kc, :], kc == 0, kc == 7)
            fw.cp("act", glT[:, :], ps[0:16, :])
            for ti in range(4):
                r0 = t0 + ti * 128
                tsl = slice(ti * 128, (ti + 1) * 128)
                psZ = c.ps.next()
                fw.mm(psZ[:, :], glT[:, tsl], wg2[:, :])
                z = zs.next()
                fw.tt("dve", z[:, :], psZ[:, :], bb[:, :], ALU.add)
                L = Ls.next()
                fw.act(z[:, :], z[:, :], AF.Exp, scale=-1.0)
                fw.act(L[:, :], z[:, :], AF.Ln, bias=1.0)
                psG = c.ps.next()
                for h in range(4):
                    fw.mm(psG[:, h * 128:(h + 1) * 128], L[:, h * 128:(h + 1) * 128], triU[:, :])
                EG, EnG = EGs.next(), EnGs.next()
                pg3 = V([psG], psG.t[:, :].rearrange("p (h t) -> p h t", h=4))
                fw.act(EG[:, :, :], pg3, AF.Exp)
                fw.act(EnG[:, :, :], pg3, AF.Exp, scale=-1.0)
                psGc = c.ps.next()
                fw.mm(psGc[:, :], triS[:, :], L[:, :])
                EGc = EGcs.next()
                fw.act(EGc[:, :], psGc[:, :], AF.Exp)
                qd, ki = qds.next(), kis.next()
                fw.stt("dve", qd[:, :, :], qkT[:, 0:4, tsl], 128.0 ** -0.5, EG[:, :, :], ALU.mult, ALU.mult)
                fw.tt("pool", ki[:, :, :], qkT[:, 4:8, tsl], EnG[:, :, :], ALU.mult)
                psK = c.ps.next()
                for kc in range(8):
                    fw.mm(psK[:, :], hnT[:, kc, tsl], w[:, kc, 512:1024], kc == 0, kc == 7)
                ke = kes.next()
                fw.tt("dve", ke[:, :], psK[:, :], EGc[:, :], ALU.mult)
                v = vs_.next()
                for hf in range(2):
                    ps = c.ps.next()
                    for kc in range(8):
                        fw.mm(ps[:, :], hnT[:, kc, tsl], w[:, kc, 1024 + hf * 512:1536 + hf * 512], kc == 0, kc == 7)
                    fw.cp(("act", "dve")[hf], v[:, hf * 512:(hf + 1) * 512], ps[:, :])
                psA = c.ps.next()
                for h in range(4):
                    fw.mm(psA[:, h * 128:(h + 1) * 128], ki[:, h, :], qd[:, h, :])
                at = ats.next()
                fw.tt("dve", at[:, :, :], V([psA], psA.t[:, :].rearrange("p (h t) -> p h t", h=4)), maskU[:, :, :],
                      ALU.mult)
                pO = [c.psx.next(), c.psx.next()]
                for h in range(4):
                    ov = pO[h // 2][:, (h % 2) * 256:(h % 2 + 1) * 256]
                    fw.mm(ov, at[:, h, :], v[:, h * 256:(h + 1) * 256], True, False)
                    fw.mm(ov, qd[:, h, :], Sbf[:, h, :], False, True)
                gs = gss.next()
                for hf in range(2):
                    ps = c.ps.next()
                    for kc in range(8):
                        fw.mm(ps[:, :], hnT[:, kc, tsl], w[:, kc, 2048 + hf * 512:2560 + hf * 512], kc == 0, kc == 7)
                    fw.act(gs[:, hf * 512:(hf + 1) * 512], ps[:, :], AF.Silu)
                for hf in range(2):
                    ps = c.ps.next()
                    for hh in range(2):
                        h = hf * 2 + hh
                        fw.mm(ps[:, hh * 256:(hh + 1) * 256], ke[:, h * 128:(h + 1) * 128], v[:, h * 256:(h + 1) * 256])
                    for hh in range(2):
                        h = hf * 2 + hh
                        fw.stt("dve", S32[:, h, :], S32[:, h, :], EG[:, h, 127:128], ps[:, hh * 256:(hh + 1) * 256],
                               ALU.mult, ALU.add)
                        fw.cp("act", Sbf[:, h, :], S32[:, h, :])
                st = sts.next()
                fw.memset("pool", st[:, 0:4], 0.0)
                for h in range(4):
                    fw.act(c.junk[:, h * 256:(h + 1) * 256], pO[h // 2][:, (h % 2) * 256:(h % 2 + 1) * 256], AF.Square,
                           accum=st[:, h:h + 1])
                fw.ts("dve", st[:, 4:8], st[:, 0:4], 1.0 / 256, EPS, ALU.mult, ALU.add)
                fw.act(st[:, 8:12], st[:, 4:8], AF.Sqrt)
                fw.recip(st[:, 12:16], st[:, 8:12])
                on = ons.next()
                for h in range(4):
                    fw.stt("dve", on[:, h * 256:(h + 1) * 256], pO[h // 2][:, (h % 2) * 256:(h % 2 + 1) * 256],
                           st[:, 12 + h:13 + h], onb[:, :], ALU.mult, ALU.mult)
                ob = obs.next()
                fw.tt("pool", ob[:, :], on[:, :], gs[:, :], ALU.mult)
                obT = obTs.next()
                transpose_to(c, ob, obT, 0)
                psY = [c.ps.next(), c.ps.next()]
                for nh in range(2):
                    for kc in range(8):
                        fw.mm(psY[nh][:, :], obT[:, kc, :], wo[:, kc, nh * 512:(nh + 1) * 512], kc == 0, kc == 7)
                postnorm_store(c, psY[0], psY[1], xl[ti], G, xos.next(), sts.next(), xout.rows(r0, r0 + 128))
        fw.barrier()


EM05 = float(np.exp(-0.5))


def block_mask(c, dst, kind, val, CH=32):
    fw = c.fw
    fw.memset("pool", dst, val)
    for cc in range(128 // CH):
        v = dst[:, cc * CH:(cc + 1) * CH]
        lo = cc * CH
        if kind == "IU":
            fw.asel(v, v, [[1, CH]], ALU.is_ge, 0.0, lo, -1)
            fw.asel(v, v, [[0, CH]], ALU.is_ge, 0.0, -lo, 1)
        elif kind == "SU":
            fw.asel(v, v, [[1, CH]], ALU.is_gt, 0.0, lo, -1)
            fw.asel(v, v, [[0, CH]], ALU.is_ge, 0.0, -lo, 1)
        else:
            fw.asel(v, v, [[-1, CH]], ALU.is_gt, 0.0, -lo, 1)
            fw.asel(v, v, [[0, CH]], ALU.is_ge, 0.0, lo + CH - 1, -1)


def chunk_ind(c, dst, val, CH=32):
    fw = c.fw
    fw.memset("pool", dst, val)
    for cc in range(128 // CH):
        v = dst[:, cc:cc + 1]
        fw.asel(v, v, [[0, 1]], ALU.is_ge, 0.0, -cc * CH, 1)
        fw.asel(v, v, [[0, 1]], ALU.is_ge, 0.0, cc * CH + CH - 1, -1)


def mixer_d(c, li, xin, xout):
    import os
    fw = c.fw
    nc = c.nc
    H = 16
    NC = 4
    names = ["At", "Rt", "Bh", "Kh", "Bt", "Kt", "Vv", "BON", "GATE"]
    dd = {n: DramT(nc.dram_tensor("d_" + n, [S, 1024], BF16, kind="Internal").ap()) for n in names}
    GLd_ap = nc.dram_tensor("d_GL", [NT * 64, 64], F32, kind="Internal").ap()
    GLd = DramT(GLd_ap, blk=64)
    Yd = DramT(nc.dram_tensor("d_Y", [S, 1024], F32, kind="Internal").ap())
    dstage = int(os.environ.get("D_STAGE", "9"))
    with ExitStack() as es:
        Wr = sb(c, es, "d_Wr", [128, 8, 1024], BF16)
        Wk = sb(c, es, "d_Wk", [128, 8, 1024], BF16)
        Wv = sb(c, es, "d_Wv", [128, 8, 1024], BF16)
        w1 = sb(c, es, "d_w1", [128, 8, 64], BF16)
        a1 = sb(c, es, "d_a1", [128, 8, 64], BF16)
        g1 = sb(c, es, "d_g1", [128, 8, 128], BF16)
        w2 = sb(c, es, "d_w2", [64, 1, 1024], BF16)
        a2 = sb(c, es, "d_a2", [64, 1, 1024], BF16)
        g2 = sb(c, es, "d_g2", [128, 1, 1024], BF16)
        mixc = sb(c, es, "d_mix", [128, 48], F32)
        bc = {}
        for n in ("d_w0", "d_a0", "d_k_k", "d_k_a"):
            bc[n] = sb(c, es, n + "b", [128, 1024], F32)
            bcast(c, bc[n][:, :], c.d[n])
        rkb = sb(c, es, "d_rkb", [128, 1024], F32)
        bcast(c, rkb[:, :], c.d["d_r_k"].rearrange("h e -> (h e)"))
        omka = sb(c, es, "d_omka", [128, 1024], F32)
        fw.ts("pool", omka[:, :], bc["d_k_a"][:, :], -1.0, 1.0, ALU.mult, ALU.add)
        triC = sb(c, es, "d_triC", [128, 128], F32)
        triD = sb(c, es, "d_triD", [128, 128], F32)
        indC = sb(c, es, "d_indC", [128, NC], F32)
        block_mask(c, triC[:, :], "IU", -EM05)
        block_mask(c, triD[:, :], "SL", -EM05)
        chunk_ind(c, indC[:, :], -EM05)
        hz = sb(c, es, "d_hz", [128, 8, 128], BF16)
        fw.memset("pool", hz[:, :, :], 0.0)
        with ExitStack() as t2:
            load_cols(c, t2, mixc[:, :], c.d["d_mix"].rearrange("i (c p) -> (i c) p", p=128), 48)
            gc = gcol(c, li, 0)
            load_w(c, t2, Wr, c.d["d_w_rkv"][0], D, 1024, gcol=gc)
            load_w(c, t2, Wk, c.d["d_w_rkv"][1], D, 1024, gcol=gc)
            load_w(c, t2, Wv, c.d["d_w_rkv"][2], D, 1024, gcol=gc)
            load_w(c, t2, w1, c.d["d_w1"], D, 64, gcol=gc)
            load_w(c, t2, a1, c.d["d_a1"], D, 64, gcol=gc)
            load_w(c, t2, g1, c.d["d_g1"], D, 128, gcol=gc)
            for dst, src, kk_ in ((w2, "d_w2", 64), (a2, "d_a2", 64), (g2, "d_g2", 128)):
                st_ = sb(c, t2, "wst", [128, 1024], F32)
                fw.dma("sp", st_[0:kk_, :], RO(c.d[src]))
                fw.cp("dve", dst[0:kk_, 0, :], st_[0:kk_, :])
            fw.barrier()
        xts = rot(c, es, "xt", 2, [128, 1024], F32)
        hns = rot(c, es, "hn", 2, [128, 1024], BF16)
        sts = rot(c, es, "st", 4, [128, 16], F32)
        hnTs = rot(c, es, "hnT", 2, [128, 8, 128], BF16)
        DTs = rot(c, es, "DT", 1, [128, 8, 128], BF16)
        Xs = rot(c, es, "X", 7, [128, 8, 128], BF16)
        smT = rot(c, es, "smT", 4, [128, 128], BF16)
        f32s = rot(c, es, "f", 9, [128, 1024], F32)
        b16s = rot(c, es, "o", 12, [128, 1024], BF16)
        s16 = rot(c, es, "s16", 4, [128, 64], F32)
        glts = rot(c, es, "glt", 2, [64, 64], F32)
        prev = hz

        def v3(buf):
            return V([buf], buf.t[:, :].rearrange("p (h e) -> p h e", h=H))

        def b3(vw):
            return V(vw.bufs, vw.ap.unsqueeze(2).to_broadcast([128, H, 64]))

        for ti in range(NT if dstage >= 1 else 0):
            r0 = ti * 128
            xt = xts.next()
            fw.dma("sp", xt[:, :], xin.rows(r0, r0 + 128))
            hn = hns.next()
            norm_tile(c, xt, hn, sts.next())
            cur = hnTs.next()
            transpose_to(c, hn, cur, 0)
            DT = DTs.next()
            fw.tt("pool", DT[:, :, 0:1], prev[:, :, 127:128], cur[:, :, 0:1], ALU.subtract)
            fw.tt("pool", DT[:, :, 1:128], cur[:, :, 0:127], cur[:, :, 1:128], ALU.subtract)
            X = []
            for i in range(6):
                Xi = Xs.next()
                for kc in range(8):
                    mcol = mixc[:, i * 8 + kc:i * 8 + kc + 1]
                    if (i * 8 + kc) % 3 != 2:
                        fw.stt("dve", Xi[:, kc, :], DT[:, kc, :], mcol, cur[:, kc, :], ALU.mult, ALU.add)
                    else:
                        fw.act(Xi[:, kc, :], DT[:, kc, :], AF.Copy, scale=mcol)
                        fw.tt("pool", Xi[:, kc, :], Xi[:, kc, :], cur[:, kc, :], ALU.add)
                X.append(Xi)
            Xr, Xw, Xk, Xv, Xa, Xg = X
            prev = cur

            def proj2(Xi, W, nh):
                ps = c.ps.next()
                for kc in range(8):
                    fw.mm(ps[:, :], Xi[:, kc, :], W[:, kc, nh * 512:(nh + 1) * 512], kc == 0, kc == 7)
                return ps

            def low(Xi, Wl, M, func):
                ps = c.ps.next()
                for kc in range(8):
                    fw.mm(ps[0:M, 0:128], Wl[:, kc, :], Xi[:, kc, :], kc == 0, kc == 7)
                o = smT.next()
                fw.act(o[0:M, :], ps[0:M, 0:128], func)
                return o

            twT = low(Xw, w1, 64, AF.Tanh)
            aaT = low(Xa, a1, 64, AF.Copy)
            ggT = low(Xg, g1, 128, AF.Sigmoid)
            sig = f32s.next()
            av = f32s.next()
            for nh in range(2):
                cs_ = slice(nh * 512, (nh + 1) * 512)
                ps = c.ps.next()
                fw.mm(ps[:, :], twT[0:64, :], w2[0:64, 0, cs_])
                fw.tt("dve", sig[:, cs_], ps[:, :], bc["d_w0"][:, cs_], ALU.add)
                ps = c.ps.next()
                fw.mm(ps[:, :], aaT[0:64, :], a2[0:64, 0, cs_])
                fw.tt("dve", av[:, cs_], ps[:, :], bc["d_a0"][:, cs_], ALU.add)
            fw.act(sig[:, :], sig[:, :], AF.Sigmoid)
            fw.act(av[:, :], av[:, :], AF.Sigmoid)
            gate = b16s.next()
            for nh in range(2):
                ps = c.ps.next()
                fw.mm(ps[:, :], ggT[:, :], g2[:, 0, nh * 512:(nh + 1) * 512])
                fw.cp("act", gate[:, nh * 512:(nh + 1) * 512], ps[:, :])
            fw.dma("pool", dd["GATE"].rows(r0, r0 + 128), gate[:, :])
            rr = f32s.next()
            kx = f32s.next()
            vv = f32s.next()
            for nh in range(2):
                cs_ = slice(nh * 512, (nh + 1) * 512)
                fw.cp("act", rr[:, cs_], proj2(Xr, Wr, nh)[:, :])
                fw.cp("act", kx[:, cs_], proj2(Xk, Wk, nh)[:, :])
                fw.cp("act", vv[:, cs_], proj2(Xv, Wv, nh)[:, :])
            Vb = b16s.next()
            fw.cp("pool", Vb[:, :], vv[:, :])
            fw.dma("pool", dd["Vv"].rows(r0, r0 + 128), Vb[:, :])
            kkx = f32s.next()
            tmp = f32s.next()
            fw.tt("dve", kkx[:, :], kx[:, :], bc["d_k_k"][:, :], ALU.mult)
            fw.tt("pool", tmp[:, :], kkx[:, :], kkx[:, :], ALU.mult)
            sm = s16.next()
            fw.red("dve", sm[:, 0:16], v3(tmp), ALU.add)
            fw.act(sm[:, 16:32], sm[:, 0:16], AF.Sqrt)
            fw.ts("dve", sm[:, 16:32], sm[:, 16:32], 1e-12, None, ALU.max)
            fw.recip(sm[:, 32:48], sm[:, 16:32])
            fw.tt("dve", v3(kkx), v3(kkx), b3(sm[:, 32:48]), ALU.mult)
            fw.tt("pool", tmp[:, :], av[:, :], bc["d_k_a"][:, :], ALU.mult)
            fw.tt("pool", tmp[:, :], tmp[:, :], omka[:, :], ALU.add)
            fw.tt("dve", kx[:, :], kx[:, :], tmp[:, :], ALU.mult)
            fw.tt("pool", tmp[:, :], rr[:, :], rkb[:, :], ALU.mult)
            fw.tt("pool", tmp[:, :], tmp[:, :], kx[:, :], ALU.mult)
            fw.red("dve", sm[:, 48:64], v3(tmp), ALU.add)
            bon = b16s.next()
            fw.tt("dve", v3(bon), v3(vv), b3(sm[:, 48:64]), ALU.mult)
            fw.dma("pool", dd["BON"].rows(r0, r0 + 128), bon[:, :])
            fw.tt("pool", av[:, :], kkx[:, :], av[:, :], ALU.mult)
            E1 = f32s.next()
            E2 = f32s.next()
            E3 = tmp
            E4 = vv
            for nh in range(2):
                cs_ = slice(nh * 512, (nh + 1) * 512)
                ps = c.ps.next()
                fw.mm(ps[:, :], triC[:, :], sig[:, cs_])
                fw.act(E1[:, cs_], ps[:, :], AF.Exp)
                fw.act(E2[:, cs_], ps[:, :], AF.Exp, scale=-1.0)
                fw.stt("dve", E3[:, cs_], sig[:, cs_], EM05, ps[:, :], ALU.mult, ALU.add)
                ps = c.ps.next()
                fw.mm(ps[:, :], triD[:, :], sig[:, cs_])
                fw.act(E4[:, cs_], ps[:, :], AF.Exp)
            fw.act(E3[:, :], E3[:, :], AF.Exp)
            outs = {}
            for n in ("At", "Rt", "Bh", "Kh", "Bt", "Kt"):
                outs[n] = b16s.next()
            fw.stt("dve", outs["At"][:, :], kkx[:, :], -1.0, E3[:, :], ALU.mult, ALU.mult)
            fw.tt("pool", outs["Rt"][:, :], rr[:, :], E1[:, :], ALU.mult)
            fw.tt("dve", outs["Bh"][:, :], av[:, :], E2[:, :], ALU.mult)
            fw.tt("pool", outs["Kh"][:, :], kx[:, :], E2[:, :], ALU.mult)
            fw.tt("dve", outs["Bt"][:, :], av[:, :], E4[:, :], ALU.mult)
            fw.tt("pool", outs["Kt"][:, :], kx[:, :], E4[:, :], ALU.mult)
            for n in ("At", "Rt", "Bh", "Kh", "Bt", "Kt"):
                fw.dma("pool", dd[n].rows(r0, r0 + 128), outs[n][:, :])
            ps = c.ps.next()
            for h in range(H):
                fw.mm(ps[0:64, h * NC:(h + 1) * NC], sig[:, h * 64:(h + 1) * 64], indC[:, :])
            glt = glts.next()
            fw.act(glt[:, :], ps[0:64, 0:64], AF.Exp)
            fw.dma("pool", GLd.rows(ti * 64, ti * 64 + 64), glt[:, :])
        fw.barrier()
    with ExitStack() as es:
        mask1 = sb(c, es, "d_m1", [128, 384], F32)
        mask2 = sb(c, es, "d_m2", [128, 256], F32)
        II = sb(c, es, "d_II", [128, 256], BF16)
        CM = sb(c, es, "d_CM", [128, NC], F32)
        CMb = sb(c, es, "d_CMb", [128, NC], BF16)
        block_mask(c, mask1[:, 0:128], "SU", 1.0)
        block_mask(c, mask1[:, 128:256], "SL", 1.0)
        block_mask(c, mask1[:, 256:384], "SU", 1.0)
        block_mask(c, mask2[:, 0:128], "IU", 1.0)
        block_mask(c, mask2[:, 128:256], "IU", 1.0)
        chunk_ind(c, CM[:, :], 1.0)
        fw.cp("pool", CMb[:, :], CM[:, :])
        fw.cp("pool", II[:, 0:128], c.identb[:, :])
        fw.cp("pool", II[:, 128:256], c.identb[:, :])
        I64 = c.identf[0:64, 0:64]
        GS = 8
        lds = {n: rot(c, es, "l" + n, 2, [128, 1024], BF16) for n in ("At", "Rt", "Bh", "Kh", "Bt", "Kt", "Vv")}
        XTs = rot(c, es, "XT", 1, [64, H, 4, 128], BF16)
        GLs = rot(c, es, "GL", 2, [64, 64], F32)
        NAs = rot(c, es, "NA", GS, [128, 384], BF16)
        RBs = rot(c, es, "RB", GS, [128, 256], BF16)
        MMs = rot(c, es, "MM", 2 * GS, [128, 256], BF16)
        PPs = rot(c, es, "PP", 2 * GS, [128, 256], BF16)
        MFs = rot(c, es, "MF", GS, [128, 128], BF16)
        AWs = rot(c, es, "AW", GS, [128, 128], BF16)
        U0s = rot(c, es, "U0", GS, [128, 64], BF16)
        RcTs = rot(c, es, "RcT", GS, [64, 128], F32)
        Bms = rot(c, es, "Bm", GS, [128, NC, 64], BF16)
        Kms = rot(c, es, "Km", GS, [128, NC, 64], BF16)
        Gds = rot(c, es, "Gd", GS, [64, NC, 64], F32)
        Y0s = rot(c, es, "Y0", GS, [64, 128], F32)
        PhTs = rot(c, es, "PhT", GS, [64, NC, 64], F32)
        PsTs = rot(c, es, "PsT", GS, [64, NC, 64], F32)
        YTs = rot(c, es, "YT", GS, [64, 128], F32)
        Yts = rot(c, es, "Yt", 2, [128, 1024], F32)
        STs = [rot(c, es, "ST%d" % h, 3, [64, 64], F32) for h in range(H)]
        ST = []
        for h in range(H):
            b_ = STs[h].next()
            fw.memset("pool", b_[:, :], 0.0)
            ST.append(b_)
        for ti in range(NT if dstage >= 2 else 0):
            r0 = ti * 128
            L = {}
            for n in lds:
                L[n] = lds[n].next()
                fw.dma("sp", L[n][:, :], dd[n].rows(r0, r0 + 128))
            GL = GLs.next()
            fw.dma("sp", GL[:, :], GLd.rows(ti * 64, ti * 64 + 64))
            XT = XTs.next()
            for h2 in range(H // 2):
                ps = c.ps.next()
                psb = ps.t[:, :].bitcast(BF16)
                for hh in range(2):
                    h = h2 * 2 + hh
                    for j, n in enumerate(("At", "Rt", "Bh", "Kh")):
                        col = (hh * 4 + j) * 128
                        fw.tr(V([ps], psb[0:64, col:col + 128]), L[n][:, h * 64:(h + 1) * 64], c.identb[:, :])
                fw.cp(("act", "dve")[h2 % 2], XT[:, h2 * 2:h2 * 2 + 2, :, :],
                      V([ps], psb[0:64, :].rearrange("p (h j t) -> p h j t", h=2, j=4)))
            Yt = Yts.next()
            for g0 in range(0, H, GS):
                hs = list(range(g0, g0 + GS))
                NA, RB, MM, PP, MF, AW, U0, RcT, Bm, Km, Gd, Y0, PhT, PsT = ({} for _ in range(14))
                for h in hs:
                    AtT, RtT, BhT, KhT = (XT[:, h, j, :] for j in range(4))
                    b1, b2 = c.ps.next(), c.ps.next()
                    fw.mm(b1[:, 0:128], BhT, AtT)
                    fw.mm(b1[:, 128:256], AtT, BhT)
                    fw.mm(b1[:, 256:384], KhT, AtT)
                    fw.mm(b2[:, 0:128], BhT, RtT)
                    fw.mm(b2[:, 128:256], KhT, RtT)
                    NA[h], RB[h], MM[h] = NAs.next(), RBs.next(), MMs.next()
                    fw.tt("dve", NA[h][:, :], b1[:, 0:384], mask1[:, :], ALU.mult)
                    fw.tt("dve", RB[h][:, :], b2[:, 0:256], mask2[:, :], ALU.mult)
                    fw.tt("pool", MM[h][:, :], NA[h][:, 0:256], II[:, :], ALU.add)
                    PP[h] = NA[h]
                for lev in range(3):
                    for h in hs:
                        P_, PT_ = PP[h][:, 0:128], PP[h][:, 128:256]
                        bp = c.ps.next()
                        fw.mm(bp[:, 0:128], PT_, P_)
                        fw.mm(bp[:, 128:256], P_, PT_)
                        npp = PPs.next()
                        fw.cp("act", npp[:, :], bp[:, 0:256])
                        PP[h] = npp
                    for h in hs:
                        P_ = PP[h][:, 0:128]
                        M_, MT_ = MM[h][:, 0:128], MM[h][:, 128:256]
                        bm = c.ps.next()
                        fw.mm(bm[:, 0:128], MT_, P_)
                        fw.mm(bm[:, 128:256], P_, MT_)
                        nmm = MMs.next()
                        fw.tt("dve", nmm[:, :], bm[:, 0:256], MM[h][:, :], ALU.add)
                        MM[h] = nmm
                for h in hs:
                    bp = c.ps.next()
                    fw.mm(bp[:, 0:128], PP[h][:, 128:256], PP[h][:, 0:128])
                    npp = PPs.next()
                    fw.cp("act", npp[:, 0:128], bp[:, 0:128])
                    PP[h] = npp
                for h in hs:
                    bm = c.ps.next()
                    fw.mm(bm[:, 0:128], MM[h][:, 128:256], PP[h][:, 0:128])
                    MF[h] = MFs.next()
                    fw.tt("dve", MF[h][:, :], bm[:, 0:128], MM[h][:, 0:128], ALU.add)
                for h in hs:
                    hc = slice(h * 64, (h + 1) * 64)
                    b_ = c.ps.next()
                    fw.mm(b_[:, 0:64], MF[h][:, :], L["At"][:, hc])
                    fw.mm(b_[:, 64:128], NA[h][:, 256:384], L["Vv"][:, hc])
                    AW[h] = AWs.next()
                    fw.cp("act", AW[h][:, :], b_[:, 0:128])
                    Bm[h], Km[h], Gd[h] = Bms.next(), Kms.next(), Gds.next()
                    cmb = V([CMb], CMb.t[:, :].unsqueeze(2).to_broadcast([128, NC, 64]))
                    fw.tt("pool", Bm[h][:, :, :], V([L["Bt"]], L["Bt"].t[:, hc].unsqueeze(1).to_broadcast([128, NC, 64])),
                          cmb, ALU.mult)
                    fw.tt("pool", Km[h][:, :, :], V([L["Kt"]], L["Kt"].t[:, hc].unsqueeze(1).to_broadcast([128, NC, 64])),
                          cmb, ALU.mult)
                    fw.tt("pool", Gd[h][:, :, :],
                          V(I64.bufs, I64.ap.unsqueeze(1).to_broadcast([64, NC, 64])),
                          V([GL], GL.t[:, h * NC:(h + 1) * NC].unsqueeze(2).to_broadcast([64, NC, 64])), ALU.mult)
                for h in hs:
                    b_ = c.ps.next()
                    fw.mm(b_[:, 0:64], MF[h][:, :], AW[h][:, 64:128])
                    fw.mm(b_[0:64, 64:192], AW[h][:, 0:64], RB[h][:, 0:128])
                    U0[h], RcT[h] = U0s.next(), RcTs.next()
                    fw.cp("act", U0[h][:, :], b_[:, 0:64])
                    fw.tt("dve", RcT[h][:, :], b_[0:64, 64:192], XT[:, h, 1, :], ALU.add)
                for h in hs:
                    hc = slice(h * 64, (h + 1) * 64)
                    b1, b2 = c.ps.next(), c.ps.next()
                    fw.mm(b1[0:64, 0:128], U0[h][:, :], RB[h][:, 0:128], True, False)
                    fw.mm(b1[0:64, 0:128], L["Vv"][:, hc], RB[h][:, 128:256], False, True)
                    bmv = V([Bm[h]], Bm[h].t[:, :, :].rearrange("p c e -> p (c e)"))
                    kmv = V([Km[h]], Km[h].t[:, :, :].rearrange("p c e -> p (c e)"))
                    fw.mm(b1[0:64, 128:384], AW[h][:, 0:64], bmv)
                    fw.mm(b2[0:64, 0:256], U0[h][:, :], bmv, True, False)
                    fw.mm(b2[0:64, 0:256], L["Vv"][:, hc], kmv, False, True)
                    Y0[h], PhT[h], PsT[h] = Y0s.next(), PhTs.next(), PsTs.next()
                    fw.cp("act", Y0[h][:, :], b1[0:64, 0:128])
                    fw.tt("dve", V([PhT[h]], PhT[h].t[:, :, :].rearrange("p c e -> p (c e)")), b1[0:64, 128:384],
                          V([Gd[h]], Gd[h].t[:, :, :].rearrange("p c e -> p (c e)")), ALU.add)
                    fw.cp("act", V([PsT[h]], PsT[h].t[:, :, :].rearrange("p c e -> p (c e)")), b2[0:64, 0:256])
                yb = [c.psx.next(), c.psx.next()]
                for cc in range(NC):
                    for h in hs:
                        hl = h - g0
                        yv = yb[hl // 4][0:64, (hl % 4) * 128 + cc * 32:(hl % 4) * 128 + cc * 32 + 32]
                        fw.mm(yv, ST[h][:, :], RcT[h][:, cc * 32:(cc + 1) * 32])
                        bs = c.ps.next()
                        fw.mm(bs[0:64, 0:64], PhT[h][:, cc, :], ST[h][:, :], True, False)
                        fw.mm(bs[0:64, 0:64], PsT[h][:, cc, :], I64, False, True)
                        ns = STs[h].next()
                        fw.cp(("act", "dve")[h % 2], ns[:, :], bs[0:64, 0:64])
                        ST[h] = ns
                pt = c.ps.next()
                for h in hs:
                    hl = h - g0
                    YT = YTs.next()
                    fw.tt("dve", YT[:, :], yb[hl // 4][0:64, (hl % 4) * 128:(hl % 4) * 128 + 128], Y0[h][:, :], ALU.add)
                    fw.tr(pt[:, hl * 64:(hl + 1) * 64], YT[:, :], I64)
                fw.cp("act", Yt[:, g0 * 64:(g0 + GS) * 64], pt[:, :])
            fw.dma("pool", Yd.rows(r0, r0 + 128), Yt[:, :])
        fw.barrier()
    with ExitStack() as es:
        wo = sb(c, es, "d_wo", [128, 8, 1024], BF16)
        G = sb(c, es, "G", [128, 1024], F32)
        lw_ = sb(c, es, "d_lnw", [128, 1024], F32)
        lb_ = sb(c, es, "d_lnb", [128, 1024], F32)
        bcast(c, G[:, :], c.d["ln_gains"][li, 1])
        bcast(c, lw_[:, :], c.d["d_lnx_w"])
        bcast(c, lb_[:, :], c.d["d_lnx_b"])
        with ExitStack() as t2:
            load_w(c, t2, wo, c.d["d_w_o"], 1024, 1024)
            fw.barrier()
        xts = rot(c, es, "xt", 3, [128, 1024], F32)
        ys = rot(c, es, "y", 2, [128, 1024], F32)
        sqs = rot(c, es, "sq", 2, [128, 1024], F32)
        bons = rot(c, es, "bon", 2, [128, 1024], BF16)
        gts = rot(c, es, "gt", 2, [128, 1024], BF16)
        obs = rot(c, es, "ob", 2, [128, 1024], BF16)
        obTs = rot(c, es, "obT", 2, [128, 8, 128], BF16)
        sts = rot(c, es, "st", 4, [128, 16], F32)
        sms = rot(c, es, "sm", 2, [128, 96], F32)
        xos = rot(c, es, "xo", 2, [128, 1024], F32)
        c.pn_tmp = rot(c, es, "pnt", 2, [128, 1024], F32)

        def v3(buf):
            return V([buf], buf.t[:, :].rearrange("p (h e) -> p h e", h=H))

        def b3(vw):
            return V(vw.bufs, vw.ap.unsqueeze(2).to_broadcast([128, H, 64]))

        for ti in range(NT if dstage >= 3 else 0):
            r0 = ti * 128
            xt, y, bon, gt = xts.next(), ys.next(), bons.next(), gts.next()
            fw.dma("sp", xt[:, :], xin.rows(r0, r0 + 128))
            fw.dma("sp", y[:, :], Yd.rows(r0, r0 + 128))
            fw.dma("sp", bon[:, :], dd["BON"].rows(r0, r0 + 128))
            fw.dma("sp", gt[:, :], dd["GATE"].rows(r0, r0 + 128))
            sm = sms.next()
            sq = sqs.next()
            fw.red("dve", sm[:, 0:16], v3(y), ALU.add)
            fw.tt("pool", sq[:, :], y[:, :], y[:, :], ALU.mult)
            fw.red("dve", sm[:, 16:32], v3(sq), ALU.add)
            fw.ts("dve", sm[:, 32:48], sm[:, 0:16], 1.0 / 64)
            fw.tt("dve", sm[:, 48:64], sm[:, 32:48], sm[:, 32:48], ALU.mult)
            fw.stt("dve", sm[:, 64:80], sm[:, 16:32], 1.0 / 64, sm[:, 48:64], ALU.mult, ALU.subtract)
            fw.ts("dve", sm[:, 64:80], sm[:, 64:80], 64e-5, None, ALU.add)
            fw.act(sm[:, 64:80], sm[:, 64:80], AF.Sqrt)
            fw.recip(sm[:, 80:96], sm[:, 64:80])
            fw.tt("dve", v3(y), v3(y), b3(sm[:, 32:48]), ALU.subtract)
            fw.tt("dve", v3(y), v3(y), b3(sm[:, 80:96]), ALU.mult)
            fw.tt("pool", y[:, :], y[:, :], lw_[:, :], ALU.mult)
            fw.tt("pool", y[:, :], y[:, :], lb_[:, :], ALU.add)
            fw.tt("dve", y[:, :], y[:, :], bon[:, :], ALU.add)
            ob = obs.next()
            fw.tt("pool", ob[:, :], y[:, :], gt[:, :], ALU.mult)
            obT = obTs.next()
            transpose_to(c, ob, obT, 0)
            psY = [c.ps.next(), c.ps.next()]
            for nh in range(2):
                for kc in range(8):
                    fw.mm(psY[nh][:, :], obT[:, kc, :], wo[:, kc, nh * 512:(nh + 1) * 512], kc == 0, kc == 7)
            postnorm_store(c, psY[0], psY[1], xt, G, xos.next(), sts.next(), xout.rows(r0, r0 + 128))
        fw.barrier()


MIXERS = {0: mixer_a, 1: mixer_b, 2: mixer_c, 3: mixer_d}

_CACHE = {}


def run(inputs, subs=None, cores=8):
    key = tuple(subs) if subs is not None else None
    if key not in _CACHE:
        _CACHE[key] = build(subs)
    nc, c = _CACHE[key]
    in_maps = []
    for ci in range(cores):
        b = ci % 4
        m = {}
        hc = host_consts()
        for name, shape, dt in IN_SPECS:
            a = np.asarray(hc[name] if name in hc else inputs[name])
            if name in ("x", "mem", "positions"):
                a = a[b]
            a = np.ascontiguousarray(a).reshape(shape)
            m[name] = a
        in_maps.append(m)
    res = run_bass_kernel_spmd(nc, in_maps, core_ids=list(range(cores)))
    return [r["out"] for r in res.results]


def kernel(**inputs):
    outs = run(inputs)
    return np.stack(outs[0:4], axis=0).astype(np.float32)
```

```python
import numpy as np
from contextlib import ExitStack
import concourse.bass as bass
import concourse.mybir as mybir
from concourse.bass_utils import run_bass_kernel_spmd

F32 = mybir.dt.float32
BF16 = mybir.dt.bfloat16
I32 = mybir.dt.int32
AF = mybir.ActivationFunctionType
ALU = mybir.AluOpType
AX = mybir.AxisListType

D = 1024
S = 8192
NT = S // 128
EPS = 1e-6
MEM = 256
DFF = 2816


class Buf:
    __slots__ = ("t", "w", "r", "excl")

    def __init__(self, t, excl=False):
        self.t = t
        self.w = None
        self.r = {}
        self.excl = excl

    def __getitem__(self, idx):
        return V([self], self.t[idx])

    def v(self, ap):
        return V([self], ap)


class V:
    __slots__ = ("bufs", "ap")

    def __init__(self, bufs, ap):
        self.bufs = bufs
        self.ap = ap

    def __getitem__(self, idx):
        return V(self.bufs, self.ap[idx])


class DramT:
    def __init__(self, ap, blk=128, tracked=True):
        self.ap = ap
        self.blk = blk
        n = (ap.shape[0] + blk - 1) // blk
        self.blocks = [Buf(None) for _ in range(n)] if tracked else None

    def rows(self, r0, r1, cols=None):
        ap = self.ap[r0:r1] if cols is None else self.ap[r0:r1, cols[0]:cols[1]]
        if self.blocks is None:
            return V([], ap)
        return V(self.blocks[r0 // self.blk:(r1 - 1) // self.blk + 1], ap)

    def view(self, ap, r0, r1):
        if self.blocks is None:
            return V([], ap)
        return V(self.blocks[r0 // self.blk:(r1 - 1) // self.blk + 1], ap)


def RO(ap):
    return V([], ap)


COMPUTE = ("pe", "act", "dve", "pool")


class FW:
    def __init__(self, nc, es, n_dma=32):
        self.nc = nc
        self.engs = {"pe": nc.tensor, "act": nc.scalar, "dve": nc.vector, "pool": nc.gpsimd, "sp": nc.sync}
        self.sem = {k: es.enter_context(nc.semaphore("s_" + k)) for k in COMPUTE}
        self.cnt = {k: 0 for k in COMPUTE}
        self.waited = {k: {} for k in self.engs}
        self.dsem = [es.enter_context(nc.semaphore("d%d" % i)) for i in range(n_dma)]
        self.dcnt = [0] * n_dma
        self.dnext = 0
        self.nd = n_dma
        self.ninst = 0

    def _wait(self, eng, key, val):
        if eng == "pe" and key == "pe":
            return
        w = self.waited[eng]
        if w.get(key, 0) >= val:
            return
        sem = self.sem[key] if isinstance(key, str) else self.dsem[key]
        self.engs[eng].wait_ge(sem, val)
        w[key] = val
        self.ninst += 1

    def _deps(self, eng, reads, writes):
        for v in reads:
            for b in v.bufs:
                if b.w is not None:
                    self._wait(eng, b.w[0], b.w[1])
                if b.excl:
                    for k, val in b.r.items():
                        if k != eng:
                            self._wait(eng, k, val)
        for v in writes:
            for b in v.bufs:
                if b.w is not None:
                    self._wait(eng, b.w[0], b.w[1])
                for k, val in b.r.items():
                    self._wait(eng, k, val)

    def _done(self, key, val, reads, writes):
        for v in reads:
            for b in v.bufs:
                b.r[key] = val
        for v in writes:
            for b in v.bufs:
                b.w = (key, val)
                b.r = {}

    def op(self, eng, fn, reads, writes):
        self._deps(eng, reads, writes)
        ins = fn()
        self.cnt[eng] += 1
        ins.then_inc(self.sem[eng], 1)
        self._done(eng, self.cnt[eng], reads, writes)
        self.ninst += 1

    def dma(self, q, out, in_, **kw):
        s = self.dnext
        self.dnext = (s + 1) % self.nd
        if self.dcnt[s]:
            self._wait(q, s, self.dcnt[s])
        self._deps(q, [in_], [out])
        ins = self.engs[q].dma_start(out=out.ap, in_=in_.ap, **kw)
        self.dcnt[s] += 16
        ins.then_inc(self.dsem[s], 16)
        self._done(s, self.dcnt[s], [in_], [out])
        self.ninst += 1

    def barrier(self):
        for eng in self.engs:
            for k in COMPUTE:
                if self.cnt[k]:
                    self._wait(eng, k, self.cnt[k])
            for s in range(self.nd):
                if self.dcnt[s]:
                    self._wait(eng, s, self.dcnt[s])

    def mm(self, out, lhsT, rhs, start=True, stop=True):
        self.op("pe", lambda: self.nc.tensor.matmul(out.ap, lhsT.ap, rhs.ap, start=start, stop=stop),
                [lhsT, rhs], [out])

    def tr(self, out, in_, ident):
        self.op("pe", lambda: self.nc.tensor.transpose(out.ap, in_.ap, ident.ap), [in_, ident], [out])

    def act(self, out, in_, func, bias=None, scale=None, accum=None):
        kw = {}
        reads = [in_]
        writes = [out]
        if bias is not None:
            if isinstance(bias, V):
                kw["bias"] = bias.ap
                reads.append(bias)
            else:
                kw["bias"] = bias
        if scale is not None:
            if isinstance(scale, V):
                kw["scale"] = scale.ap
                reads.append(scale)
            else:
                kw["scale"] = scale
        if accum is not None:
            kw["accum_out"] = accum.ap
            writes.append(accum)
        self.op("act", lambda: self.nc.scalar.activation(out=out.ap, in_=in_.ap, func=func, **kw), reads, writes)

    def _e(self, eng):
        return self.engs[eng]

    def tt(self, eng, out, a, b, op):
        self.op(eng, lambda: self._e(eng).tensor_tensor(out.ap, a.ap, b.ap, op), [a, b], [out])

    def ts(self, eng, out, a, s1, s2=None, op0=ALU.mult, op1=None, accum=None):
        reads = [a]
        writes = [out]
        a1 = s1
        a2 = s2
        if isinstance(s1, V):
            reads.append(s1)
            a1 = s1.ap
        if isinstance(s2, V):
            reads.append(s2)
            a2 = s2.ap
        kw = {}
        if accum is not None:
            kw["accum_out"] = accum.ap
            writes.append(accum)
        if op1 is None:
            self.op(eng, lambda: self._e(eng).tensor_scalar(out.ap, a.ap, a1, None, op0, **kw), reads, writes)
        else:
            self.op(eng, lambda: self._e(eng).tensor_scalar(out.ap, a.ap, a1, a2, op0, op1, **kw), reads, writes)

    def stt(self, eng, out, a, s, b, op0, op1):
        reads = [a, b]
        sa = s
        if isinstance(s, V):
            reads.append(s)
            sa = s.ap
        self.op(eng, lambda: self._e(eng).scalar_tensor_tensor(out.ap, a.ap, sa, b.ap, op0, op1), reads, [out])

    def cp(self, eng, out, in_):
        if eng == "act":
            self.act(out, in_, AF.Copy)
        else:
            self.op(eng, lambda: self._e(eng).tensor_copy(out.ap, in_.ap), [in_], [out])

    def recip(self, out, in_):
        self.op("dve", lambda: self.nc.vector.reciprocal(out.ap, in_.ap), [in_], [out])

    def red(self, eng, out, in_, op, axis=AX.X):
        self.op(eng, lambda: self._e(eng).tensor_reduce(out.ap, in_.ap, axis, op), [in_], [out])

    def memset(self, eng, out, val):
        self.op(eng, lambda: self._e(eng).memset(out.ap, val), [], [out])

    def asel(self, out, in_, pattern, cmp, fill, base, cm):
        self.op("pool", lambda: self.nc.gpsimd.affine_select(out=out.ap, in_=in_.ap, pattern=pattern,
                                                             compare_op=cmp, fill=fill, base=base,
                                                             channel_multiplier=cm), [in_], [out])


class Ctx:
    pass


def sb(c, es, name, shape, dt=F32):
    c.uid += 1
    return Buf(es.enter_context(c.nc.sbuf_tensor("%s_%d" % (name, c.uid), shape, dt)))


class Rot:
    def __init__(self, bufs):
        self.bufs = bufs
        self.i = 0

    def next(self):
        b = self.bufs[self.i]
        self.i = (self.i + 1) % len(self.bufs)
        return b


def rot(c, es, name, n, shape, dt=F32):
    return Rot([sb(c, es, name, shape, dt) for _ in range(n)])


def load_w(c, es_tmp, dst, src_ap, K, N, gcol=None, n0=0, q="sp"):
    fw = c.fw
    CH = 1408 if N % 1408 == 0 else (1024 if N % 1024 == 0 else N)
    if CH > 2048:
        CH = N // ((N + 2047) // 2048)
        assert N % CH == 0
    stg = rot(c, es_tmp, "wstg", 3, [128, CH], F32)
    i = 0
    for kc in range(K // 128):
        for n1 in range(0, N, CH):
            st = stg.next()
            fw.dma(q, st[:, :], RO(src_ap[kc * 128:(kc + 1) * 128, n1:n1 + CH]))
            eng = ("dve", "pool", "act")[i % 3]
            i += 1
            o = dst[:, kc, n0 + n1:n0 + n1 + CH]
            if gcol is None:
                fw.cp(eng, o, st[:, :])
            elif eng == "act":
                fw.act(o, st[:, :], AF.Copy, scale=gcol[:, kc:kc + 1])
            else:
                fw.ts(eng, o, st[:, :], gcol[:, kc:kc + 1])


def load_cols(c, es_tmp, dst_v, src2d_ap, R):
    fw = c.fw
    st = sb(c, es_tmp, "lc", [128, 128], F32)
    fw.dma("sp", st[0:R, :], RO(src2d_ap))
    ps = c.ps.next()
    fw.tr(ps[:, 0:R], st[0:R, :], c.identf[0:R, 0:R])
    fw.cp("dve", dst_v, ps[:, 0:R])


def bcast(c, dst, vec_ap):
    c.fw.dma("sp", dst, RO(vec_ap.partition_broadcast(128)))


def rms_rstd(c, ss_v, n, out_v, tmp):
    fw = c.fw
    k = ss_v.ap.shape[1]
    fw.ts("dve", tmp[:, 0:k], ss_v, 1.0 / n, EPS, ALU.mult, ALU.add)
    fw.act(tmp[:, k:2 * k], tmp[:, 0:k], AF.Sqrt)
    fw.recip(out_v, tmp[:, k:2 * k])


def norm_tile(c, xt, hn, st):
    fw = c.fw
    fw.memset("pool", st[:, 0:1], 0.0)
    fw.act(c.junk[:, :], xt[:, :], AF.Square, accum=st[:, 0:1])
    rms_rstd(c, st[:, 0:1], D, st[:, 3:4], c.mk_tmp(st))
    fw.act(hn[:, :], xt[:, :], AF.Copy, scale=st[:, 3:4])


def transpose_to(c, hn, dstT, t0, nchunk=8):
    fw = c.fw
    ps = c.ps.next()
    psb = ps.v(ps.t[:, :].bitcast(BF16))
    for ch in range(nchunk):
        fw.tr(V([ps], psb.ap[:, ch * 128:(ch + 1) * 128]), hn[:, ch * 128:(ch + 1) * 128], c.identb[:, :])
    eng = c.evac_eng()
    fw.cp(eng, dstT[:, 0:nchunk, t0:t0 + 128],
          V([ps], psb.ap[:, 0:nchunk * 128].rearrange("p (c t) -> p c t", c=nchunk)))


def postnorm_store(c, psA, psB, xt, G, xo, st, dst_rows):
    fw = c.fw
    fw.memset("pool", st[:, 0:2], 0.0)
    fw.act(c.junk[:, 0:512], psA[:, :], AF.Square, accum=st[:, 0:1])
    fw.act(c.junk[:, 512:1024], psB[:, :], AF.Square, accum=st[:, 1:2])
    fw.tt("dve", st[:, 2:3], st[:, 0:1], st[:, 1:2], ALU.add)
    rms_rstd(c, st[:, 2:3], D, st[:, 3:4], c.mk_tmp(st))
    tmp = c.pn_tmp.next() if c.pn_tmp is not None else xo
    fw.stt("dve", tmp[:, 0:512], psA[:, :], st[:, 3:4], G[:, 0:512], ALU.mult, ALU.mult)
    fw.stt("dve", tmp[:, 512:1024], psB[:, :], st[:, 3:4], G[:, 512:1024], ALU.mult, ALU.mult)
    fw.tt("pool", xo[:, :], xt[:, :], tmp[:, :], ALU.add)
    fw.dma("pool", dst_rows, xo[:, :])


def phase_consts(c, es):
    fw = c.fw
    c.identf = sb(c, es, "identf", [128, 128], F32)
    c.identb = sb(c, es, "identb", [128, 128], BF16)
    fw.memset("pool", c.identf[:, :], 1.0)
    fw.asel(c.identf[:, :], c.identf[:, :], [[-1, 128]], ALU.is_equal, 0.0, 0, 1)
    fw.cp("pool", c.identb[:, :], c.identf[:, :])
    c.junk = sb(c, es, "junk", [128, 1024], BF16)
    c.gT = sb(c, es, "gT", [128, 192], F32)
    with ExitStack() as tmp:
        g2 = c.d["ln_gains"].rearrange("l s (c p) -> (l s c) p", p=128)
        load_cols(c, tmp, c.gT[:, 0:96], g2[0:96, :], 96)
        load_cols(c, tmp, c.gT[:, 96:192], g2[96:192, :], 96)
        fw.barrier()


def gcol(c, li, si):
    o = (li * 6 + si) * 8
    return c.gT[:, o:o + 8]


def phase_memkv(c, es):
    fw = c.fw
    c.mem_kT = sb(c, es, "memkT", [128, 4, 256], BF16)
    c.mem_v = sb(c, es, "memv", [128, 2, 512], BF16)
    with ExitStack() as tmp:
        w = sb(c, tmp, "wkv", [128, 8, 1024], BF16)
        gm = sb(c, tmp, "gm", [128, 8], F32)
        load_cols(c, tmp, gm[:, :], c.d["mem_norm"].rearrange("(c p) -> c p", p=128), 8)
        with ExitStack() as t2:
            load_w(c, t2, w, c.d["mem_w_kv"], D, 1024, gcol=gm)
            fw.barrier()
        hnT = sb(c, tmp, "mhnT", [128, 8, 256], BF16)
        for ti in range(2):
            xt = sb(c, tmp, "mx", [128, 1024], F32)
            hn = sb(c, tmp, "mhn", [128, 1024], BF16)
            st = sb(c, tmp, "mst", [128, 16], F32)
            fw.dma("sp", xt[:, :], RO(c.d["mem"][ti * 128:(ti + 1) * 128, :]))
            norm_tile(c, xt, hn, st)
            transpose_to(c, hn, hnT, ti * 128)
        for h in range(4):
            ps = c.ps.next()
            for kc in range(8):
                fw.mm(ps[:, 0:256], w[:, kc, h * 128:(h + 1) * 128], hnT[:, kc, :], kc == 0, kc == 7)
            fw.cp("act", c.mem_kT[:, h, :], ps[:, 0:256])
        for mc in range(2):
            ps = c.ps.next()
            for kc in range(8):
                fw.mm(ps[:, :], hnT[:, kc, mc * 128:(mc + 1) * 128], w[:, kc, 512:1024], kc == 0, kc == 7)
            fw.cp("act", c.mem_v[:, mc, :], ps[:, :])
        fw.barrier()


def phase_mem(c, li, xin, xout):
    fw = c.fw
    TB = 512
    with ExitStack() as es:
        wq = sb(c, es, "wq", [128, 8, 512], BF16)
        wo = sb(c, es, "wo", [128, 4, 1024], BF16)
        G = sb(c, es, "G", [128, 1024], F32)
        bcast(c, G[:, :], c.d["ln_gains"][li, 3])
        with ExitStack() as t2:
            load_w(c, t2, wq, c.d["mem_w_q"][li], D, 512, gcol=gcol(c, li, 2))
            load_w(c, t2, wo, c.d["mem_w_o"][li], 512, 1024)
            fw.barrier()
        xts = rot(c, es, "xt", 6, [128, 1024], F32)
        hns = rot(c, es, "hn", 2, [128, 1024], BF16)
        sts = rot(c, es, "st", 4, [128, 16], F32)
        hnTs = rot(c, es, "hnT", 2, [128, 8, TB], BF16)
        qTs = rot(c, es, "qT", 2, [128, 4, TB], BF16)
        ps_ = rot(c, es, "p", 2, [128, 4, 256], F32)
        pns = rot(c, es, "pn", 2, [128, 4, 256], BF16)
        pTs = rot(c, es, "pT", 2, [128, 8, 128], BF16)
        oTs = rot(c, es, "oT", 2, [128, 4, 128], BF16)
        xos = rot(c, es, "xo", 2, [128, 1024], F32)
        c.pn_tmp = rot(c, es, "pnt", 2, [128, 1024], F32)
        sms = rot(c, es, "sm", 4, [128, 16], F32)
        scale = 128.0 ** -0.5
        for blk in range(S // TB):
            hnT = hnTs.next()
            xl = []
            for ti in range(TB // 128):
                r0 = blk * TB + ti * 128
                xt = xts.next()
                xl.append(xt)
                fw.dma("sp", xt[:, :], xin.rows(r0, r0 + 128))
                hn = hns.next()
                norm_tile(c, xt, hn, sts.next())
                transpose_to(c, hn, hnT, ti * 128)
            qT = qTs.next()
            for h in range(4):
                ps = c.ps.next()
                for kc in range(8):
                    fw.mm(ps[:, :], wq[:, kc, h * 128:(h + 1) * 128], hnT[:, kc, :], kc == 0, kc == 7)
                fw.act(qT[:, h, :], ps[:, :], AF.Copy, scale=scale)
            for ti in range(TB // 128):
                r0 = blk * TB + ti * 128
                tsl = slice(ti * 128, (ti + 1) * 128)
                sm = sms.next()
                pss = [c.ps.next(), c.ps.next()]
                for h in range(4):
                    fw.mm(pss[h // 2][:, (h % 2) * 256:(h % 2 + 1) * 256], qT[:, h, tsl], c.mem_kT[:, h, :])
                for j in range(2):
                    fw.red("dve", sm[:, 2 * j:2 * j + 2],
                           V([pss[j]], pss[j].t[:, :].rearrange("p (h m) -> p h m", h=2)), ALU.max)
                fw.ts("dve", sm[:, 4:8], sm[:, 0:4], -1.0)
                fw.memset("pool", sm[:, 8:12], 0.0)
                p = ps_.next()
                for h in range(4):
                    fw.act(p[:, h, :], pss[h // 2][:, (h % 2) * 256:(h % 2 + 1) * 256], AF.Exp,
                           bias=sm[:, 4 + h:5 + h], accum=sm[:, 8 + h:9 + h])
                fw.recip(sm[:, 12:16], sm[:, 8:12])
                pn = pns.next()
                for h in range(4):
                    fw.ts(("dve", "pool")[h % 2], pn[:, h, :], p[:, h, :], sm[:, 12 + h:13 + h])
                psT = c.ps.next()
                psTb = psT.t[:, :].bitcast(BF16)
                for h in range(4):
                    for mc in range(2):
                        j = h * 2 + mc
                        fw.tr(V([psT], psTb[:, j * 128:(j + 1) * 128]), pn[:, h, mc * 128:(mc + 1) * 128],
                              c.identb[:, :])
                pT = pTs.next()
                fw.cp("act", pT[:, :, :], V([psT], psTb.rearrange("p (j t) -> p j t", j=8)))
                psO = c.ps.next()
                for h in range(4):
                    for mc in range(2):
                        fw.mm(psO[:, h * 128:(h + 1) * 128], c.mem_v[:, mc, h * 128:(h + 1) * 128],
                              pT[:, h * 2 + mc, :], mc == 0, mc == 1)
                oT = oTs.next()
                fw.cp("act", oT[:, :, :], V([psO], psO.t[:, :].rearrange("p (h t) -> p h t", h=4)))
                psY = [c.ps.next(), c.ps.next()]
                for nh in range(2):
                    for h in range(4):
                        fw.mm(psY[nh][:, :], oT[:, h, :], wo[:, h, nh * 512:(nh + 1) * 512], h == 0, h == 3)
                postnorm_store(c, psY[0], psY[1], xl[ti], G, xos.next(), sts.next(), xout.rows(r0, r0 + 128))
        fw.barrier()


def phase_ffn(c, li, xin, xout):
    fw = c.fw
    TB = 256
    NTI = TB // 128
    NBK = S // TB
    with ExitStack() as es:
        w1 = sb(c, es, "w1", [128, 8, 2 * DFF], BF16)
        w2 = sb(c, es, "w2", [128, 22, 1024], BF16)
        G = sb(c, es, "G", [128, 1024], F32)
        cw = sb(c, es, "cw", [128, 3, 44], F32)
        cb = sb(c, es, "cb", [128, 44], F32)
        bcast(c, G[:, :], c.d["ln_gains"][li, 5])
        with ExitStack() as t2:
            for j in range(3):
                load_cols(c, t2, cw[:, j, :], c.d["ffn_conv_w"][li, j].rearrange("(c p) -> c p", p=128), 44)
            load_cols(c, t2, cb[:, :], c.d["ffn_conv_b"][li].rearrange("(c p) -> c p", p=128), 44)
            load_w(c, t2, w1, c.d["ffn_w_in"][li], D, 2 * DFF, gcol=gcol(c, li, 4))
            load_w(c, t2, w2, c.d["ffn_w_out"][li], DFF, 1024)
            fw.barrier()
        xts = rot(c, es, "xt", 4, [128, 1024], F32)
        hns = rot(c, es, "hn", 2, [128, 1024], BF16)
        sts = rot(c, es, "st", 6, [128, 16], F32)
        hnTs = rot(c, es, "hnT", 2, [128, 8, TB + 2], BF16)
        tbs = rot(c, es, "tb", 8, [128, TB], F32)
        sgs = rot(c, es, "sg", 4, [128, TB], F32)
        aTs = rot(c, es, "aT", 1, [128, 22, TB], BF16)
        xos = rot(c, es, "xo", 2, [128, 1024], F32)
        c.pn_tmp = None
        for hb in hnTs.bufs:
            fw.memset("pool", hb[:, :, :], 0.0)

        def prologue(blk, prevT):
            hnT = hnTs.next()
            if prevT is not None:
                fw.cp("pool", hnT[:, :, 0:2], prevT[:, :, TB:TB + 2])
            xl = []
            for ti in range(NTI):
                r0 = blk * TB + ti * 128
                xt = xts.next()
                xl.append(xt)
                fw.dma("sp", xt[:, :], xin.rows(r0, r0 + 128))
                hn = hns.next()
                norm_tile(c, xt, hn, sts.next())
                transpose_to(c, hn, hnT, 2 + ti * 128)
            return hnT, xl

        nxt = prologue(0, None)
        for blk in range(NBK):
            hnT, xl = nxt
            aT = aTs.next()
            for j in range(22):
                tv = []
                for half, ch in ((0, j), (1, 22 + j)):
                    ps = c.ps.next()
                    for kc in range(8):
                        fw.mm(ps[:, 0:TB + 2], w1[:, kc, ch * 128:(ch + 1) * 128], hnT[:, kc, :], kc == 0, kc == 7)
                    t = tbs.next()
                    fw.act(t[:, :], ps[:, 2:TB + 2], AF.Identity, bias=cb[:, ch:ch + 1], scale=cw[:, 2, ch:ch + 1])
                    fw.stt("dve", t[:, :], ps[:, 0:TB], cw[:, 0, ch:ch + 1], t[:, :], ALU.mult, ALU.add)
                    fw.stt("dve", t[:, :], ps[:, 1:TB + 1], cw[:, 1, ch:ch + 1], t[:, :], ALU.mult, ALU.add)
                    tv.append(t)
                sg = sgs.next()
                fw.act(sg[:, :], tv[0][:, :], AF.Silu)
                fw.tt("pool", aT[:, j, :], sg[:, :], tv[1][:, :], ALU.mult)
            if blk + 1 < NBK:
                nxt = prologue(blk + 1, hnT)
            for ti in range(NTI):
                r0 = blk * TB + ti * 128
                psY = [c.ps.next(), c.ps.next()]
                for nh in range(2):
                    for kc in range(22):
                        fw.mm(psY[nh][:, :], aT[:, kc, ti * 128:(ti + 1) * 128], w2[:, kc, nh * 512:(nh + 1) * 512],
                              kc == 0, kc == 21)
                postnorm_store(c, psY[0], psY[1], xl[ti], G, xos.next(), sts.next(), xout.rows(r0, r0 + 128))
        fw.barrier()


IN_SPECS = [
    ("x", [S, D], F32), ("mem", [MEM, D], F32), ("positions", [S], I32),
    ("ln_gains", [4, 6, D], F32), ("mem_norm", [D], F32), ("mem_w_kv", [D, 1024], F32),
    ("mem_w_q", [4, D, 512], F32), ("mem_w_o", [4, 512, D], F32),
    ("ffn_w_in", [4, D, 2 * DFF], F32), ("ffn_conv_w", [4, 3, 2 * DFF], F32), ("ffn_conv_b", [4, 2 * DFF], F32),
    ("ffn_w_out", [4, DFF, D], F32),
    ("a_w_qkv", [D, 4608], F32), ("a_w_o", [512, D], F32),
    ("b_w_in", [D, 3088], F32), ("b_w_gate2", [16, 512], F32), ("b_gate_bias", [512], F32),
    ("b_o_norm", [256], F32), ("b_w_o", [D, D], F32),
    ("c_w_in", [D, 672], F32), ("c_q_norm", [384], F32), ("c_w_uq", [384, 1536], F32),
    ("c_kv_norm", [256], F32), ("c_w_ukv", [256, 2048], F32), ("c_w_o", [D, D], F32),
    ("d_mix", [6, D], F32), ("d_w_rkv", [3, D, D], F32), ("d_w0", [D], F32), ("d_w1", [D, 64], F32),
    ("d_w2", [64, D], F32), ("d_a0", [D], F32), ("d_a1", [D, 64], F32), ("d_a2", [64, D], F32),
    ("d_g1", [D, 128], F32), ("d_g2", [128, D], F32), ("d_k_k", [D], F32), ("d_k_a", [D], F32),
    ("d_r_k", [16, 64], F32), ("d_lnx_w", [D], F32), ("d_lnx_b", [D], F32), ("d_w_o", [D, D], F32),
    ("k_invf_c", [128, 1], F32),
]


def host_consts():
    invf = np.zeros((128, 1), np.float32)
    for i in range(32):
        invf[64 + i, 0] = INVF_C[i % 16]
    return {"k_invf_c": invf}

ALL_SUBS = [(k, li) for li in range(4) for k in ("mix", "mem", "ffn")]


def build(subs=None):
    if subs is None:
        subs = ALL_SUBS
    nc = bass.Bass("TRN2", target_bir_lowering=False)
    c = Ctx()
    c.nc = nc
    c.uid = 0
    c.d = {}
    for name, shape, dt in IN_SPECS:
        c.d[name] = nc.dram_tensor(name, shape, dt, kind="ExternalInput").ap()
    out = nc.dram_tensor("out", [S, D], F32, kind="ExternalOutput").ap()
    scr = [nc.dram_tensor("xs%d" % i, [S, D], F32, kind="Internal").ap() for i in range(2)]
    with ExitStack() as es:
        fw = FW(nc, es)
        c.fw = fw
        banks = [Buf(es.enter_context(nc.psum_tensor("ps%d" % i, [128, 512], F32)), excl=True) for i in range(8)]
        c.ps = Rot(banks[0:6])
        c.psx = Rot(banks[6:8])
        c._ev = 0

        def evac_eng():
            c._ev += 1
            return ("act", "dve")[c._ev % 2]
        c.evac_eng = evac_eng
        c.mk_tmp = _Tmp
        phase_consts(c, es)
        phase_memkv(c, es)
        cur = DramT(c.d["x"], tracked=False)
        for i, (kind, li) in enumerate(subs):
            last = i == len(subs) - 1
            dst = DramT(out) if last else DramT(scr[i % 2])
            if kind == "mem":
                phase_mem(c, li, cur, dst)
            elif kind == "ffn":
                phase_ffn(c, li, cur, dst)
            else:
                MIXERS[li](c, li, cur, dst)
            cur = dst
        fw.barrier()
    c.ninst = fw.ninst
    return nc, c


class _Tmp:
    def __init__(self, st):
        self.st = st

    def __getitem__(self, idx):
        p, f = idx
        return self.st[p, slice(8 + f.start, 8 + f.stop)]


INVF_A = [float(np.float32(500000.0) ** np.float32(-(np.float32(i) * np.float32(2.0 / 16)))) for i in range(8)]
INVF_C = [float(np.float32(500000.0) ** np.float32(-(np.float32(i) * np.float32(2.0 / 32)))) for i in range(16)]
TWO_PI = float(2 * np.pi)


def rope_tables(c, COS, SIN, es_tmp, pf, npair, invf):
    fw = c.fw
    nf = len(invf)
    n = npair * nf
    ang = sb(c, es_tmp, "ang", [128, npair, nf], F32)
    u = sb(c, es_tmp, "u", [128, n], F32)
    ni = sb(c, es_tmp, "ni", [128, n], I32)
    nfl = sb(c, es_tmp, "nfl", [128, n], F32)
    r = sb(c, es_tmp, "r", [128, n], F32)
    for f in range(nf):
        fw.ts("dve", ang[:, :, f], pf[:, :], invf[f])
    angf = ang.v(ang.t[:, :, :].rearrange("p a b -> p (a b)"))
    for off, dst in ((0.0, SIN), (0.25, COS)):
        fw.ts("dve", u[:, :], angf, 1.0 / TWO_PI, off, ALU.mult, ALU.add)
        fw.cp("dve", ni[:, :], u[:, :])
        fw.cp("dve", nfl[:, :], ni[:, :])
        if off:
            fw.ts("dve", u[:, :], angf, float(np.pi / 2), None, ALU.add)
            fw.stt("dve", r[:, :], nfl[:, :], -TWO_PI, u[:, :], ALU.mult, ALU.add)
        else:
            fw.stt("dve", r[:, :], nfl[:, :], -TWO_PI, angf, ALU.mult, ALU.add)
        fw.ts("dve", r[:, :], r[:, :], float(np.pi), float(-np.pi), ALU.min, ALU.max)
        fw.act(dst.v(dst.t[:, :, :].rearrange("p a b -> p (a b)")), r[:, :], AF.Sin)
    return COS, SIN


def rope_apply(c, xs, o, cosv, sinv, tmps, nh, hd, half):
    fw = c.fw
    x3 = V(xs.bufs, xs.ap.rearrange("p (h d) -> p h d", h=nh))
    o3 = V(o.bufs, o.ap.rearrange("p (h d) -> p h d", h=nh))
    cb = V(cosv.bufs, cosv.ap.unsqueeze(1).to_broadcast([128, nh, half]))
    sbv = V(sinv.bufs, sinv.ap.unsqueeze(1).to_broadcast([128, nh, half]))
    t = [tmps.next() for _ in range(4)]
    tv = [V([b], b.t[:, 0:nh * half].rearrange("p (h d) -> p h d", h=nh)) for b in t]
    x1 = x3[:, :, 0:half]
    x2 = x3[:, :, half:2 * half]
    fw.tt("dve", tv[0], x1, cb, ALU.mult)
    fw.tt("pool", tv[1], x2, sbv, ALU.mult)
    fw.tt("dve", o3[:, :, 0:half], tv[0], tv[1], ALU.subtract)
    fw.tt("dve", tv[2], x2, cb, ALU.mult)
    fw.tt("pool", tv[3], x1, sbv, ALU.mult)
    fw.tt("pool", o3[:, :, half:2 * half], tv[2], tv[3], ALU.add)
    if hd > 2 * half:
        fw.cp("act", o3[:, :, 2 * half:hd], x3[:, :, 2 * half:hd])


def load_pos(c, dst_i, src_ap_iJ, nJ):
    fw = c.fw
    step = 8
    for j0 in range(0, nJ, step):
        fw.dma("sp", dst_i[:, j0:j0 + step], RO(src_ap_iJ[:, j0:j0 + step]), allow_slow_non_contiguous=True)


def mixer_a(c, li, xin, xout):
    import os
    stage = int(os.environ.get('A_STAGE', '9'))
    fw = c.fw
    nc = c.nc
    U = 2048
    NU = S // U
    GROUPS = [(0, 1), (1, 4), (2, 16)]
    nds = [DramT(nc.dram_tensor("a_nd%d" % g, [S, 520], F32, kind="Internal").ap()) for g in range(3)]
    for g, d in GROUPS:
        nblk = 16 // d
        nJ = 64 // d
        with ExitStack() as es:
            w = sb(c, es, "wa", [128, 8, 1536], BF16)
            with ExitStack() as t2:
                for s3 in range(3):
                    col0 = (s3 * 3 + g) * 512
                    load_w(c, t2, w, c.d["a_w_qkv"][:, col0:col0 + 512], D, 512, gcol=gcol(c, li, 0), n0=s3 * 512)
                fw.barrier()
            COS = sb(c, es, "cos", [128, 64, 8], F32)
            SIN = sb(c, es, "sin", [128, 64, 8], F32)
            with ExitStack() as t2:
                pi = sb(c, t2, "pi", [128, 64], I32)
                pf = sb(c, t2, "pf", [128, 64], F32)
                if d == 1:
                    load_pos(c, pi, c.d["positions"].rearrange("(J i) -> i J", i=128), 64)
                else:
                    src = c.d["positions"].rearrange("(J i r) -> i J r", i=128, r=d)
                    step = max(1, 8 // d * 1)
                    piv = pi.t[:, :].rearrange("i (J r) -> i J r", r=d)
                    for j0 in range(0, nJ, 2):
                        fw.dma("sp", V([pi], piv[:, j0:j0 + 2, :]), RO(src[:, j0:j0 + 2, :]),
                               allow_slow_non_contiguous=True)
                fw.cp("dve", pf[:, :], pi[:, :])
                rope_tables(c, COS, SIN, t2, pf, 64, INVF_A)
                fw.barrier()
            hnT = sb(c, es, "hnTu", [128, 8, U], BF16)
            xts = rot(c, es, "xt", 2, [128, 1024], F32)
            hns = rot(c, es, "hn", 2, [128, 1024], BF16)
            sts = rot(c, es, "st", 4, [128, 16], F32)
            xss = rot(c, es, "xs", 3, [128, 512], F32)
            qrs = rot(c, es, "qr", 3, [128, 512], BF16)
            qTs = rot(c, es, "qT", 2, [128, 4, 128], BF16)
            tms = rot(c, es, "tm", 8, [128, 64], F32)
            pes = rot(c, es, "pe", 3, [128, 512], BF16)
            pms = rot(c, es, "pm", 3, [128, 512], BF16)
            stg = rot(c, es, "stg", 2, [128, 520], F32)
            nbuf = d + 3
            kfree = [sb(c, es, "kT", [128, 8, 128], BF16) for _ in range(nbuf)]
            for kb in kfree:
                fw.memset("pool", kb[:, :, :], 0.0)
            vfree = [sb(c, es, "v65", [128, 8, 80], BF16) for _ in range(nbuf)]
            for vb in vfree:
                fw.memset("pool", vb[:, :, 64:65], 1.0)
            kz = sb(c, es, "kz", [128, 8, 128], BF16)
            vz = sb(c, es, "vz", [128, 8, 80], BF16)
            fw.memset("pool", kz[:, :, :], 0.0)
            fw.memset("pool", vz[:, :, :], 0.0)
            mN = sb(c, es, "mN", [128, 512], BF16)
            mF = sb(c, es, "mF", [128, 512], BF16)
            with ExitStack() as t2:
                mt = sb(c, t2, "mt", [128, 512], F32)
                fw.memset("pool", mt[:, :], 1.0)
                for hh in range(2):
                    fw.asel(mt[:, hh * 256:hh * 256 + 128], mt[:, hh * 256:hh * 256 + 128], [[-1, 128]],
                            ALU.is_ge, 0.0, 0, 1)
                    fw.asel(mt[:, hh * 256 + 128:hh * 256 + 256], mt[:, hh * 256 + 128:hh * 256 + 256], [[1, 128]],
                            ALU.is_ge, 0.0, 0, -1)
                fw.cp("pool", mN[:, :], mt[:, :])
                for hh in range(2):
                    fw.memset("pool", mt[:, hh * 256:hh * 256 + 128], 0.0)
                fw.cp("pool", mF[:, :], mt[:, :])
                fw.barrier()
            carry = {}
            ndv = nds[g].ap.rearrange("(J i r) c -> r J i c", i=128, r=d)
            for u in range(NU if stage >= 2 else 0):
                for ti in range(U // 128):
                    r0 = u * U + ti * 128
                    xt = xts.next()
                    fw.dma("sp", xt[:, :], xin.rows(r0, r0 + 128))
                    hn = hns.next()
                    norm_tile(c, xt, hn, sts.next())
                    transpose_to(c, hn, hnT, ti * 128)
                for r in range(d if stage >= 3 else 0):
                    for jl in range(nblk):
                        J = u * nblk + jl
                        pidx = J * d + r
                        banks = []
                        for s3 in range(3):
                            ps = c.ps.next()
                            banks.append(ps)
                            for kc in range(8):
                                hv = hnT.t[:, kc, :].rearrange("p (j i r) -> p r j i", i=128, r=d)[:, r, jl, :]
                                fw.mm(ps[:, :], V([hnT], hv), w[:, kc, s3 * 512:(s3 + 1) * 512], kc == 0, kc == 7)
                        sub = int(os.environ.get('A_SUB', '9'))
                        if sub < 1:
                            continue
                        psT = c.ps.next()
                        psTb = psT.t[:, :].bitcast(BF16)
                        for s3 in range(2):
                            xs = xss.next()
                            fw.cp("act", xs[:, :], banks[s3][:, :])
                            qr = qrs.next()
                            if sub >= 2:
                                rope_apply(c, xs[:, :], qr[:, :], COS[:, pidx, :], SIN[:, pidx, :], tms, 8, 64, 8)
                            else:
                                fw.cp("dve", qr[:, :], xs[:, :])
                            for ch in range(4 if sub != 11 else 0):
                                jj = s3 * 4 + ch
                                fw.tr(V([psT], psTb[:, jj * 128:(jj + 1) * 128]), qr[:, ch * 128:(ch + 1) * 128],
                                      c.identb[:, :])
                        qT = qTs.next()
                        curK = kfree.pop()
                        curV = vfree.pop()
                        if sub not in (11, 12):
                            fw.cp("act", qT[:, :, :], V([psT], psTb[:, 0:512].rearrange("p (c t) -> p c t", c=4)))
                        if sub not in (11, 12, 13):
                            kv3 = psTb[:, 512:1024].rearrange("p (c t) -> p c t", c=4)
                            k4 = curK.t[:, :, :].rearrange("p (c e) t -> p c e t", e=2)
                            fw.cp("dve", V([curK], k4[0:64, :, 0, :]), V([psT], kv3[0:64]))
                            fw.cp("act", V([curK], k4[64:128, :, 1, :]), V([psT], kv3[64:128]))
                        if sub not in (11, 12, 13, 14):
                            fw.cp("dve", curV[:, :, 0:64], V([banks[2]], banks[2].t[:, :].rearrange("p (h e) -> p h e", h=8)))
                        if J == 0:
                            prevK, prevV, mask = kz, vz, mF
                        else:
                            prevK, prevV = carry[r]
                            mask = mN
                        pO = [c.psx.next(), c.psx.next()]
                        pend = []
                        for hp in range(5):
                            if hp < 4:
                                ps = c.ps.next()
                                for hh in range(2):
                                    fw.mm(ps[:, hh * 256:hh * 256 + 128], prevK[:, hp * 2 + hh, :], qT[:, hp, :])
                                    fw.mm(ps[:, hh * 256 + 128:hh * 256 + 256], curK[:, hp * 2 + hh, :], qT[:, hp, :])
                                pe = pes.next()
                                fw.act(pe[:, :], ps[:, :], AF.Exp, scale=0.125)
                                pm = pms.next()
                                fw.tt(("dve", "pool")[hp % 2], pm[:, :], pe[:, :], mask[:, :], ALU.mult)
                                pend.append((hp, pm))
                            if hp >= 1:
                                hp0, pm0 = pend.pop(0)
                                for hh in range(2):
                                    h = hp0 * 2 + hh
                                    ov = pO[h // 4][:, (h % 4) * 128:(h % 4) * 128 + 65]
                                    fw.mm(ov, pm0[:, hh * 256:hh * 256 + 128], prevV[:, h, 0:65], True, False)
                                    fw.mm(ov, pm0[:, hh * 256 + 128:hh * 256 + 256], curV[:, h, 0:65], False, True)
                        st = stg.next()
                        if stage < 4 or sub in (41, 42, 43, 44):
                            carry[r] = (curK, curV)
                            if J > 0:
                                kfree.append(prevK)
                                vfree.append(prevV)
                            continue
                        fw.cp("act", V([st], st.t[:, 0:260].rearrange("p (h e) -> p h e", h=4)),
                              V([pO[0]], pO[0].t[:, :].rearrange("p (h e) -> p h e", h=4)[:, :, 0:65]))
                        fw.cp("dve", V([st], st.t[:, 260:520].rearrange("p (h e) -> p h e", h=4)),
                              V([pO[1]], pO[1].t[:, :].rearrange("p (h e) -> p h e", h=4)[:, :, 0:65]))
                        if sub != 45:
                            fw.dma("pool", nds[g].view(ndv[r, J], 0, S), st[:, :])
                        if J > 0:
                            kfree.append(prevK)
                            vfree.append(prevV)
                        carry[r] = (curK, curV)
            fw.barrier()
    with ExitStack() as es:
        wo = sb(c, es, "woa", [128, 4, 1024], BF16)
        G = sb(c, es, "G", [128, 1024], F32)
        bcast(c, G[:, :], c.d["ln_gains"][li, 1])
        with ExitStack() as t2:
            load_w(c, t2, wo, c.d["a_w_o"], 512, 1024)
            fw.barrier()
        xts = rot(c, es, "xt", 3, [128, 1024], F32)
        n0s = rot(c, es, "n0", 2, [128, 520], F32)
        n1s = rot(c, es, "n1", 2, [128, 520], F32)
        n2s = rot(c, es, "n2", 2, [128, 520], F32)
        obs = rot(c, es, "ob", 2, [128, 512], BF16)
        oTs = rot(c, es, "oT", 2, [128, 4, 128], BF16)
        sts = rot(c, es, "st", 4, [128, 16], F32)
        xos = rot(c, es, "xo", 2, [128, 1024], F32)
        c.pn_tmp = rot(c, es, "pnt", 2, [128, 1024], F32)
        for ti in range(NT):
            r0 = ti * 128
            xt = xts.next()
            fw.dma("sp", xt[:, :], xin.rows(r0, r0 + 128))
            n0, n1, n2 = n0s.next(), n1s.next(), n2s.next()
            fw.dma("sp", n0[:, :], nds[0].rows(r0, r0 + 128))
            fw.dma("sp", n1[:, :], nds[1].rows(r0, r0 + 128))
            fw.dma("sp", n2[:, :], nds[2].rows(r0, r0 + 128))
            fw.tt("pool", n0[:, :], n0[:, :], n1[:, :], ALU.add)
            fw.tt("dve", n0[:, :], n0[:, :], n2[:, :], ALU.add)
            st = sts.next()
            n3 = V([n0], n0.t[:, :].rearrange("p (h e) -> p h e", h=8))
            fw.recip(st[:, 0:8], n3[:, :, 64])
            ob = obs.next()
            fw.tt("dve", V([ob], ob.t[:, :].rearrange("p (h e) -> p h e", h=8)), n3[:, :, 0:64],
                  V([st], st.t[:, 0:8].unsqueeze(2).to_broadcast([128, 8, 64])), ALU.mult)
            oT = oTs.next()
            transpose_to(c, ob, oT, 0, nchunk=4)
            psY = [c.ps.next(), c.ps.next()]
            for nh in range(2):
                for kc in range(4):
                    fw.mm(psY[nh][:, :], oT[:, kc, :], wo[:, kc, nh * 512:(nh + 1) * 512], kc == 0, kc == 3)
            postnorm_store(c, psY[0], psY[1], xt, G, xos.next(), sts.next(), xout.rows(r0, r0 + 128))
        fw.barrier()


def range_reduce_sin(c, dst, ang, off, u, ni, nfl, r):
    fw = c.fw
    fw.ts("dve", u, ang, 1.0 / TWO_PI, off, ALU.mult, ALU.add)
    fw.cp("dve", ni, u)
    fw.cp("dve", nfl, ni)
    if off:
        fw.ts("dve", u, ang, float(off * TWO_PI), None, ALU.add)
        fw.stt("dve", r, nfl, -TWO_PI, u, ALU.mult, ALU.add)
    else:
        fw.stt("dve", r, nfl, -TWO_PI, ang, ALU.mult, ALU.add)
    fw.ts("dve", r, r, float(np.pi), float(-np.pi), ALU.min, ALU.max)
    fw.act(dst, r, AF.Sin)


def mixer_c(c, li, xin, xout):
    fw = c.fw
    nc = c.nc
    TB = 512
    NB = S // TB
    H = 16
    QTd = nc.dram_tensor("c_qt", [H, 96, S], BF16, kind="Internal").ap()
    KTd = nc.dram_tensor("c_kt", [H, 96, S], BF16, kind="Internal").ap()
    Vd = nc.dram_tensor("c_v", [S, 1024], BF16, kind="Internal").ap()
    OTd = nc.dram_tensor("c_ot", [8, 128, S], BF16, kind="Internal").ap()
    QT = DramT(QTd.rearrange("h r s -> (h r) s"), blk=96)
    KT = DramT(KTd.rearrange("h r s -> (h r) s"), blk=96)
    VD = DramT(Vd)
    OT = DramT(OTd.rearrange("h r s -> (h r) s"), blk=128)
    with ExitStack() as es:
        win = sb(c, es, "c_win", [128, 8, 672], BF16)
        wkp = sb(c, es, "c_wkp", [128, 8, 96], BF16)
        wkp2 = sb(c, es, "c_wkp2", [128, 8, 96], BF16)
        wq = sb(c, es, "c_wq", [128, 3, 1536], BF16)
        wq2 = sb(c, es, "c_wq2", [128, 3, 1536], BF16)
        wkv = sb(c, es, "c_wkv", [128, 2, 2048], BF16)
        invf = sb(c, es, "c_invf", [128, 1], F32)
        gq = sb(c, es, "c_gq", [128, 3], F32)
        gkv = sb(c, es, "c_gkv", [128, 2], F32)
        fw.dma("sp", invf[:, :], RO(c.d["k_invf_c"]))
        with ExitStack() as t2:
            load_cols(c, t2, gq[:, :], c.d["c_q_norm"].rearrange("(c p) -> c p", p=128), 3)
            load_cols(c, t2, gkv[:, :], c.d["c_kv_norm"].rearrange("(c p) -> c p", p=128), 2)
            load_w(c, t2, win, c.d["c_w_in"], D, 672, gcol=gcol(c, li, 0))
            load_w(c, t2, wq, c.d["c_w_uq"], 384, 1536, gcol=gq)
            load_w(c, t2, wkv, c.d["c_w_ukv"], 256, 2048, gcol=gkv)
            fw.memset("pool", wkp[:, :, :], 0.0)
            fw.memset("pool", wkp2[:, :, :], 0.0)
            fw.memset("pool", wq2[:, :, :], 0.0)
            fw.cp("pool", wkp[:, :, 64:96], win[:, :, 640:672])
            fw.cp("pool", wkp2[:, :, 80:96], win[:, :, 640:656])
            fw.ts("pool", wkp2[:, :, 64:80], win[:, :, 656:672], -1.0)
            wq4 = wq.t[:, :, :].rearrange("p k (h e) -> p k h e", h=H)
            wq24 = wq2.t[:, :, :].rearrange("p k (h e) -> p k h e", h=H)
            for kc in range(3):
                fw.cp("pool", V([wq2], wq24[:, kc, :, 80:96]), V([wq], wq4[:, kc, :, 64:80]))
                fw.ts("pool", V([wq2], wq24[:, kc, :, 64:80]), V([wq], wq4[:, kc, :, 80:96]), -1.0)
            fw.barrier()
        xts = rot(c, es, "xt", 2, [128, 1024], F32)
        hns = rot(c, es, "hn", 2, [128, 1024], BF16)
        sts = rot(c, es, "st", 4, [128, 16], F32)
        hnTs = rot(c, es, "hnT", 2, [128, 8, TB], BF16)
        cqTs = rot(c, es, "cqT", 2, [128, 5, TB], BF16)
        cns = rot(c, es, "cn", 2, [128, 640], BF16)
        pis = rot(c, es, "pi", 1, [128, TB], I32)
        pfs = rot(c, es, "pf", 1, [128, TB], F32)
        angs = rot(c, es, "ang", 1, [128, TB], F32)
        CTs = rot(c, es, "CT", 2, [128, TB], F32)
        STs = rot(c, es, "ST", 2, [128, TB], F32)
        us = rot(c, es, "u", 1, [128, TB], F32)
        nis = rot(c, es, "ni", 1, [128, TB], I32)
        nfs = rot(c, es, "nf", 1, [128, TB], F32)
        rs = rot(c, es, "r", 1, [128, TB], F32)
        kpes = rot(c, es, "kpe", 2, [128, TB], BF16)
        t1s = rot(c, es, "t1", 3, [128, TB], F32)
        t2s = rot(c, es, "t2", 3, [128, TB], F32)
        qst = rot(c, es, "qst", 4, [128, TB], BF16)
        kst = rot(c, es, "kst", 4, [128, TB], BF16)
        vst = rot(c, es, "vst", 3, [128, 1024], BF16)
        wkv4 = wkv.t[:, :, :].rearrange("p k (h e) -> p k h e", h=H)
        for blk in range(NB):
            t0 = blk * TB
            hnT = hnTs.next()
            cqT = cqTs.next()
            pi, pf, ang = pis.next(), pfs.next(), angs.next()
            fw.dma("sp", pi[:, :], RO(c.d["positions"][t0:t0 + TB].partition_broadcast(128)))
            fw.cp("pool", pf[:, :], pi[:, :])
            fw.ts("dve", ang[:, :], pf[:, :], invf[:, 0:1])
            CT, ST = CTs.next(), STs.next()
            u, ni, nfl, r = us.next(), nis.next(), nfs.next(), rs.next()
            range_reduce_sin(c, ST[:, :], ang[:, :], 0.0, u[:, :], ni[:, :], nfl[:, :], r[:, :])
            range_reduce_sin(c, CT[:, :], ang[:, :], 0.25, u[:, :], ni[:, :], nfl[:, :], r[:, :])
            for ti in range(TB // 128):
                r0 = t0 + ti * 128
                xt = xts.next()
                fw.dma("sp", xt[:, :], xin.rows(r0, r0 + 128))
                hn = hns.next()
                st = sts.next()
                norm_tile(c, xt, hn, st)
                transpose_to(c, hn, hnT, ti * 128)
                psA, psB = c.ps.next(), c.ps.next()
                for kc in range(8):
                    fw.mm(psA[:, 0:384], hnT[:, kc, ti * 128:(ti + 1) * 128], win[:, kc, 0:384], kc == 0, kc == 7)
                for kc in range(8):
                    fw.mm(psB[:, 0:256], hnT[:, kc, ti * 128:(ti + 1) * 128], win[:, kc, 384:640], kc == 0, kc == 7)
                st2 = sts.next()
                fw.memset("pool", st2[:, 0:2], 0.0)
                fw.act(c.junk[:, 0:384], psA[:, 0:384], AF.Square, accum=st2[:, 0:1])
                fw.act(c.junk[:, 384:640], psB[:, 0:256], AF.Square, accum=st2[:, 1:2])
                fw.ts("dve", st2[:, 2:3], st2[:, 0:1], 1.0 / 384, EPS, ALU.mult, ALU.add)
                fw.ts("dve", st2[:, 3:4], st2[:, 1:2], 1.0 / 256, EPS, ALU.mult, ALU.add)
                fw.act(st2[:, 4:6], st2[:, 2:4], AF.Sqrt)
                fw.recip(st2[:, 6:8], st2[:, 4:6])
                cn = cns.next()
                fw.act(cn[:, 0:384], psA[:, 0:384], AF.Copy, scale=st2[:, 6:7])
                fw.act(cn[:, 384:640], psB[:, 0:256], AF.Copy, scale=st2[:, 7:8])
                transpose_to(c, cn, cqT, ti * 128, nchunk=5)
            psK, psK2 = c.ps.next(), c.ps.next()
            for kc in range(8):
                fw.mm(psK[0:96, :], wkp[:, kc, :], hnT[:, kc, :], kc == 0, kc == 7)
            for kc in range(8):
                fw.mm(psK2[0:96, :], wkp2[:, kc, :], hnT[:, kc, :], kc == 0, kc == 7)
            t1, t2 = t1s.next(), t2s.next()
            kpe = kpes.next()
            fw.tt("dve", t1[0:96, :], psK[0:96, :], CT[0:96, :], ALU.mult)
            fw.tt("dve", t2[0:96, :], psK2[0:96, :], ST[0:96, :], ALU.mult)
            fw.tt("pool", kpe[0:96, :], t1[0:96, :], t2[0:96, :], ALU.add)
            for h in range(H):
                psQ, psQ2, psKn = c.ps.next(), c.ps.next(), c.ps.next()
                for kc in range(3):
                    fw.mm(psQ[0:96, :], wq[:, kc, h * 96:(h + 1) * 96], cqT[:, kc, :], kc == 0, kc == 2)
                for kc in range(3):
                    fw.mm(psQ2[0:96, :], wq2[:, kc, h * 96:(h + 1) * 96], cqT[:, kc, :], kc == 0, kc == 2)
                for kc in range(2):
                    fw.mm(psKn[0:64, :], V([wkv], wkv4[:, kc, h, 0:64]), cqT[:, 3 + kc, :], kc == 0, kc == 1)
                t1, t2 = t1s.next(), t2s.next()
                qs = qst.next()
                fw.tt("dve", t1[0:96, :], psQ[0:96, :], CT[0:96, :], ALU.mult)
                fw.tt("dve", t2[0:96, :], psQ2[0:96, :], ST[0:96, :], ALU.mult)
                fw.tt("pool", qs[0:96, :], t1[0:96, :], t2[0:96, :], ALU.add)
                fw.dma("pool", QT.view(QTd[h, :, t0:t0 + TB], h * 96, (h + 1) * 96), qs[0:96, :])
                ks = kst.next()
                fw.cp("act", ks[0:64, :], psKn[0:64, :])
                fw.cp("pool", ks[64:96, :], kpe[64:96, :])
                fw.dma("pool", KT.view(KTd[h, :, t0:t0 + TB], h * 96, (h + 1) * 96), ks[0:96, :])
            for ti in range(TB // 128):
                r0 = t0 + ti * 128
                vs = vst.next()
                for hf in range(2):
                    ps = c.ps.next()
                    for kc in range(2):
                        fw.mm(V([ps], ps.t[:, :].rearrange("p (h e) -> p h e", h=8)),
                              cqT[:, 3 + kc, ti * 128:(ti + 1) * 128],
                              V([wkv], wkv4[:, kc, hf * 8:(hf + 1) * 8, 64:128]), kc == 0, kc == 1)
                    fw.cp(("act", "dve")[hf], vs[:, hf * 512:(hf + 1) * 512], ps[:, :])
                fw.dma("pool", VD.rows(r0, r0 + 128), vs[:, :])
        fw.barrier()
    with ExitStack() as es:
        qbufs = rot(c, es, "qh", 2, [128, S], BF16)
        kbufs = rot(c, es, "kh", 2, [128, S], BF16)
        vbufs = rot(c, es, "vh", 2, [128, NT, 80], BF16)
        for vb in vbufs.bufs:
            fw.memset("pool", vb[:, :, 64:65], 1.0)
        masks = sb(c, es, "cmask", [128, 4, TB], BF16)
        sel = sb(c, es, "csel", [128, 64], BF16)
        with ExitStack() as t2:
            mt = sb(c, t2, "mt", [128, TB], F32)
            for j in range(4):
                fw.memset("pool", mt[:, :], 1.0)
                fw.asel(mt[:, :], mt[:, :], [[1, TB]], ALU.is_ge, 0.0, -128 * j, -1)
                fw.cp("pool", masks[:, j, :], mt[:, :])
            fw.memset("pool", mt[:, 0:64], 0.0)
            fw.memset("pool", mt[64:96, 0:64], 1.0)
            fw.cp("pool", sel[:, :], mt[:, 0:64])
            fw.barrier()
        pes = rot(c, es, "pe", 5, [128, TB], BF16)
        pms = rot(c, es, "pm", 4, [128, TB], BF16)
        oas = rot(c, es, "oa", 2, [128, TB], BF16)
        ofs = rot(c, es, "of", 2, [128, TB], F32)
        rds = rot(c, es, "rd", 2, [128, TB], F32)
        ons = rot(c, es, "on", 3, [128, TB], BF16)
        scale = 96.0 ** -0.5
        for h in range(H):
            qh, kh, vh = qbufs.next(), kbufs.next(), vbufs.next()
            for q4 in range(4):
                cs = slice(q4 * 2048, (q4 + 1) * 2048)
                fw.dma("sp", qh[0:96, cs], QT.view(QTd[h, :, cs], h * 96, (h + 1) * 96))
                fw.dma("sp", kh[0:96, cs], KT.view(KTd[h, :, cs], h * 96, (h + 1) * 96))
            for q4 in range(4):
                fw.dma("sp", vh[:, q4 * 16:(q4 + 1) * 16, 0:64],
                       VD.view(Vd[q4 * 2048:(q4 + 1) * 2048, h * 64:(h + 1) * 64].rearrange("(t p) e -> p t e", p=128),
                               q4 * 2048, (q4 + 1) * 2048))
            for qb in range(NB):
                acc = c.psx.next()
                nk = 4 * qb + 4
                LOOK = 2
                pend = []
                for kt in range(nk + LOOK):
                    if kt < nk:
                        ps = c.ps.next()
                        fw.mm(ps[:, :], kh[0:96, kt * 128:(kt + 1) * 128], qh[0:96, qb * TB:(qb + 1) * TB])
                        pe = pes.next()
                        fw.act(pe[:, :], ps[:, :], AF.Exp, scale=scale)
                        if kt >= 4 * qb:
                            pm = pms.next()
                            fw.tt(("dve", "pool")[kt % 2], pm[:, :], pe[:, :], masks[:, kt - 4 * qb, :], ALU.mult)
                            pe = pm
                        pend.append((kt, pe))
                    if kt >= LOOK:
                        k0, pe0 = pend.pop(0)
                        fw.mm(acc[0:65, :], vh[:, k0, 0:65], pe0[:, :], k0 == 0, k0 == nk - 1)
                oa = oas.next()
                of = ofs.next()
                fw.cp("act", oa[0:65, :], acc[0:65, :])
                fw.cp("dve", of[0:64, :], acc[0:64, :])
                psd = c.ps.next()
                fw.mm(psd[0:64, :], sel[0:65, :], oa[0:65, :])
                rd = rds.next()
                fw.recip(rd[0:64, :], psd[0:64, :])
                on = ons.next()
                fw.tt("pool", on[0:64, :], of[0:64, :], rd[0:64, :], ALU.mult)
                hp, hh = h // 2, h % 2
                fw.dma("pool", OT.view(OTd[hp, hh * 64:(hh + 1) * 64, qb * TB:(qb + 1) * TB], hp * 128, (hp + 1) * 128),
                       on[0:64, :])
        fw.barrier()
    with ExitStack() as es:
        wo = sb(c, es, "c_wo", [128, 8, 1024], BF16)
        G = sb(c, es, "G", [128, 1024], F32)
        bcast(c, G[:, :], c.d["ln_gains"][li, 1])
        with ExitStack() as t2:
            load_w(c, t2, wo, c.d["c_w_o"], 1024, 1024)
            fw.barrier()
        xts = rot(c, es, "xt", 3, [128, 1024], F32)
        oTs = rot(c, es, "oT", 3, [128, 8, 128], BF16)
        sts = rot(c, es, "st", 4, [128, 16], F32)
        xos = rot(c, es, "xo", 2, [128, 1024], F32)
        c.pn_tmp = rot(c, es, "pnt", 2, [128, 1024], F32)
        for ti in range(NT):
            r0 = ti * 128
            xt = xts.next()
            fw.dma("sp", xt[:, :], xin.rows(r0, r0 + 128))
            oT = oTs.next()
            fw.dma("sp", oT[:, :, :], OT.view(OTd[:, :, r0:r0 + 128].rearrange("h r s -> r h s"), 0, 1024))
            psY = [c.ps.next(), c.ps.next()]
            for nh in range(2):
                for kc in range(8):
                    fw.mm(psY[nh][:, :], oT[:, kc, :], wo[:, kc, nh * 512:(nh + 1) * 512], kc == 0, kc == 7)
            postnorm_store(c, psY[0], psY[1], xt, G, xos.next(), sts.next(), xout.rows(r0, r0 + 128))
        fw.barrier()


def mixer_b(c, li, xin, xout):
    fw = c.fw
    TB = 512
    NB = S // TB
    with ExitStack() as es:
        w = sb(c, es, "b_w", [128, 8, 3088], BF16)
        wo = sb(c, es, "b_wo", [128, 8, 1024], BF16)
        wg2 = sb(c, es, "b_wg2", [16, 512], F32)
        bb = sb(c, es, "b_bias", [128, 512], F32)
        onb = sb(c, es, "b_onb", [128, 256], F32)
        G = sb(c, es, "G", [128, 1024], F32)
        triU = sb(c, es, "b_triU", [128, 128], F32)
        triS = sb(c, es, "b_triS", [128, 128], F32)
        maskU = sb(c, es, "b_maskU", [128, 4, 128], F32)
        S32 = sb(c, es, "b_S32", [128, 4, 256], F32)
        Sbf = sb(c, es, "b_Sbf", [128, 4, 256], BF16)
        bcast(c, G[:, :], c.d["ln_gains"][li, 1])
        bcast(c, bb[:, :], c.d["b_gate_bias"])
        bcast(c, onb[:, :], c.d["b_o_norm"])
        fw.dma("sp", wg2[:, :], RO(c.d["b_w_gate2"]))
        fw.memset("pool", S32[:, :, :], 0.0)
        fw.memset("pool", Sbf[:, :, :], 0.0)
        fw.memset("pool", triU[:, :], -1.0 / 16)
        fw.asel(triU[:, :], triU[:, :], [[1, 128]], ALU.is_ge, 0.0, 0, -1)
        fw.memset("pool", triS[:, :], -1.0 / 16)
        fw.asel(triS[:, :], triS[:, :], [[-1, 128]], ALU.is_gt, 0.0, 0, 1)
        fw.memset("pool", maskU[:, :, :], 1.0)
        for h in range(4):
            fw.asel(maskU[:, h, :], maskU[:, h, :], [[1, 128]], ALU.is_ge, 0.0, 0, -1)
        with ExitStack() as t2:
            load_w(c, t2, w, c.d["b_w_in"], D, 3088, gcol=gcol(c, li, 0))
            load_w(c, t2, wo, c.d["b_w_o"], 1024, 1024)
            fw.barrier()
        xts = rot(c, es, "xt", 5, [128, 1024], F32)
        hns = rot(c, es, "hn", 2, [128, 1024], BF16)
        sts = rot(c, es, "st", 6, [128, 16], F32)
        hnTs = rot(c, es, "hnT", 1, [128, 8, TB], BF16)
        qkTs = rot(c, es, "qkT", 2, [128, 8, TB], BF16)
        glTs = rot(c, es, "glT", 2, [16, TB], F32)
        zs = rot(c, es, "z", 2, [128, 512], F32)
        Ls = rot(c, es, "L", 2, [128, 512], F32)
        EGs = rot(c, es, "EG", 2, [128, 4, 128], F32)
        EnGs = rot(c, es, "EnG", 2, [128, 4, 128], F32)
        EGcs = rot(c, es, "EGc", 2, [128, 512], F32)
        qds = rot(c, es, "qd", 2, [128, 4, 128], BF16)
        kis = rot(c, es, "ki", 2, [128, 4, 128], BF16)
        kes = rot(c, es, "ke", 2, [128, 512], BF16)
        vs_ = rot(c, es, "v", 2, [128, 1024], BF16)
        ats = rot(c, es, "at", 2, [128, 4, 128], BF16)
        gss = rot(c, es, "gs", 1, [128, 1024], F32)
        ons = rot(c, es, "on", 1, [128, 1024], F32)
        obs = rot(c, es, "ob", 2, [128, 1024], BF16)
        obTs = rot(c, es, "obT", 2, [128, 8, 128], BF16)
        xos = rot(c, es, "xo", 2, [128, 1024], F32)
        c.pn_tmp = rot(c, es, "pnt", 1, [128, 1024], F32)
        for blk in range(NB):
            t0 = blk * TB
            hnT = hnTs.next()
            xl = []
            for ti in range(4):
                r0 = t0 + ti * 128
                xt = xts.next()
                xl.append(xt)
                fw.dma("sp", xt[:, :], xin.rows(r0, r0 + 128))
                hn = hns.next()
                norm_tile(c, xt, hn, sts.next())
                transpose_to(c, hn, hnT, ti * 128)
            qkT = qkTs.next()
            for j in range(8):
                ps = c.ps.next()
                for kc in range(8):
                    fw.mm(ps[:, :], w[:, kc, j * 128:(j + 1) * 128], hnT[:, kc, :], kc == 0, kc == 7)
                fw.cp(("act", "dve")[j % 2], qkT[:, j, :], ps[:, :])
            glT = glTs.next()
            ps = c.ps.next()
            for kc in range(8):
                fw.mm(ps[0:16, :], w[:, kc, 3072:3088], hnT[:, kc, :], kc == 0, kc == 7)
            fw.cp("act", glT[:, :], ps[0:16, :])
            for ti in range(4):
                r0 = t0 + ti * 128
                tsl = slice(ti * 128, (ti + 1) * 128)
                psZ = c.ps.next()
                fw.mm(psZ[:, :], glT[:, tsl], wg2[:, :])
                z = zs.next()
                fw.tt("dve", z[:, :], psZ[:, :], bb[:, :], ALU.add)
                L = Ls.next()
                fw.act(z[:, :], z[:, :], AF.Exp, scale=-1.0)
                fw.act(L[:, :], z[:, :], AF.Ln, bias=1.0)
                psG = c.ps.next()
                for h in range(4):
                    fw.mm(psG[:, h * 128:(h + 1) * 128], L[:, h * 128:(h + 1) * 128], triU[:, :])
                EG, EnG = EGs.next(), EnGs.next()
                pg3 = V([psG], psG.t[:, :].rearrange("p (h t) -> p h t", h=4))
                fw.act(EG[:, :, :], pg3, AF.Exp)
                fw.act(EnG[:, :, :], pg3, AF.Exp, scale=-1.0)
                psGc = c.ps.next()
                fw.mm(psGc[:, :], triS[:, :], L[:, :])
                EGc = EGcs.next()
                fw.act(EGc[:, :], psGc[:, :], AF.Exp)
                qd, ki = qds.next(), kis.next()
                fw.stt("dve", qd[:, :, :], qkT[:, 0:4, tsl], 128.0 ** -0.5, EG[:, :, :], ALU.mult, ALU.mult)
                fw.tt("pool", ki[:, :, :], qkT[:, 4:8, tsl], EnG[:, :, :], ALU.mult)
                psK = c.ps.next()
                for kc in range(8):
                    fw.mm(psK[:, :], hnT[:, kc, tsl], w[:, kc, 512:1024], kc == 0, kc == 7)
                ke = kes.next()
                fw.tt("dve", ke[:, :], psK[:, :], EGc[:, :], ALU.mult)
                v = vs_.next()
                for hf in range(2):
                    ps = c.ps.next()
                    for kc in range(8):
                        fw.mm(ps[:, :], hnT[:, kc, tsl], w[:, kc, 1024 + hf * 512:1536 + hf * 512], kc == 0, kc == 7)
                    fw.cp(("act", "dve")[hf], v[:, hf * 512:(hf + 1) * 512], ps[:, :])
                psA = c.ps.next()
                for h in range(4):
                    fw.mm(psA[:, h * 128:(h + 1) * 128], ki[:, h, :], qd[:, h, :])
                at = ats.next()
                fw.tt("dve", at[:, :, :], V([psA], psA.t[:, :].rearrange("p (h t) -> p h t", h=4)), maskU[:, :, :],
                      ALU.mult)
                pO = [c.psx.next(), c.psx.next()]
                for h in range(4):
                    ov = pO[h // 2][:, (h % 2) * 256:(h % 2 + 1) * 256]
                    fw.mm(ov, at[:, h, :], v[:, h * 256:(h + 1) * 256], True, False)
                    fw.mm(ov, qd[:, h, :], Sbf[:, h, :], False, True)
                gs = gss.next()
                for hf in range(2):
                    ps = c.ps.next()
                    for kc in range(8):
                        fw.mm(ps[:, :], hnT[:, kc, tsl], w[:, kc, 2048 + hf * 512:2560 + hf * 512], kc == 0, kc == 7)
                    fw.act(gs[:, hf * 512:(hf + 1) * 512], ps[:, :], AF.Silu)
                for hf in range(2):
                    ps = c.ps.next()
                    for hh in range(2):
                        h = hf * 2 + hh
                        fw.mm(ps[:, hh * 256:(hh + 1) * 256], ke[:, h * 128:(h + 1) * 128], v[:, h * 256:(h + 1) * 256])
                    for hh in range(2):
                        h = hf * 2 + hh
                        fw.stt("dve", S32[:, h, :], S32[:, h, :], EG[:, h, 127:128], ps[:, hh * 256:(hh + 1) * 256],
                               ALU.mult, ALU.add)
                        fw.cp("act", Sbf[:, h, :], S32[:, h, :])
                st = sts.next()
                fw.memset("pool", st[:, 0:4], 0.0)
                for h in range(4):
                    fw.act(c.junk[:, h * 256:(h + 1) * 256], pO[h // 2][:, (h % 2) * 256:(h % 2 + 1) * 256], AF.Square,
                           accum=st[:, h:h + 1])
                fw.ts("dve", st[:, 4:8], st[:, 0:4], 1.0 / 256, EPS, ALU.mult, ALU.add)
                fw.act(st[:, 8:12], st[:, 4:8], AF.Sqrt)
                fw.recip(st[:, 12:16], st[:, 8:12])
                on = ons.next()
                for h in range(4):
                    fw.stt("dve", on[:, h * 256:(h + 1) * 256], pO[h // 2][:, (h % 2) * 256:(h % 2 + 1) * 256],
                           st[:, 12 + h:13 + h], onb[:, :], ALU.mult, ALU.mult)
                ob = obs.next()
                fw.tt("pool", ob[:, :], on[:, :], gs[:, :], ALU.mult)
                obT = obTs.next()
                transpose_to(c, ob, obT, 0)
                psY = [c.ps.next(), c.ps.next()]
                for nh in range(2):
                    for kc in range(8):
                        fw.mm(psY[nh][:, :], obT[:, kc, :], wo[:, kc, nh * 512:(nh + 1) * 512], kc == 0, kc == 7)
                postnorm_store(c, psY[0], psY[1], xl[ti], G, xos.next(), sts.next(), xout.rows(r0, r0 + 128))
        fw.barrier()


EM05 = float(np.exp(-0.5))


def block_mask(c, dst, kind, val, CH=32):
    fw = c.fw
    fw.memset("pool", dst, val)
    for cc in range(128 // CH):
        v = dst[:, cc * CH:(cc + 1) * CH]
        lo = cc * CH
        if kind == "IU":
            fw.asel(v, v, [[1, CH]], ALU.is_ge, 0.0, lo, -1)
            fw.asel(v, v, [[0, CH]], ALU.is_ge, 0.0, -lo, 1)
        elif kind == "SU":
            fw.asel(v, v, [[1, CH]], ALU.is_gt, 0.0, lo, -1)
            fw.asel(v, v, [[0, CH]], ALU.is_ge, 0.0, -lo, 1)
        else:
            fw.asel(v, v, [[-1, CH]], ALU.is_gt, 0.0, -lo, 1)
            fw.asel(v, v, [[0, CH]], ALU.is_ge, 0.0, lo + CH - 1, -1)


def chunk_ind(c, dst, val, CH=32):
    fw = c.fw
    fw.memset("pool", dst, val)
    for cc in range(128 // CH):
        v = dst[:, cc:cc + 1]
        fw.asel(v, v, [[0, 1]], ALU.is_ge, 0.0, -cc * CH, 1)
        fw.asel(v, v, [[0, 1]], ALU.is_ge, 0.0, cc * CH + CH - 1, -1)


def mixer_d(c, li, xin, xout):
    import os
    fw = c.fw
    nc = c.nc
    H = 16
    NC = 4
    names = ["At", "Rt", "Bh", "Kh", "Bt", "Kt", "Vv", "BON", "GATE"]
    dd = {n: DramT(nc.dram_tensor("d_" + n, [S, 1024], BF16, kind="Internal").ap()) for n in names}
    GLd_ap = nc.dram_tensor("d_GL", [NT * 64, 64], F32, kind="Internal").ap()
    GLd = DramT(GLd_ap, blk=64)
    Yd = DramT(nc.dram_tensor("d_Y", [S, 1024], F32, kind="Internal").ap())
    dstage = int(os.environ.get("D_STAGE", "9"))
    with ExitStack() as es:
        Wr = sb(c, es, "d_Wr", [128, 8, 1024], BF16)
        Wk = sb(c, es, "d_Wk", [128, 8, 1024], BF16)
        Wv = sb(c, es, "d_Wv", [128, 8, 1024], BF16)
        w1 = sb(c, es, "d_w1", [128, 8, 64], BF16)
        a1 = sb(c, es, "d_a1", [128, 8, 64], BF16)
        g1 = sb(c, es, "d_g1", [128, 8, 128], BF16)
        w2 = sb(c, es, "d_w2", [64, 1, 1024], BF16)
        a2 = sb(c, es, "d_a2", [64, 1, 1024], BF16)
        g2 = sb(c, es, "d_g2", [128, 1, 1024], BF16)
        mixc = sb(c, es, "d_mix", [128, 48], F32)
        bc = {}
        for n in ("d_w0", "d_a0", "d_k_k", "d_k_a"):
            bc[n] = sb(c, es, n + "b", [128, 1024], F32)
            bcast(c, bc[n][:, :], c.d[n])
        rkb = sb(c, es, "d_rkb", [128, 1024], F32)
        bcast(c, rkb[:, :], c.d["d_r_k"].rearrange("h e -> (h e)"))
        omka = sb(c, es, "d_omka", [128, 1024], F32)
        fw.ts("pool", omka[:, :], bc["d_k_a"][:, :], -1.0, 1.0, ALU.mult, ALU.add)
        triC = sb(c, es, "d_triC", [128, 128], F32)
        triD = sb(c, es, "d_triD", [128, 128], F32)
        indC = sb(c, es, "d_indC", [128, NC], F32)
        block_mask(c, triC[:, :], "IU", -EM05)
        block_mask(c, triD[:, :], "SL", -EM05)
        chunk_ind(c, indC[:, :], -EM05)
        hz = sb(c, es, "d_hz", [128, 8, 128], BF16)
        fw.memset("pool", hz[:, :, :], 0.0)
        with ExitStack() as t2:
            load_cols(c, t2, mixc[:, :], c.d["d_mix"].rearrange("i (c p) -> (i c) p", p=128), 48)
            gc = gcol(c, li, 0)
            load_w(c, t2, Wr, c.d["d_w_rkv"][0], D, 1024, gcol=gc)
            load_w(c, t2, Wk, c.d["d_w_rkv"][1], D, 1024, gcol=gc)
            load_w(c, t2, Wv, c.d["d_w_rkv"][2], D, 1024, gcol=gc)
            load_w(c, t2, w1, c.d["d_w1"], D, 64, gcol=gc)
            load_w(c, t2, a1, c.d["d_a1"], D, 64, gcol=gc)
            load_w(c, t2, g1, c.d["d_g1"], D, 128, gcol=gc)
            for dst, src, kk_ in ((w2, "d_w2", 64), (a2, "d_a2", 64), (g2, "d_g2", 128)):
                st_ = sb(c, t2, "wst", [128, 1024], F32)
                fw.dma("sp", st_[0:kk_, :], RO(c.d[src]))
                fw.cp("dve", dst[0:kk_, 0, :], st_[0:kk_, :])
            fw.barrier()
        xts = rot(c, es, "xt", 2, [128, 1024], F32)
        hns = rot(c, es, "hn", 2, [128, 1024], BF16)
        sts = rot(c, es, "st", 4, [128, 16], F32)
        hnTs = rot(c, es, "hnT", 2, [128, 8, 128], BF16)
        DTs = rot(c, es, "DT", 1, [128, 8, 128], BF16)
        Xs = rot(c, es, "X", 7, [128, 8, 128], BF16)
        smT = rot(c, es, "smT", 4, [128, 128], BF16)
        f32s = rot(c, es, "f", 9, [128, 1024], F32)
        b16s = rot(c, es, "o", 12, [128, 1024], BF16)
        s16 = rot(c, es, "s16", 4, [128, 64], F32)
        glts = rot(c, es, "glt", 2, [64, 64], F32)
        prev = hz

        def v3(buf):
            return V([buf], buf.t[:, :].rearrange("p (h e) -> p h e", h=H))

        def b3(vw):
            return V(vw.bufs, vw.ap.unsqueeze(2).to_broadcast([128, H, 64]))

        for ti in range(NT if dstage >= 1 else 0):
            r0 = ti * 128
            xt = xts.next()
            fw.dma("sp", xt[:, :], xin.rows(r0, r0 + 128))
            hn = hns.next()
            norm_tile(c, xt, hn, sts.next())
            cur = hnTs.next()
            transpose_to(c, hn, cur, 0)
            DT = DTs.next()
            fw.tt("pool", DT[:, :, 0:1], prev[:, :, 127:128], cur[:, :, 0:1], ALU.subtract)
            fw.tt("pool", DT[:, :, 1:128], cur[:, :, 0:127], cur[:, :, 1:128], ALU.subtract)
            X = []
            for i in range(6):
                Xi = Xs.next()
                for kc in range(8):
                    mcol = mixc[:, i * 8 + kc:i * 8 + kc + 1]
                    if (i * 8 + kc) % 3 != 2:
                        fw.stt("dve", Xi[:, kc, :], DT[:, kc, :], mcol, cur[:, kc, :], ALU.mult, ALU.add)
                    else:
                        fw.act(Xi[:, kc, :], DT[:, kc, :], AF.Copy, scale=mcol)
                        fw.tt("pool", Xi[:, kc, :], Xi[:, kc, :], cur[:, kc, :], ALU.add)
                X.append(Xi)
            Xr, Xw, Xk, Xv, Xa, Xg = X
            prev = cur

            def proj2(Xi, W, nh):
                ps = c.ps.next()
                for kc in range(8):
                    fw.mm(ps[:, :], Xi[:, kc, :], W[:, kc, nh * 512:(nh + 1) * 512], kc == 0, kc == 7)
                return ps

            def low(Xi, Wl, M, func):
                ps = c.ps.next()
                for kc in range(8):
                    fw.mm(ps[0:M, 0:128], Wl[:, kc, :], Xi[:, kc, :], kc == 0, kc == 7)
                o = smT.next()
                fw.act(o[0:M, :], ps[0:M, 0:128], func)
                return o

            twT = low(Xw, w1, 64, AF.Tanh)
            aaT = low(Xa, a1, 64, AF.Copy)
            ggT = low(Xg, g1, 128, AF.Sigmoid)
            sig = f32s.next()
            av = f32s.next()
            for nh in range(2):
                cs_ = slice(nh * 512, (nh + 1) * 512)
                ps = c.ps.next()
                fw.mm(ps[:, :], twT[0:64, :], w2[0:64, 0, cs_])
                fw.tt("dve", sig[:, cs_], ps[:, :], bc["d_w0"][:, cs_], ALU.add)
                ps = c.ps.next()
                fw.mm(ps[:, :], aaT[0:64, :], a2[0:64, 0, cs_])
                fw.tt("dve", av[:, cs_], ps[:, :], bc["d_a0"][:, cs_], ALU.add)
            fw.act(sig[:, :], sig[:, :], AF.Sigmoid)
            fw.act(av[:, :], av[:, :], AF.Sigmoid)
            gate = b16s.next()
            for nh in range(2):
                ps = c.ps.next()
                fw.mm(ps[:, :], ggT[:, :], g2[:, 0, nh * 512:(nh + 1) * 512])
                fw.cp("act", gate[:, nh * 512:(nh + 1) * 512], ps[:, :])
            fw.dma("pool", dd["GATE"].rows(r0, r0 + 128), gate[:, :])
            rr = f32s.next()
            kx = f32s.next()
            vv = f32s.next()
            for nh in range(2):
                cs_ = slice(nh * 512, (nh + 1) * 512)
                fw.cp("act", rr[:, cs_], proj2(Xr, Wr, nh)[:, :])
                fw.cp("act", kx[:, cs_], proj2(Xk, Wk, nh)[:, :])
                fw.cp("act", vv[:, cs_], proj2(Xv, Wv, nh)[:, :])
            Vb = b16s.next()
            fw.cp("pool", Vb[:, :], vv[:, :])
            fw.dma("pool", dd["Vv"].rows(r0, r0 + 128), Vb[:, :])
            kkx = f32s.next()
            tmp = f32s.next()
            fw.tt("dve", kkx[:, :], kx[:, :], bc["d_k_k"][:, :], ALU.mult)
            fw.tt("pool", tmp[:, :], kkx[:, :], kkx[:, :], ALU.mult)
            sm = s16.next()
            fw.red("dve", sm[:, 0:16], v3(tmp), ALU.add)
            fw.act(sm[:, 16:32], sm[:, 0:16], AF.Sqrt)
            fw.ts("dve", sm[:, 16:32], sm[:, 16:32], 1e-12, None, ALU.max)
            fw.recip(sm[:, 32:48], sm[:, 16:32])
            fw.tt("dve", v3(kkx), v3(kkx), b3(sm[:, 32:48]), ALU.mult)
            fw.tt("pool", tmp[:, :], av[:, :], bc["d_k_a"][:, :], ALU.mult)
            fw.tt("pool", tmp[:, :], tmp[:, :], omka[:, :], ALU.add)
            fw.tt("dve", kx[:, :], kx[:, :], tmp[:, :], ALU.mult)
            fw.tt("pool", tmp[:, :], rr[:, :], rkb[:, :], ALU.mult)
            fw.tt("pool", tmp[:, :], tmp[:, :], kx[:, :], ALU.mult)
            fw.red("dve", sm[:, 48:64], v3(tmp), ALU.add)
            bon = b16s.next()
            fw.tt("dve", v3(bon), v3(vv), b3(sm[:, 48:64]), ALU.mult)
            fw.dma("pool", dd["BON"].rows(r0, r0 + 128), bon[:, :])
            fw.tt("pool", av[:, :], kkx[:, :], av[:, :], ALU.mult)
            E1 = f32s.next()
            E2 = f32s.next()
            E3 = tmp
            E4 = vv
            for nh in range(2):
                cs_ = slice(nh * 512, (nh + 1) * 512)
                ps = c.ps.next()
                fw.mm(ps[:, :], triC[:, :], sig[:, cs_])
                fw.act(E1[:, cs_], ps[:, :], AF.Exp)
                fw.act(E2[:, cs_], ps[:, :], AF.Exp, scale=-1.0)
                fw.stt("dve", E3[:, cs_], sig[:, cs_], EM05, ps[:, :], ALU.mult, ALU.add)
                ps = c.ps.next()
                fw.mm(ps[:, :], triD[:, :], sig[:, cs_])
                fw.act(E4[:, cs_], ps[:, :], AF.Exp)
            fw.act(E3[:, :], E3[:, :], AF.Exp)
            outs = {}
            for n in ("At", "Rt", "Bh", "Kh", "Bt", "Kt"):
                outs[n] = b16s.next()
            fw.stt("dve", outs["At"][:, :], kkx[:, :], -1.0, E3[:, :], ALU.mult, ALU.mult)
            fw.tt("pool", outs["Rt"][:, :], rr[:, :], E1[:, :], ALU.mult)
            fw.tt("dve", outs["Bh"][:, :], av[:, :], E2[:, :], ALU.mult)
            fw.tt("pool", outs["Kh"][:, :], kx[:, :], E2[:, :], ALU.mult)
            fw.tt("dve", outs["Bt"][:, :], av[:, :], E4[:, :], ALU.mult)
            fw.tt("pool", outs["Kt"][:, :], kx[:, :], E4[:, :], ALU.mult)
            for n in ("At", "Rt", "Bh", "Kh", "Bt", "Kt"):
                fw.dma("pool", dd[n].rows(r0, r0 + 128), outs[n][:, :])
            ps = c.ps.next()
            for h in range(H):
                fw.mm(ps[0:64, h * NC:(h + 1) * NC], sig[:, h * 64:(h + 1) * 64], indC[:, :])
            glt = glts.next()
            fw.act(glt[:, :], ps[0:64, 0:64], AF.Exp)
            fw.dma("pool", GLd.rows(ti * 64, ti * 64 + 64), glt[:, :])
        fw.barrier()
    with ExitStack() as es:
        mask1 = sb(c, es, "d_m1", [128, 384], F32)
        mask2 = sb(c, es, "d_m2", [128, 256], F32)
        II = sb(c, es, "d_II", [128, 256], BF16)
        CM = sb(c, es, "d_CM", [128, NC], F32)
        CMb = sb(c, es, "d_CMb", [128, NC], BF16)
        block_mask(c, mask1[:, 0:128], "SU", 1.0)
        block_mask(c, mask1[:, 128:256], "SL", 1.0)
        block_mask(c, mask1[:, 256:384], "SU", 1.0)
        block_mask(c, mask2[:, 0:128], "IU", 1.0)
        block_mask(c, mask2[:, 128:256], "IU", 1.0)
        chunk_ind(c, CM[:, :], 1.0)
        fw.cp("pool", CMb[:, :], CM[:, :])
        fw.cp("pool", II[:, 0:128], c.identb[:, :])
        fw.cp("pool", II[:, 128:256], c.identb[:, :])
        I64 = c.identf[0:64, 0:64]
        GS = 8
        lds = {n: rot(c, es, "l" + n, 2, [128, 1024], BF16) for n in ("At", "Rt", "Bh", "Kh", "Bt", "Kt", "Vv")}
        XTs = rot(c, es, "XT", 1, [64, H, 4, 128], BF16)
        GLs = rot(c, es, "GL", 2, [64, 64], F32)
        NAs = rot(c, es, "NA", GS, [128, 384], BF16)
        RBs = rot(c, es, "RB", GS, [128, 256], BF16)
        MMs = rot(c, es, "MM", 2 * GS, [128, 256], BF16)
        PPs = rot(c, es, "PP", 2 * GS, [128, 256], BF16)
        MFs = rot(c, es, "MF", GS, [128, 128], BF16)
        AWs = rot(c, es, "AW", GS, [128, 128], BF16)
        U0s = rot(c, es, "U0", GS, [128, 64], BF16)
        RcTs = rot(c, es, "RcT", GS, [64, 128], F32)
        Bms = rot(c, es, "Bm", GS, [128, NC, 64], BF16)
        Kms = rot(c, es, "Km", GS, [128, NC, 64], BF16)
        Gds = rot(c, es, "Gd", GS, [64, NC, 64], F32)
        Y0s = rot(c, es, "Y0", GS, [64, 128], F32)
        PhTs = rot(c, es, "PhT", GS, [64, NC, 64], F32)
        PsTs = rot(c, es, "PsT", GS, [64, NC, 64], F32)
        YTs = rot(c, es, "YT", GS, [64, 128], F32)
        Yts = rot(c, es, "Yt", 2, [128, 1024], F32)
        STs = [rot(c, es, "ST%d" % h, 3, [64, 64], F32) for h in range(H)]
        ST = []
        for h in range(H):
            b_ = STs[h].next()
            fw.memset("pool", b_[:, :], 0.0)
            ST.append(b_)
        for ti in range(NT if dstage >= 2 else 0):
            r0 = ti * 128
            L = {}
            for n in lds:
                L[n] = lds[n].next()
                fw.dma("sp", L[n][:, :], dd[n].rows(r0, r0 + 128))
            GL = GLs.next()
            fw.dma("sp", GL[:, :], GLd.rows(ti * 64, ti * 64 + 64))
            XT = XTs.next()
            for h2 in range(H // 2):
                ps = c.ps.next()
                psb = ps.t[:, :].bitcast(BF16)
                for hh in range(2):
                    h = h2 * 2 + hh
                    for j, n in enumerate(("At", "Rt", "Bh", "Kh")):
                        col = (hh * 4 + j) * 128
                        fw.tr(V([ps], psb[0:64, col:col + 128]), L[n][:, h * 64:(h + 1) * 64], c.identb[:, :])
                fw.cp(("act", "dve")[h2 % 2], XT[:, h2 * 2:h2 * 2 + 2, :, :],
                      V([ps], psb[0:64, :].rearrange("p (h j t) -> p h j t", h=2, j=4)))
            Yt = Yts.next()
            for g0 in range(0, H, GS):
                hs = list(range(g0, g0 + GS))
                NA, RB, MM, PP, MF, AW, U0, RcT, Bm, Km, Gd, Y0, PhT, PsT = ({} for _ in range(14))
                for h in hs:
                    AtT, RtT, BhT, KhT = (XT[:, h, j, :] for j in range(4))
                    b1, b2 = c.ps.next(), c.ps.next()
                    fw.mm(b1[:, 0:128], BhT, AtT)
                    fw.mm(b1[:, 128:256], AtT, BhT)
                    fw.mm(b1[:, 256:384], KhT, AtT)
                    fw.mm(b2[:, 0:128], BhT, RtT)
                    fw.mm(b2[:, 128:256], KhT, RtT)
                    NA[h], RB[h], MM[h] = NAs.next(), RBs.next(), MMs.next()
                    fw.tt("dve", NA[h][:, :], b1[:, 0:384], mask1[:, :], ALU.mult)
                    fw.tt("dve", RB[h][:, :], b2[:, 0:256], mask2[:, :], ALU.mult)
                    fw.tt("pool", MM[h][:, :], NA[h][:, 0:256], II[:, :], ALU.add)
                    PP[h] = NA[h]
                for lev in range(3):
                    for h in hs:
                        P_, PT_ = PP[h][:, 0:128], PP[h][:, 128:256]
                        bp = c.ps.next()
                        fw.mm(bp[:, 0:128], PT_, P_)
                        fw.mm(bp[:, 128:256], P_, PT_)
                        npp = PPs.next()
                        fw.cp("act", npp[:, :], bp[:, 0:256])
                        PP[h] = npp
                    for h in hs:
                        P_ = PP[h][:, 0:128]
                        M_, MT_ = MM[h][:, 0:128], MM[h][:, 128:256]
                        bm = c.ps.next()
                        fw.mm(bm[:, 0:128], MT_, P_)
                        fw.mm(bm[:, 128:256], P_, MT_)
                        nmm = MMs.next()
                        fw.tt("dve", nmm[:, :], bm[:, 0:256], MM[h][:, :], ALU.add)
                        MM[h] = nmm
                for h in hs:
                    bp = c.ps.next()
                    fw.mm(bp[:, 0:128], PP[h][:, 128:256], PP[h][:, 0:128])
                    npp = PPs.next()
                    fw.cp("act", npp[:, 0:128], bp[:, 0:128])
                    PP[h] = npp
                for h in hs:
                    bm = c.ps.next()
                    fw.mm(bm[:, 0:128], MM[h][:, 128:256], PP[h][:, 0:128])
                    MF[h] = MFs.next()
                    fw.tt("dve", MF[h][:, :], bm[:, 0:128], MM[h][:, 0:128], ALU.add)
                for h in hs:
                    hc = slice(h * 64, (h + 1) * 64)
                    b_ = c.ps.next()
                    fw.mm(b_[:, 0:64], MF[h][:, :], L["At"][:, hc])
                    fw.mm(b_[:, 64:128], NA[h][:, 256:384], L["Vv"][:, hc])
                    AW[h] = AWs.next()
                    fw.cp("act", AW[h][:, :], b_[:, 0:128])
                    Bm[h], Km[h], Gd[h] = Bms.next(), Kms.next(), Gds.next()
                    cmb = V([CMb], CMb.t[:, :].unsqueeze(2).to_broadcast([128, NC, 64]))
                    fw.tt("pool", Bm[h][:, :, :], V([L["Bt"]], L["Bt"].t[:, hc].unsqueeze(1).to_broadcast([128, NC, 64])),
                          cmb, ALU.mult)
                    fw.tt("pool", Km[h][:, :, :], V([L["Kt"]], L["Kt"].t[:, hc].unsqueeze(1).to_broadcast([128, NC, 64])),
                          cmb, ALU.mult)
                    fw.tt("pool", Gd[h][:, :, :],
                          V(I64.bufs, I64.ap.unsqueeze(1).to_broadcast([64, NC, 64])),
                          V([GL], GL.t[:, h * NC:(h + 1) * NC].unsqueeze(2).to_broadcast([64, NC, 64])), ALU.mult)
                for h in hs:
                    b_ = c.ps.next()
                    fw.mm(b_[:, 0:64], MF[h][:, :], AW[h][:, 64:128])
                    fw.mm(b_[0:64, 64:192], AW[h][:, 0:64], RB[h][:, 0:128])
                    U0[h], RcT[h] = U0s.next(), RcTs.next()
                    fw.cp("act", U0[h][:, :], b_[:, 0:64])
                    fw.tt("dve", RcT[h][:, :], b_[0:64, 64:192], XT[:, h, 1, :], ALU.add)
                for h in hs:
                    hc = slice(h * 64, (h + 1) * 64)
                    b1, b2 = c.ps.next(), c.ps.next()
                    fw.mm(b1[0:64, 0:128], U0[h][:, :], RB[h][:, 0:128], True, False)
                    fw.mm(b1[0:64, 0:128], L["Vv"][:, hc], RB[h][:, 128:256], False, True)
                    bmv = V([Bm[h]], Bm[h].t[:, :, :].rearrange("p c e -> p (c e)"))
                    kmv = V([Km[h]], Km[h].t[:, :, :].rearrange("p c e -> p (c e)"))
                    fw.mm(b1[0:64, 128:384], AW[h][:, 0:64], bmv)
                    fw.mm(b2[0:64, 0:256], U0[h][:, :], bmv, True, False)
                    fw.mm(b2[0:64, 0:256], L["Vv"][:, hc], kmv, False, True)
                    Y0[h], PhT[h], PsT[h] = Y0s.next(), PhTs.next(), PsTs.next()
                    fw.cp("act", Y0[h][:, :], b1[0:64, 0:128])
                    fw.tt("dve", V([PhT[h]], PhT[h].t[:, :, :].rearrange("p c e -> p (c e)")), b1[0:64, 128:384],
                          V([Gd[h]], Gd[h].t[:, :, :].rearrange("p c e -> p (c e)")), ALU.add)
                    fw.cp("act", V([PsT[h]], PsT[h].t[:, :, :].rearrange("p c e -> p (c e)")), b2[0:64, 0:256])
                yb = [c.psx.next(), c.psx.next()]
                for cc in range(NC):
                    for h in hs:
                        hl = h - g0
                        yv = yb[hl // 4][0:64, (hl % 4) * 128 + cc * 32:(hl % 4) * 128 + cc * 32 + 32]
                        fw.mm(yv, ST[h][:, :], RcT[h][:, cc * 32:(cc + 1) * 32])
                        bs = c.ps.next()
                        fw.mm(bs[0:64, 0:64], PhT[h][:, cc, :], ST[h][:, :], True, False)
                        fw.mm(bs[0:64, 0:64], PsT[h][:, cc, :], I64, False, True)
                        ns = STs[h].next()
                        fw.cp(("act", "dve")[h % 2], ns[:, :], bs[0:64, 0:64])
                        ST[h] = ns
                pt = c.ps.next()
                for h in hs:
                    hl = h - g0
                    YT = YTs.next()
                    fw.tt("dve", YT[:, :], yb[hl // 4][0:64, (hl % 4) * 128:(hl % 4) * 128 + 128], Y0[h][:, :], ALU.add)
                    fw.tr(pt[:, hl * 64:(hl + 1) * 64], YT[:, :], I64)
                fw.cp("act", Yt[:, g0 * 64:(g0 + GS) * 64], pt[:, :])
            fw.dma("pool", Yd.rows(r0, r0 + 128), Yt[:, :])
        fw.barrier()
    with ExitStack() as es:
        wo = sb(c, es, "d_wo", [128, 8, 1024], BF16)
        G = sb(c, es, "G", [128, 1024], F32)
        lw_ = sb(c, es, "d_lnw", [128, 1024], F32)
        lb_ = sb(c, es, "d_lnb", [128, 1024], F32)
        bcast(c, G[:, :], c.d["ln_gains"][li, 1])
        bcast(c, lw_[:, :], c.d["d_lnx_w"])
        bcast(c, lb_[:, :], c.d["d_lnx_b"])
        with ExitStack() as t2:
            load_w(c, t2, wo, c.d["d_w_o"], 1024, 1024)
            fw.barrier()
        xts = rot(c, es, "xt", 3, [128, 1024], F32)
        ys = rot(c, es, "y", 2, [128, 1024], F32)
        sqs = rot(c, es, "sq", 2, [128, 1024], F32)
        bons = rot(c, es, "bon", 2, [128, 1024], BF16)
        gts = rot(c, es, "gt", 2, [128, 1024], BF16)
        obs = rot(c, es, "ob", 2, [128, 1024], BF16)
        obTs = rot(c, es, "obT", 2, [128, 8, 128], BF16)
        sts = rot(c, es, "st", 4, [128, 16], F32)
        sms = rot(c, es, "sm", 2, [128, 96], F32)
        xos = rot(c, es, "xo", 2, [128, 1024], F32)
        c.pn_tmp = rot(c, es, "pnt", 2, [128, 1024], F32)

        def v3(buf):
            return V([buf], buf.t[:, :].rearrange("p (h e) -> p h e", h=H))

        def b3(vw):
            return V(vw.bufs, vw.ap.unsqueeze(2).to_broadcast([128, H, 64]))

        for ti in range(NT if dstage >= 3 else 0):
            r0 = ti * 128
            xt, y, bon, gt = xts.next(), ys.next(), bons.next(), gts.next()
            fw.dma("sp", xt[:, :], xin.rows(r0, r0 + 128))
            fw.dma("sp", y[:, :], Yd.rows(r0, r0 + 128))
            fw.dma("sp", bon[:, :], dd["BON"].rows(r0, r0 + 128))
            fw.dma("sp", gt[:, :], dd["GATE"].rows(r0, r0 + 128))
            sm = sms.next()
            sq = sqs.next()
            fw.red("dve", sm[:, 0:16], v3(y), ALU.add)
            fw.tt("pool", sq[:, :], y[:, :], y[:, :], ALU.mult)
            fw.red("dve", sm[:, 16:32], v3(sq), ALU.add)
            fw.ts("dve", sm[:, 32:48], sm[:, 0:16], 1.0 / 64)
            fw.tt("dve", sm[:, 48:64], sm[:, 32:48], sm[:, 32:48], ALU.mult)
            fw.stt("dve", sm[:, 64:80], sm[:, 16:32], 1.0 / 64, sm[:, 48:64], ALU.mult, ALU.subtract)
            fw.ts("dve", sm[:, 64:80], sm[:, 64:80], 64e-5, None, ALU.add)
            fw.act(sm[:, 64:80], sm[:, 64:80], AF.Sqrt)
            fw.recip(sm[:, 80:96], sm[:, 64:80])
            fw.tt("dve", v3(y), v3(y), b3(sm[:, 32:48]), ALU.subtract)
            fw.tt("dve", v3(y), v3(y), b3(sm[:, 80:96]), ALU.mult)
            fw.tt("pool", y[:, :], y[:, :], lw_[:, :], ALU.mult)
            fw.tt("pool", y[:, :], y[:, :], lb_[:, :], ALU.add)
            fw.tt("dve", y[:, :], y[:, :], bon[:, :], ALU.add)
            ob = obs.next()
            fw.tt("pool", ob[:, :], y[:, :], gt[:, :], ALU.mult)
            obT = obTs.next()
            transpose_to(c, ob, obT, 0)
            psY = [c.ps.next(), c.ps.next()]
            for nh in range(2):
                for kc in range(8):
                    fw.mm(psY[nh][:, :], obT[:, kc, :], wo[:, kc, nh * 512:(nh + 1) * 512], kc == 0, kc == 7)
            postnorm_store(c, psY[0], psY[1], xt, G, xos.next(), sts.next(), xout.rows(r0, r0 + 128))
        fw.barrier()


MIXERS = {0: mixer_a, 1: mixer_b, 2: mixer_c, 3: mixer_d}

_CACHE = {}


def run(inputs, subs=None, cores=8):
    key = tuple(subs) if subs is not None else None
    if key not in _CACHE:
        _CACHE[key] = build(subs)
    nc, c = _CACHE[key]
    in_maps = []
    for ci in range(cores):
        b = ci % 4
        m = {}
        hc = host_consts()
        for name, shape, dt in IN_SPECS:
            a = np.asarray(hc[name] if name in hc else inputs[name])
            if name in ("x", "mem", "positions"):
                a = a[b]
            a = np.ascontiguousarray(a).reshape(shape)
            m[name] = a
        in_maps.append(m)
    res = run_bass_kernel_spmd(nc, in_maps, core_ids=list(range(cores)))
    return [r["out"] for r in res.results]


def kernel(**inputs):
    outs = run(inputs)
    return np.stack(outs[0:4], axis=0).astype(np.float32)
```

```python
import numpy as np
from contextlib import ExitStack
import concourse.bass as bass
import concourse.mybir as mybir
from concourse.bass_utils import run_bass_kernel_spmd

F32 = mybir.dt.float32
BF16 = mybir.dt.bfloat16
I32 = mybir.dt.int32
AF = mybir.ActivationFunctionType
ALU = mybir.AluOpType
AX = mybir.AxisListType

D = 1024
S = 8192
NT = S // 128
EPS = 1e-6
MEM = 256
DFF = 2816


class Buf:
    __slots__ = ("t", "w", "r", "excl")

    def __init__(self, t, excl=False):
        self.t = t
        self.w = None
        self.r = {}
        self.excl = excl

    def __getitem__(self, idx):
        return V([self], self.t[idx])

    def v(self, ap):
        return V([self], ap)


class V:
    __slots__ = ("bufs", "ap")

    def __init__(self, bufs, ap):
        self.bufs = bufs
        self.ap = ap

    def __getitem__(self, idx):
        return V(self.bufs, self.ap[idx])


class DramT:
    def __init__(self, ap, blk=128, tracked=True):
        self.ap = ap
        self.blk = blk
        n = (ap.shape[0] + blk - 1) // blk
        self.blocks = [Buf(None) for _ in range(n)] if tracked else None

    def rows(self, r0, r1, cols=None):
        ap = self.ap[r0:r1] if cols is None else self.ap[r0:r1, cols[0]:cols[1]]
        if self.blocks is None:
            return V([], ap)
        return V(self.blocks[r0 // self.blk:(r1 - 1) // self.blk + 1], ap)

    def view(self, ap, r0, r1):
        if self.blocks is None:
            return V([], ap)
        return V(self.blocks[r0 // self.blk:(r1 - 1) // self.blk + 1], ap)


def RO(ap):
    return V([], ap)


COMPUTE = ("pe", "act", "dve", "pool")


class FW:
    def __init__(self, nc, es, n_dma=32):
        self.nc = nc
        self.engs = {"pe": nc.tensor, "act": nc.scalar, "dve": nc.vector, "pool": nc.gpsimd, "sp": nc.sync}
        self.sem = {k: es.enter_context(nc.semaphore("s_" + k)) for k in COMPUTE}
        self.cnt = {k: 0 for k in COMPUTE}
        self.waited = {k: {} for k in self.engs}
        self.dsem = [es.enter_context(nc.semaphore("d%d" % i)) for i in range(n_dma)]
        self.dcnt = [0] * n_dma
        self.dnext = 0
        self.nd = n_dma
        self.ninst = 0

    def _wait(self, eng, key, val):
        if eng == "pe" and key == "pe":
            return
        w = self.waited[eng]
        if w.get(key, 0) >= val:
            return
        sem = self.sem[key] if isinstance(key, str) else self.dsem[key]
        self.engs[eng].wait_ge(sem, val)
        w[key] = val
        self.ninst += 1

    def _deps(self, eng, reads, writes):
        for v in reads:
            for b in v.bufs:
                if b.w is not None:
                    self._wait(eng, b.w[0], b.w[1])
                if b.excl:
                    for k, val in b.r.items():
                        if k != eng:
                            self._wait(eng, k, val)
        for v in writes:
            for b in v.bufs:
                if b.w is not None:
                    self._wait(eng, b.w[0], b.w[1])
                for k, val in b.r.items():
                    self._wait(eng, k, val)

    def _done(self, key, val, reads, writes):
        for v in reads:
            for b in v.bufs:
                b.r[key] = val
        for v in writes:
            for b in v.bufs:
                b.w = (key, val)
                b.r = {}

    def op(self, eng, fn, reads, writes):
        self._deps(eng, reads, writes)
        ins = fn()
        self.cnt[eng] += 1
        ins.then_inc(self.sem[eng], 1)
        self._done(eng, self.cnt[eng], reads, writes)
        self.ninst += 1

    def dma(self, q, out, in_, **kw):
        s = self.dnext
        self.dnext = (s + 1) % self.nd
        if self.dcnt[s]:
            self._wait(q, s, self.dcnt[s])
        self._deps(q, [in_], [out])
        ins = self.engs[q].dma_start(out=out.ap, in_=in_.ap, **kw)
        self.dcnt[s] += 16
        ins.then_inc(self.dsem[s], 16)
        self._done(s, self.dcnt[s], [in_], [out])
        self.ninst += 1

    def barrier(self):
        for eng in self.engs:
            for k in COMPUTE:
                if self.cnt[k]:
                    self._wait(eng, k, self.cnt[k])
            for s in range(self.nd):
                if self.dcnt[s]:
                    self._wait(eng, s, self.dcnt[s])

    def mm(self, out, lhsT, rhs, start=True, stop=True):
        self.op("pe", lambda: self.nc.tensor.matmul(out.ap, lhsT.ap, rhs.ap, start=start, stop=stop),
                [lhsT, rhs], [out])

    def tr(self, out, in_, ident):
        self.op("pe", lambda: self.nc.tensor.transpose(out.ap, in_.ap, ident.ap), [in_, ident], [out])

    def act(self, out, in_, func, bias=None, scale=None, accum=None):
        kw = {}
        reads = [in_]
        writes = [out]
        if bias is not None:
            if isinstance(bias, V):
                kw["bias"] = bias.ap
                reads.append(bias)
            else:
                kw["bias"] = bias
        if scale is not None:
            if isinstance(scale, V):
                kw["scale"] = scale.ap
                reads.append(scale)
            else:
                kw["scale"] = scale
        if accum is not None:
            kw["accum_out"] = accum.ap
            writes.append(accum)
        self.op("act", lambda: self.nc.scalar.activation(out=out.ap, in_=in_.ap, func=func, **kw), reads, writes)

    def _e(self, eng):
        return self.engs[eng]

    def tt(self, eng, out, a, b, op):
        self.op(eng, lambda: self._e(eng).tensor_tensor(out.ap, a.ap, b.ap, op), [a, b], [out])

    def ts(self, eng, out, a, s1, s2=None, op0=ALU.mult, op1=None, accum=None):
        reads = [a]
        writes = [out]
        a1 = s1
        a2 = s2
        if isinstance(s1, V):
            reads.append(s1)
            a1 = s1.ap
        if isinstance(s2, V):
            reads.append(s2)
            a2 = s2.ap
        kw = {}
        if accum is not None:
            kw["accum_out"] = accum.ap
            writes.append(accum)
        if op1 is None:
            self.op(eng, lambda: self._e(eng).tensor_scalar(out.ap, a.ap, a1, None, op0, **kw), reads, writes)
        else:
            self.op(eng, lambda: self._e(eng).tensor_scalar(out.ap, a.ap, a1, a2, op0, op1, **kw), reads, writes)

    def stt(self, eng, out, a, s, b, op0, op1):
        reads = [a, b]
        sa = s
        if isinstance(s, V):
            reads.append(s)
            sa = s.ap
        self.op(eng, lambda: self._e(eng).scalar_tensor_tensor(out.ap, a.ap, sa, b.ap, op0, op1), reads, [out])

    def cp(self, eng, out, in_):
        if eng == "act":
            self.act(out, in_, AF.Copy)
        else:
            self.op(eng, lambda: self._e(eng).tensor_copy(out.ap, in_.ap), [in_], [out])

    def recip(self, out, in_):
        self.op("dve", lambda: self.nc.vector.reciprocal(out.ap, in_.ap), [in_], [out])

    def red(self, eng, out, in_, op, axis=AX.X):
        self.op(eng, lambda: self._e(eng).tensor_reduce(out.ap, in_.ap, axis, op), [in_], [out])

    def memset(self, eng, out, val):
        self.op(eng, lambda: self._e(eng).memset(out.ap, val), [], [out])

    def asel(self, out, in_, pattern, cmp, fill, base, cm):
        self.op("pool", lambda: self.nc.gpsimd.affine_select(out=out.ap, in_=in_.ap, pattern=pattern,
                                                             compare_op=cmp, fill=fill, base=base,
                                                             channel_multiplier=cm), [in_], [out])


class Ctx:
    pass


def sb(c, es, name, shape, dt=F32):
    c.uid += 1
    return Buf(es.enter_context(c.nc.sbuf_tensor("%s_%d" % (name, c.uid), shape, dt)))


class Rot:
    def __init__(self, bufs):
        self.bufs = bufs
        self.i = 0

    def next(self):
        b = self.bufs[self.i]
        self.i = (self.i + 1) % len(self.bufs)
        return b


def rot(c, es, name, n, shape, dt=F32):
    return Rot([sb(c, es, name, shape, dt) for _ in range(n)])


def prefetched(n, load):
    nxt = load(0) if n else None
    for i in range(n):
        cur = nxt
        nxt = load(i + 1) if i + 1 < n else None
        yield i, cur


def load_w(c, es_tmp, dst, src_ap, K, N, gcol=None, n0=0, q="sp"):
    fw = c.fw
    CH = 1408 if N % 1408 == 0 else (1024 if N % 1024 == 0 else N)
    if CH > 2048:
        CH = N // ((N + 2047) // 2048)
        assert N % CH == 0
    stg = rot(c, es_tmp, "wstg", 3, [128, CH], F32)
    i = 0
    for kc in range(K // 128):
        for n1 in range(0, N, CH):
            st = stg.next()
            fw.dma(q, st[:, :], RO(src_ap[kc * 128:(kc + 1) * 128, n1:n1 + CH]))
            eng = ("dve", "pool", "act")[i % 3]
            i += 1
            o = dst[:, kc, n0 + n1:n0 + n1 + CH]
            if gcol is None:
                fw.cp(eng, o, st[:, :])
            elif eng == "act":
                fw.act(o, st[:, :], AF.Copy, scale=gcol[:, kc:kc + 1])
            else:
                fw.ts(eng, o, st[:, :], gcol[:, kc:kc + 1])


def load_cols(c, es_tmp, dst_v, src2d_ap, R):
    fw = c.fw
    st = sb(c, es_tmp, "lc", [128, 128], F32)
    fw.dma("sp", st[0:R, :], RO(src2d_ap))
    ps = c.ps.next()
    fw.tr(ps[:, 0:R], st[0:R, :], c.identf[0:R, 0:R])
    fw.cp("dve", dst_v, ps[:, 0:R])


def bcast(c, dst, vec_ap):
    c.fw.dma("sp", dst, RO(vec_ap.partition_broadcast(128)))


def rms_rstd(c, ss_v, n, out_v, tmp):
    fw = c.fw
    k = ss_v.ap.shape[1]
    fw.ts("dve", tmp[:, 0:k], ss_v, 1.0 / n, EPS, ALU.mult, ALU.add)
    fw.act(tmp[:, k:2 * k], tmp[:, 0:k], AF.Sqrt)
    fw.recip(out_v, tmp[:, k:2 * k])


def norm_tile(c, xt, hn, st):
    fw = c.fw
    fw.memset("pool", st[:, 0:1], 0.0)
    fw.act(c.junk[:, :], xt[:, :], AF.Square, accum=st[:, 0:1])
    rms_rstd(c, st[:, 0:1], D, st[:, 3:4], c.mk_tmp(st))
    fw.act(hn[:, :], xt[:, :], AF.Copy, scale=st[:, 3:4])


def transpose_to(c, hn, dstT, t0, nchunk=8):
    fw = c.fw
    ps = c.ps.next()
    psb = ps.v(ps.t[:, :].bitcast(BF16))
    for ch in range(nchunk):
        fw.tr(V([ps], psb.ap[:, ch * 128:(ch + 1) * 128]), hn[:, ch * 128:(ch + 1) * 128], c.identb[:, :])
    eng = c.evac_eng()
    fw.cp(eng, dstT[:, 0:nchunk, t0:t0 + 128],
          V([ps], psb.ap[:, 0:nchunk * 128].rearrange("p (c t) -> p c t", c=nchunk)))


def postnorm_store(c, psA, psB, xt, G, xo, st, dst_rows):
    fw = c.fw
    fw.memset("pool", st[:, 0:2], 0.0)
    fw.act(c.junk[:, 0:512], psA[:, :], AF.Square, accum=st[:, 0:1])
    fw.act(c.junk[:, 512:1024], psB[:, :], AF.Square, accum=st[:, 1:2])
    fw.tt("dve", st[:, 2:3], st[:, 0:1], st[:, 1:2], ALU.add)
    rms_rstd(c, st[:, 2:3], D, st[:, 3:4], c.mk_tmp(st))
    tmp = c.pn_tmp.next() if c.pn_tmp is not None else xo
    fw.stt("dve", tmp[:, 0:512], psA[:, :], st[:, 3:4], G[:, 0:512], ALU.mult, ALU.mult)
    fw.stt("dve", tmp[:, 512:1024], psB[:, :], st[:, 3:4], G[:, 512:1024], ALU.mult, ALU.mult)
    fw.tt("pool", xo[:, :], xt[:, :], tmp[:, :], ALU.add)
    fw.dma("sp", dst_rows, xo[:, :])


def phase_consts(c, es):
    fw = c.fw
    c.identf = sb(c, es, "identf", [128, 128], F32)
    c.identb = sb(c, es, "identb", [128, 128], BF16)
    fw.memset("pool", c.identf[:, :], 1.0)
    fw.asel(c.identf[:, :], c.identf[:, :], [[-1, 128]], ALU.is_equal, 0.0, 0, 1)
    fw.cp("pool", c.identb[:, :], c.identf[:, :])
    c.junk = sb(c, es, "junk", [128, 1024], BF16)
    c.gT = sb(c, es, "gT", [128, 192], F32)
    with ExitStack() as tmp:
        g2 = c.d["ln_gains"].rearrange("l s (c p) -> (l s c) p", p=128)
        load_cols(c, tmp, c.gT[:, 0:96], g2[0:96, :], 96)
        load_cols(c, tmp, c.gT[:, 96:192], g2[96:192, :], 96)
        fw.barrier()


def gcol(c, li, si):
    o = (li * 6 + si) * 8
    return c.gT[:, o:o + 8]


def phase_memkv(c, es):
    fw = c.fw
    c.mem_kT = sb(c, es, "memkT", [128, 4, 256], BF16)
    c.mem_v = sb(c, es, "memv", [128, 2, 512], BF16)
    with ExitStack() as tmp:
        w = sb(c, tmp, "wkv", [128, 8, 1024], BF16)
        gm = sb(c, tmp, "gm", [128, 8], F32)
        load_cols(c, tmp, gm[:, :], c.d["mem_norm"].rearrange("(c p) -> c p", p=128), 8)
        with ExitStack() as t2:
            load_w(c, t2, w, c.d["mem_w_kv"], D, 1024, gcol=gm)
            fw.barrier()
        hnT = sb(c, tmp, "mhnT", [128, 8, 256], BF16)
        for ti in range(2):
            xt = sb(c, tmp, "mx", [128, 1024], F32)
            hn = sb(c, tmp, "mhn", [128, 1024], BF16)
            st = sb(c, tmp, "mst", [128, 16], F32)
            fw.dma("sp", xt[:, :], RO(c.d["mem"][ti * 128:(ti + 1) * 128, :]))
            norm_tile(c, xt, hn, st)
            transpose_to(c, hn, hnT, ti * 128)
        for h in range(4):
            ps = c.ps.next()
            for kc in range(8):
                fw.mm(ps[:, 0:256], w[:, kc, h * 128:(h + 1) * 128], hnT[:, kc, :], kc == 0, kc == 7)
            fw.cp("act", c.mem_kT[:, h, :], ps[:, 0:256])
        for mc in range(2):
            ps = c.ps.next()
            for kc in range(8):
                fw.mm(ps[:, :], hnT[:, kc, mc * 128:(mc + 1) * 128], w[:, kc, 512:1024], kc == 0, kc == 7)
            fw.cp("act", c.mem_v[:, mc, :], ps[:, :])
        fw.barrier()


def phase_mem(c, li, xin, xout):
    fw = c.fw
    TB = 512
    NBK = S // TB
    with ExitStack() as es:
        wq = sb(c, es, "wq", [128, 8, 512], BF16)
        wo = sb(c, es, "wo", [128, 4, 1024], BF16)
        G = sb(c, es, "G", [128, 1024], F32)
        bcast(c, G[:, :], c.d["ln_gains"][li, 3])
        with ExitStack() as t2:
            load_w(c, t2, wq, c.d["mem_w_q"][li], D, 512, gcol=gcol(c, li, 2))
            load_w(c, t2, wo, c.d["mem_w_o"][li], 512, 1024)
            fw.barrier()
        xts = rot(c, es, "xt", 12, [128, 1024], F32)
        hns = rot(c, es, "hn", 2, [128, 1024], BF16)
        sts = rot(c, es, "st", 8, [128, 16], F32)
        hnTs = rot(c, es, "hnT", 2, [128, 8, TB], BF16)
        qTs = rot(c, es, "qT", 3, [128, 4, TB], BF16)
        ps_ = rot(c, es, "p", 3, [128, 4, 256], F32)
        pns = rot(c, es, "pn", 3, [128, 4, 256], BF16)
        pTs = rot(c, es, "pT", 3, [128, 8, 128], BF16)
        oTs = rot(c, es, "oT", 3, [128, 4, 128], BF16)
        xos = rot(c, es, "xo", 3, [128, 1024], F32)
        c.pn_tmp = None
        sms = rot(c, es, "sm", 8, [128, 16], F32)
        scale = 128.0 ** -0.5
        blkst = {}
        tst = {}

        def P(blk):
            hnT = hnTs.next()
            xl = []
            for ti in range(TB // 128):
                r0 = blk * TB + ti * 128
                xt = xts.next()
                xl.append(xt)
                fw.dma("sp", xt[:, :], xin.rows(r0, r0 + 128))
                hn = hns.next()
                norm_tile(c, xt, hn, sts.next())
                transpose_to(c, hn, hnT, ti * 128)
            qT = qTs.next()
            for h in range(4):
                ps = c.ps.next()
                for kc in range(8):
                    fw.mm(ps[:, :], wq[:, kc, h * 128:(h + 1) * 128], hnT[:, kc, :], kc == 0, kc == 7)
                fw.act(qT[:, h, :], ps[:, :], AF.Copy, scale=scale)
            blkst[blk] = (xl, qT)

        def A(i):
            blk, ti = divmod(i, TB // 128)
            xl, qT = blkst[blk]
            tsl = slice(ti * 128, (ti + 1) * 128)
            sm = sms.next()
            pss = [c.ps.next(), c.ps.next()]
            for h in range(4):
                fw.mm(pss[h // 2][:, (h % 2) * 256:(h % 2 + 1) * 256], qT[:, h, tsl], c.mem_kT[:, h, :])
            for j in range(2):
                fw.red("dve", sm[:, 2 * j:2 * j + 2],
                       V([pss[j]], pss[j].t[:, :].rearrange("p (h m) -> p h m", h=2)), ALU.max)
            fw.ts("dve", sm[:, 4:8], sm[:, 0:4], -1.0)
            fw.memset("pool", sm[:, 8:12], 0.0)
            p = ps_.next()
            for h in range(4):
                fw.act(p[:, h, :], pss[h // 2][:, (h % 2) * 256:(h % 2 + 1) * 256], AF.Exp,
                       bias=sm[:, 4 + h:5 + h], accum=sm[:, 8 + h:9 + h])
            fw.recip(sm[:, 12:16], sm[:, 8:12])
            pn = pns.next()
            for h in range(4):
                fw.ts(("dve", "pool")[h % 2], pn[:, h, :], p[:, h, :], sm[:, 12 + h:13 + h])
            tst[i] = pn

        def B(i):
            pn = tst[i]
            psT = c.ps.next()
            psTb = psT.t[:, :].bitcast(BF16)
            for h in range(4):
                for mc in range(2):
                    j = h * 2 + mc
                    fw.tr(V([psT], psTb[:, j * 128:(j + 1) * 128]), pn[:, h, mc * 128:(mc + 1) * 128],
                          c.identb[:, :])
            pT = pTs.next()
            fw.cp("act", pT[:, :, :], V([psT], psTb.rearrange("p (j t) -> p j t", j=8)))
            psO = c.ps.next()
            for h in range(4):
                for mc in range(2):
                    fw.mm(psO[:, h * 128:(h + 1) * 128], c.mem_v[:, mc, h * 128:(h + 1) * 128],
                          pT[:, h * 2 + mc, :], mc == 0, mc == 1)
            oT = oTs.next()
            fw.cp("dve", oT[:, :, :], V([psO], psO.t[:, :].rearrange("p (h t) -> p h t", h=4)))
            tst[i] = oT

        def C(i):
            blk, ti = divmod(i, TB // 128)
            xl, qT = blkst[blk]
            oT = tst.pop(i)
            r0 = i * 128
            psY = [c.psx.next(), c.psx.next()]
            for nh in range(2):
                for h in range(4):
                    fw.mm(psY[nh][:, :], oT[:, h, :], wo[:, h, nh * 512:(nh + 1) * 512], h == 0, h == 3)
            postnorm_store(c, psY[0], psY[1], xl[ti], G, xos.next(), sts.next(), xout.rows(r0, r0 + 128))

        P(0)
        for step in range(NT + 2):
            if step % 4 == 1 and step // 4 + 1 < NBK:
                P(step // 4 + 1)
            if 0 <= step - 2 < NT:
                C(step - 2)
            if 0 <= step - 1 < NT:
                B(step - 1)
            if step < NT:
                A(step)
        fw.barrier()


def phase_ffn(c, li, xin, xout):
    fw = c.fw
    TB = 256
    NTI = TB // 128
    NBK = S // TB
    with ExitStack() as es:
        w1 = sb(c, es, "w1", [128, 8, 2 * DFF], BF16)
        w2 = sb(c, es, "w2", [128, 22, 1024], BF16)
        G = sb(c, es, "G", [128, 1024], F32)
        cw = sb(c, es, "cw", [128, 3, 44], F32)
        cb = sb(c, es, "cb", [128, 44], F32)
        bcast(c, G[:, :], c.d["ln_gains"][li, 5])
        with ExitStack() as t2:
            for j in range(3):
                load_cols(c, t2, cw[:, j, :], c.d["ffn_conv_w"][li, j].rearrange("(c p) -> c p", p=128), 44)
            load_cols(c, t2, cb[:, :], c.d["ffn_conv_b"][li].rearrange("(c p) -> c p", p=128), 44)
            load_w(c, t2, w1, c.d["ffn_w_in"][li], D, 2 * DFF, gcol=gcol(c, li, 4))
            load_w(c, t2, w2, c.d["ffn_w_out"][li], DFF, 1024)
            fw.barrier()
        xts = rot(c, es, "xt", 4, [128, 1024], F32)
        hns = rot(c, es, "hn", 2, [128, 1024], BF16)
        sts = rot(c, es, "st", 6, [128, 16], F32)
        hnTs = rot(c, es, "hnT", 2, [128, 8, TB + 2], BF16)
        tbs = rot(c, es, "tb", 8, [128, TB], F32)
        sgs = rot(c, es, "sg", 4, [128, TB], F32)
        aTs = rot(c, es, "aT", 1, [128, 22, TB], BF16)
        xos = rot(c, es, "xo", 2, [128, 1024], F32)
        c.pn_tmp = None
        for hb in hnTs.bufs:
            fw.memset("pool", hb[:, :, :], 0.0)

        def prologue(blk, prevT):
            hnT = hnTs.next()
            if prevT is not None:
                fw.cp("pool", hnT[:, :, 0:2], prevT[:, :, TB:TB + 2])
            xl = []
            for ti in range(NTI):
                r0 = blk * TB + ti * 128
                xt = xts.next()
                xl.append(xt)
                fw.dma("sp", xt[:, :], xin.rows(r0, r0 + 128))
                hn = hns.next()
                norm_tile(c, xt, hn, sts.next())
                transpose_to(c, hn, hnT, 2 + ti * 128)
            return hnT, xl

        nxt = prologue(0, None)
        for blk in range(NBK):
            hnT, xl = nxt
            aT = aTs.next()
            for j in range(22):
                tv = []
                for half, ch in ((0, j), (1, 22 + j)):
                    ps = c.ps.next()
                    for kc in range(8):
                        fw.mm(ps[:, 0:TB + 2], w1[:, kc, ch * 128:(ch + 1) * 128], hnT[:, kc, :], kc == 0, kc == 7)
                    t = tbs.next()
                    fw.act(t[:, :], ps[:, 2:TB + 2], AF.Identity, bias=cb[:, ch:ch + 1], scale=cw[:, 2, ch:ch + 1])
                    fw.stt("dve", t[:, :], ps[:, 0:TB], cw[:, 0, ch:ch + 1], t[:, :], ALU.mult, ALU.add)
                    fw.stt("dve", t[:, :], ps[:, 1:TB + 1], cw[:, 1, ch:ch + 1], t[:, :], ALU.mult, ALU.add)
                    tv.append(t)
                sg = sgs.next()
                fw.act(sg[:, :], tv[0][:, :], AF.Silu)
                fw.tt("pool", aT[:, j, :], sg[:, :], tv[1][:, :], ALU.mult)
            if blk + 1 < NBK:
                nxt = prologue(blk + 1, hnT)
            for ti in range(NTI):
                r0 = blk * TB + ti * 128
                psY = [c.ps.next(), c.ps.next()]
                for nh in range(2):
                    for kc in range(22):
                        fw.mm(psY[nh][:, :], aT[:, kc, ti * 128:(ti + 1) * 128], w2[:, kc, nh * 512:(nh + 1) * 512],
                              kc == 0, kc == 21)
                postnorm_store(c, psY[0], psY[1], xl[ti], G, xos.next(), sts.next(), xout.rows(r0, r0 + 128))
        fw.barrier()


IN_SPECS = [
    ("x", [S, D], F32), ("mem", [MEM, D], F32), ("positions", [S], I32),
    ("ln_gains", [4, 6, D], F32), ("mem_norm", [D], F32), ("mem_w_kv", [D, 1024], F32),
    ("mem_w_q", [4, D, 512], F32), ("mem_w_o", [4, 512, D], F32),
    ("ffn_w_in", [4, D, 2 * DFF], F32), ("ffn_conv_w", [4, 3, 2 * DFF], F32), ("ffn_conv_b", [4, 2 * DFF], F32),
    ("ffn_w_out", [4, DFF, D], F32),
    ("a_w_qkv", [D, 4608], F32), ("a_w_o", [512, D], F32),
    ("b_w_in", [D, 3088], F32), ("b_w_gate2", [16, 512], F32), ("b_gate_bias", [512], F32),
    ("b_o_norm", [256], F32), ("b_w_o", [D, D], F32),
    ("c_w_in", [D, 672], F32), ("c_q_norm", [384], F32), ("c_w_uq", [384, 1536], F32),
    ("c_kv_norm", [256], F32), ("c_w_ukv", [256, 2048], F32), ("c_w_o", [D, D], F32),
    ("d_mix", [6, D], F32), ("d_w_rkv", [3, D, D], F32), ("d_w0", [D], F32), ("d_w1", [D, 64], F32),
    ("d_w2", [64, D], F32), ("d_a0", [D], F32), ("d_a1", [D, 64], F32), ("d_a2", [64, D], F32),
    ("d_g1", [D, 128], F32), ("d_g2", [128, D], F32), ("d_k_k", [D], F32), ("d_k_a", [D], F32),
    ("d_r_k", [16, 64], F32), ("d_lnx_w", [D], F32), ("d_lnx_b", [D], F32), ("d_w_o", [D, D], F32),
    ("k_invf_c", [128, 1], F32),
]


def host_consts():
    invf = np.zeros((128, 1), np.float32)
    for i in range(32):
        invf[64 + i, 0] = INVF_C[i % 16]
    return {"k_invf_c": invf}

ALL_SUBS = [(k, li) for li in range(4) for k in ("mix", "mem", "ffn")]


def build(subs=None):
    if subs is None:
        subs = ALL_SUBS
    nc = bass.Bass("TRN2", target_bir_lowering=False)
    c = Ctx()
    c.nc = nc
    c.uid = 0
    c.d = {}
    for name, shape, dt in IN_SPECS:
        c.d[name] = nc.dram_tensor(name, shape, dt, kind="ExternalInput").ap()
    out = nc.dram_tensor("out", [S, D], F32, kind="ExternalOutput").ap()
    scr = [nc.dram_tensor("xs%d" % i, [S, D], F32, kind="Internal").ap() for i in range(2)]
    with ExitStack() as es:
        fw = FW(nc, es)
        c.fw = fw
        banks = [Buf(es.enter_context(nc.psum_tensor("ps%d" % i, [128, 512], F32)), excl=True) for i in range(8)]
        c.ps = Rot(banks[0:6])
        c.psx = Rot(banks[6:8])
        c._ev = 0

        def evac_eng():
            c._ev += 1
            return ("act", "dve")[c._ev % 2]
        c.evac_eng = evac_eng
        c.mk_tmp = _Tmp
        phase_consts(c, es)
        phase_memkv(c, es)
        cur = DramT(c.d["x"], tracked=False)
        for i, (kind, li) in enumerate(subs):
            last = i == len(subs) - 1
            dst = DramT(out) if last else DramT(scr[i % 2])
            if kind == "mem":
                phase_mem(c, li, cur, dst)
            elif kind == "ffn":
                phase_ffn(c, li, cur, dst)
            else:
                MIXERS[li](c, li, cur, dst)
            cur = dst
        fw.barrier()
    c.ninst = fw.ninst
    return nc, c


class _Tmp:
    def __init__(self, st):
        self.st = st

    def __getitem__(self, idx):
        p, f = idx
        return self.st[p, slice(8 + f.start, 8 + f.stop)]


INVF_A = [float(np.float32(500000.0) ** np.float32(-(np.float32(i) * np.float32(2.0 / 16)))) for i in range(8)]
INVF_C = [float(np.float32(500000.0) ** np.float32(-(np.float32(i) * np.float32(2.0 / 32)))) for i in range(16)]
TWO_PI = float(2 * np.pi)


def rope_tables(c, COS, SIN, es_tmp, pf, npair, invf):
    fw = c.fw
    nf = len(invf)
    n = npair * nf
    ang = sb(c, es_tmp, "ang", [128, npair, nf], F32)
    u = sb(c, es_tmp, "u", [128, n], F32)
    ni = sb(c, es_tmp, "ni", [128, n], I32)
    nfl = sb(c, es_tmp, "nfl", [128, n], F32)
    r = sb(c, es_tmp, "r", [128, n], F32)
    for f in range(nf):
        fw.ts("dve", ang[:, :, f], pf[:, :], invf[f])
    angf = ang.v(ang.t[:, :, :].rearrange("p a b -> p (a b)"))
    for off, dst in ((0.0, SIN), (0.25, COS)):
        fw.ts("dve", u[:, :], angf, 1.0 / TWO_PI, off, ALU.mult, ALU.add)
        fw.cp("dve", ni[:, :], u[:, :])
        fw.cp("dve", nfl[:, :], ni[:, :])
        if off:
            fw.ts("dve", u[:, :], angf, float(np.pi / 2), None, ALU.add)
            fw.stt("dve", r[:, :], nfl[:, :], -TWO_PI, u[:, :], ALU.mult, ALU.add)
        else:
            fw.stt("dve", r[:, :], nfl[:, :], -TWO_PI, angf, ALU.mult, ALU.add)
        fw.ts("dve", r[:, :], r[:, :], float(np.pi), float(-np.pi), ALU.min, ALU.max)
        fw.act(dst.v(dst.t[:, :, :].rearrange("p a b -> p (a b)")), r[:, :], AF.Sin)
    return COS, SIN


def rope_apply(c, xs, o, cosv, sinv, tmps, nh, hd, half):
    fw = c.fw
    x3 = V(xs.bufs, xs.ap.rearrange("p (h d) -> p h d", h=nh))
    o3 = V(o.bufs, o.ap.rearrange("p (h d) -> p h d", h=nh))
    cb = V(cosv.bufs, cosv.ap.unsqueeze(1).to_broadcast([128, nh, half]))
    sbv = V(sinv.bufs, sinv.ap.unsqueeze(1).to_broadcast([128, nh, half]))
    t = [tmps.next() for _ in range(4)]
    tv = [V([b], b.t[:, 0:nh * half].rearrange("p (h d) -> p h d", h=nh)) for b in t]
    x1 = x3[:, :, 0:half]
    x2 = x3[:, :, half:2 * half]
    fw.tt("dve", tv[0], x1, cb, ALU.mult)
    fw.tt("pool", tv[1], x2, sbv, ALU.mult)
    fw.tt("dve", o3[:, :, 0:half], tv[0], tv[1], ALU.subtract)
    fw.tt("dve", tv[2], x2, cb, ALU.mult)
    fw.tt("pool", tv[3], x1, sbv, ALU.mult)
    fw.tt("pool", o3[:, :, half:2 * half], tv[2], tv[3], ALU.add)
    if hd > 2 * half:
        fw.cp("act", o3[:, :, 2 * half:hd], x3[:, :, 2 * half:hd])


def load_pos(c, dst_i, src_ap_iJ, nJ):
    fw = c.fw
    step = 8
    for j0 in range(0, nJ, step):
        fw.dma("sp", dst_i[:, j0:j0 + step], RO(src_ap_iJ[:, j0:j0 + step]), allow_slow_non_contiguous=True)


def mixer_a(c, li, xin, xout):
    import os
    stage = int(os.environ.get('A_STAGE', '9'))
    fw = c.fw
    nc = c.nc
    U = 2048
    NU = S // U
    GROUPS = [(0, 1), (1, 4), (2, 16)]
    nds = [DramT(nc.dram_tensor("a_nd%d" % g, [S, 520], F32, kind="Internal").ap()) for g in range(3)]
    for g, d in GROUPS:
        nblk = 16 // d
        nJ = 64 // d
        with ExitStack() as es:
            w = sb(c, es, "wa", [128, 8, 1536], BF16)
            with ExitStack() as t2:
                for s3 in range(3):
                    col0 = (s3 * 3 + g) * 512
                    load_w(c, t2, w, c.d["a_w_qkv"][:, col0:col0 + 512], D, 512, gcol=gcol(c, li, 0), n0=s3 * 512)
                fw.barrier()
            COS = sb(c, es, "cos", [128, 64, 8], F32)
            SIN = sb(c, es, "sin", [128, 64, 8], F32)
            with ExitStack() as t2:
                pi = sb(c, t2, "pi", [128, 64], I32)
                pf = sb(c, t2, "pf", [128, 64], F32)
                if d == 1:
                    load_pos(c, pi, c.d["positions"].rearrange("(J i) -> i J", i=128), 64)
                else:
                    src = c.d["positions"].rearrange("(J i r) -> i J r", i=128, r=d)
                    step = max(1, 8 // d * 1)
                    piv = pi.t[:, :].rearrange("i (J r) -> i J r", r=d)
                    for j0 in range(0, nJ, 2):
                        fw.dma("sp", V([pi], piv[:, j0:j0 + 2, :]), RO(src[:, j0:j0 + 2, :]),
                               allow_slow_non_contiguous=True)
                fw.cp("dve", pf[:, :], pi[:, :])
                rope_tables(c, COS, SIN, t2, pf, 64, INVF_A)
                fw.barrier()
            hnT = sb(c, es, "hnTu", [128, 8, U], BF16)
            xts = rot(c, es, "xt", 2, [128, 1024], F32)
            hns = rot(c, es, "hn", 2, [128, 1024], BF16)
            sts = rot(c, es, "st", 4, [128, 16], F32)
            xss = rot(c, es, "xs", 3, [128, 512], F32)
            qrs = rot(c, es, "qr", 3, [128, 512], BF16)
            qTs = rot(c, es, "qT", 2, [128, 4, 128], BF16)
            tms = rot(c, es, "tm", 8, [128, 64], F32)
            pes = rot(c, es, "pe", 3, [128, 512], BF16)
            pms = rot(c, es, "pm", 3, [128, 512], BF16)
            stg = rot(c, es, "stg", 2, [128, 520], F32)
            nbuf = d + 3
            kfree = [sb(c, es, "kT", [128, 8, 128], BF16) for _ in range(nbuf)]
            for kb in kfree:
                fw.memset("pool", kb[:, :, :], 0.0)
            vfree = [sb(c, es, "v65", [128, 8, 80], BF16) for _ in range(nbuf)]
            for vb in vfree:
                fw.memset("pool", vb[:, :, 64:65], 1.0)
            kz = sb(c, es, "kz", [128, 8, 128], BF16)
            vz = sb(c, es, "vz", [128, 8, 80], BF16)
            fw.memset("pool", kz[:, :, :], 0.0)
            fw.memset("pool", vz[:, :, :], 0.0)
            mN = sb(c, es, "mN", [128, 512], BF16)
            mF = sb(c, es, "mF", [128, 512], BF16)
            with ExitStack() as t2:
                mt = sb(c, t2, "mt", [128, 512], F32)
                fw.memset("pool", mt[:, :], 1.0)
                for hh in range(2):
                    fw.asel(mt[:, hh * 256:hh * 256 + 128], mt[:, hh * 256:hh * 256 + 128], [[-1, 128]],
                            ALU.is_ge, 0.0, 0, 1)
                    fw.asel(mt[:, hh * 256 + 128:hh * 256 + 256], mt[:, hh * 256 + 128:hh * 256 + 256], [[1, 128]],
                            ALU.is_ge, 0.0, 0, -1)
                fw.cp("pool", mN[:, :], mt[:, :])
                for hh in range(2):
                    fw.memset("pool", mt[:, hh * 256:hh * 256 + 128], 0.0)
                fw.cp("pool", mF[:, :], mt[:, :])
                fw.barrier()
            carry = {}
            ndv = nds[g].ap.rearrange("(J i r) c -> r J i c", i=128, r=d)
            for u in range(NU if stage >= 2 else 0):
                for ti in range(U // 128):
                    r0 = u * U + ti * 128
                    xt = xts.next()
                    fw.dma("sp", xt[:, :], xin.rows(r0, r0 + 128))
                    hn = hns.next()
                    norm_tile(c, xt, hn, sts.next())
                    transpose_to(c, hn, hnT, ti * 128)
                for r in range(d if stage >= 3 else 0):
                    for jl in range(nblk):
                        J = u * nblk + jl
                        pidx = J * d + r
                        banks = []
                        for s3 in range(3):
                            ps = c.ps.next()
                            banks.append(ps)
                            for kc in range(8):
                                hv = hnT.t[:, kc, :].rearrange("p (j i r) -> p r j i", i=128, r=d)[:, r, jl, :]
                                fw.mm(ps[:, :], V([hnT], hv), w[:, kc, s3 * 512:(s3 + 1) * 512], kc == 0, kc == 7)
                        sub = int(os.environ.get('A_SUB', '9'))
                        if sub < 1:
                            continue
                        psT = c.ps.next()
                        psTb = psT.t[:, :].bitcast(BF16)
                        for s3 in range(2):
                            xs = xss.next()
                            fw.cp("act", xs[:, :], banks[s3][:, :])
                            qr = qrs.next()
                            if sub >= 2:
                                rope_apply(c, xs[:, :], qr[:, :], COS[:, pidx, :], SIN[:, pidx, :], tms, 8, 64, 8)
                            else:
                                fw.cp("dve", qr[:, :], xs[:, :])
                            for ch in range(4 if sub != 11 else 0):
                                jj = s3 * 4 + ch
                                fw.tr(V([psT], psTb[:, jj * 128:(jj + 1) * 128]), qr[:, ch * 128:(ch + 1) * 128],
                                      c.identb[:, :])
                        qT = qTs.next()
                        curK = kfree.pop()
                        curV = vfree.pop()
                        if sub not in (11, 12):
                            fw.cp("act", qT[:, :, :], V([psT], psTb[:, 0:512].rearrange("p (c t) -> p c t", c=4)))
                        if sub not in (11, 12, 13):
                            kv3 = psTb[:, 512:1024].rearrange("p (c t) -> p c t", c=4)
                            k4 = curK.t[:, :, :].rearrange("p (c e) t -> p c e t", e=2)
                            fw.cp("dve", V([curK], k4[0:64, :, 0, :]), V([psT], kv3[0:64]))
                            fw.cp("act", V([curK], k4[64:128, :, 1, :]), V([psT], kv3[64:128]))
                        if sub not in (11, 12, 13, 14):
                            fw.cp("dve", curV[:, :, 0:64], V([banks[2]], banks[2].t[:, :].rearrange("p (h e) -> p h e", h=8)))
                        if J == 0:
                            prevK, prevV, mask = kz, vz, mF
                        else:
                            prevK, prevV = carry[r]
                            mask = mN
                        pO = [c.psx.next(), c.psx.next()]
                        pend = []
                        for hp in range(5):
                            if hp < 4:
                                ps = c.ps.next()
                                for hh in range(2):
                                    fw.mm(ps[:, hh * 256:hh * 256 + 128], prevK[:, hp * 2 + hh, :], qT[:, hp, :])
                                    fw.mm(ps[:, hh * 256 + 128:hh * 256 + 256], curK[:, hp * 2 + hh, :], qT[:, hp, :])
                                pe = pes.next()
                                fw.act(pe[:, :], ps[:, :], AF.Exp, scale=0.125)
                                pm = pms.next()
                                fw.tt(("dve", "pool")[hp % 2], pm[:, :], pe[:, :], mask[:, :], ALU.mult)
                                pend.append((hp, pm))
                            if hp >= 1:
                                hp0, pm0 = pend.pop(0)
                                for hh in range(2):
                                    h = hp0 * 2 + hh
                                    ov = pO[h // 4][:, (h % 4) * 128:(h % 4) * 128 + 65]
                                    fw.mm(ov, pm0[:, hh * 256:hh * 256 + 128], prevV[:, h, 0:65], True, False)
                                    fw.mm(ov, pm0[:, hh * 256 + 128:hh * 256 + 256], curV[:, h, 0:65], False, True)
                        st = stg.next()
                        if stage < 4 or sub in (41, 42, 43, 44):
                            carry[r] = (curK, curV)
                            if J > 0:
                                kfree.append(prevK)
                                vfree.append(prevV)
                            continue
                        fw.cp("act", V([st], st.t[:, 0:260].rearrange("p (h e) -> p h e", h=4)),
                              V([pO[0]], pO[0].t[:, :].rearrange("p (h e) -> p h e", h=4)[:, :, 0:65]))
                        fw.cp("dve", V([st], st.t[:, 260:520].rearrange("p (h e) -> p h e", h=4)),
                              V([pO[1]], pO[1].t[:, :].rearrange("p (h e) -> p h e", h=4)[:, :, 0:65]))
                        if sub != 45:
                            fw.dma("sp", nds[g].view(ndv[r, J], 0, S), st[:, :])
                        if J > 0:
                            kfree.append(prevK)
                            vfree.append(prevV)
                        carry[r] = (curK, curV)
            fw.barrier()
    with ExitStack() as es:
        wo = sb(c, es, "woa", [128, 4, 1024], BF16)
        G = sb(c, es, "G", [128, 1024], F32)
        bcast(c, G[:, :], c.d["ln_gains"][li, 1])
        with ExitStack() as t2:
            load_w(c, t2, wo, c.d["a_w_o"], 512, 1024)
            fw.barrier()
        xts = rot(c, es, "xt", 4, [128, 1024], F32)
        n0s = rot(c, es, "n0", 3, [128, 520], F32)
        n1s = rot(c, es, "n1", 3, [128, 520], F32)
        n2s = rot(c, es, "n2", 3, [128, 520], F32)
        obs = rot(c, es, "ob", 2, [128, 512], BF16)
        oTs = rot(c, es, "oT", 2, [128, 4, 128], BF16)
        sts = rot(c, es, "st", 4, [128, 16], F32)
        xos = rot(c, es, "xo", 2, [128, 1024], F32)
        c.pn_tmp = rot(c, es, "pnt", 2, [128, 1024], F32)
        def ld_a2(ti):
            r0 = ti * 128
            xt = xts.next()
            fw.dma("sp", xt[:, :], xin.rows(r0, r0 + 128))
            n0, n1, n2 = n0s.next(), n1s.next(), n2s.next()
            fw.dma("sp", n0[:, :], nds[0].rows(r0, r0 + 128))
            fw.dma("sp", n1[:, :], nds[1].rows(r0, r0 + 128))
            fw.dma("sp", n2[:, :], nds[2].rows(r0, r0 + 128))
            return xt, n0, n1, n2

        for ti, (xt, n0, n1, n2) in prefetched(NT, ld_a2):
            r0 = ti * 128
            fw.tt("pool", n0[:, :], n0[:, :], n1[:, :], ALU.add)
            fw.tt("dve", n0[:, :], n0[:, :], n2[:, :], ALU.add)
            st = sts.next()
            n3 = V([n0], n0.t[:, :].rearrange("p (h e) -> p h e", h=8))
            fw.recip(st[:, 0:8], n3[:, :, 64])
            ob = obs.next()
            fw.tt("dve", V([ob], ob.t[:, :].rearrange("p (h e) -> p h e", h=8)), n3[:, :, 0:64],
                  V([st], st.t[:, 0:8].unsqueeze(2).to_broadcast([128, 8, 64])), ALU.mult)
            oT = oTs.next()
            transpose_to(c, ob, oT, 0, nchunk=4)
            psY = [c.ps.next(), c.ps.next()]
            for nh in range(2):
                for kc in range(4):
                    fw.mm(psY[nh][:, :], oT[:, kc, :], wo[:, kc, nh * 512:(nh + 1) * 512], kc == 0, kc == 3)
            postnorm_store(c, psY[0], psY[1], xt, G, xos.next(), sts.next(), xout.rows(r0, r0 + 128))
        fw.barrier()


def range_reduce_sin(c, dst, ang, off, u, ni, nfl, r):
    fw = c.fw
    fw.ts("dve", u, ang, 1.0 / TWO_PI, off, ALU.mult, ALU.add)
    fw.cp("dve", ni, u)
    fw.cp("dve", nfl, ni)
    if off:
        fw.ts("dve", u, ang, float(off * TWO_PI), None, ALU.add)
        fw.stt("dve", r, nfl, -TWO_PI, u, ALU.mult, ALU.add)
    else:
        fw.stt("dve", r, nfl, -TWO_PI, ang, ALU.mult, ALU.add)
    fw.ts("dve", r, r, float(np.pi), float(-np.pi), ALU.min, ALU.max)
    fw.act(dst, r, AF.Sin)


def mixer_c(c, li, xin, xout):
    fw = c.fw
    nc = c.nc
    TB = 512
    NB = S // TB
    H = 16
    QTd = nc.dram_tensor("c_qt", [H, 96, S], BF16, kind="Internal").ap()
    KTd = nc.dram_tensor("c_kt", [H, 96, S], BF16, kind="Internal").ap()
    Vd = nc.dram_tensor("c_v", [S, 1024], BF16, kind="Internal").ap()
    OTd = nc.dram_tensor("c_ot", [8, 128, S], BF16, kind="Internal").ap()
    QT = DramT(QTd.rearrange("h r s -> (h r) s"), blk=96)
    KT = DramT(KTd.rearrange("h r s -> (h r) s"), blk=96)
    VD = DramT(Vd)
    OT = DramT(OTd.rearrange("h r s -> (h r) s"), blk=128)
    with ExitStack() as es:
        win = sb(c, es, "c_win", [128, 8, 672], BF16)
        wkp = sb(c, es, "c_wkp", [128, 8, 96], BF16)
        wkp2 = sb(c, es, "c_wkp2", [128, 8, 96], BF16)
        wq = sb(c, es, "c_wq", [128, 3, 1536], BF16)
        wq2 = sb(c, es, "c_wq2", [128, 3, 1536], BF16)
        wkv = sb(c, es, "c_wkv", [128, 2, 2048], BF16)
        invf = sb(c, es, "c_invf", [128, 1], F32)
        gq = sb(c, es, "c_gq", [128, 3], F32)
        gkv = sb(c, es, "c_gkv", [128, 2], F32)
        fw.dma("sp", invf[:, :], RO(c.d["k_invf_c"]))
        with ExitStack() as t2:
            load_cols(c, t2, gq[:, :], c.d["c_q_norm"].rearrange("(c p) -> c p", p=128), 3)
            load_cols(c, t2, gkv[:, :], c.d["c_kv_norm"].rearrange("(c p) -> c p", p=128), 2)
            load_w(c, t2, win, c.d["c_w_in"], D, 672, gcol=gcol(c, li, 0))
            load_w(c, t2, wq, c.d["c_w_uq"], 384, 1536, gcol=gq)
            load_w(c, t2, wkv, c.d["c_w_ukv"], 256, 2048, gcol=gkv)
            fw.memset("pool", wkp[:, :, :], 0.0)
            fw.memset("pool", wkp2[:, :, :], 0.0)
            fw.memset("pool", wq2[:, :, :], 0.0)
            fw.cp("pool", wkp[:, :, 64:96], win[:, :, 640:672])
            fw.cp("pool", wkp2[:, :, 80:96], win[:, :, 640:656])
            fw.ts("pool", wkp2[:, :, 64:80], win[:, :, 656:672], -1.0)
            wq4 = wq.t[:, :, :].rearrange("p k (h e) -> p k h e", h=H)
            wq24 = wq2.t[:, :, :].rearrange("p k (h e) -> p k h e", h=H)
            for kc in range(3):
                fw.cp("pool", V([wq2], wq24[:, kc, :, 80:96]), V([wq], wq4[:, kc, :, 64:80]))
                fw.ts("pool", V([wq2], wq24[:, kc, :, 64:80]), V([wq], wq4[:, kc, :, 80:96]), -1.0)
            fw.barrier()
        xts = rot(c, es, "xt", 8, [128, 1024], F32)
        hns = rot(c, es, "hn", 2, [128, 1024], BF16)
        sts = rot(c, es, "st", 4, [128, 16], F32)
        hnTs = rot(c, es, "hnT", 2, [128, 8, TB], BF16)
        cqTs = rot(c, es, "cqT", 2, [128, 5, TB], BF16)
        cns = rot(c, es, "cn", 2, [128, 640], BF16)
        pis = rot(c, es, "pi", 1, [128, TB], I32)
        pfs = rot(c, es, "pf", 1, [128, TB], F32)
        angs = rot(c, es, "ang", 1, [128, TB], F32)
        CTs = rot(c, es, "CT", 2, [128, TB], F32)
        STs = rot(c, es, "ST", 2, [128, TB], F32)
        us = rot(c, es, "u", 1, [128, TB], F32)
        nis = rot(c, es, "ni", 1, [128, TB], I32)
        nfs = rot(c, es, "nf", 1, [128, TB], F32)
        rs = rot(c, es, "r", 1, [128, TB], F32)
        kpes = rot(c, es, "kpe", 2, [128, TB], BF16)
        t1s = rot(c, es, "t1", 3, [128, TB], F32)
        t2s = rot(c, es, "t2", 3, [128, TB], F32)
        qst = rot(c, es, "qst", 4, [128, TB], BF16)
        kst = rot(c, es, "kst", 4, [128, TB], BF16)
        vst = rot(c, es, "vst", 3, [128, 1024], BF16)
        wkv4 = wkv.t[:, :, :].rearrange("p k (h e) -> p k h e", h=H)
        for blk in range(NB):
            t0 = blk * TB
            hnT = hnTs.next()
            cqT = cqTs.next()
            pi, pf, ang = pis.next(), pfs.next(), angs.next()
            fw.dma("sp", pi[:, :], RO(c.d["positions"][t0:t0 + TB].partition_broadcast(128)))
            fw.cp("pool", pf[:, :], pi[:, :])
            fw.ts("dve", ang[:, :], pf[:, :], invf[:, 0:1])
            CT, ST = CTs.next(), STs.next()
            u, ni, nfl, r = us.next(), nis.next(), nfs.next(), rs.next()
            range_reduce_sin(c, ST[:, :], ang[:, :], 0.0, u[:, :], ni[:, :], nfl[:, :], r[:, :])
            range_reduce_sin(c, CT[:, :], ang[:, :], 0.25, u[:, :], ni[:, :], nfl[:, :], r[:, :])
            if blk == 0:
                xnext = []
                for ti in range(TB // 128):
                    xt = xts.next()
                    fw.dma("sp", xt[:, :], xin.rows(ti * 128, ti * 128 + 128))
                    xnext.append(xt)
            xcur = xnext
            xnext = []
            if blk + 1 < NB:
                for ti in range(TB // 128):
                    r1 = t0 + TB + ti * 128
                    xt = xts.next()
                    fw.dma("sp", xt[:, :], xin.rows(r1, r1 + 128))
                    xnext.append(xt)
            for ti in range(TB // 128):
                r0 = t0 + ti * 128
                xt = xcur[ti]
                hn = hns.next()
                st = sts.next()
                norm_tile(c, xt, hn, st)
                transpose_to(c, hn, hnT, ti * 128)
                psA, psB = c.ps.next(), c.ps.next()
                for kc in range(8):
                    fw.mm(psA[:, 0:384], hnT[:, kc, ti * 128:(ti + 1) * 128], win[:, kc, 0:384], kc == 0, kc == 7)
                for kc in range(8):
                    fw.mm(psB[:, 0:256], hnT[:, kc, ti * 128:(ti + 1) * 128], win[:, kc, 384:640], kc == 0, kc == 7)
                st2 = sts.next()
                fw.memset("pool", st2[:, 0:2], 0.0)
                fw.act(c.junk[:, 0:384], psA[:, 0:384], AF.Square, accum=st2[:, 0:1])
                fw.act(c.junk[:, 384:640], psB[:, 0:256], AF.Square, accum=st2[:, 1:2])
                fw.ts("dve", st2[:, 2:3], st2[:, 0:1], 1.0 / 384, EPS, ALU.mult, ALU.add)
                fw.ts("dve", st2[:, 3:4], st2[:, 1:2], 1.0 / 256, EPS, ALU.mult, ALU.add)
                fw.act(st2[:, 4:6], st2[:, 2:4], AF.Sqrt)
                fw.recip(st2[:, 6:8], st2[:, 4:6])
                cn = cns.next()
                fw.act(cn[:, 0:384], psA[:, 0:384], AF.Copy, scale=st2[:, 6:7])
                fw.act(cn[:, 384:640], psB[:, 0:256], AF.Copy, scale=st2[:, 7:8])
                transpose_to(c, cn, cqT, ti * 128, nchunk=5)
            psK, psK2 = c.ps.next(), c.ps.next()
            for kc in range(8):
                fw.mm(psK[0:96, :], wkp[:, kc, :], hnT[:, kc, :], kc == 0, kc == 7)
            for kc in range(8):
                fw.mm(psK2[0:96, :], wkp2[:, kc, :], hnT[:, kc, :], kc == 0, kc == 7)
            t1, t2 = t1s.next(), t2s.next()
            kpe = kpes.next()
            fw.tt("dve", t1[0:96, :], psK[0:96, :], CT[0:96, :], ALU.mult)
            fw.tt("dve", t2[0:96, :], psK2[0:96, :], ST[0:96, :], ALU.mult)
            fw.tt("pool", kpe[0:96, :], t1[0:96, :], t2[0:96, :], ALU.add)
            for h in range(H):
                psQ, psQ2, psKn = c.ps.next(), c.ps.next(), c.ps.next()
                for kc in range(3):
                    fw.mm(psQ[0:96, :], wq[:, kc, h * 96:(h + 1) * 96], cqT[:, kc, :], kc == 0, kc == 2)
                for kc in range(3):
                    fw.mm(psQ2[0:96, :], wq2[:, kc, h * 96:(h + 1) * 96], cqT[:, kc, :], kc == 0, kc == 2)
                for kc in range(2):
                    fw.mm(psKn[0:64, :], V([wkv], wkv4[:, kc, h, 0:64]), cqT[:, 3 + kc, :], kc == 0, kc == 1)
                t1, t2 = t1s.next(), t2s.next()
                qs = qst.next()
                fw.tt("dve", t1[0:96, :], psQ[0:96, :], CT[0:96, :], ALU.mult)
                fw.tt("dve", t2[0:96, :], psQ2[0:96, :], ST[0:96, :], ALU.mult)
                fw.tt("pool", qs[0:96, :], t1[0:96, :], t2[0:96, :], ALU.add)
                fw.dma("sp", QT.view(QTd[h, :, t0:t0 + TB], h * 96, (h + 1) * 96), qs[0:96, :])
                ks = kst.next()
                fw.cp("act", ks[0:64, :], psKn[0:64, :])
                fw.cp("pool", ks[64:96, :], kpe[64:96, :])
                fw.dma("sp", KT.view(KTd[h, :, t0:t0 + TB], h * 96, (h + 1) * 96), ks[0:96, :])
            for ti in range(TB // 128):
                r0 = t0 + ti * 128
                vs = vst.next()
                for hf in range(2):
                    ps = c.ps.next()
                    for kc in range(2):
                        fw.mm(V([ps], ps.t[:, :].rearrange("p (h e) -> p h e", h=8)),
                              cqT[:, 3 + kc, ti * 128:(ti + 1) * 128],
                              V([wkv], wkv4[:, kc, hf * 8:(hf + 1) * 8, 64:128]), kc == 0, kc == 1)
                    fw.cp(("act", "dve")[hf], vs[:, hf * 512:(hf + 1) * 512], ps[:, :])
                fw.dma("sp", VD.rows(r0, r0 + 128), vs[:, :])
        fw.barrier()
    with ExitStack() as es:
        qbufs = rot(c, es, "qh", 2, [128, S], BF16)
        kbufs = rot(c, es, "kh", 2, [128, S], BF16)
        vbufs = rot(c, es, "vh", 2, [128, NT, 80], BF16)
        for vb in vbufs.bufs:
            fw.memset("pool", vb[:, :, 64:65], 1.0)
        masks = sb(c, es, "cmask", [128, 4, TB], BF16)
        sel = sb(c, es, "csel", [128, 64], BF16)
        with ExitStack() as t2:
            mt = sb(c, t2, "mt", [128, TB], F32)
            for j in range(4):
                fw.memset("pool", mt[:, :], 1.0)
                fw.asel(mt[:, :], mt[:, :], [[1, TB]], ALU.is_ge, 0.0, -128 * j, -1)
                fw.cp("pool", masks[:, j, :], mt[:, :])
            fw.memset("pool", mt[:, 0:64], 0.0)
            fw.memset("pool", mt[64:96, 0:64], 1.0)
            fw.cp("pool", sel[:, :], mt[:, 0:64])
            fw.barrier()
        pes = rot(c, es, "pe", 5, [128, TB], BF16)
        pms = rot(c, es, "pm", 4, [128, TB], BF16)
        oas = rot(c, es, "oa", 2, [128, TB], BF16)
        ofs = rot(c, es, "of", 2, [128, TB], F32)
        rds = rot(c, es, "rd", 2, [128, TB], F32)
        ons = rot(c, es, "on", 3, [128, TB], BF16)
        scale = 96.0 ** -0.5
        def ld_c2(h):
            qh, kh, vh = qbufs.next(), kbufs.next(), vbufs.next()
            for q4 in range(4):
                cs = slice(q4 * 2048, (q4 + 1) * 2048)
                fw.dma("sp", qh[0:96, cs], QT.view(QTd[h, :, cs], h * 96, (h + 1) * 96))
                fw.dma("sp", kh[0:96, cs], KT.view(KTd[h, :, cs], h * 96, (h + 1) * 96))
            for q4 in range(4):
                fw.dma("sp", vh[:, q4 * 16:(q4 + 1) * 16, 0:64],
                       VD.view(Vd[q4 * 2048:(q4 + 1) * 2048, h * 64:(h + 1) * 64].rearrange("(t p) e -> p t e", p=128),
                               q4 * 2048, (q4 + 1) * 2048))
            return qh, kh, vh

        for h, (qh, kh, vh) in prefetched(H, ld_c2):
            for qb in range(NB):
                acc = c.psx.next()
                nk = 4 * qb + 4
                LOOK = 2
                pend = []
                for kt in range(nk + LOOK):
                    if kt < nk:
                        ps = c.ps.next()
                        fw.mm(ps[:, :], kh[0:96, kt * 128:(kt + 1) * 128], qh[0:96, qb * TB:(qb + 1) * TB])
                        pe = pes.next()
                        fw.act(pe[:, :], ps[:, :], AF.Exp, scale=scale)
                        if kt >= 4 * qb:
                            pm = pms.next()
                            fw.tt(("dve", "pool")[kt % 2], pm[:, :], pe[:, :], masks[:, kt - 4 * qb, :], ALU.mult)
                            pe = pm
                        pend.append((kt, pe))
                    if kt >= LOOK:
                        k0, pe0 = pend.pop(0)
                        fw.mm(acc[0:65, :], vh[:, k0, 0:65], pe0[:, :], k0 == 0, k0 == nk - 1)
                oa = oas.next()
                of = ofs.next()
                fw.cp("act", oa[0:65, :], acc[0:65, :])
                fw.cp("dve", of[0:64, :], acc[0:64, :])
                psd = c.ps.next()
                fw.mm(psd[0:64, :], sel[0:65, :], oa[0:65, :])
                rd = rds.next()
                fw.recip(rd[0:64, :], psd[0:64, :])
                on = ons.next()
                fw.tt("pool", on[0:64, :], of[0:64, :], rd[0:64, :], ALU.mult)
                hp, hh = h // 2, h % 2
                fw.dma("sp", OT.view(OTd[hp, hh * 64:(hh + 1) * 64, qb * TB:(qb + 1) * TB], hp * 128, (hp + 1) * 128),
                       on[0:64, :])
        fw.barrier()
    with ExitStack() as es:
        wo = sb(c, es, "c_wo", [128, 8, 1024], BF16)
        G = sb(c, es, "G", [128, 1024], F32)
        bcast(c, G[:, :], c.d["ln_gains"][li, 1])
        with ExitStack() as t2:
            load_w(c, t2, wo, c.d["c_w_o"], 1024, 1024)
            fw.barrier()
        xts = rot(c, es, "xt", 4, [128, 1024], F32)
        oTs = rot(c, es, "oT", 4, [128, 8, 128], BF16)
        sts = rot(c, es, "st", 4, [128, 16], F32)
        xos = rot(c, es, "xo", 2, [128, 1024], F32)
        c.pn_tmp = rot(c, es, "pnt", 2, [128, 1024], F32)
        def ld_c3(ti):
            r0 = ti * 128
            xt = xts.next()
            fw.dma("sp", xt[:, :], xin.rows(r0, r0 + 128))
            oT = oTs.next()
            fw.dma("sp", oT[:, :, :], OT.view(OTd[:, :, r0:r0 + 128].rearrange("h r s -> r h s"), 0, 1024))
            return xt, oT

        for ti, (xt, oT) in prefetched(NT, ld_c3):
            r0 = ti * 128
            psY = [c.ps.next(), c.ps.next()]
            for nh in range(2):
                for kc in range(8):
                    fw.mm(psY[nh][:, :], oT[:, kc, :], wo[:, kc, nh * 512:(nh + 1) * 512], kc == 0, kc == 7)
            postnorm_store(c, psY[0], psY[1], xt, G, xos.next(), sts.next(), xout.rows(r0, r0 + 128))
        fw.barrier()


def mixer_c(c, li, xin, xout):
    fw = c.fw
    nc = c.nc
    TB = 512
    NB = S // TB
    H = 16
    QTd = nc.dram_tensor("c_qt", [H, 96, S], BF16, kind="Internal").ap()
    KTd = nc.dram_tensor("c_kt", [H, 96, S], BF16, kind="Internal").ap()
    Vd = nc.dram_tensor("c_v", [S, 1024], BF16, kind="Internal").ap()
    OTd = nc.dram_tensor("c_ot", [8, 128, S], BF16, kind="Internal").ap()
    QT = DramT(QTd.rearrange("h r s -> (h r) s"), blk=96)
    KT = DramT(KTd.rearrange("h r s -> (h r) s"), blk=96)
    VD = DramT(Vd)
    OT = DramT(OTd.rearrange("h r s -> (h r) s"), blk=128)
    with ExitStack() as es:
        win = sb(c, es, "c_win", [128, 8, 672], BF16)
        wkp = sb(c, es, "c_wkp", [128, 8, 96], BF16)
        wkp2 = sb(c, es, "c_wkp2", [128, 8, 96], BF16)
        wq = sb(c, es, "c_wq", [128, 3, 1536], BF16)
        wq2 = sb(c, es, "c_wq2", [128, 3, 1536], BF16)
        wkv = sb(c, es, "c_wkv", [128, 2, 2048], BF16)
        invf = sb(c, es, "c_invf", [128, 1], F32)
        gq = sb(c, es, "c_gq", [128, 3], F32)
        gkv = sb(c, es, "c_gkv", [128, 2], F32)
        fw.dma("sp", invf[:, :], RO(c.d["k_invf_c"]))
        with ExitStack() as t2:
            load_cols(c, t2, gq[:, :], c.d["c_q_norm"].rearrange("(c p) -> c p", p=128), 3)
            load_cols(c, t2, gkv[:, :], c.d["c_kv_norm"].rearrange("(c p) -> c p", p=128), 2)
            load_w(c, t2, win, c.d["c_w_in"], D, 672, gcol=gcol(c, li, 0))
            load_w(c, t2, wq, c.d["c_w_uq"], 384, 1536, gcol=gq)
            load_w(c, t2, wkv, c.d["c_w_ukv"], 256, 2048, gcol=gkv)
            fw.memset("pool", wkp[:, :, :], 0.0)
            fw.memset("pool", wkp2[:, :, :], 0.0)
            fw.memset("pool", wq2[:, :, :], 0.0)
            fw.cp("pool", wkp[:, :, 64:96], win[:, :, 640:672])
            fw.cp("pool", wkp2[:, :, 80:96], win[:, :, 640:656])
            fw.ts("pool", wkp2[:, :, 64:80], win[:, :, 656:672], -1.0)
            wq4 = wq.t[:, :, :].rearrange("p k (h e) -> p k h e", h=H)
            wq24 = wq2.t[:, :, :].rearrange("p k (h e) -> p k h e", h=H)
            for kc in range(3):
                fw.cp("pool", V([wq2], wq24[:, kc, :, 80:96]), V([wq], wq4[:, kc, :, 64:80]))
                fw.ts("pool", V([wq2], wq24[:, kc, :, 64:80]), V([wq], wq4[:, kc, :, 80:96]), -1.0)
            fw.barrier()
        xts = rot(c, es, "xt", 8, [128, 1024], F32)
        hns = rot(c, es, "hn", 2, [128, 1024], BF16)
        sts = rot(c, es, "st", 4, [128, 16], F32)
        hnTs = rot(c, es, "hnT", 2, [128, 8, TB], BF16)
        cqTs = rot(c, es, "cqT", 2, [128, 5, TB], BF16)
        cns = rot(c, es, "cn", 2, [128, 640], BF16)
        pis = rot(c, es, "pi", 1, [128, TB], I32)
        pfs = rot(c, es, "pf", 1, [128, TB], F32)
        angs = rot(c, es, "ang", 1, [128, TB], F32)
        CTs = rot(c, es, "CT", 2, [128, TB], F32)
        STs = rot(c, es, "ST", 2, [128, TB], F32)
        us = rot(c, es, "u", 1, [128, TB], F32)
        nis = rot(c, es, "ni", 1, [128, TB], I32)
        nfs = rot(c, es, "nf", 1, [128, TB], F32)
        rs = rot(c, es, "r", 1, [128, TB], F32)
        kpes = rot(c, es, "kpe", 2, [128, TB], BF16)
        t1s = rot(c, es, "t1", 3, [128, TB], F32)
        t2s = rot(c, es, "t2", 3, [128, TB], F32)
        qst = rot(c, es, "qst", 4, [128, TB], BF16)
        kst = rot(c, es, "kst", 4, [128, TB], BF16)
        vst = rot(c, es, "vst", 3, [128, 1024], BF16)
        wkv4 = wkv.t[:, :, :].rearrange("p k (h e) -> p k h e", h=H)
        for blk in range(NB):
            t0 = blk * TB
            hnT = hnTs.next()
            cqT = cqTs.next()
            pi, pf, ang = pis.next(), pfs.next(), angs.next()
            fw.dma("sp", pi[:, :], RO(c.d["positions"][t0:t0 + TB].partition_broadcast(128)))
            fw.cp("pool", pf[:, :], pi[:, :])
            fw.ts("dve", ang[:, :], pf[:, :], invf[:, 0:1])
            CT, ST = CTs.next(), STs.next()
            u, ni, nfl, r = us.next(), nis.next(), nfs.next(), rs.next()
            range_reduce_sin(c, ST[:, :], ang[:, :], 0.0, u[:, :], ni[:, :], nfl[:, :], r[:, :])
            range_reduce_sin(c, CT[:, :], ang[:, :], 0.25, u[:, :], ni[:, :], nfl[:, :], r[:, :])
            if blk == 0:
                xnext = []
                for ti in range(TB // 128):
                    xt = xts.next()
                    fw.dma("sp", xt[:, :], xin.rows(ti * 128, ti * 128 + 128))
                    xnext.append(xt)
            xcur = xnext
            xnext = []
            if blk + 1 < NB:
                for ti in range(TB // 128):
                    r1 = t0 + TB + ti * 128
                    xt = xts.next()
                    fw.dma("sp", xt[:, :], xin.rows(r1, r1 + 128))
                    xnext.append(xt)
            for ti in range(TB // 128):
                r0 = t0 + ti * 128
                xt = xcur[ti]
                hn = hns.next()
                st = sts.next()
                norm_tile(c, xt, hn, st)
                transpose_to(c, hn, hnT, ti * 128)
                psA, psB = c.ps.next(), c.ps.next()
                for kc in range(8):
                    fw.mm(psA[:, 0:384], hnT[:, kc, ti * 128:(ti + 1) * 128], win[:, kc, 0:384], kc == 0, kc == 7)
                for kc in range(8):
                    fw.mm(psB[:, 0:256], hnT[:, kc, ti * 128:(ti + 1) * 128], win[:, kc, 384:640], kc == 0, kc == 7)
                st2 = sts.next()
                fw.memset("pool", st2[:, 0:2], 0.0)
                fw.act(c.junk[:, 0:384], psA[:, 0:384], AF.Square, accum=st2[:, 0:1])
                fw.act(c.junk[:, 384:640], psB[:, 0:256], AF.Square, accum=st2[:, 1:2])
                fw.ts("dve", st2[:, 2:3], st2[:, 0:1], 1.0 / 384, EPS, ALU.mult, ALU.add)
                fw.ts("dve", st2[:, 3:4], st2[:, 1:2], 1.0 / 256, EPS, ALU.mult, ALU.add)
                fw.act(st2[:, 4:6], st2[:, 2:4], AF.Sqrt)
                fw.recip(st2[:, 6:8], st2[:, 4:6])
                cn = cns.next()
                fw.act(cn[:, 0:384], psA[:, 0:384], AF.Copy, scale=st2[:, 6:7])
                fw.act(cn[:, 384:640], psB[:, 0:256], AF.Copy, scale=st2[:, 7:8])
                transpose_to(c, cn, cqT, ti * 128, nchunk=5)
            psK, psK2 = c.ps.next(), c.ps.next()
            for kc in range(8):
                fw.mm(psK[0:96, :], wkp[:, kc, :], hnT[:, kc, :], kc == 0, kc == 7)
            for kc in range(8):
                fw.mm(psK2[0:96, :], wkp2[:, kc, :], hnT[:, kc, :], kc == 0, kc == 7)
            t1, t2 = t1s.next(), t2s.next()
            kpe = kpes.next()
            fw.tt("dve", t1[0:96, :], psK[0:96, :], CT[0:96, :], ALU.mult)
            fw.tt("dve", t2[0:96, :], psK2[0:96, :], ST[0:96, :], ALU.mult)
            fw.tt("pool", kpe[0:96, :], t1[0:96, :], t2[0:96, :], ALU.add)
            for h in range(H):
                psQ, psQ2, psKn = c.ps.next(), c.ps.next(), c.ps.next()
                for kc in range(3):
                    fw.mm(psQ[0:96, :], wq[:, kc, h * 96:(h + 1) * 96], cqT[:, kc, :], kc == 0, kc == 2)
                for kc in range(3):
                    fw.mm(psQ2[0:96, :], wq2[:, kc, h * 96:(h + 1) * 96], cqT[:, kc, :], kc == 0, kc == 2)
                for kc in range(2):
                    fw.mm(psKn[0:64, :], V([wkv], wkv4[:, kc, h, 0:64]), cqT[:, 3 + kc, :], kc == 0, kc == 1)
                t1, t2 = t1s.next(), t2s.next()
                qs = qst.next()
                fw.tt("dve", t1[0:96, :], psQ[0:96, :], CT[0:96, :], ALU.mult)
                fw.tt("dve", t2[0:96, :], psQ2[0:96, :], ST[0:96, :], ALU.mult)
                fw.tt("pool", qs[0:96, :], t1[0:96, :], t2[0:96, :], ALU.add)
                fw.dma("sp", QT.view(QTd[h, :, t0:t0 + TB], h * 96, (h + 1) * 96), qs[0:96, :])
                ks = kst.next()
                fw.cp("act", ks[0:64, :], psKn[0:64, :])
                fw.cp("pool", ks[64:96, :], kpe[64:96, :])
                fw.dma("sp", KT.view(KTd[h, :, t0:t0 + TB], h * 96, (h + 1) * 96), ks[0:96, :])
            for ti in range(TB // 128):
                r0 = t0 + ti * 128
                vs = vst.next()
                for hf in range(2):
                    ps = c.ps.next()
                    for kc in range(2):
                        fw.mm(V([ps], ps.t[:, :].rearrange("p (h e) -> p h e", h=8)),
                              cqT[:, 3 + kc, ti * 128:(ti + 1) * 128],
                              V([wkv], wkv4[:, kc, hf * 8:(hf + 1) * 8, 64:128]), kc == 0, kc == 1)
                    fw.cp(("act", "dve")[hf], vs[:, hf * 512:(hf + 1) * 512], ps[:, :])
                fw.dma("sp", VD.rows(r0, r0 + 128), vs[:, :])
        fw.barrier()
    with ExitStack() as es:
        qbufs = rot(c, es, "qh", 2, [128, S], BF16)
        kbufs = rot(c, es, "kh", 2, [128, S], BF16)
        vbufs = rot(c, es, "vh", 2, [128, NT, 80], BF16)
        for vb in vbufs.bufs:
            fw.memset("pool", vb[:, :, 64:65], 1.0)
        masks = sb(c, es, "cmask", [128, 4, TB], BF16)
        sel = sb(c, es, "csel", [128, 64], BF16)
        with ExitStack() as t2:
            mt = sb(c, t2, "mt", [128, TB], F32)
            for j in range(4):
                fw.memset("pool", mt[:, :], 1.0)
                fw.asel(mt[:, :], mt[:, :], [[1, TB]], ALU.is_ge, 0.0, -128 * j, -1)
                fw.cp("pool", masks[:, j, :], mt[:, :])
            fw.memset("pool", mt[:, 0:64], 0.0)
            fw.memset("pool", mt[64:96, 0:64], 1.0)
            fw.cp("pool", sel[:, :], mt[:, 0:64])
            fw.barrier()
        pes = rot(c, es, "pe", 5, [128, TB], BF16)
        pms = rot(c, es, "pm", 4, [128, TB], BF16)
        oas = rot(c, es, "oa", 2, [128, TB], BF16)
        ofs = rot(c, es, "of", 2, [128, TB], F32)
        rds = rot(c, es, "rd", 2, [128, TB], F32)
        ons = rot(c, es, "on", 3, [128, TB], BF16)
        scale = 96.0 ** -0.5
        def ld_c2(h):
            qh, kh, vh = qbufs.next(), kbufs.next(), vbufs.next()
            for q4 in range(4):
                cs = slice(q4 * 2048, (q4 + 1) * 2048)
                fw.dma("sp", qh[0:96, cs], QT.view(QTd[h, :, cs], h * 96, (h + 1) * 96))
                fw.dma("sp", kh[0:96, cs], KT.view(KTd[h, :, cs], h * 96, (h + 1) * 96))
            for q4 in range(4):
                fw.dma("sp", vh[:, q4 * 16:(q4 + 1) * 16, 0:64],
                       VD.view(Vd[q4 * 2048:(q4 + 1) * 2048, h * 64:(h + 1) * 64].rearrange("(t p) e -> p t e", p=128),
                               q4 * 2048, (q4 + 1) * 2048))
            return qh, kh, vh

        for h, (qh, kh, vh) in prefetched(H, ld_c2):
            for qb in range(NB):
                acc = c.psx.next()
                nk = 4 * qb + 4
                LOOK = 2
                pend = []
                for kt in range(nk + LOOK):
                    if kt < nk:
                        ps = c.ps.next()
                        fw.mm(ps[:, :], kh[0:96, kt * 128:(kt + 1) * 128], qh[0:96, qb * TB:(qb + 1) * TB])
                        pe = pes.next()
                        fw.act(pe[:, :], ps[:, :], AF.Exp, scale=scale)
                        if kt >= 4 * qb:
                            pm = pms.next()
                            fw.tt(("dve", "pool")[kt % 2], pm[:, :], pe[:, :], masks[:, kt - 4 * qb, :], ALU.mult)
                            pe = pm
                        pend.append((kt, pe))
                    if kt >= LOOK:
                        k0, pe0 = pend.pop(0)
                        fw.mm(acc[0:65, :], vh[:, k0, 0:65], pe0[:, :], k0 == 0, k0 == nk - 1)
                oa = oas.next()
                of = ofs.next()
                fw.cp("act", oa[0:65, :], acc[0:65, :])
                fw.cp("dve", of[0:64, :], acc[0:64, :])
                psd = c.ps.next()
                fw.mm(psd[0:64, :], sel[0:65, :], oa[0:65, :])
                rd = rds.next()
                fw.recip(rd[0:64, :], psd[0:64, :])
                on = ons.next()
                fw.tt("pool", on[0:64, :], of[0:64, :], rd[0:64, :], ALU.mult)
                hp, hh = h // 2, h % 2
                fw.dma("sp", OT.view(OTd[hp, hh * 64:(hh + 1) * 64, qb * TB:(qb + 1) * TB], hp * 128, (hp + 1) * 128),
                       on[0:64, :])
        fw.barrier()
    with ExitStack() as es:
        wo = sb(c, es, "c_wo", [128, 8, 1024], BF16)
        G = sb(c, es, "G", [128, 1024], F32)
        bcast(c, G[:, :], c.d["ln_gains"][li, 1])
        with ExitStack() as t2:
            load_w(c, t2, wo, c.d["c_w_o"], 1024, 1024)
            fw.barrier()
        xts = rot(c, es, "xt", 4, [128, 1024], F32)
        oTs = rot(c, es, "oT", 4, [128, 8, 128], BF16)
        sts = rot(c, es, "st", 4, [128, 16], F32)
        xos = rot(c, es, "xo", 2, [128, 1024], F32)
        c.pn_tmp = rot(c, es, "pnt", 2, [128, 1024], F32)
        def ld_c3(ti):
            r0 = ti * 128
            xt = xts.next()
            fw.dma("sp", xt[:, :], xin.rows(r0, r0 + 128))
            oT = oTs.next()
            fw.dma("sp", oT[:, :, :], OT.view(OTd[:, :, r0:r0 + 128].rearrange("h r s -> r h s"), 0, 1024))
            return xt, oT

        for ti, (xt, oT) in prefetched(NT, ld_c3):
            r0 = ti * 128
            psY = [c.ps.next(), c.ps.next()]
            for nh in range(2):
                for kc in range(8):
                    fw.mm(psY[nh][:, :], oT[:, kc, :], wo[:, kc, nh * 512:(nh + 1) * 512], kc == 0, kc == 7)
            postnorm_store(c, psY[0], psY[1], xt, G, xos.next(), sts.next(), xout.rows(r0, r0 + 128))
        fw.barrier()


def mixer_b(c, li, xin, xout):
    fw = c.fw
    TB = 512
    NB = S // TB
    with ExitStack() as es:
        w = sb(c, es, "b_w", [128, 8, 3088], BF16)
        wo = sb(c, es, "b_wo", [128, 8, 1024], BF16)
        wg2 = sb(c, es, "b_wg2", [16, 512], F32)
        bb = sb(c, es, "b_bias", [128, 512], F32)
        onb = sb(c, es, "b_onb", [128, 256], F32)
        G = sb(c, es, "G", [128, 1024], F32)
        triU = sb(c, es, "b_triU", [128, 128], F32)
        triS = sb(c, es, "b_triS", [128, 128], F32)
        maskU = sb(c, es, "b_maskU", [128, 4, 128], F32)
        S32 = sb(c, es, "b_S32", [128, 4, 256], F32)
        Sbf = sb(c, es, "b_Sbf", [128, 4, 256], BF16)
        bcast(c, G[:, :], c.d["ln_gains"][li, 1])
        bcast(c, bb[:, :], c.d["b_gate_bias"])
        bcast(c, onb[:, :], c.d["b_o_norm"])
        fw.dma("sp", wg2[:, :], RO(c.d["b_w_gate2"]))
        fw.memset("pool", S32[:, :, :], 0.0)
        fw.memset("pool", Sbf[:, :, :], 0.0)
        fw.memset("pool", triU[:, :], -1.0 / 16)
        fw.asel(triU[:, :], triU[:, :], [[1, 128]], ALU.is_ge, 0.0, 0, -1)
        fw.memset("pool", triS[:, :], -1.0 / 16)
        fw.asel(triS[:, :], triS[:, :], [[-1, 128]], ALU.is_gt, 0.0, 0, 1)
        fw.memset("pool", maskU[:, :, :], 1.0)
        for h in range(4):
            fw.asel(maskU[:, h, :], maskU[:, h, :], [[1, 128]], ALU.is_ge, 0.0, 0, -1)
        with ExitStack() as t2:
            load_w(c, t2, w, c.d["b_w_in"], D, 3088, gcol=gcol(c, li, 0))
            load_w(c, t2, wo, c.d["b_w_o"], 1024, 1024)
            fw.barrier()
        xts = rot(c, es, "xt", 8, [128, 1024], F32)
        hns = rot(c, es, "hn", 2, [128, 1024], BF16)
        sts = rot(c, es, "st", 6, [128, 16], F32)
        hnTs = rot(c, es, "hnT", 1, [128, 8, TB], BF16)
        qkTs = rot(c, es, "qkT", 1, [128, 8, TB], BF16)
        glTs = rot(c, es, "glT", 2, [16, TB], F32)
        zs = rot(c, es, "z", 2, [128, 512], F32)
        Ls = rot(c, es, "L", 2, [128, 512], F32)
        EGs = rot(c, es, "EG", 2, [128, 4, 128], F32)
        EnGs = rot(c, es, "EnG", 2, [128, 4, 128], F32)
        EGcs = rot(c, es, "EGc", 2, [128, 512], F32)
        qds = rot(c, es, "qd", 2, [128, 4, 128], BF16)
        kis = rot(c, es, "ki", 2, [128, 4, 128], BF16)
        kes = rot(c, es, "ke", 2, [128, 512], BF16)
        vs_ = rot(c, es, "v", 2, [128, 1024], BF16)
        ats = rot(c, es, "at", 2, [128, 4, 128], BF16)
        gss = rot(c, es, "gs", 1, [128, 1024], F32)
        ons = rot(c, es, "on", 1, [128, 1024], F32)
        obs = rot(c, es, "ob", 1, [128, 1024], BF16)
        obTs = rot(c, es, "obT", 2, [128, 8, 128], BF16)
        xos = rot(c, es, "xo", 2, [128, 1024], F32)
        c.pn_tmp = None
        for blk in range(NB):
            t0 = blk * TB
            hnT = hnTs.next()
            if blk == 0:
                xnext = []
                for ti in range(4):
                    xt = xts.next()
                    fw.dma("sp", xt[:, :], xin.rows(ti * 128, ti * 128 + 128))
                    xnext.append(xt)
            xl = xnext
            xnext = []
            if blk + 1 < NB:
                for ti in range(4):
                    r1 = t0 + TB + ti * 128
                    xt = xts.next()
                    fw.dma("sp", xt[:, :], xin.rows(r1, r1 + 128))
                    xnext.append(xt)
            for ti in range(4):
                hn = hns.next()
                norm_tile(c, xl[ti], hn, sts.next())
                transpose_to(c, hn, hnT, ti * 128)
            qkT = qkTs.next()
            for j in range(8):
                ps = c.ps.next()
                for kc in range(8):
                    fw.mm(ps[:, :], w[:, kc, j * 128:(j + 1) * 128], hnT[:, kc, :], kc == 0, kc == 7)
                fw.cp(("act", "dve")[j % 2], qkT[:, j, :], ps[:, :])
            glT = glTs.next()
            ps = c.ps.next()
            for kc in range(8):
                fw.mm(ps[0:16, :], w[:, kc, 3072:3088], hnT[:, kc, :], kc == 0, kc == 7)
            fw.cp("act", glT[:, :], ps[0:16, :])
            for ti in range(4):
                r0 = t0 + ti * 128
                tsl = slice(ti * 128, (ti + 1) * 128)
                psZ = c.ps.next()
                fw.mm(psZ[:, :], glT[:, tsl], wg2[:, :])
                z = zs.next()
                fw.tt("dve", z[:, :], psZ[:, :], bb[:, :], ALU.add)
                L = Ls.next()
                fw.act(z[:, :], z[:, :], AF.Exp, scale=-1.0)
                fw.act(L[:, :], z[:, :], AF.Ln, bias=1.0)
                psG = c.ps.next()
                for h in range(4):
                    fw.mm(psG[:, h * 128:(h + 1) * 128], L[:, h * 128:(h + 1) * 128], triU[:, :])
                EG, EnG = EGs.next(), EnGs.next()
                pg3 = V([psG], psG.t[:, :].rearrange("p (h t) -> p h t", h=4))
                fw.act(EG[:, :, :], pg3, AF.Exp)
                fw.act(EnG[:, :, :], pg3, AF.Exp, scale=-1.0)
                psGc = c.ps.next()
                fw.mm(psGc[:, :], triS[:, :], L[:, :])
                EGc = EGcs.next()
                fw.act(EGc[:, :], psGc[:, :], AF.Exp)
                qd, ki = qds.next(), kis.next()
                fw.stt("dve", qd[:, :, :], qkT[:, 0:4, tsl], 128.0 ** -0.5, EG[:, :, :], ALU.mult, ALU.mult)
                fw.tt("pool", ki[:, :, :], qkT[:, 4:8, tsl], EnG[:, :, :], ALU.mult)
                psK = c.ps.next()
                for kc in range(8):
                    fw.mm(psK[:, :], hnT[:, kc, tsl], w[:, kc, 512:1024], kc == 0, kc == 7)
                ke = kes.next()
                fw.tt("dve", ke[:, :], psK[:, :], EGc[:, :], ALU.mult)
                v = vs_.next()
                for hf in range(2):
                    ps = c.ps.next()
                    for kc in range(8):
                        fw.mm(ps[:, :], hnT[:, kc, tsl], w[:, kc, 1024 + hf * 512:1536 + hf * 512], kc == 0, kc == 7)
                    fw.cp(("act", "dve")[hf], v[:, hf * 512:(hf + 1) * 512], ps[:, :])
                psA = c.ps.next()
                for h in range(4):
                    fw.mm(psA[:, h * 128:(h + 1) * 128], ki[:, h, :], qd[:, h, :])
                at = ats.next()
                fw.tt("dve", at[:, :, :], V([psA], psA.t[:, :].rearrange("p (h t) -> p h t", h=4)), maskU[:, :, :],
                      ALU.mult)
                pO = [c.psx.next(), c.psx.next()]
                for h in range(4):
                    ov = pO[h // 2][:, (h % 2) * 256:(h % 2 + 1) * 256]
                    fw.mm(ov, at[:, h, :], v[:, h * 256:(h + 1) * 256], True, False)
                    fw.mm(ov, qd[:, h, :], Sbf[:, h, :], False, True)
                gs = gss.next()
                for hf in range(2):
                    ps = c.ps.next()
                    for kc in range(8):
                        fw.mm(ps[:, :], hnT[:, kc, tsl], w[:, kc, 2048 + hf * 512:2560 + hf * 512], kc == 0, kc == 7)
                    fw.act(gs[:, hf * 512:(hf + 1) * 512], ps[:, :], AF.Silu)
                for hf in range(2):
                    ps = c.ps.next()
                    for hh in range(2):
                        h = hf * 2 + hh
                        fw.mm(ps[:, hh * 256:(hh + 1) * 256], ke[:, h * 128:(h + 1) * 128], v[:, h * 256:(h + 1) * 256])
                    for hh in range(2):
                        h = hf * 2 + hh
                        fw.stt("dve", S32[:, h, :], S32[:, h, :], EG[:, h, 127:128], ps[:, hh * 256:(hh + 1) * 256],
                               ALU.mult, ALU.add)
                        fw.cp("act", Sbf[:, h, :], S32[:, h, :])
                st = sts.next()
                fw.memset("pool", st[:, 0:4], 0.0)
                for h in range(4):
                    fw.act(c.junk[:, h * 256:(h + 1) * 256], pO[h // 2][:, (h % 2) * 256:(h % 2 + 1) * 256], AF.Square,
                           accum=st[:, h:h + 1])
                fw.ts("dve", st[:, 4:8], st[:, 0:4], 1.0 / 256, EPS, ALU.mult, ALU.add)
                fw.act(st[:, 8:12], st[:, 4:8], AF.Sqrt)
                fw.recip(st[:, 12:16], st[:, 8:12])
                on = ons.next()
                for h in range(4):
                    fw.stt("dve", on[:, h * 256:(h + 1) * 256], pO[h // 2][:, (h % 2) * 256:(h % 2 + 1) * 256],
                           st[:, 12 + h:13 + h], onb[:, :], ALU.mult, ALU.mult)
                ob = obs.next()
                fw.tt("pool", ob[:, :], on[:, :], gs[:, :], ALU.mult)
                obT = obTs.next()
                transpose_to(c, ob, obT, 0)
                psY = [c.ps.next(), c.ps.next()]
                for nh in range(2):
                    for kc in range(8):
                        fw.mm(psY[nh][:, :], obT[:, kc, :], wo[:, kc, nh * 512:(nh + 1) * 512], kc == 0, kc == 7)
                postnorm_store(c, psY[0], psY[1], xl[ti], G, xos.next(), sts.next(), xout.rows(r0, r0 + 128))
        fw.barrier()


EM05 = float(np.exp(-0.5))


def block_mask(c, dst, kind, val, CH=32):
    fw = c.fw
    fw.memset("pool", dst, val)
    for cc in range(128 // CH):
        v = dst[:, cc * CH:(cc + 1) * CH]
        lo = cc * CH
        if kind == "IU":
            fw.asel(v, v, [[1, CH]], ALU.is_ge, 0.0, lo, -1)
            fw.asel(v, v, [[0, CH]], ALU.is_ge, 0.0, -lo, 1)
        elif kind == "SU":
            fw.asel(v, v, [[1, CH]], ALU.is_gt, 0.0, lo, -1)
            fw.asel(v, v, [[0, CH]], ALU.is_ge, 0.0, -lo, 1)
        else:
            fw.asel(v, v, [[-1, CH]], ALU.is_gt, 0.0, -lo, 1)
            fw.asel(v, v, [[0, CH]], ALU.is_ge, 0.0, lo + CH - 1, -1)


def chunk_ind(c, dst, val, CH=32):
    fw = c.fw
    fw.memset("pool", dst, val)
    for cc in range(128 // CH):
        v = dst[:, cc:cc + 1]
        fw.asel(v, v, [[0, 1]], ALU.is_ge, 0.0, -cc * CH, 1)
        fw.asel(v, v, [[0, 1]], ALU.is_ge, 0.0, cc * CH + CH - 1, -1)


def mixer_d(c, li, xin, xout):
    import os
    fw = c.fw
    nc = c.nc
    H = 16
    NC = 4
    names = ["At", "Rt", "Bh", "Kh", "Bt", "Kt", "Vv", "BON", "GATE"]
    dd = {n: DramT(nc.dram_tensor("d_" + n, [S, 1024], BF16, kind="Internal").ap()) for n in names}
    GLd_ap = nc.dram_tensor("d_GL", [NT * 64, 64], F32, kind="Internal").ap()
    GLd = DramT(GLd_ap, blk=64)
    Yd = DramT(nc.dram_tensor("d_Y", [S, 1024], F32, kind="Internal").ap())
    dstage = int(os.environ.get("D_STAGE", "9"))
    with ExitStack() as es:
        Wr = sb(c, es, "d_Wr", [128, 8, 1024], BF16)
        Wk = sb(c, es, "d_Wk", [128, 8, 1024], BF16)
        Wv = sb(c, es, "d_Wv", [128, 8, 1024], BF16)
        w1 = sb(c, es, "d_w1", [128, 8, 64], BF16)
        a1 = sb(c, es, "d_a1", [128, 8, 64], BF16)
        g1 = sb(c, es, "d_g1", [128, 8, 128], BF16)
        w2 = sb(c, es, "d_w2", [64, 1, 1024], BF16)
        a2 = sb(c, es, "d_a2", [64, 1, 1024], BF16)
        g2 = sb(c, es, "d_g2", [128, 1, 1024], BF16)
        mixc = sb(c, es, "d_mix", [128, 48], F32)
        bc = {}
        for n in ("d_w0", "d_a0", "d_k_k", "d_k_a"):
            bc[n] = sb(c, es, n + "b", [128, 1024], F32)
            bcast(c, bc[n][:, :], c.d[n])
        rkb = sb(c, es, "d_rkb", [128, 1024], F32)
        bcast(c, rkb[:, :], c.d["d_r_k"].rearrange("h e -> (h e)"))
        omka = sb(c, es, "d_omka", [128, 1024], F32)
        fw.ts("pool", omka[:, :], bc["d_k_a"][:, :], -1.0, 1.0, ALU.mult, ALU.add)
        triC = sb(c, es, "d_triC", [128, 128], F32)
        triD = sb(c, es, "d_triD", [128, 128], F32)
        indC = sb(c, es, "d_indC", [128, NC], F32)
        block_mask(c, triC[:, :], "IU", -EM05)
        block_mask(c, triD[:, :], "SL", -EM05)
        chunk_ind(c, indC[:, :], -EM05)
        hz = sb(c, es, "d_hz", [128, 8, 128], BF16)
        fw.memset("pool", hz[:, :, :], 0.0)
        with ExitStack() as t2:
            load_cols(c, t2, mixc[:, :], c.d["d_mix"].rearrange("i (c p) -> (i c) p", p=128), 48)
            gc = gcol(c, li, 0)
            load_w(c, t2, Wr, c.d["d_w_rkv"][0], D, 1024, gcol=gc)
            load_w(c, t2, Wk, c.d["d_w_rkv"][1], D, 1024, gcol=gc)
            load_w(c, t2, Wv, c.d["d_w_rkv"][2], D, 1024, gcol=gc)
            load_w(c, t2, w1, c.d["d_w1"], D, 64, gcol=gc)
            load_w(c, t2, a1, c.d["d_a1"], D, 64, gcol=gc)
            load_w(c, t2, g1, c.d["d_g1"], D, 128, gcol=gc)
            for dst, src, kk_ in ((w2, "d_w2", 64), (a2, "d_a2", 64), (g2, "d_g2", 128)):
                st_ = sb(c, t2, "wst", [128, 1024], F32)
                fw.dma("sp", st_[0:kk_, :], RO(c.d[src]))
                fw.cp("dve", dst[0:kk_, 0, :], st_[0:kk_, :])
            fw.barrier()
        xts = rot(c, es, "xt", 3, [128, 1024], F32)
        hns = rot(c, es, "hn", 2, [128, 1024], BF16)
        sts = rot(c, es, "st", 4, [128, 16], F32)
        hnTs = rot(c, es, "hnT", 2, [128, 8, 128], BF16)
        DTs = rot(c, es, "DT", 1, [128, 8, 128], BF16)
        Xs = rot(c, es, "X", 7, [128, 8, 128], BF16)
        smT = rot(c, es, "smT", 4, [128, 128], BF16)
        f32s = rot(c, es, "f", 9, [128, 1024], F32)
        b16s = rot(c, es, "o", 12, [128, 1024], BF16)
        s16 = rot(c, es, "s16", 4, [128, 64], F32)
        glts = rot(c, es, "glt", 2, [64, 64], F32)
        prev = hz

        def v3(buf):
            return V([buf], buf.t[:, :].rearrange("p (h e) -> p h e", h=H))

        def b3(vw):
            return V(vw.bufs, vw.ap.unsqueeze(2).to_broadcast([128, H, 64]))

        def ld_d1(ti):
            xt = xts.next()
            fw.dma("sp", xt[:, :], xin.rows(ti * 128, ti * 128 + 128))
            return xt

        for ti, xt in prefetched(NT if dstage >= 1 else 0, ld_d1):
            r0 = ti * 128
            hn = hns.next()
            norm_tile(c, xt, hn, sts.next())
            cur = hnTs.next()
            transpose_to(c, hn, cur, 0)
            DT = DTs.next()
            fw.tt("pool", DT[:, :, 0:1], prev[:, :, 127:128], cur[:, :, 0:1], ALU.subtract)
            fw.tt("pool", DT[:, :, 1:128], cur[:, :, 0:127], cur[:, :, 1:128], ALU.subtract)
            X = []
            for i in range(6):
                Xi = Xs.next()
                for kc in range(8):
                    mcol = mixc[:, i * 8 + kc:i * 8 + kc + 1]
                    if (i * 8 + kc) % 3 != 2:
                        fw.stt("dve", Xi[:, kc, :], DT[:, kc, :], mcol, cur[:, kc, :], ALU.mult, ALU.add)
                    else:
                        fw.act(Xi[:, kc, :], DT[:, kc, :], AF.Copy, scale=mcol)
                        fw.tt("pool", Xi[:, kc, :], Xi[:, kc, :], cur[:, kc, :], ALU.add)
                X.append(Xi)
            Xr, Xw, Xk, Xv, Xa, Xg = X
            prev = cur

            def proj2(Xi, W, nh):
                ps = c.ps.next()
                for kc in range(8):
                    fw.mm(ps[:, :], Xi[:, kc, :], W[:, kc, nh * 512:(nh + 1) * 512], kc == 0, kc == 7)
                return ps

            def low(Xi, Wl, M, func):
                ps = c.ps.next()
                for kc in range(8):
                    fw.mm(ps[0:M, 0:128], Wl[:, kc, :], Xi[:, kc, :], kc == 0, kc == 7)
                o = smT.next()
                fw.act(o[0:M, :], ps[0:M, 0:128], func)
                return o

            twT = low(Xw, w1, 64, AF.Tanh)
            aaT = low(Xa, a1, 64, AF.Copy)
            ggT = low(Xg, g1, 128, AF.Sigmoid)
            sig = f32s.next()
            av = f32s.next()
            for nh in range(2):
                cs_ = slice(nh * 512, (nh + 1) * 512)
                ps = c.ps.next()
                fw.mm(ps[:, :], twT[0:64, :], w2[0:64, 0, cs_])
                fw.tt("dve", sig[:, cs_], ps[:, :], bc["d_w0"][:, cs_], ALU.add)
                ps = c.ps.next()
                fw.mm(ps[:, :], aaT[0:64, :], a2[0:64, 0, cs_])
                fw.tt("dve", av[:, cs_], ps[:, :], bc["d_a0"][:, cs_], ALU.add)
            fw.act(sig[:, :], sig[:, :], AF.Sigmoid)
            fw.act(av[:, :], av[:, :], AF.Sigmoid)
            gate = b16s.next()
            for nh in range(2):
                ps = c.ps.next()
                fw.mm(ps[:, :], ggT[:, :], g2[:, 0, nh * 512:(nh + 1) * 512])
                fw.cp("act", gate[:, nh * 512:(nh + 1) * 512], ps[:, :])
            fw.dma("sp", dd["GATE"].rows(r0, r0 + 128), gate[:, :])
            rr = f32s.next()
            kx = f32s.next()
            vv = f32s.next()
            for nh in range(2):
                cs_ = slice(nh * 512, (nh + 1) * 512)
                fw.cp("act", rr[:, cs_], proj2(Xr, Wr, nh)[:, :])
                fw.cp("act", kx[:, cs_], proj2(Xk, Wk, nh)[:, :])
                fw.cp("act", vv[:, cs_], proj2(Xv, Wv, nh)[:, :])
            Vb = b16s.next()
            fw.cp("act", Vb[:, :], vv[:, :])
            fw.dma("sp", dd["Vv"].rows(r0, r0 + 128), Vb[:, :])
            kkx = f32s.next()
            tmp = f32s.next()
            fw.tt("dve", kkx[:, :], kx[:, :], bc["d_k_k"][:, :], ALU.mult)
            fw.act(tmp[:, :], kkx[:, :], AF.Square)
            sm = s16.next()
            fw.red("dve", sm[:, 0:16], v3(tmp), ALU.add)
            fw.act(sm[:, 16:32], sm[:, 0:16], AF.Sqrt)
            fw.ts("dve", sm[:, 16:32], sm[:, 16:32], 1e-12, None, ALU.max)
            fw.recip(sm[:, 32:48], sm[:, 16:32])
            fw.tt("dve", v3(kkx), v3(kkx), b3(sm[:, 32:48]), ALU.mult)
            fw.tt("pool", tmp[:, :], av[:, :], bc["d_k_a"][:, :], ALU.mult)
            fw.tt("pool", tmp[:, :], tmp[:, :], omka[:, :], ALU.add)
            fw.tt("dve", kx[:, :], kx[:, :], tmp[:, :], ALU.mult)
            fw.tt("dve", tmp[:, :], rr[:, :], rkb[:, :], ALU.mult)
            fw.tt("pool", tmp[:, :], tmp[:, :], kx[:, :], ALU.mult)
            fw.red("dve", sm[:, 48:64], v3(tmp), ALU.add)
            bon = b16s.next()
            fw.tt("dve", v3(bon), v3(vv), b3(sm[:, 48:64]), ALU.mult)
            fw.dma("sp", dd["BON"].rows(r0, r0 + 128), bon[:, :])
            fw.tt("pool", av[:, :], kkx[:, :], av[:, :], ALU.mult)
            E1 = f32s.next()
            E2 = f32s.next()
            E3 = tmp
            E4 = vv
            for nh in range(2):
                cs_ = slice(nh * 512, (nh + 1) * 512)
                ps = c.ps.next()
                fw.mm(ps[:, :], triC[:, :], sig[:, cs_])
                fw.act(E1[:, cs_], ps[:, :], AF.Exp)
                fw.act(E2[:, cs_], ps[:, :], AF.Exp, scale=-1.0)
                fw.stt("dve", E3[:, cs_], sig[:, cs_], EM05, ps[:, :], ALU.mult, ALU.add)
                ps = c.ps.next()
                fw.mm(ps[:, :], triD[:, :], sig[:, cs_])
                fw.act(E4[:, cs_], ps[:, :], AF.Exp)
            fw.act(E3[:, :], E3[:, :], AF.Exp)
            outs = {}
            for n in ("At", "Rt", "Bh", "Kh", "Bt", "Kt"):
                outs[n] = b16s.next()
            fw.stt("dve", outs["At"][:, :], kkx[:, :], -1.0, E3[:, :], ALU.mult, ALU.mult)
            fw.tt("pool", outs["Rt"][:, :], rr[:, :], E1[:, :], ALU.mult)
            fw.tt("dve", outs["Bh"][:, :], av[:, :], E2[:, :], ALU.mult)
            fw.tt("pool", outs["Kh"][:, :], kx[:, :], E2[:, :], ALU.mult)
            fw.tt("dve", outs["Bt"][:, :], av[:, :], E4[:, :], ALU.mult)
            fw.tt("pool", outs["Kt"][:, :], kx[:, :], E4[:, :], ALU.mult)
            for n in ("At", "Rt", "Bh", "Kh", "Bt", "Kt"):
                fw.dma("sp", dd[n].rows(r0, r0 + 128), outs[n][:, :])
            ps = c.ps.next()
            for h in range(H):
                fw.mm(ps[0:64, h * NC:(h + 1) * NC], sig[:, h * 64:(h + 1) * 64], indC[:, :])
            glt = glts.next()
            fw.act(glt[:, :], ps[0:64, 0:64], AF.Exp)
            fw.dma("sp", GLd.rows(ti * 64, ti * 64 + 64), glt[:, :])
        fw.barrier()
    with ExitStack() as es:
        mask1 = sb(c, es, "d_m1", [128, 384], F32)
        mask2 = sb(c, es, "d_m2", [128, 256], F32)
        II = sb(c, es, "d_II", [128, 256], BF16)
        CM = sb(c, es, "d_CM", [128, NC], F32)
        CMb = sb(c, es, "d_CMb", [128, NC], BF16)
        block_mask(c, mask1[:, 0:128], "SU", 1.0)
        block_mask(c, mask1[:, 128:256], "SL", 1.0)
        block_mask(c, mask1[:, 256:384], "SU", 1.0)
        block_mask(c, mask2[:, 0:128], "IU", 1.0)
        block_mask(c, mask2[:, 128:256], "IU", 1.0)
        chunk_ind(c, CM[:, :], 1.0)
        fw.cp("pool", CMb[:, :], CM[:, :])
        fw.cp("pool", II[:, 0:128], c.identb[:, :])
        fw.cp("pool", II[:, 128:256], c.identb[:, :])
        I64 = c.identf[0:64, 0:64]
        GS = 8
        lds = {n: rot(c, es, "l" + n, 3, [128, 1024], BF16) for n in ("At", "Rt", "Bh", "Kh", "Bt", "Kt", "Vv")}
        XTs = rot(c, es, "XT", 1, [64, H, 4, 128], BF16)
        GLs = rot(c, es, "GL", 3, [64, 64], F32)
        NAs = rot(c, es, "NA", GS, [128, 384], BF16)
        RBs = rot(c, es, "RB", GS, [128, 256], BF16)
        MMs = rot(c, es, "MM", 2 * GS, [128, 256], BF16)
        PPs = rot(c, es, "PP", 2 * GS, [128, 256], BF16)
        MFs = rot(c, es, "MF", GS, [128, 128], BF16)
        AWs = rot(c, es, "AW", GS, [128, 128], BF16)
        U0s = rot(c, es, "U0", GS, [128, 64], BF16)
        RcTs = rot(c, es, "RcT", GS, [64, 128], F32)
        Bms = rot(c, es, "Bm", GS, [128, NC, 64], BF16)
        Kms = rot(c, es, "Km", GS, [128, NC, 64], BF16)
        Gds = rot(c, es, "Gd", GS, [64, NC, 64], F32)
        Y0s = rot(c, es, "Y0", GS, [64, 128], F32)
        PhTs = rot(c, es, "PhT", GS, [64, NC, 64], F32)
        PsTs = rot(c, es, "PsT", GS, [64, NC, 64], F32)
        YTs = rot(c, es, "YT", GS, [64, 128], F32)
        Yts = rot(c, es, "Yt", 2, [128, 1024], F32)
        STs = [rot(c, es, "ST%d" % h, 3, [64, 64], F32) for h in range(H)]
        ST = []
        for h in range(H):
            b_ = STs[h].next()
            fw.memset("pool", b_[:, :], 0.0)
            ST.append(b_)
        def ld_d2(ti):
            r0 = ti * 128
            L = {}
            for n in lds:
                L[n] = lds[n].next()
                fw.dma("sp", L[n][:, :], dd[n].rows(r0, r0 + 128))
            GL = GLs.next()
            fw.dma("sp", GL[:, :], GLd.rows(ti * 64, ti * 64 + 64))
            return L, GL

        for ti, (L, GL) in prefetched(NT if dstage >= 2 else 0, ld_d2):
            r0 = ti * 128
            XT = XTs.next()
            for h2 in range(H // 2):
                ps = c.ps.next()
                psb = ps.t[:, :].bitcast(BF16)
                for hh in range(2):
                    h = h2 * 2 + hh
                    for j, n in enumerate(("At", "Rt", "Bh", "Kh")):
                        col = (hh * 4 + j) * 128
                        fw.tr(V([ps], psb[0:64, col:col + 128]), L[n][:, h * 64:(h + 1) * 64], c.identb[:, :])
                fw.cp(("act", "dve")[h2 % 2], XT[:, h2 * 2:h2 * 2 + 2, :, :],
                      V([ps], psb[0:64, :].rearrange("p (h j t) -> p h j t", h=2, j=4)))
            Yt = Yts.next()
            for g0 in range(0, H, GS):
                hs = list(range(g0, g0 + GS))
                NA, RB, MM, PP, MF, AW, U0, RcT, Bm, Km, Gd, Y0, PhT, PsT = ({} for _ in range(14))
                for h in hs:
                    AtT, RtT, BhT, KhT = (XT[:, h, j, :] for j in range(4))
                    b1, b2 = c.ps.next(), c.ps.next()
                    fw.mm(b1[:, 0:128], BhT, AtT)
                    fw.mm(b1[:, 128:256], AtT, BhT)
                    fw.mm(b1[:, 256:384], KhT, AtT)
                    fw.mm(b2[:, 0:128], BhT, RtT)
                    fw.mm(b2[:, 128:256], KhT, RtT)
                    NA[h], RB[h], MM[h] = NAs.next(), RBs.next(), MMs.next()
                    fw.tt("dve", NA[h][:, :], b1[:, 0:384], mask1[:, :], ALU.mult)
                    fw.tt("dve", RB[h][:, :], b2[:, 0:256], mask2[:, :], ALU.mult)
                    fw.tt("pool", MM[h][:, :], NA[h][:, 0:256], II[:, :], ALU.add)
                    PP[h] = NA[h]
                for lev in range(3):
                    for h in hs:
                        P_, PT_ = PP[h][:, 0:128], PP[h][:, 128:256]
                        bp = c.ps.next()
                        fw.mm(bp[:, 0:128], PT_, P_)
                        fw.mm(bp[:, 128:256], P_, PT_)
                        npp = PPs.next()
                        fw.cp("act", npp[:, :], bp[:, 0:256])
                        PP[h] = npp
                    for h in hs:
                        P_ = PP[h][:, 0:128]
                        M_, MT_ = MM[h][:, 0:128], MM[h][:, 128:256]
                        bm = c.ps.next()
                        fw.mm(bm[:, 0:128], MT_, P_)
                        fw.mm(bm[:, 128:256], P_, MT_)
                        nmm = MMs.next()
                        fw.tt("dve", nmm[:, :], bm[:, 0:256], MM[h][:, :], ALU.add)
                        MM[h] = nmm
                for h in hs:
                    bp = c.ps.next()
                    fw.mm(bp[:, 0:128], PP[h][:, 128:256], PP[h][:, 0:128])
                    npp = PPs.next()
                    fw.cp("act", npp[:, 0:128], bp[:, 0:128])
                    PP[h] = npp
                for h in hs:
                    bm = c.ps.next()
                    fw.mm(bm[:, 0:128], MM[h][:, 128:256], PP[h][:, 0:128])
                    MF[h] = MFs.next()
                    fw.tt("dve", MF[h][:, :], bm[:, 0:128], MM[h][:, 0:128], ALU.add)
                for h in hs:
                    hc = slice(h * 64, (h + 1) * 64)
                    b_ = c.ps.next()
                    fw.mm(b_[:, 0:64], MF[h][:, :], L["At"][:, hc])
                    fw.mm(b_[:, 64:128], NA[h][:, 256:384], L["Vv"][:, hc])
                    AW[h] = AWs.next()
                    fw.cp("act", AW[h][:, :], b_[:, 0:128])
                    Bm[h], Km[h], Gd[h] = Bms.next(), Kms.next(), Gds.next()
                    cmb = V([CMb], CMb.t[:, :].unsqueeze(2).to_broadcast([128, NC, 64]))
                    fw.tt("pool", Bm[h][:, :, :], V([L["Bt"]], L["Bt"].t[:, hc].unsqueeze(1).to_broadcast([128, NC, 64])),
                          cmb, ALU.mult)
                    fw.tt("pool", Km[h][:, :, :], V([L["Kt"]], L["Kt"].t[:, hc].unsqueeze(1).to_broadcast([128, NC, 64])),
                          cmb, ALU.mult)
                    fw.tt("pool", Gd[h][:, :, :],
                          V(I64.bufs, I64.ap.unsqueeze(1).to_broadcast([64, NC, 64])),
                          V([GL], GL.t[:, h * NC:(h + 1) * NC].unsqueeze(2).to_broadcast([64, NC, 64])), ALU.mult)
                for h in hs:
                    b_ = c.ps.next()
                    fw.mm(b_[:, 0:64], MF[h][:, :], AW[h][:, 64:128])
                    fw.mm(b_[0:64, 64:192], AW[h][:, 0:64], RB[h][:, 0:128])
                    U0[h], RcT[h] = U0s.next(), RcTs.next()
                    fw.cp("act", U0[h][:, :], b_[:, 0:64])
                    fw.tt("dve", RcT[h][:, :], b_[0:64, 64:192], XT[:, h, 1, :], ALU.add)
                for h in hs:
                    hc = slice(h * 64, (h + 1) * 64)
                    b1, b2 = c.ps.next(), c.ps.next()
                    fw.mm(b1[0:64, 0:128], U0[h][:, :], RB[h][:, 0:128], True, False)
                    fw.mm(b1[0:64, 0:128], L["Vv"][:, hc], RB[h][:, 128:256], False, True)
                    bmv = V([Bm[h]], Bm[h].t[:, :, :].rearrange("p c e -> p (c e)"))
                    kmv = V([Km[h]], Km[h].t[:, :, :].rearrange("p c e -> p (c e)"))
                    fw.mm(b1[0:64, 128:384], AW[h][:, 0:64], bmv)
                    fw.mm(b2[0:64, 0:256], U0[h][:, :], bmv, True, False)
                    fw.mm(b2[0:64, 0:256], L["Vv"][:, hc], kmv, False, True)
                    Y0[h], PhT[h], PsT[h] = Y0s.next(), PhTs.next(), PsTs.next()
                    fw.cp("act", Y0[h][:, :], b1[0:64, 0:128])
                    fw.tt("dve", V([PhT[h]], PhT[h].t[:, :, :].rearrange("p c e -> p (c e)")), b1[0:64, 128:384],
                          V([Gd[h]], Gd[h].t[:, :, :].rearrange("p c e -> p (c e)")), ALU.add)
                    fw.cp("act", V([PsT[h]], PsT[h].t[:, :, :].rearrange("p c e -> p (c e)")), b2[0:64, 0:256])
                yb = [c.psx.next(), c.psx.next()]
                for cc in range(NC):
                    for h in hs:
                        hl = h - g0
                        yv = yb[hl // 4][0:64, (hl % 4) * 128 + cc * 32:(hl % 4) * 128 + cc * 32 + 32]
                        fw.mm(yv, ST[h][:, :], RcT[h][:, cc * 32:(cc + 1) * 32])
                        bs = c.ps.next()
                        fw.mm(bs[0:64, 0:64], PhT[h][:, cc, :], ST[h][:, :], True, False)
                        fw.mm(bs[0:64, 0:64], PsT[h][:, cc, :], I64, False, True)
                        ns = STs[h].next()
                        fw.cp(("act", "dve")[h % 2], ns[:, :], bs[0:64, 0:64])
                        ST[h] = ns
                pt = c.ps.next()
                for h in hs:
                    hl = h - g0
                    YT = YTs.next()
                    fw.tt("dve", YT[:, :], yb[hl // 4][0:64, (hl % 4) * 128:(hl % 4) * 128 + 128], Y0[h][:, :], ALU.add)
                    fw.tr(pt[:, hl * 64:(hl + 1) * 64], YT[:, :], I64)
                fw.cp("act", Yt[:, g0 * 64:(g0 + GS) * 64], pt[:, :])
            fw.dma("sp", Yd.rows(r0, r0 + 128), Yt[:, :])
        fw.barrier()
    with ExitStack() as es:
        wo = sb(c, es, "d_wo", [128, 8, 1024], BF16)
        G = sb(c, es, "G", [128, 1024], F32)
        lw_ = sb(c, es, "d_lnw", [128, 1024], F32)
        lb_ = sb(c, es, "d_lnb", [128, 1024], F32)
        bcast(c, G[:, :], c.d["ln_gains"][li, 1])
        bcast(c, lw_[:, :], c.d["d_lnx_w"])
        bcast(c, lb_[:, :], c.d["d_lnx_b"])
        with ExitStack() as t2:
            load_w(c, t2, wo, c.d["d_w_o"], 1024, 1024)
            fw.barrier()
        xts = rot(c, es, "xt", 4, [128, 1024], F32)
        ys = rot(c, es, "y", 3, [128, 1024], F32)
        sqs = rot(c, es, "sq", 2, [128, 1024], F32)
        bons = rot(c, es, "bon", 3, [128, 1024], BF16)
        gts = rot(c, es, "gt", 3, [128, 1024], BF16)
        obs = rot(c, es, "ob", 2, [128, 1024], BF16)
        obTs = rot(c, es, "obT", 2, [128, 8, 128], BF16)
        sts = rot(c, es, "st", 4, [128, 16], F32)
        sms = rot(c, es, "sm", 2, [128, 96], F32)
        xos = rot(c, es, "xo", 2, [128, 1024], F32)
        c.pn_tmp = rot(c, es, "pnt", 2, [128, 1024], F32)

        def v3(buf):
            return V([buf], buf.t[:, :].rearrange("p (h e) -> p h e", h=H))

        def b3(vw):
            return V(vw.bufs, vw.ap.unsqueeze(2).to_broadcast([128, H, 64]))

        def ld_d3(ti):
            r0 = ti * 128
            xt, y, bon, gt = xts.next(), ys.next(), bons.next(), gts.next()
            fw.dma("sp", xt[:, :], xin.rows(r0, r0 + 128))
            fw.dma("sp", y[:, :], Yd.rows(r0, r0 + 128))
            fw.dma("sp", bon[:, :], dd["BON"].rows(r0, r0 + 128))
            fw.dma("sp", gt[:, :], dd["GATE"].rows(r0, r0 + 128))
            return xt, y, bon, gt

        for ti, (xt, y, bon, gt) in prefetched(NT if dstage >= 3 else 0, ld_d3):
            r0 = ti * 128
            sm = sms.next()
            sq = sqs.next()
            fw.red("dve", sm[:, 0:16], v3(y), ALU.add)
            fw.act(sq[:, :], y[:, :], AF.Square)
            fw.red("dve", sm[:, 16:32], v3(sq), ALU.add)
            fw.ts("dve", sm[:, 32:48], sm[:, 0:16], 1.0 / 64)
            fw.tt("dve", sm[:, 48:64], sm[:, 32:48], sm[:, 32:48], ALU.mult)
            fw.stt("dve", sm[:, 64:80], sm[:, 16:32], 1.0 / 64, sm[:, 48:64], ALU.mult, ALU.subtract)
            fw.ts("dve", sm[:, 64:80], sm[:, 64:80], 64e-5, None, ALU.add)
            fw.act(sm[:, 64:80], sm[:, 64:80], AF.Sqrt)
            fw.recip(sm[:, 80:96], sm[:, 64:80])
            fw.tt("dve", v3(y), v3(y), b3(sm[:, 32:48]), ALU.subtract)
            fw.tt("dve", v3(y), v3(y), b3(sm[:, 80:96]), ALU.mult)
            fw.tt("pool", y[:, :], y[:, :], lw_[:, :], ALU.mult)
            fw.tt("pool", y[:, :], y[:, :], lb_[:, :], ALU.add)
            fw.tt("dve", y[:, :], y[:, :], bon[:, :], ALU.add)
            ob = obs.next()
            fw.tt("pool", ob[:, :], y[:, :], gt[:, :], ALU.mult)
            obT = obTs.next()
            transpose_to(c, ob, obT, 0)
            psY = [c.ps.next(), c.ps.next()]
            for nh in range(2):
                for kc in range(8):
                    fw.mm(psY[nh][:, :], obT[:, kc, :], wo[:, kc, nh * 512:(nh + 1) * 512], kc == 0, kc == 7)
            postnorm_store(c, psY[0], psY[1], xt, G, xos.next(), sts.next(), xout.rows(r0, r0 + 128))
        fw.barrier()


MIXERS = {0: mixer_a, 1: mixer_b, 2: mixer_c, 3: mixer_d}

_CACHE = {}


def run(inputs, subs=None, cores=8):
    key = tuple(subs) if subs is not None else None
    if key not in _CACHE:
        _CACHE[key] = build(subs)
    nc, c = _CACHE[key]
    in_maps = []
    for ci in range(cores):
        b = ci % 4
        m = {}
        hc = host_consts()
        for name, shape, dt in IN_SPECS:
            a = np.asarray(hc[name] if name in hc else inputs[name])
            if name in ("x", "mem", "positions"):
                a = a[b]
            a = np.ascontiguousarray(a).reshape(shape)
            m[name] = a
        in_maps.append(m)
    res = run_bass_kernel_spmd(nc, in_maps, core_ids=list(range(cores)))
    return [r["out"] for r in res.results]


def kernel(**inputs):
    outs = run(inputs)
    return np.stack(outs[0:4], axis=0).astype(np.float32)
```

```python
import numpy as np
from contextlib import ExitStack
import concourse.bass as bass
import concourse.mybir as mybir
from concourse.bass_utils import run_bass_kernel_spmd

F32 = mybir.dt.float32
BF16 = mybir.dt.bfloat16
I32 = mybir.dt.int32
AF = mybir.ActivationFunctionType
ALU = mybir.AluOpType
AX = mybir.AxisListType

D = 1024
S = 8192
NT = S // 128
EPS = 1e-6
MEM = 256
DFF = 2816


class Buf:
    __slots__ = ("t", "w", "r", "excl")

    def __init__(self, t, excl=False):
        self.t = t
        self.w = None
        self.r = {}
        self.excl = excl

    def __getitem__(self, idx):
        return V([self], self.t[idx])

    def v(self, ap):
        return V([self], ap)


class V:
    __slots__ = ("bufs", "ap")

    def __init__(self, bufs, ap):
        self.bufs = bufs
        self.ap = ap

    def __getitem__(self, idx):
        return V(self.bufs, self.ap[idx])


class DramT:
    def __init__(self, ap, blk=128, tracked=True):
        self.ap = ap
        self.blk = blk
        n = (ap.shape[0] + blk - 1) // blk
        self.blocks = [Buf(None) for _ in range(n)] if tracked else None

    def rows(self, r0, r1, cols=None):
        ap = self.ap[r0:r1] if cols is None else self.ap[r0:r1, cols[0]:cols[1]]
        if self.blocks is None:
            return V([], ap)
        return V(self.blocks[r0 // self.blk:(r1 - 1) // self.blk + 1], ap)

    def view(self, ap, r0, r1):
        if self.blocks is None:
            return V([], ap)
        return V(self.blocks[r0 // self.blk:(r1 - 1) // self.blk + 1], ap)


def RO(ap):
    return V([], ap)


COMPUTE = ("pe", "act", "dve", "pool")


class FW:
    def __init__(self, nc, es, n_dma=32):
        self.nc = nc
        self.engs = {"pe": nc.tensor, "act": nc.scalar, "dve": nc.vector, "pool": nc.gpsimd, "sp": nc.sync}
        self.sem = {k: es.enter_context(nc.semaphore("s_" + k)) for k in COMPUTE}
        self.cnt = {k: 0 for k in COMPUTE}
        self.waited = {k: {} for k in self.engs}
        self.dsem = [es.enter_context(nc.semaphore("d%d" % i)) for i in range(n_dma)]
        self.dcnt = [0] * n_dma
        self.dnext = 0
        self.nd = n_dma
        self.ninst = 0

    def _wait(self, eng, key, val):
        if eng == "pe" and key == "pe":
            return
        w = self.waited[eng]
        if w.get(key, 0) >= val:
            return
        sem = self.sem[key] if isinstance(key, str) else self.dsem[key]
        self.engs[eng].wait_ge(sem, val)
        w[key] = val
        self.ninst += 1

    def _deps(self, eng, reads, writes):
        for v in reads:
            for b in v.bufs:
                if b.w is not None:
                    self._wait(eng, b.w[0], b.w[1])
                if b.excl:
                    for k, val in b.r.items():
                        if k != eng:
                            self._wait(eng, k, val)
        for v in writes:
            for b in v.bufs:
                if b.w is not None:
                    self._wait(eng, b.w[0], b.w[1])
                for k, val in b.r.items():
                    self._wait(eng, k, val)

    def _done(self, key, val, reads, writes):
        for v in reads:
            for b in v.bufs:
                b.r[key] = val
        for v in writes:
            for b in v.bufs:
                b.w = (key, val)
                b.r = {}

    def op(self, eng, fn, reads, writes):
        self._deps(eng, reads, writes)
        ins = fn()
        self.cnt[eng] += 1
        ins.then_inc(self.sem[eng], 1)
        self._done(eng, self.cnt[eng], reads, writes)
        self.ninst += 1

    def dma(self, q, out, in_, **kw):
        s = self.dnext
        self.dnext = (s + 1) % self.nd
        if self.dcnt[s]:
            self._wait(q, s, self.dcnt[s])
        self._deps(q, [in_], [out])
        ins = self.engs[q].dma_start(out=out.ap, in_=in_.ap, **kw)
        self.dcnt[s] += 16
        ins.then_inc(self.dsem[s], 16)
        self._done(s, self.dcnt[s], [in_], [out])
        self.ninst += 1

    def barrier(self):
        for eng in self.engs:
            for k in COMPUTE:
                if self.cnt[k]:
                    self._wait(eng, k, self.cnt[k])
            for s in range(self.nd):
                if self.dcnt[s]:
                    self._wait(eng, s, self.dcnt[s])

    def mm(self, out, lhsT, rhs, start=True, stop=True):
        self.op("pe", lambda: self.nc.tensor.matmul(out.ap, lhsT.ap, rhs.ap, start=start, stop=stop),
                [lhsT, rhs], [out])

    def tr(self, out, in_, ident):
        self.op("pe", lambda: self.nc.tensor.transpose(out.ap, in_.ap, ident.ap), [in_, ident], [out])

    def act(self, out, in_, func, bias=None, scale=None, accum=None):
        kw = {}
        reads = [in_]
        writes = [out]
        if bias is not None:
            if isinstance(bias, V):
                kw["bias"] = bias.ap
                reads.append(bias)
            else:
                kw["bias"] = bias
        if scale is not None:
            if isinstance(scale, V):
                kw["scale"] = scale.ap
                reads.append(scale)
            else:
                kw["scale"] = scale
        if accum is not None:
            kw["accum_out"] = accum.ap
            writes.append(accum)
        self.op("act", lambda: self.nc.scalar.activation(out=out.ap, in_=in_.ap, func=func, **kw), reads, writes)

    def _e(self, eng):
        return self.engs[eng]

    def tt(self, eng, out, a, b, op):
        self.op(eng, lambda: self._e(eng).tensor_tensor(out.ap, a.ap, b.ap, op), [a, b], [out])

    def ts(self, eng, out, a, s1, s2=None, op0=ALU.mult, op1=None, accum=None):
        reads = [a]
        writes = [out]
        a1 = s1
        a2 = s2
        if isinstance(s1, V):
            reads.append(s1)
            a1 = s1.ap
        if isinstance(s2, V):
            reads.append(s2)
            a2 = s2.ap
        kw = {}
        if accum is not None:
            kw["accum_out"] = accum.ap
            writes.append(accum)
        if op1 is None:
            self.op(eng, lambda: self._e(eng).tensor_scalar(out.ap, a.ap, a1, None, op0, **kw), reads, writes)
        else:
            self.op(eng, lambda: self._e(eng).tensor_scalar(out.ap, a.ap, a1, a2, op0, op1, **kw), reads, writes)

    def stt(self, eng, out, a, s, b, op0, op1):
        reads = [a, b]
        sa = s
        if isinstance(s, V):
            reads.append(s)
            sa = s.ap
        self.op(eng, lambda: self._e(eng).scalar_tensor_tensor(out.ap, a.ap, sa, b.ap, op0, op1), reads, [out])

    def cp(self, eng, out, in_):
        if eng == "act":
            self.act(out, in_, AF.Copy)
        else:
            self.op(eng, lambda: self._e(eng).tensor_copy(out.ap, in_.ap), [in_], [out])

    def recip(self, out, in_):
        self.op("dve", lambda: self.nc.vector.reciprocal(out.ap, in_.ap), [in_], [out])

    def red(self, eng, out, in_, op, axis=AX.X):
        self.op(eng, lambda: self._e(eng).tensor_reduce(out.ap, in_.ap, axis, op), [in_], [out])

    def memset(self, eng, out, val):
        self.op(eng, lambda: self._e(eng).memset(out.ap, val), [], [out])

    def asel(self, out, in_, pattern, cmp, fill, base, cm):
        self.op("pool", lambda: self.nc.gpsimd.affine_select(out=out.ap, in_=in_.ap, pattern=pattern,
                                                             compare_op=cmp, fill=fill, base=base,
                                                             channel_multiplier=cm), [in_], [out])


class Ctx:
    pass


def sb(c, es, name, shape, dt=F32):
    c.uid += 1
    return Buf(es.enter_context(c.nc.sbuf_tensor("%s_%d" % (name, c.uid), shape, dt)))


class Rot:
    def __init__(self, bufs):
        self.bufs = bufs
        self.i = 0

    def next(self):
        b = self.bufs[self.i]
        self.i = (self.i + 1) % len(self.bufs)
        return b


def rot(c, es, name, n, shape, dt=F32):
    return Rot([sb(c, es, name, shape, dt) for _ in range(n)])


def prefetched(n, load):
    nxt = load(0) if n else None
    for i in range(n):
        cur = nxt
        nxt = load(i + 1) if i + 1 < n else None
        yield i, cur


def load_w(c, es_tmp, dst, src_ap, K, N, gcol=None, n0=0, q="sp"):
    fw = c.fw
    CH = 1408 if N % 1408 == 0 else (1024 if N % 1024 == 0 else N)
    if CH > 2048:
        CH = N // ((N + 2047) // 2048)
        assert N % CH == 0
    stg = rot(c, es_tmp, "wstg", 3, [128, CH], F32)
    i = 0
    for kc in range(K // 128):
        for n1 in range(0, N, CH):
            st = stg.next()
            fw.dma(q, st[:, :], RO(src_ap[kc * 128:(kc + 1) * 128, n1:n1 + CH]))
            eng = ("dve", "pool", "act")[i % 3]
            i += 1
            o = dst[:, kc, n0 + n1:n0 + n1 + CH]
            if gcol is None:
                fw.cp(eng, o, st[:, :])
            elif eng == "act":
                fw.act(o, st[:, :], AF.Copy, scale=gcol[:, kc:kc + 1])
            else:
                fw.ts(eng, o, st[:, :], gcol[:, kc:kc + 1])


def load_cols(c, es_tmp, dst_v, src2d_ap, R):
    fw = c.fw
    st = sb(c, es_tmp, "lc", [128, 128], F32)
    fw.dma("sp", st[0:R, :], RO(src2d_ap))
    ps = c.ps.next()
    fw.tr(ps[:, 0:R], st[0:R, :], c.identf[0:R, 0:R])
    fw.cp("dve", dst_v, ps[:, 0:R])


def bcast(c, dst, vec_ap):
    c.fw.dma("sp", dst, RO(vec_ap.partition_broadcast(128)))


def rms_rstd(c, ss_v, n, out_v, tmp):
    fw = c.fw
    k = ss_v.ap.shape[1]
    fw.ts("dve", tmp[:, 0:k], ss_v, 1.0 / n, EPS, ALU.mult, ALU.add)
    fw.act(tmp[:, k:2 * k], tmp[:, 0:k], AF.Sqrt)
    fw.recip(out_v, tmp[:, k:2 * k])


def norm_tile(c, xt, hn, st):
    fw = c.fw
    fw.memset("pool", st[:, 0:1], 0.0)
    fw.act(c.junk[:, :], xt[:, :], AF.Square, accum=st[:, 0:1])
    rms_rstd(c, st[:, 0:1], D, st[:, 3:4], c.mk_tmp(st))
    fw.act(hn[:, :], xt[:, :], AF.Copy, scale=st[:, 3:4])


def transpose_to(c, hn, dstT, t0, nchunk=8):
    fw = c.fw
    ps = c.ps.next()
    psb = ps.v(ps.t[:, :].bitcast(BF16))
    for ch in range(nchunk):
        fw.tr(V([ps], psb.ap[:, ch * 128:(ch + 1) * 128]), hn[:, ch * 128:(ch + 1) * 128], c.identb[:, :])
    eng = c.evac_eng()
    fw.cp(eng, dstT[:, 0:nchunk, t0:t0 + 128],
          V([ps], psb.ap[:, 0:nchunk * 128].rearrange("p (c t) -> p c t", c=nchunk)))


def postnorm_store(c, psA, psB, xt, G, xo, st, dst_rows):
    fw = c.fw
    fw.memset("pool", st[:, 0:2], 0.0)
    fw.act(c.junk[:, 0:512], psA[:, :], AF.Square, accum=st[:, 0:1])
    fw.act(c.junk[:, 512:1024], psB[:, :], AF.Square, accum=st[:, 1:2])
    fw.tt("dve", st[:, 2:3], st[:, 0:1], st[:, 1:2], ALU.add)
    rms_rstd(c, st[:, 2:3], D, st[:, 3:4], c.mk_tmp(st))
    tmp = c.pn_tmp.next() if c.pn_tmp is not None else xo
    fw.stt("dve", tmp[:, 0:512], psA[:, :], st[:, 3:4], G[:, 0:512], ALU.mult, ALU.mult)
    fw.stt("dve", tmp[:, 512:1024], psB[:, :], st[:, 3:4], G[:, 512:1024], ALU.mult, ALU.mult)
    fw.tt("pool", xo[:, :], xt[:, :], tmp[:, :], ALU.add)
    fw.dma("sp", dst_rows, xo[:, :])


def phase_consts(c, es):
    fw = c.fw
    c.identf = sb(c, es, "identf", [128, 128], F32)
    c.identb = sb(c, es, "identb", [128, 128], BF16)
    fw.memset("pool", c.identf[:, :], 1.0)
    fw.asel(c.identf[:, :], c.identf[:, :], [[-1, 128]], ALU.is_equal, 0.0, 0, 1)
    fw.cp("pool", c.identb[:, :], c.identf[:, :])
    c.junk = sb(c, es, "junk", [128, 1024], BF16)
    c.gT = sb(c, es, "gT", [128, 192], F32)
    with ExitStack() as tmp:
        g2 = c.d["ln_gains"].rearrange("l s (c p) -> (l s c) p", p=128)
        load_cols(c, tmp, c.gT[:, 0:96], g2[0:96, :], 96)
        load_cols(c, tmp, c.gT[:, 96:192], g2[96:192, :], 96)
        fw.barrier()


def gcol(c, li, si):
    o = (li * 6 + si) * 8
    return c.gT[:, o:o + 8]


def phase_memkv(c, es):
    fw = c.fw
    c.mem_kT = sb(c, es, "memkT", [128, 4, 256], BF16)
    c.mem_v = sb(c, es, "memv", [128, 2, 512], BF16)
    with ExitStack() as tmp:
        w = sb(c, tmp, "wkv", [128, 8, 1024], BF16)
        gm = sb(c, tmp, "gm", [128, 8], F32)
        load_cols(c, tmp, gm[:, :], c.d["mem_norm"].rearrange("(c p) -> c p", p=128), 8)
        with ExitStack() as t2:
            load_w(c, t2, w, c.d["mem_w_kv"], D, 1024, gcol=gm)
            fw.barrier()
        hnT = sb(c, tmp, "mhnT", [128, 8, 256], BF16)
        for ti in range(2):
            xt = sb(c, tmp, "mx", [128, 1024], F32)
            hn = sb(c, tmp, "mhn", [128, 1024], BF16)
            st = sb(c, tmp, "mst", [128, 16], F32)
            fw.dma("sp", xt[:, :], RO(c.d["mem"][ti * 128:(ti + 1) * 128, :]))
            norm_tile(c, xt, hn, st)
            transpose_to(c, hn, hnT, ti * 128)
        for h in range(4):
            ps = c.ps.next()
            for kc in range(8):
                fw.mm(ps[:, 0:256], w[:, kc, h * 128:(h + 1) * 128], hnT[:, kc, :], kc == 0, kc == 7)
            fw.cp("act", c.mem_kT[:, h, :], ps[:, 0:256])
        for mc in range(2):
            ps = c.ps.next()
            for kc in range(8):
                fw.mm(ps[:, :], hnT[:, kc, mc * 128:(mc + 1) * 128], w[:, kc, 512:1024], kc == 0, kc == 7)
            fw.cp("act", c.mem_v[:, mc, :], ps[:, :])
        fw.barrier()


def phase_mem(c, li, xin, xout):
    fw = c.fw
    TB = 512
    NBK = S // TB
    with ExitStack() as es:
        wq = sb(c, es, "wq", [128, 8, 512], BF16)
        wo = sb(c, es, "wo", [128, 4, 1024], BF16)
        G = sb(c, es, "G", [128, 1024], F32)
        ones = sb(c, es, "ones", [128, 128], BF16)
        fw.memset("pool", ones[:, :], 1.0)
        bcast(c, G[:, :], c.d["ln_gains"][li, 3])
        with ExitStack() as t2:
            load_w(c, t2, wq, c.d["mem_w_q"][li], D, 512, gcol=gcol(c, li, 2))
            load_w(c, t2, wo, c.d["mem_w_o"][li], 512, 1024)
            fw.barrier()
        xts = rot(c, es, "xt", 12, [128, 1024], F32)
        hns = rot(c, es, "hn", 8, [128, 1024], BF16)
        sts = rot(c, es, "st", 12, [128, 16], F32)
        hnTs = rot(c, es, "hnT", 2, [128, 8, TB], BF16)
        qTs = rot(c, es, "qT", 2, [128, 4, TB], BF16)
        pTs = rot(c, es, "pT", 6, [128, TB], BF16)
        rds = rot(c, es, "rd", 3, [128, TB], F32)
        oTs = rot(c, es, "oT", 2, [128, 4, TB], BF16)
        xos = rot(c, es, "xo", 3, [128, 1024], F32)
        c.pn_tmp = None
        scale = 128.0 ** -0.5

        def P1(blk):
            xl, hl = [], []
            for ti in range(4):
                r0 = blk * TB + ti * 128
                xt = xts.next()
                fw.dma("sp", xt[:, :], xin.rows(r0, r0 + 128))
                hn = hns.next()
                norm_tile(c, xt, hn, sts.next())
                xl.append(xt)
                hl.append(hn)
            return xl, hl

        def P2(hl):
            hnT = hnTs.next()
            for ti in range(4):
                transpose_to(c, hl[ti], hnT, ti * 128)
            qT = qTs.next()
            for h in range(4):
                ps = c.ps.next()
                for kc in range(8):
                    fw.mm(ps[:, :], wq[:, kc, h * 128:(h + 1) * 128], hnT[:, kc, :], kc == 0, kc == 7)
                fw.cp(("act", "dve")[h % 2], qT[:, h, :], ps[:, :])
            return qT

        def scores(qT, h):
            out = []
            for mc in range(2):
                ps = c.ps.next()
                fw.mm(ps[:, :], c.mem_kT[:, h, mc * 128:(mc + 1) * 128], qT[:, h, :])
                pT = pTs.next()
                fw.act(pT[:, :], ps[:, :], AF.Exp, scale=scale)
                out.append(pT)
            return out

        def pv(oT, h, pl):
            acc, den = c.psx.next(), c.psx.next()
            for mc in range(2):
                fw.mm(acc[:, :], c.mem_v[:, mc, h * 128:(h + 1) * 128], pl[mc][:, :], mc == 0, mc == 1)
            for mc in range(2):
                fw.mm(den[:, :], ones[:, :], pl[mc][:, :], mc == 0, mc == 1)
            rd = rds.next()
            fw.recip(rd[:, :], den[:, :])
            fw.tt("dve", oT[:, h, :], acc[:, :], rd[:, :], ALU.mult)

        xl, hl = P1(0)
        qT = P2(hl)
        for blk in range(NBK):
            nxt = P1(blk + 1) if blk + 1 < NBK else None
            oT = oTs.next()
            pl = scores(qT, 0)
            for h in range(4):
                pln = scores(qT, h + 1) if h + 1 < 4 else None
                pv(oT, h, pl)
                pl = pln
            if nxt is not None:
                qTn = P2(nxt[1])
            for ti in range(4):
                r0 = blk * TB + ti * 128
                psY = [c.ps.next(), c.ps.next()]
                for nh in range(2):
                    for h in range(4):
                        fw.mm(psY[nh][:, :], oT[:, h, ti * 128:(ti + 1) * 128], wo[:, h, nh * 512:(nh + 1) * 512],
                              h == 0, h == 3)
                postnorm_store(c, psY[0], psY[1], xl[ti], G, xos.next(), sts.next(), xout.rows(r0, r0 + 128))
            if nxt is not None:
                xl, qT = nxt[0], qTn
        fw.barrier()


def phase_ffn(c, li, xin, xout):
    fw = c.fw
    TB = 256
    NTI = TB // 128
    NBK = S // TB
    with ExitStack() as es:
        w1 = sb(c, es, "w1", [128, 8, 2 * DFF], BF16)
        w2 = sb(c, es, "w2", [128, 22, 1024], BF16)
        G = sb(c, es, "G", [128, 1024], F32)
        cw = sb(c, es, "cw", [128, 3, 44], F32)
        cb = sb(c, es, "cb", [128, 44], F32)
        bcast(c, G[:, :], c.d["ln_gains"][li, 5])
        with ExitStack() as t2:
            for j in range(3):
                load_cols(c, t2, cw[:, j, :], c.d["ffn_conv_w"][li, j].rearrange("(c p) -> c p", p=128), 44)
            load_cols(c, t2, cb[:, :], c.d["ffn_conv_b"][li].rearrange("(c p) -> c p", p=128), 44)
            load_w(c, t2, w1, c.d["ffn_w_in"][li], D, 2 * DFF, gcol=gcol(c, li, 4))
            load_w(c, t2, w2, c.d["ffn_w_out"][li], DFF, 1024)
            fw.barrier()
        xts = rot(c, es, "xt", 4, [128, 1024], F32)
        hns = rot(c, es, "hn", 2, [128, 1024], BF16)
        sts = rot(c, es, "st", 6, [128, 16], F32)
        hnTs = rot(c, es, "hnT", 2, [128, 8, TB + 2], BF16)
        tbs = rot(c, es, "tb", 8, [128, TB], F32)
        sgs = rot(c, es, "sg", 4, [128, TB], F32)
        aTs = rot(c, es, "aT", 1, [128, 22, TB], BF16)
        xos = rot(c, es, "xo", 2, [128, 1024], F32)
        c.pn_tmp = None
        for hb in hnTs.bufs:
            fw.memset("pool", hb[:, :, :], 0.0)

        def prologue(blk, prevT):
            hnT = hnTs.next()
            if prevT is not None:
                fw.cp("pool", hnT[:, :, 0:2], prevT[:, :, TB:TB + 2])
            xl = []
            for ti in range(NTI):
                r0 = blk * TB + ti * 128
                xt = xts.next()
                xl.append(xt)
                fw.dma("sp", xt[:, :], xin.rows(r0, r0 + 128))
                hn = hns.next()
                norm_tile(c, xt, hn, sts.next())
                transpose_to(c, hn, hnT, 2 + ti * 128)
            return hnT, xl

        nxt = prologue(0, None)
        for blk in range(NBK):
            hnT, xl = nxt
            aT = aTs.next()
            for j in range(22):
                tv = []
                for half, ch in ((0, j), (1, 22 + j)):
                    ps = c.ps.next()
                    for kc in range(8):
                        fw.mm(ps[:, 0:TB + 2], w1[:, kc, ch * 128:(ch + 1) * 128], hnT[:, kc, :], kc == 0, kc == 7)
                    t = tbs.next()
                    fw.act(t[:, :], ps[:, 2:TB + 2], AF.Identity, bias=cb[:, ch:ch + 1], scale=cw[:, 2, ch:ch + 1])
                    fw.stt("dve", t[:, :], ps[:, 0:TB], cw[:, 0, ch:ch + 1], t[:, :], ALU.mult, ALU.add)
                    fw.stt("dve", t[:, :], ps[:, 1:TB + 1], cw[:, 1, ch:ch + 1], t[:, :], ALU.mult, ALU.add)
                    tv.append(t)
                sg = sgs.next()
                fw.act(sg[:, :], tv[0][:, :], AF.Silu)
                fw.tt("pool", aT[:, j, :], sg[:, :], tv[1][:, :], ALU.mult)
            if blk + 1 < NBK:
                nxt = prologue(blk + 1, hnT)
            for ti in range(NTI):
                r0 = blk * TB + ti * 128
                psY = [c.ps.next(), c.ps.next()]
                for nh in range(2):
                    for kc in range(22):
                        fw.mm(psY[nh][:, :], aT[:, kc, ti * 128:(ti + 1) * 128], w2[:, kc, nh * 512:(nh + 1) * 512],
                              kc == 0, kc == 21)
                postnorm_store(c, psY[0], psY[1], xl[ti], G, xos.next(), sts.next(), xout.rows(r0, r0 + 128))
        fw.barrier()


IN_SPECS = [
    ("x", [S, D], F32), ("mem", [MEM, D], F32), ("positions", [S], I32),
    ("ln_gains", [4, 6, D], F32), ("mem_norm", [D], F32), ("mem_w_kv", [D, 1024], F32),
    ("mem_w_q", [4, D, 512], F32), ("mem_w_o", [4, 512, D], F32),
    ("ffn_w_in", [4, D, 2 * DFF], F32), ("ffn_conv_w", [4, 3, 2 * DFF], F32), ("ffn_conv_b", [4, 2 * DFF], F32),
    ("ffn_w_out", [4, DFF, D], F32),
    ("a_w_qkv", [D, 4608], F32), ("a_w_o", [512, D], F32),
    ("b_w_in", [D, 3088], F32), ("b_w_gate2", [16, 512], F32), ("b_gate_bias", [512], F32),
    ("b_o_norm", [256], F32), ("b_w_o", [D, D], F32),
    ("c_w_in", [D, 672], F32), ("c_q_norm", [384], F32), ("c_w_uq", [384, 1536], F32),
    ("c_kv_norm", [256], F32), ("c_w_ukv", [256, 2048], F32), ("c_w_o", [D, D], F32),
    ("d_mix", [6, D], F32), ("d_w_rkv", [3, D, D], F32), ("d_w0", [D], F32), ("d_w1", [D, 64], F32),
    ("d_w2", [64, D], F32), ("d_a0", [D], F32), ("d_a1", [D, 64], F32), ("d_a2", [64, D], F32),
    ("d_g1", [D, 128], F32), ("d_g2", [128, D], F32), ("d_k_k", [D], F32), ("d_k_a", [D], F32),
    ("d_r_k", [16, 64], F32), ("d_lnx_w", [D], F32), ("d_lnx_b", [D], F32), ("d_w_o", [D, D], F32),
    ("k_invf_c", [128, 1], F32),
]


def host_consts():
    invf = np.zeros((128, 1), np.float32)
    for i in range(32):
        invf[64 + i, 0] = INVF_C[i % 16]
    return {"k_invf_c": invf}

ALL_SUBS = [(k, li) for li in range(4) for k in ("mix", "mem", "ffn")]


def build(subs=None):
    if subs is None:
        subs = ALL_SUBS
    nc = bass.Bass("TRN2", target_bir_lowering=False)
    c = Ctx()
    c.nc = nc
    c.uid = 0
    c.d = {}
    for name, shape, dt in IN_SPECS:
        c.d[name] = nc.dram_tensor(name, shape, dt, kind="ExternalInput").ap()
    out = nc.dram_tensor("out", [S, D], F32, kind="ExternalOutput").ap()
    scr = [nc.dram_tensor("xs%d" % i, [S, D], F32, kind="Internal").ap() for i in range(2)]
    with ExitStack() as es:
        fw = FW(nc, es)
        c.fw = fw
        banks = [Buf(es.enter_context(nc.psum_tensor("ps%d" % i, [128, 512], F32)), excl=True) for i in range(8)]
        c.ps = Rot(banks[0:6])
        c.psx = Rot(banks[6:8])
        c._ev = 0

        def evac_eng():
            c._ev += 1
            return ("act", "dve")[c._ev % 2]
        c.evac_eng = evac_eng
        c.mk_tmp = _Tmp
        phase_consts(c, es)
        phase_memkv(c, es)
        cur = DramT(c.d["x"], tracked=False)
        for i, (kind, li) in enumerate(subs):
            last = i == len(subs) - 1
            dst = DramT(out) if last else DramT(scr[i % 2])
            if kind == "mem":
                phase_mem(c, li, cur, dst)
            elif kind == "ffn":
                phase_ffn(c, li, cur, dst)
            else:
                MIXERS[li](c, li, cur, dst)
            cur = dst
        fw.barrier()
    c.ninst = fw.ninst
    return nc, c


class _Tmp:
    def __init__(self, st):
        self.st = st

    def __getitem__(self, idx):
        p, f = idx
        return self.st[p, slice(8 + f.start, 8 + f.stop)]


INVF_A = [float(np.float32(500000.0) ** np.float32(-(np.float32(i) * np.float32(2.0 / 16)))) for i in range(8)]
INVF_C = [float(np.float32(500000.0) ** np.float32(-(np.float32(i) * np.float32(2.0 / 32)))) for i in range(16)]
TWO_PI = float(2 * np.pi)


def rope_tables(c, COS, SIN, es_tmp, pf, npair, invf):
    fw = c.fw
    nf = len(invf)
    n = npair * nf
    ang = sb(c, es_tmp, "ang", [128, npair, nf], F32)
    u = sb(c, es_tmp, "u", [128, n], F32)
    ni = sb(c, es_tmp, "ni", [128, n], I32)
    nfl = sb(c, es_tmp, "nfl", [128, n], F32)
    r = sb(c, es_tmp, "r", [128, n], F32)
    for f in range(nf):
        fw.ts("dve", ang[:, :, f], pf[:, :], invf[f])
    angf = ang.v(ang.t[:, :, :].rearrange("p a b -> p (a b)"))
    for off, dst in ((0.0, SIN), (0.25, COS)):
        fw.ts("dve", u[:, :], angf, 1.0 / TWO_PI, off, ALU.mult, ALU.add)
        fw.cp("dve", ni[:, :], u[:, :])
        fw.cp("dve", nfl[:, :], ni[:, :])
        if off:
            fw.ts("dve", u[:, :], angf, float(np.pi / 2), None, ALU.add)
            fw.stt("dve", r[:, :], nfl[:, :], -TWO_PI, u[:, :], ALU.mult, ALU.add)
        else:
            fw.stt("dve", r[:, :], nfl[:, :], -TWO_PI, angf, ALU.mult, ALU.add)
        fw.ts("dve", r[:, :], r[:, :], float(np.pi), float(-np.pi), ALU.min, ALU.max)
        fw.act(dst.v(dst.t[:, :, :].rearrange("p a b -> p (a b)")), r[:, :], AF.Sin)
    return COS, SIN


def rope_apply(c, xs, o, cosv, sinv, tmps, nh, hd, half):
    fw = c.fw
    x3 = V(xs.bufs, xs.ap.rearrange("p (h d) -> p h d", h=nh))
    o3 = V(o.bufs, o.ap.rearrange("p (h d) -> p h d", h=nh))
    cb = V(cosv.bufs, cosv.ap.unsqueeze(1).to_broadcast([128, nh, half]))
    sbv = V(sinv.bufs, sinv.ap.unsqueeze(1).to_broadcast([128, nh, half]))
    t = [tmps.next() for _ in range(4)]
    tv = [V([b], b.t[:, 0:nh * half].rearrange("p (h d) -> p h d", h=nh)) for b in t]
    x1 = x3[:, :, 0:half]
    x2 = x3[:, :, half:2 * half]
    fw.tt("dve", tv[0], x1, cb, ALU.mult)
    fw.tt("pool", tv[1], x2, sbv, ALU.mult)
    fw.tt("dve", o3[:, :, 0:half], tv[0], tv[1], ALU.subtract)
    fw.tt("dve", tv[2], x2, cb, ALU.mult)
    fw.tt("pool", tv[3], x1, sbv, ALU.mult)
    fw.tt("pool", o3[:, :, half:2 * half], tv[2], tv[3], ALU.add)
    if hd > 2 * half:
        fw.cp("act", o3[:, :, 2 * half:hd], x3[:, :, 2 * half:hd])


def load_pos(c, dst_i, src_ap_iJ, nJ):
    fw = c.fw
    step = 8
    for j0 in range(0, nJ, step):
        fw.dma("sp", dst_i[:, j0:j0 + step], RO(src_ap_iJ[:, j0:j0 + step]), allow_slow_non_contiguous=True)


def mixer_a(c, li, xin, xout):
    import os
    stage = int(os.environ.get('A_STAGE', '9'))
    fw = c.fw
    nc = c.nc
    U = 2048
    NU = S // U
    GROUPS = [(0, 1), (1, 4), (2, 16)]
    nds = [DramT(nc.dram_tensor("a_nd%d" % g, [S, 520], F32, kind="Internal").ap()) for g in range(3)]
    for g, d in GROUPS:
        nblk = 16 // d
        nJ = 64 // d
        with ExitStack() as es:
            w = sb(c, es, "wa", [128, 8, 1536], BF16)
            with ExitStack() as t2:
                for s3 in range(3):
                    col0 = (s3 * 3 + g) * 512
                    load_w(c, t2, w, c.d["a_w_qkv"][:, col0:col0 + 512], D, 512, gcol=gcol(c, li, 0), n0=s3 * 512)
                fw.barrier()
            COS = sb(c, es, "cos", [128, 64, 8], F32)
            SIN = sb(c, es, "sin", [128, 64, 8], F32)
            with ExitStack() as t2:
                pi = sb(c, t2, "pi", [128, 64], I32)
                pf = sb(c, t2, "pf", [128, 64], F32)
                if d == 1:
                    load_pos(c, pi, c.d["positions"].rearrange("(J i) -> i J", i=128), 64)
                else:
                    src = c.d["positions"].rearrange("(J i r) -> i J r", i=128, r=d)
                    step = max(1, 8 // d * 1)
                    piv = pi.t[:, :].rearrange("i (J r) -> i J r", r=d)
                    for j0 in range(0, nJ, 2):
                        fw.dma("sp", V([pi], piv[:, j0:j0 + 2, :]), RO(src[:, j0:j0 + 2, :]),
                               allow_slow_non_contiguous=True)
                fw.cp("dve", pf[:, :], pi[:, :])
                rope_tables(c, COS, SIN, t2, pf, 64, INVF_A)
                fw.barrier()
            hnT = sb(c, es, "hnTu", [128, 8, U], BF16)
            xts = rot(c, es, "xt", 2, [128, 1024], F32)
            hns = rot(c, es, "hn", 2, [128, 1024], BF16)
            sts = rot(c, es, "st", 4, [128, 16], F32)
            xss = rot(c, es, "xs", 3, [128, 512], F32)
            qrs = rot(c, es, "qr", 3, [128, 512], BF16)
            qTs = rot(c, es, "qT", 2, [128, 4, 128], BF16)
            tms = rot(c, es, "tm", 8, [128, 64], F32)
            pes = rot(c, es, "pe", 3, [128, 512], BF16)
            pms = rot(c, es, "pm", 3, [128, 512], BF16)
            stg = rot(c, es, "stg", 2, [128, 520], F32)
            nbuf = d + 3
            kfree = [sb(c, es, "kT", [128, 8, 128], BF16) for _ in range(nbuf)]
            for kb in kfree:
                fw.memset("pool", kb[:, :, :], 0.0)
            vfree = [sb(c, es, "v65", [128, 8, 80], BF16) for _ in range(nbuf)]
            for vb in vfree:
                fw.memset("pool", vb[:, :, 64:65], 1.0)
            kz = sb(c, es, "kz", [128, 8, 128], BF16)
            vz = sb(c, es, "vz", [128, 8, 80], BF16)
            fw.memset("pool", kz[:, :, :], 0.0)
            fw.memset("pool", vz[:, :, :], 0.0)
            mN = sb(c, es, "mN", [128, 512], BF16)
            mF = sb(c, es, "mF", [128, 512], BF16)
            with ExitStack() as t2:
                mt = sb(c, t2, "mt", [128, 512], F32)
                fw.memset("pool", mt[:, :], 1.0)
                for hh in range(2):
                    fw.asel(mt[:, hh * 256:hh * 256 + 128], mt[:, hh * 256:hh * 256 + 128], [[-1, 128]],
                            ALU.is_ge, 0.0, 0, 1)
                    fw.asel(mt[:, hh * 256 + 128:hh * 256 + 256], mt[:, hh * 256 + 128:hh * 256 + 256], [[1, 128]],
                            ALU.is_ge, 0.0, 0, -1)
                fw.cp("pool", mN[:, :], mt[:, :])
                for hh in range(2):
                    fw.memset("pool", mt[:, hh * 256:hh * 256 + 128], 0.0)
                fw.cp("pool", mF[:, :], mt[:, :])
                fw.barrier()
            carry = {}
            ndv = nds[g].ap.rearrange("(J i r) c -> r J i c", i=128, r=d)
            for u in range(NU if stage >= 2 else 0):
                for ti in range(U // 128):
                    r0 = u * U + ti * 128
                    xt = xts.next()
                    fw.dma("sp", xt[:, :], xin.rows(r0, r0 + 128))
                    hn = hns.next()
                    norm_tile(c, xt, hn, sts.next())
                    transpose_to(c, hn, hnT, ti * 128)
                for r in range(d if stage >= 3 else 0):
                    for jl in range(nblk):
                        J = u * nblk + jl
                        pidx = J * d + r
                        banks = []
                        for s3 in range(3):
                            ps = c.ps.next()
                            banks.append(ps)
                            for kc in range(8):
                                hv = hnT.t[:, kc, :].rearrange("p (j i r) -> p r j i", i=128, r=d)[:, r, jl, :]
                                fw.mm(ps[:, :], V([hnT], hv), w[:, kc, s3 * 512:(s3 + 1) * 512], kc == 0, kc == 7)
                        sub = int(os.environ.get('A_SUB', '9'))
                        if sub < 1:
                            continue
                        psT = c.ps.next()
                        psTb = psT.t[:, :].bitcast(BF16)
                        for s3 in range(2):
                            xs = xss.next()
                            fw.cp("act", xs[:, :], banks[s3][:, :])
                            qr = qrs.next()
                            if sub >= 2:
                                rope_apply(c, xs[:, :], qr[:, :], COS[:, pidx, :], SIN[:, pidx, :], tms, 8, 64, 8)
                            else:
                                fw.cp("dve", qr[:, :], xs[:, :])
                            for ch in range(4 if sub != 11 else 0):
                                jj = s3 * 4 + ch
                                fw.tr(V([psT], psTb[:, jj * 128:(jj + 1) * 128]), qr[:, ch * 128:(ch + 1) * 128],
                                      c.identb[:, :])
                        qT = qTs.next()
                        curK = kfree.pop()
                        curV = vfree.pop()
                        if sub not in (11, 12):
                            fw.cp("act", qT[:, :, :], V([psT], psTb[:, 0:512].rearrange("p (c t) -> p c t", c=4)))
                        if sub not in (11, 12, 13):
                            kv3 = psTb[:, 512:1024].rearrange("p (c t) -> p c t", c=4)
                            k4 = curK.t[:, :, :].rearrange("p (c e) t -> p c e t", e=2)
                            fw.cp("dve", V([curK], k4[0:64, :, 0, :]), V([psT], kv3[0:64]))
                            fw.cp("act", V([curK], k4[64:128, :, 1, :]), V([psT], kv3[64:128]))
                        if sub not in (11, 12, 13, 14):
                            fw.cp("dve", curV[:, :, 0:64], V([banks[2]], banks[2].t[:, :].rearrange("p (h e) -> p h e", h=8)))
                        if J == 0:
                            prevK, prevV, mask = kz, vz, mF
                        else:
                            prevK, prevV = carry[r]
                            mask = mN
                        pO = [c.psx.next(), c.psx.next()]
                        pend = []
                        for hp in range(5):
                            if hp < 4:
                                ps = c.ps.next()
                                for hh in range(2):
                                    fw.mm(ps[:, hh * 256:hh * 256 + 128], prevK[:, hp * 2 + hh, :], qT[:, hp, :])
                                    fw.mm(ps[:, hh * 256 + 128:hh * 256 + 256], curK[:, hp * 2 + hh, :], qT[:, hp, :])
                                pe = pes.next()
                                fw.act(pe[:, :], ps[:, :], AF.Exp, scale=0.125)
                                pm = pms.next()
                                fw.tt(("dve", "pool")[hp % 2], pm[:, :], pe[:, :], mask[:, :], ALU.mult)
                                pend.append((hp, pm))
                            if hp >= 1:
                                hp0, pm0 = pend.pop(0)
                                for hh in range(2):
                                    h = hp0 * 2 + hh
                                    ov = pO[h // 4][:, (h % 4) * 128:(h % 4) * 128 + 65]
                                    fw.mm(ov, pm0[:, hh * 256:hh * 256 + 128], prevV[:, h, 0:65], True, False)
                                    fw.mm(ov, pm0[:, hh * 256 + 128:hh * 256 + 256], curV[:, h, 0:65], False, True)
                        st = stg.next()
                        if stage < 4 or sub in (41, 42, 43, 44):
                            carry[r] = (curK, curV)
                            if J > 0:
                                kfree.append(prevK)
                                vfree.append(prevV)
                            continue
                        fw.cp("act", V([st], st.t[:, 0:260].rearrange("p (h e) -> p h e", h=4)),
                              V([pO[0]], pO[0].t[:, :].rearrange("p (h e) -> p h e", h=4)[:, :, 0:65]))
                        fw.cp("dve", V([st], st.t[:, 260:520].rearrange("p (h e) -> p h e", h=4)),
                              V([pO[1]], pO[1].t[:, :].rearrange("p (h e) -> p h e", h=4)[:, :, 0:65]))
                        if sub != 45:
                            fw.dma("sp", nds[g].view(ndv[r, J], 0, S), st[:, :])
                        if J > 0:
                            kfree.append(prevK)
                            vfree.append(prevV)
                        carry[r] = (curK, curV)
            fw.barrier()
    with ExitStack() as es:
        wo = sb(c, es, "woa", [128, 4, 1024], BF16)
        G = sb(c, es, "G", [128, 1024], F32)
        bcast(c, G[:, :], c.d["ln_gains"][li, 1])
        with ExitStack() as t2:
            load_w(c, t2, wo, c.d["a_w_o"], 512, 1024)
            fw.barrier()
        xts = rot(c, es, "xt", 4, [128, 1024], F32)
        n0s = rot(c, es, "n0", 3, [128, 520], F32)
        n1s = rot(c, es, "n1", 3, [128, 520], F32)
        n2s = rot(c, es, "n2", 3, [128, 520], F32)
        obs = rot(c, es, "ob", 2, [128, 512], BF16)
        oTs = rot(c, es, "oT", 2, [128, 4, 128], BF16)
        sts = rot(c, es, "st", 4, [128, 16], F32)
        xos = rot(c, es, "xo", 2, [128, 1024], F32)
        c.pn_tmp = rot(c, es, "pnt", 2, [128, 1024], F32)
        def ld_a2(ti):
            r0 = ti * 128
            xt = xts.next()
            fw.dma("sp", xt[:, :], xin.rows(r0, r0 + 128))
            n0, n1, n2 = n0s.next(), n1s.next(), n2s.next()
            fw.dma("sp", n0[:, :], nds[0].rows(r0, r0 + 128))
            fw.dma("sp", n1[:, :], nds[1].rows(r0, r0 + 128))
            fw.dma("sp", n2[:, :], nds[2].rows(r0, r0 + 128))
            return xt, n0, n1, n2

        for ti, (xt, n0, n1, n2) in prefetched(NT, ld_a2):
            r0 = ti * 128
            fw.tt("pool", n0[:, :], n0[:, :], n1[:, :], ALU.add)
            fw.tt("dve", n0[:, :], n0[:, :], n2[:, :], ALU.add)
            st = sts.next()
            n3 = V([n0], n0.t[:, :].rearrange("p (h e) -> p h e", h=8))
            fw.recip(st[:, 0:8], n3[:, :, 64])
            ob = obs.next()
            fw.tt("dve", V([ob], ob.t[:, :].rearrange("p (h e) -> p h e", h=8)), n3[:, :, 0:64],
                  V([st], st.t[:, 0:8].unsqueeze(2).to_broadcast([128, 8, 64])), ALU.mult)
            oT = oTs.next()
            transpose_to(c, ob, oT, 0, nchunk=4)
            psY = [c.ps.next(), c.ps.next()]
            for nh in range(2):
                for kc in range(4):
                    fw.mm(psY[nh][:, :], oT[:, kc, :], wo[:, kc, nh * 512:(nh + 1) * 512], kc == 0, kc == 3)
            postnorm_store(c, psY[0], psY[1], xt, G, xos.next(), sts.next(), xout.rows(r0, r0 + 128))
        fw.barrier()


def range_reduce_sin(c, dst, ang, off, u, ni, nfl, r):
    fw = c.fw
    fw.ts("dve", u, ang, 1.0 / TWO_PI, off, ALU.mult, ALU.add)
    fw.cp("dve", ni, u)
    fw.cp("dve", nfl, ni)
    if off:
        fw.ts("dve", u, ang, float(off * TWO_PI), None, ALU.add)
        fw.stt("dve", r, nfl, -TWO_PI, u, ALU.mult, ALU.add)
    else:
        fw.stt("dve", r, nfl, -TWO_PI, ang, ALU.mult, ALU.add)
    fw.ts("dve", r, r, float(np.pi), float(-np.pi), ALU.min, ALU.max)
    fw.act(dst, r, AF.Sin)


def mixer_c(c, li, xin, xout):
    fw = c.fw
    nc = c.nc
    TB = 512
    NB = S // TB
    H = 16
    QTd = nc.dram_tensor("c_qt", [H, 96, S], BF16, kind="Internal").ap()
    KTd = nc.dram_tensor("c_kt", [H, 96, S], BF16, kind="Internal").ap()
    Vd = nc.dram_tensor("c_v", [S, 1024], BF16, kind="Internal").ap()
    OTd = nc.dram_tensor("c_ot", [8, 128, S], BF16, kind="Internal").ap()
    QT = DramT(QTd.rearrange("h r s -> (h r) s"), blk=96)
    KT = DramT(KTd.rearrange("h r s -> (h r) s"), blk=96)
    VD = DramT(Vd)
    OT = DramT(OTd.rearrange("h r s -> (h r) s"), blk=128)
    with ExitStack() as es:
        win = sb(c, es, "c_win", [128, 8, 672], BF16)
        wkp = sb(c, es, "c_wkp", [128, 8, 96], BF16)
        wkp2 = sb(c, es, "c_wkp2", [128, 8, 96], BF16)
        wq = sb(c, es, "c_wq", [128, 3, 1536], BF16)
        wq2 = sb(c, es, "c_wq2", [128, 3, 1536], BF16)
        wkv = sb(c, es, "c_wkv", [128, 2, 2048], BF16)
        invf = sb(c, es, "c_invf", [128, 1], F32)
        gq = sb(c, es, "c_gq", [128, 3], F32)
        gkv = sb(c, es, "c_gkv", [128, 2], F32)
        fw.dma("sp", invf[:, :], RO(c.d["k_invf_c"]))
        with ExitStack() as t2:
            load_cols(c, t2, gq[:, :], c.d["c_q_norm"].rearrange("(c p) -> c p", p=128), 3)
            load_cols(c, t2, gkv[:, :], c.d["c_kv_norm"].rearrange("(c p) -> c p", p=128), 2)
            load_w(c, t2, win, c.d["c_w_in"], D, 672, gcol=gcol(c, li, 0))
            load_w(c, t2, wq, c.d["c_w_uq"], 384, 1536, gcol=gq)
            load_w(c, t2, wkv, c.d["c_w_ukv"], 256, 2048, gcol=gkv)
            fw.memset("pool", wkp[:, :, :], 0.0)
            fw.memset("pool", wkp2[:, :, :], 0.0)
            fw.memset("pool", wq2[:, :, :], 0.0)
            fw.cp("pool", wkp[:, :, 64:96], win[:, :, 640:672])
            fw.cp("pool", wkp2[:, :, 80:96], win[:, :, 640:656])
            fw.ts("pool", wkp2[:, :, 64:80], win[:, :, 656:672], -1.0)
            wq4 = wq.t[:, :, :].rearrange("p k (h e) -> p k h e", h=H)
            wq24 = wq2.t[:, :, :].rearrange("p k (h e) -> p k h e", h=H)
            for kc in range(3):
                fw.cp("pool", V([wq2], wq24[:, kc, :, 80:96]), V([wq], wq4[:, kc, :, 64:80]))
                fw.ts("pool", V([wq2], wq24[:, kc, :, 64:80]), V([wq], wq4[:, kc, :, 80:96]), -1.0)
            fw.barrier()
        xts = rot(c, es, "xt", 8, [128, 1024], F32)
        hns = rot(c, es, "hn", 2, [128, 1024], BF16)
        sts = rot(c, es, "st", 4, [128, 16], F32)
        hnTs = rot(c, es, "hnT", 2, [128, 8, TB], BF16)
        cqTs = rot(c, es, "cqT", 2, [128, 5, TB], BF16)
        cns = rot(c, es, "cn", 2, [128, 640], BF16)
        pis = rot(c, es, "pi", 1, [128, TB], I32)
        pfs = rot(c, es, "pf", 1, [128, TB], F32)
        angs = rot(c, es, "ang", 1, [128, TB], F32)
        CTs = rot(c, es, "CT", 2, [128, TB], F32)
        STs = rot(c, es, "ST", 2, [128, TB], F32)
        us = rot(c, es, "u", 1, [128, TB], F32)
        nis = rot(c, es, "ni", 1, [128, TB], I32)
        nfs = rot(c, es, "nf", 1, [128, TB], F32)
        rs = rot(c, es, "r", 1, [128, TB], F32)
        kpes = rot(c, es, "kpe", 2, [128, TB], BF16)
        t1s = rot(c, es, "t1", 3, [128, TB], F32)
        t2s = rot(c, es, "t2", 3, [128, TB], F32)
        qst = rot(c, es, "qst", 4, [128, TB], BF16)
        kst = rot(c, es, "kst", 4, [128, TB], BF16)
        vst = rot(c, es, "vst", 3, [128, 1024], BF16)
        wkv4 = wkv.t[:, :, :].rearrange("p k (h e) -> p k h e", h=H)
        for blk in range(NB):
            t0 = blk * TB
            hnT = hnTs.next()
            cqT = cqTs.next()
            pi, pf, ang = pis.next(), pfs.next(), angs.next()
            fw.dma("sp", pi[:, :], RO(c.d["positions"][t0:t0 + TB].partition_broadcast(128)))
            fw.cp("pool", pf[:, :], pi[:, :])
            fw.ts("dve", ang[:, :], pf[:, :], invf[:, 0:1])
            CT, ST = CTs.next(), STs.next()
            u, ni, nfl, r = us.next(), nis.next(), nfs.next(), rs.next()
            range_reduce_sin(c, ST[:, :], ang[:, :], 0.0, u[:, :], ni[:, :], nfl[:, :], r[:, :])
            range_reduce_sin(c, CT[:, :], ang[:, :], 0.25, u[:, :], ni[:, :], nfl[:, :], r[:, :])
            if blk == 0:
                xnext = []
                for ti in range(TB // 128):
                    xt = xts.next()
                    fw.dma("sp", xt[:, :], xin.rows(ti * 128, ti * 128 + 128))
                    xnext.append(xt)
            xcur = xnext
            xnext = []
            if blk + 1 < NB:
                for ti in range(TB // 128):
                    r1 = t0 + TB + ti * 128
                    xt = xts.next()
                    fw.dma("sp", xt[:, :], xin.rows(r1, r1 + 128))
                    xnext.append(xt)
            for ti in range(TB // 128):
                r0 = t0 + ti * 128
                xt = xcur[ti]
                hn = hns.next()
                st = sts.next()
                norm_tile(c, xt, hn, st)
                transpose_to(c, hn, hnT, ti * 128)
                psA, psB = c.ps.next(), c.ps.next()
                for kc in range(8):
                    fw.mm(psA[:, 0:384], hnT[:, kc, ti * 128:(ti + 1) * 128], win[:, kc, 0:384], kc == 0, kc == 7)
                for kc in range(8):
                    fw.mm(psB[:, 0:256], hnT[:, kc, ti * 128:(ti + 1) * 128], win[:, kc, 384:640], kc == 0, kc == 7)
                st2 = sts.next()
                fw.memset("pool", st2[:, 0:2], 0.0)
                fw.act(c.junk[:, 0:384], psA[:, 0:384], AF.Square, accum=st2[:, 0:1])
                fw.act(c.junk[:, 384:640], psB[:, 0:256], AF.Square, accum=st2[:, 1:2])
                fw.ts("dve", st2[:, 2:3], st2[:, 0:1], 1.0 / 384, EPS, ALU.mult, ALU.add)
                fw.ts("dve", st2[:, 3:4], st2[:, 1:2], 1.0 / 256, EPS, ALU.mult, ALU.add)
                fw.act(st2[:, 4:6], st2[:, 2:4], AF.Sqrt)
                fw.recip(st2[:, 6:8], st2[:, 4:6])
                cn = cns.next()
                fw.act(cn[:, 0:384], psA[:, 0:384], AF.Copy, scale=st2[:, 6:7])
                fw.act(cn[:, 384:640], psB[:, 0:256], AF.Copy, scale=st2[:, 7:8])
                transpose_to(c, cn, cqT, ti * 128, nchunk=5)
            psK, psK2 = c.ps.next(), c.ps.next()
            for kc in range(8):
                fw.mm(psK[0:96, :], wkp[:, kc, :], hnT[:, kc, :], kc == 0, kc == 7)
            for kc in range(8):
                fw.mm(psK2[0:96, :], wkp2[:, kc, :], hnT[:, kc, :], kc == 0, kc == 7)
            t1, t2 = t1s.next(), t2s.next()
            kpe = kpes.next()
            fw.tt("dve", t1[0:96, :], psK[0:96, :], CT[0:96, :], ALU.mult)
            fw.tt("dve", t2[0:96, :], psK2[0:96, :], ST[0:96, :], ALU.mult)
            fw.tt("pool", kpe[0:96, :], t1[0:96, :], t2[0:96, :], ALU.add)
            for h in range(H):
                psQ, psQ2, psKn = c.ps.next(), c.ps.next(), c.ps.next()
                for kc in range(3):
                    fw.mm(psQ[0:96, :], wq[:, kc, h * 96:(h + 1) * 96], cqT[:, kc, :], kc == 0, kc == 2)
                for kc in range(3):
                    fw.mm(psQ2[0:96, :], wq2[:, kc, h * 96:(h + 1) * 96], cqT[:, kc, :], kc == 0, kc == 2)
                for kc in range(2):
                    fw.mm(psKn[0:64, :], V([wkv], wkv4[:, kc, h, 0:64]), cqT[:, 3 + kc, :], kc == 0, kc == 1)
                t1, t2 = t1s.next(), t2s.next()
                qs = qst.next()
                fw.tt("dve", t1[0:96, :], psQ[0:96, :], CT[0:96, :], ALU.mult)
                fw.tt("dve", t2[0:96, :], psQ2[0:96, :], ST[0:96, :], ALU.mult)
                fw.tt("pool", qs[0:96, :], t1[0:96, :], t2[0:96, :], ALU.add)
                fw.dma("sp", QT.view(QTd[h, :, t0:t0 + TB], h * 96, (h + 1) * 96), qs[0:96, :])
                ks = kst.next()
                fw.cp("act", ks[0:64, :], psKn[0:64, :])
                fw.cp("pool", ks[64:96, :], kpe[64:96, :])
                fw.dma("sp", KT.view(KTd[h, :, t0:t0 + TB], h * 96, (h + 1) * 96), ks[0:96, :])
            for ti in range(TB // 128):
                r0 = t0 + ti * 128
                vs = vst.next()
                for hf in range(2):
                    ps = c.ps.next()
                    for kc in range(2):
                        fw.mm(V([ps], ps.t[:, :].rearrange("p (h e) -> p h e", h=8)),
                              cqT[:, 3 + kc, ti * 128:(ti + 1) * 128],
                              V([wkv], wkv4[:, kc, hf * 8:(hf + 1) * 8, 64:128]), kc == 0, kc == 1)
                    fw.cp(("act", "dve")[hf], vs[:, hf * 512:(hf + 1) * 512], ps[:, :])
                fw.dma("sp", VD.rows(r0, r0 + 128), vs[:, :])
        fw.barrier()
    with ExitStack() as es:
        qbufs = rot(c, es, "qh", 2, [128, S], BF16)
        kbufs = rot(c, es, "kh", 2, [128, S], BF16)
        vbufs = rot(c, es, "vh", 2, [128, NT, 80], BF16)
        for vb in vbufs.bufs:
            fw.memset("pool", vb[:, :, 64:65], 1.0)
        masks = sb(c, es, "cmask", [128, 4, TB], BF16)
        sel = sb(c, es, "csel", [128, 64], BF16)
        with ExitStack() as t2:
            mt = sb(c, t2, "mt", [128, TB], F32)
            for j in range(4):
                fw.memset("pool", mt[:, :], 1.0)
                fw.asel(mt[:, :], mt[:, :], [[1, TB]], ALU.is_ge, 0.0, -128 * j, -1)
                fw.cp("pool", masks[:, j, :], mt[:, :])
            fw.memset("pool", mt[:, 0:64], 0.0)
            fw.memset("pool", mt[64:96, 0:64], 1.0)
            fw.cp("pool", sel[:, :], mt[:, 0:64])
            fw.barrier()
        pes = rot(c, es, "pe", 5, [128, TB], BF16)
        pms = rot(c, es, "pm", 4, [128, TB], BF16)
        oas = rot(c, es, "oa", 2, [128, TB], BF16)
        ofs = rot(c, es, "of", 2, [128, TB], F32)
        rds = rot(c, es, "rd", 2, [128, TB], F32)
        ons = rot(c, es, "on", 3, [128, TB], BF16)
        scale = 96.0 ** -0.5
        def ld_c2(h):
            qh, kh, vh = qbufs.next(), kbufs.next(), vbufs.next()
            for q4 in range(4):
                cs = slice(q4 * 2048, (q4 + 1) * 2048)
                fw.dma("sp", qh[0:96, cs], QT.view(QTd[h, :, cs], h * 96, (h + 1) * 96))
                fw.dma("sp", kh[0:96, cs], KT.view(KTd[h, :, cs], h * 96, (h + 1) * 96))
            for q4 in range(4):
                fw.dma("sp", vh[:, q4 * 16:(q4 + 1) * 16, 0:64],
                       VD.view(Vd[q4 * 2048:(q4 + 1) * 2048, h * 64:(h + 1) * 64].rearrange("(t p) e -> p t e", p=128),
                               q4 * 2048, (q4 + 1) * 2048))
            return qh, kh, vh

        for h, (qh, kh, vh) in prefetched(H, ld_c2):
            for qb in range(NB):
                acc = c.psx.next()
                nk = 4 * qb + 4
                LOOK = 2
                pend = []
                for kt in range(nk + LOOK):
                    if kt < nk:
                        ps = c.ps.next()
                        fw.mm(ps[:, :], kh[0:96, kt * 128:(kt + 1) * 128], qh[0:96, qb * TB:(qb + 1) * TB])
                        pe = pes.next()
                        fw.act(pe[:, :], ps[:, :], AF.Exp, scale=scale)
                        if kt >= 4 * qb:
                            pm = pms.next()
                            fw.tt(("dve", "pool")[kt % 2], pm[:, :], pe[:, :], masks[:, kt - 4 * qb, :], ALU.mult)
                            pe = pm
                        pend.append((kt, pe))
                    if kt >= LOOK:
                        k0, pe0 = pend.pop(0)
                        fw.mm(acc[0:65, :], vh[:, k0, 0:65], pe0[:, :], k0 == 0, k0 == nk - 1)
                oa = oas.next()
                of = ofs.next()
                fw.cp("act", oa[0:65, :], acc[0:65, :])
                fw.cp("dve", of[0:64, :], acc[0:64, :])
                psd = c.ps.next()
                fw.mm(psd[0:64, :], sel[0:65, :], oa[0:65, :])
                rd = rds.next()
                fw.recip(rd[0:64, :], psd[0:64, :])
                on = ons.next()
                fw.tt("pool", on[0:64, :], of[0:64, :], rd[0:64, :], ALU.mult)
                hp, hh = h // 2, h % 2
                fw.dma("sp", OT.view(OTd[hp, hh * 64:(hh + 1) * 64, qb * TB:(qb + 1) * TB], hp * 128, (hp + 1) * 128),
                       on[0:64, :])
        fw.barrier()
    with ExitStack() as es:
        wo = sb(c, es, "c_wo", [128, 8, 1024], BF16)
        G = sb(c, es, "G", [128, 1024], F32)
        bcast(c, G[:, :], c.d["ln_gains"][li, 1])
        with ExitStack() as t2:
            load_w(c, t2, wo, c.d["c_w_o"], 1024, 1024)
            fw.barrier()
        xts = rot(c, es, "xt", 4, [128, 1024], F32)
        oTs = rot(c, es, "oT", 4, [128, 8, 128], BF16)
        sts = rot(c, es, "st", 4, [128, 16], F32)
        xos = rot(c, es, "xo", 2, [128, 1024], F32)
        c.pn_tmp = rot(c, es, "pnt", 2, [128, 1024], F32)
        def ld_c3(ti):
            r0 = ti * 128
            xt = xts.next()
            fw.dma("sp", xt[:, :], xin.rows(r0, r0 + 128))
            oT = oTs.next()
            fw.dma("sp", oT[:, :, :], OT.view(OTd[:, :, r0:r0 + 128].rearrange("h r s -> r h s"), 0, 1024))
            return xt, oT

        for ti, (xt, oT) in prefetched(NT, ld_c3):
            r0 = ti * 128
            psY = [c.ps.next(), c.ps.next()]
            for nh in range(2):
                for kc in range(8):
                    fw.mm(psY[nh][:, :], oT[:, kc, :], wo[:, kc, nh * 512:(nh + 1) * 512], kc == 0, kc == 7)
            postnorm_store(c, psY[0], psY[1], xt, G, xos.next(), sts.next(), xout.rows(r0, r0 + 128))
        fw.barrier()


def mixer_c(c, li, xin, xout):
    fw = c.fw
    nc = c.nc
    TB = 512
    NB = S // TB
    H = 16
    QTd = nc.dram_tensor("c_qt", [H, 96, S], BF16, kind="Internal").ap()
    KTd = nc.dram_tensor("c_kt", [H, 96, S], BF16, kind="Internal").ap()
    Vd = nc.dram_tensor("c_v", [S, 1024], BF16, kind="Internal").ap()
    OTd = nc.dram_tensor("c_ot", [8, 128, S], BF16, kind="Internal").ap()
    QT = DramT(QTd.rearrange("h r s -> (h r) s"), blk=96)
    KT = DramT(KTd.rearrange("h r s -> (h r) s"), blk=96)
    VD = DramT(Vd)
    OT = DramT(OTd.rearrange("h r s -> (h r) s"), blk=128)
    with ExitStack() as es:
        win = sb(c, es, "c_win", [128, 8, 672], BF16)
        wkp = sb(c, es, "c_wkp", [128, 8, 96], BF16)
        wkp2 = sb(c, es, "c_wkp2", [128, 8, 96], BF16)
        wq = sb(c, es, "c_wq", [128, 3, 1536], BF16)
        wq2 = sb(c, es, "c_wq2", [128, 3, 1536], BF16)
        wkv = sb(c, es, "c_wkv", [128, 2, 2048], BF16)
        invf = sb(c, es, "c_invf", [128, 1], F32)
        gq = sb(c, es, "c_gq", [128, 3], F32)
        gkv = sb(c, es, "c_gkv", [128, 2], F32)
        fw.dma("sp", invf[:, :], RO(c.d["k_invf_c"]))
        with ExitStack() as t2:
            load_cols(c, t2, gq[:, :], c.d["c_q_norm"].rearrange("(c p) -> c p", p=128), 3)
            load_cols(c, t2, gkv[:, :], c.d["c_kv_norm"].rearrange("(c p) -> c p", p=128), 2)
            load_w(c, t2, win, c.d["c_w_in"], D, 672, gcol=gcol(c, li, 0))
            load_w(c, t2, wq, c.d["c_w_uq"], 384, 1536, gcol=gq)
            load_w(c, t2, wkv, c.d["c_w_ukv"], 256, 2048, gcol=gkv)
            fw.memset("pool", wkp[:, :, :], 0.0)
            fw.memset("pool", wkp2[:, :, :], 0.0)
            fw.memset("pool", wq2[:, :, :], 0.0)
            fw.cp("pool", wkp[:, :, 64:96], win[:, :, 640:672])
            fw.cp("pool", wkp2[:, :, 80:96], win[:, :, 640:656])
            fw.ts("pool", wkp2[:, :, 64:80], win[:, :, 656:672], -1.0)
            wq4 = wq.t[:, :, :].rearrange("p k (h e) -> p k h e", h=H)
            wq24 = wq2.t[:, :, :].rearrange("p k (h e) -> p k h e", h=H)
            for kc in range(3):
                fw.cp("pool", V([wq2], wq24[:, kc, :, 80:96]), V([wq], wq4[:, kc, :, 64:80]))
                fw.ts("pool", V([wq2], wq24[:, kc, :, 64:80]), V([wq], wq4[:, kc, :, 80:96]), -1.0)
            fw.barrier()
        xts = rot(c, es, "xt", 8, [128, 1024], F32)
        hns = rot(c, es, "hn", 2, [128, 1024], BF16)
        sts = rot(c, es, "st", 4, [128, 16], F32)
        hnTs = rot(c, es, "hnT", 2, [128, 8, TB], BF16)
        cqTs = rot(c, es, "cqT", 2, [128, 5, TB], BF16)
        cns = rot(c, es, "cn", 2, [128, 640], BF16)
        pis = rot(c, es, "pi", 1, [128, TB], I32)
        pfs = rot(c, es, "pf", 1, [128, TB], F32)
        angs = rot(c, es, "ang", 1, [128, TB], F32)
        CTs = rot(c, es, "CT", 2, [128, TB], F32)
        STs = rot(c, es, "ST", 2, [128, TB], F32)
        us = rot(c, es, "u", 1, [128, TB], F32)
        nis = rot(c, es, "ni", 1, [128, TB], I32)
        nfs = rot(c, es, "nf", 1, [128, TB], F32)
        rs = rot(c, es, "r", 1, [128, TB], F32)
        kpes = rot(c, es, "kpe", 2, [128, TB], BF16)
        t1s = rot(c, es, "t1", 3, [128, TB], F32)
        t2s = rot(c, es, "t2", 3, [128, TB], F32)
        qst = rot(c, es, "qst", 4, [128, TB], BF16)
        kst = rot(c, es, "kst", 4, [128, TB], BF16)
        vst = rot(c, es, "vst", 3, [128, 1024], BF16)
        wkv4 = wkv.t[:, :, :].rearrange("p k (h e) -> p k h e", h=H)
        for blk in range(NB):
            t0 = blk * TB
            hnT = hnTs.next()
            cqT = cqTs.next()
            pi, pf, ang = pis.next(), pfs.next(), angs.next()
            fw.dma("sp", pi[:, :], RO(c.d["positions"][t0:t0 + TB].partition_broadcast(128)))
            fw.cp("pool", pf[:, :], pi[:, :])
            fw.ts("dve", ang[:, :], pf[:, :], invf[:, 0:1])
            CT, ST = CTs.next(), STs.next()
            u, ni, nfl, r = us.next(), nis.next(), nfs.next(), rs.next()
            range_reduce_sin(c, ST[:, :], ang[:, :], 0.0, u[:, :], ni[:, :], nfl[:, :], r[:, :])
            range_reduce_sin(c, CT[:, :], ang[:, :], 0.25, u[:, :], ni[:, :], nfl[:, :], r[:, :])
            if blk == 0:
                xnext = []
                for ti in range(TB // 128):
                    xt = xts.next()
                    fw.dma("sp", xt[:, :], xin.rows(ti * 128, ti * 128 + 128))
                    xnext.append(xt)
            xcur = xnext
            xnext = []
            if blk + 1 < NB:
                for ti in range(TB // 128):
                    r1 = t0 + TB + ti * 128
                    xt = xts.next()
                    fw.dma("sp", xt[:, :], xin.rows(r1, r1 + 128))
                    xnext.append(xt)
            for ti in range(TB // 128):
                r0 = t0 + ti * 128
                xt = xcur[ti]
                hn = hns.next()
                st = sts.next()
                norm_tile(c, xt, hn, st)
                transpose_to(c, hn, hnT, ti * 128)
                psA, psB = c.ps.next(), c.ps.next()
                for kc in range(8):
                    fw.mm(psA[:, 0:384], hnT[:, kc, ti * 128:(ti + 1) * 128], win[:, kc, 0:384], kc == 0, kc == 7)
                for kc in range(8):
                    fw.mm(psB[:, 0:256], hnT[:, kc, ti * 128:(ti + 1) * 128], win[:, kc, 384:640], kc == 0, kc == 7)
                st2 = sts.next()
                fw.memset("pool", st2[:, 0:2], 0.0)
                fw.act(c.junk[:, 0:384], psA[:, 0:384], AF.Square, accum=st2[:, 0:1])
                fw.act(c.junk[:, 384:640], psB[:, 0:256], AF.Square, accum=st2[:, 1:2])
                fw.ts("dve", st2[:, 2:3], st2[:, 0:1], 1.0 / 384, EPS, ALU.mult, ALU.add)
                fw.ts("dve", st2[:, 3:4], st2[:, 1:2], 1.0 / 256, EPS, ALU.mult, ALU.add)
                fw.act(st2[:, 4:6], st2[:, 2:4], AF.Sqrt)
                fw.recip(st2[:, 6:8], st2[:, 4:6])
                cn = cns.next()
                fw.act(cn[:, 0:384], psA[:, 0:384], AF.Copy, scale=st2[:, 6:7])
                fw.act(cn[:, 384:640], psB[:, 0:256], AF.Copy, scale=st2[:, 7:8])
                transpose_to(c, cn, cqT, ti * 128, nchunk=5)
            psK, psK2 = c.ps.next(), c.ps.next()
            for kc in range(8):
                fw.mm(psK[0:96, :], wkp[:, kc, :], hnT[:, kc, :], kc == 0, kc == 7)
            for kc in range(8):
                fw.mm(psK2[0:96, :], wkp2[:, kc, :], hnT[:, kc, :], kc == 0, kc == 7)
            t1, t2 = t1s.next(), t2s.next()
            kpe = kpes.next()
            fw.tt("dve", t1[0:96, :], psK[0:96, :], CT[0:96, :], ALU.mult)
            fw.tt("dve", t2[0:96, :], psK2[0:96, :], ST[0:96, :], ALU.mult)
            fw.tt("pool", kpe[0:96, :], t1[0:96, :], t2[0:96, :], ALU.add)
            for h in range(H):
                psQ, psQ2, psKn = c.ps.next(), c.ps.next(), c.ps.next()
                for kc in range(3):
                    fw.mm(psQ[0:96, :], wq[:, kc, h * 96:(h + 1) * 96], cqT[:, kc, :], kc == 0, kc == 2)
                for kc in range(3):
                    fw.mm(psQ2[0:96, :], wq2[:, kc, h * 96:(h + 1) * 96], cqT[:, kc, :], kc == 0, kc == 2)
                for kc in range(2):
                    fw.mm(psKn[0:64, :], V([wkv], wkv4[:, kc, h, 0:64]), cqT[:, 3 + kc, :], kc == 0, kc == 1)
                t1, t2 = t1s.next(), t2s.next()
                qs = qst.next()
                fw.tt("dve", t1[0:96, :], psQ[0:96, :], CT[0:96, :], ALU.mult)
                fw.tt("dve", t2[0:96, :], psQ2[0:96, :], ST[0:96, :], ALU.mult)
                fw.tt("pool", qs[0:96, :], t1[0:96, :], t2[0:96, :], ALU.add)
                fw.dma("sp", QT.view(QTd[h, :, t0:t0 + TB], h * 96, (h + 1) * 96), qs[0:96, :])
                ks = kst.next()
                fw.cp("act", ks[0:64, :], psKn[0:64, :])
                fw.cp("pool", ks[64:96, :], kpe[64:96, :])
                fw.dma("sp", KT.view(KTd[h, :, t0:t0 + TB], h * 96, (h + 1) * 96), ks[0:96, :])
            for ti in range(TB // 128):
                r0 = t0 + ti * 128
                vs = vst.next()
                for hf in range(2):
                    ps = c.ps.next()
                    for kc in range(2):
                        fw.mm(V([ps], ps.t[:, :].rearrange("p (h e) -> p h e", h=8)),
                              cqT[:, 3 + kc, ti * 128:(ti + 1) * 128],
                              V([wkv], wkv4[:, kc, hf * 8:(hf + 1) * 8, 64:128]), kc == 0, kc == 1)
                    fw.cp(("act", "dve")[hf], vs[:, hf * 512:(hf + 1) * 512], ps[:, :])
                fw.dma("sp", VD.rows(r0, r0 + 128), vs[:, :])
        fw.barrier()
    with ExitStack() as es:
        qbufs = rot(c, es, "qh", 2, [128, S], BF16)
        kbufs = rot(c, es, "kh", 2, [128, S], BF16)
        vbufs = rot(c, es, "vh", 2, [128, NT, 80], BF16)
        for vb in vbufs.bufs:
            fw.memset("pool", vb[:, :, 64:65], 1.0)
        masks = sb(c, es, "cmask", [128, 4, TB], BF16)
        sel = sb(c, es, "csel", [128, 64], BF16)
        with ExitStack() as t2:
            mt = sb(c, t2, "mt", [128, TB], F32)
            for j in range(4):
                fw.memset("pool", mt[:, :], 1.0)
                fw.asel(mt[:, :], mt[:, :], [[1, TB]], ALU.is_ge, 0.0, -128 * j, -1)
                fw.cp("pool", masks[:, j, :], mt[:, :])
            fw.memset("pool", mt[:, 0:64], 0.0)
            fw.memset("pool", mt[64:96, 0:64], 1.0)
            fw.cp("pool", sel[:, :], mt[:, 0:64])
            fw.barrier()
        pes = rot(c, es, "pe", 5, [128, TB], BF16)
        pms = rot(c, es, "pm", 4, [128, TB], BF16)
        oas = rot(c, es, "oa", 2, [128, TB], BF16)
        ofs = rot(c, es, "of", 2, [128, TB], F32)
        rds = rot(c, es, "rd", 2, [128, TB], F32)
        ons = rot(c, es, "on", 3, [128, TB], BF16)
        scale = 96.0 ** -0.5
        def ld_c2(h):
            qh, kh, vh = qbufs.next(), kbufs.next(), vbufs.next()
            for q4 in range(4):
                cs = slice(q4 * 2048, (q4 + 1) * 2048)
                fw.dma("sp", qh[0:96, cs], QT.view(QTd[h, :, cs], h * 96, (h + 1) * 96))
                fw.dma("sp", kh[0:96, cs], KT.view(KTd[h, :, cs], h * 96, (h + 1) * 96))
            for q4 in range(4):
                fw.dma("sp", vh[:, q4 * 16:(q4 + 1) * 16, 0:64],
                       VD.view(Vd[q4 * 2048:(q4 + 1) * 2048, h * 64:(h + 1) * 64].rearrange("(t p) e -> p t e", p=128),
                               q4 * 2048, (q4 + 1) * 2048))
            return qh, kh, vh

        for h, (qh, kh, vh) in prefetched(H, ld_c2):
            for qb in range(NB):
                acc = c.psx.next()
                nk = 4 * qb + 4
                LOOK = 2
                pend = []
                for kt in range(nk + LOOK):
                    if kt < nk:
                        ps = c.ps.next()
                        fw.mm(ps[:, :], kh[0:96, kt * 128:(kt + 1) * 128], qh[0:96, qb * TB:(qb + 1) * TB])
                        pe = pes.next()
                        fw.act(pe[:, :], ps[:, :], AF.Exp, scale=scale)
                        if kt >= 4 * qb:
                            pm = pms.next()
                            fw.tt(("dve", "pool")[kt % 2], pm[:, :], pe[:, :], masks[:, kt - 4 * qb, :], ALU.mult)
                            pe = pm
                        pend.append((kt, pe))
                    if kt >= LOOK:
                        k0, pe0 = pend.pop(0)
                        fw.mm(acc[0:65, :], vh[:, k0, 0:65], pe0[:, :], k0 == 0, k0 == nk - 1)
                oa = oas.next()
                of = ofs.next()
                fw.cp("act", oa[0:65, :], acc[0:65, :])
                fw.cp("dve", of[0:64, :], acc[0:64, :])
                psd = c.ps.next()
                fw.mm(psd[0:64, :], sel[0:65, :], oa[0:65, :])
                rd = rds.next()
                fw.recip(rd[0:64, :], psd[0:64, :])
                on = ons.next()
                fw.tt("pool", on[0:64, :], of[0:64, :], rd[0:64, :], ALU.mult)
                hp, hh = h // 2, h % 2
                fw.dma("sp", OT.view(OTd[hp, hh * 64:(hh + 1) * 64, qb * TB:(qb + 1) * TB], hp * 128, (hp + 1) * 128),
                       on[0:64, :])
        fw.barrier()
    with ExitStack() as es:
        wo = sb(c, es, "c_wo", [128, 8, 1024], BF16)
        G = sb(c, es, "G", [128, 1024], F32)
        bcast(c, G[:, :], c.d["ln_gains"][li, 1])
        with ExitStack() as t2:
            load_w(c, t2, wo, c.d["c_w_o"], 1024, 1024)
            fw.barrier()
        xts = rot(c, es, "xt", 4, [128, 1024], F32)
        oTs = rot(c, es, "oT", 4, [128, 8, 128], BF16)
        sts = rot(c, es, "st", 4, [128, 16], F32)
        xos = rot(c, es, "xo", 2, [128, 1024], F32)
        c.pn_tmp = rot(c, es, "pnt", 2, [128, 1024], F32)
        def ld_c3(ti):
            r0 = ti * 128
            xt = xts.next()
            fw.dma("sp", xt[:, :], xin.rows(r0, r0 + 128))
            oT = oTs.next()
            fw.dma("sp", oT[:, :, :], OT.view(OTd[:, :, r0:r0 + 128].rearrange("h r s -> r h s"), 0, 1024))
            return xt, oT

        for ti, (xt, oT) in prefetched(NT, ld_c3):
            r0 = ti * 128
            psY = [c.ps.next(), c.ps.next()]
            for nh in range(2):
                for kc in range(8):
                    fw.mm(psY[nh][:, :], oT[:, kc, :], wo[:, kc, nh * 512:(nh + 1) * 512], kc == 0, kc == 7)
            postnorm_store(c, psY[0], psY[1], xt, G, xos.next(), sts.next(), xout.rows(r0, r0 + 128))
        fw.barrier()


def mixer_b(c, li, xin, xout):
    fw = c.fw
    TB = 512
    NB = S // TB
    with ExitStack() as es:
        w = sb(c, es, "b_w", [128, 8, 3088], BF16)
        wo = sb(c, es, "b_wo", [128, 8, 1024], BF16)
        wg2 = sb(c, es, "b_wg2", [16, 512], F32)
        bb = sb(c, es, "b_bias", [128, 512], F32)
        onb = sb(c, es, "b_onb", [128, 256], F32)
        G = sb(c, es, "G", [128, 1024], F32)
        triU = sb(c, es, "b_triU", [128, 128], F32)
        triS = sb(c, es, "b_triS", [128, 128], F32)
        maskU = sb(c, es, "b_maskU", [128, 4, 128], F32)
        S32 = sb(c, es, "b_S32", [128, 4, 256], F32)
        Sbf = sb(c, es, "b_Sbf", [128, 4, 256], BF16)
        bcast(c, G[:, :], c.d["ln_gains"][li, 1])
        bcast(c, bb[:, :], c.d["b_gate_bias"])
        bcast(c, onb[:, :], c.d["b_o_norm"])
        fw.dma("sp", wg2[:, :], RO(c.d["b_w_gate2"]))
        fw.memset("pool", S32[:, :, :], 0.0)
        fw.memset("pool", Sbf[:, :, :], 0.0)
        fw.memset("pool", triU[:, :], -1.0 / 16)
        fw.asel(triU[:, :], triU[:, :], [[1, 128]], ALU.is_ge, 0.0, 0, -1)
        fw.memset("pool", triS[:, :], -1.0 / 16)
        fw.asel(triS[:, :], triS[:, :], [[-1, 128]], ALU.is_gt, 0.0, 0, 1)
        fw.memset("pool", maskU[:, :, :], 1.0)
        for h in range(4):
            fw.asel(maskU[:, h, :], maskU[:, h, :], [[1, 128]], ALU.is_ge, 0.0, 0, -1)
        with ExitStack() as t2:
            load_w(c, t2, w, c.d["b_w_in"], D, 3088, gcol=gcol(c, li, 0))
            load_w(c, t2, wo, c.d["b_w_o"], 1024, 1024)
            fw.barrier()
        xts = rot(c, es, "xt", 8, [128, 1024], F32)
        hns = rot(c, es, "hn", 2, [128, 1024], BF16)
        sts = rot(c, es, "st", 6, [128, 16], F32)
        hnTs = rot(c, es, "hnT", 1, [128, 8, TB], BF16)
        qkTs = rot(c, es, "qkT", 1, [128, 8, TB], BF16)
        glTs = rot(c, es, "glT", 2, [16, TB], F32)
        zs = rot(c, es, "z", 2, [128, 512], F32)
        Ls = rot(c, es, "L", 2, [128, 512], F32)
        EGs = rot(c, es, "EG", 2, [128, 4, 128], F32)
        EnGs = rot(c, es, "EnG", 2, [128, 4, 128], F32)
        EGcs = rot(c, es, "EGc", 2, [128, 512], F32)
        qds = rot(c, es, "qd", 2, [128, 4, 128], BF16)
        kis = rot(c, es, "ki", 2, [128, 4, 128], BF16)
        kes = rot(c, es, "ke", 2, [128, 512], BF16)
        vs_ = rot(c, es, "v", 2, [128, 1024], BF16)
        ats = rot(c, es, "at", 2, [128, 4, 128], BF16)
        gss = rot(c, es, "gs", 1, [128, 1024], F32)
        ons = rot(c, es, "on", 1, [128, 1024], F32)
        obs = rot(c, es, "ob", 1, [128, 1024], BF16)
        obTs = rot(c, es, "obT", 2, [128, 8, 128], BF16)
        xos = rot(c, es, "xo", 2, [128, 1024], F32)
        c.pn_tmp = None
        for blk in range(NB):
            t0 = blk * TB
            hnT = hnTs.next()
            if blk == 0:
                xnext = []
                for ti in range(4):
                    xt = xts.next()
                    fw.dma("sp", xt[:, :], xin.rows(ti * 128, ti * 128 + 128))
                    xnext.append(xt)
            xl = xnext
            xnext = []
            if blk + 1 < NB:
                for ti in range(4):
                    r1 = t0 + TB + ti * 128
                    xt = xts.next()
                    fw.dma("sp", xt[:, :], xin.rows(r1, r1 + 128))
                    xnext.append(xt)
            for ti in range(4):
                hn = hns.next()
                norm_tile(c, xl[ti], hn, sts.next())
                transpose_to(c, hn, hnT, ti * 128)
            qkT = qkTs.next()
            for j in range(8):
                ps = c.ps.next()
                for kc in range(8):
                    fw.mm(ps[:, :], w[:, kc, j * 128:(j + 1) * 128], hnT[:, kc, :], kc == 0, kc == 7)
                fw.cp(("act", "dve")[j % 2], qkT[:, j, :], ps[:, :])
            glT = glTs.next()
            ps = c.ps.next()
            for kc in range(8):
                fw.mm(ps[0:16, :], w[:, kc, 3072:3088], hnT[:, kc, :], kc == 0, kc == 7)
            fw.cp("act", glT[:, :], ps[0:16, :])
            for ti in range(4):
                r0 = t0 + ti * 128
                tsl = slice(ti * 128, (ti + 1) * 128)
                psZ = c.ps.next()
                fw.mm(psZ[:, :], glT[:, tsl], wg2[:, :])
                z = zs.next()
                fw.tt("dve", z[:, :], psZ[:, :], bb[:, :], ALU.add)
                L = Ls.next()
                fw.act(z[:, :], z[:, :], AF.Exp, scale=-1.0)
                fw.act(L[:, :], z[:, :], AF.Ln, bias=1.0)
                psG = c.ps.next()
                for h in range(4):
                    fw.mm(psG[:, h * 128:(h + 1) * 128], L[:, h * 128:(h + 1) * 128], triU[:, :])
                EG, EnG = EGs.next(), EnGs.next()
                pg3 = V([psG], psG.t[:, :].rearrange("p (h t) -> p h t", h=4))
                fw.act(EG[:, :, :], pg3, AF.Exp)
                fw.act(EnG[:, :, :], pg3, AF.Exp, scale=-1.0)
                psGc = c.ps.next()
                fw.mm(psGc[:, :], triS[:, :], L[:, :])
                EGc = EGcs.next()
                fw.act(EGc[:, :], psGc[:, :], AF.Exp)
                qd, ki = qds.next(), kis.next()
                fw.stt("dve", qd[:, :, :], qkT[:, 0:4, tsl], 128.0 ** -0.5, EG[:, :, :], ALU.mult, ALU.mult)
                fw.tt("pool", ki[:, :, :], qkT[:, 4:8, tsl], EnG[:, :, :], ALU.mult)
                psK = c.ps.next()
                for kc in range(8):
                    fw.mm(psK[:, :], hnT[:, kc, tsl], w[:, kc, 512:1024], kc == 0, kc == 7)
                ke = kes.next()
                fw.tt("dve", ke[:, :], psK[:, :], EGc[:, :], ALU.mult)
                v = vs_.next()
                for hf in range(2):
                    ps = c.ps.next()
                    for kc in range(8):
                        fw.mm(ps[:, :], hnT[:, kc, tsl], w[:, kc, 1024 + hf * 512:1536 + hf * 512], kc == 0, kc == 7)
                    fw.cp(("act", "dve")[hf], v[:, hf * 512:(hf + 1) * 512], ps[:, :])
                psA = c.ps.next()
                for h in range(4):
                    fw.mm(psA[:, h * 128:(h + 1) * 128], ki[:, h, :], qd[:, h, :])
                at = ats.next()
                fw.tt("dve", at[:, :, :], V([psA], psA.t[:, :].rearrange("p (h t) -> p h t", h=4)), maskU[:, :, :],
                      ALU.mult)
                pO = [c.psx.next(), c.psx.next()]
                for h in range(4):
                    ov = pO[h // 2][:, (h % 2) * 256:(h % 2 + 1) * 256]
                    fw.mm(ov, at[:, h, :], v[:, h * 256:(h + 1) * 256], True, False)
                    fw.mm(ov, qd[:, h, :], Sbf[:, h, :], False, True)
                gs = gss.next()
                for hf in range(2):
                    ps = c.ps.next()
                    for kc in range(8):
                        fw.mm(ps[:, :], hnT[:, kc, tsl], w[:, kc, 2048 + hf * 512:2560 + hf * 512], kc == 0, kc == 7)
                    fw.act(gs[:, hf * 512:(hf + 1) * 512], ps[:, :], AF.Silu)
                for hf in range(2):
                    ps = c.ps.next()
                    for hh in range(2):
                        h = hf * 2 + hh
                        fw.mm(ps[:, hh * 256:(hh + 1) * 256], ke[:, h * 128:(h + 1) * 128], v[:, h * 256:(h + 1) * 256])
                    for hh in range(2):
                        h = hf * 2 + hh
                        fw.stt("dve", S32[:, h, :], S32[:, h, :], EG[:, h, 127:128], ps[:, hh * 256:(hh + 1) * 256],
                               ALU.mult, ALU.add)
                        fw.cp("act", Sbf[:, h, :], S32[:, h, :])
                st = sts.next()
                fw.memset("pool", st[:, 0:4], 0.0)
                for h in range(4):
                    fw.act(c.junk[:, h * 256:(h + 1) * 256], pO[h // 2][:, (h % 2) * 256:(h % 2 + 1) * 256], AF.Square,
                           accum=st[:, h:h + 1])
                fw.ts("dve", st[:, 4:8], st[:, 0:4], 1.0 / 256, EPS, ALU.mult, ALU.add)
                fw.act(st[:, 8:12], st[:, 4:8], AF.Sqrt)
                fw.recip(st[:, 12:16], st[:, 8:12])
                on = ons.next()
                for h in range(4):
                    fw.stt("dve", on[:, h * 256:(h + 1) * 256], pO[h // 2][:, (h % 2) * 256:(h % 2 + 1) * 256],
                           st[:, 12 + h:13 + h], onb[:, :], ALU.mult, ALU.mult)
                ob = obs.next()
                fw.tt("pool", ob[:, :], on[:, :], gs[:, :], ALU.mult)
                obT = obTs.next()
                transpose_to(c, ob, obT, 0)
                psY = [c.ps.next(), c.ps.next()]
                for nh in range(2):
                    for kc in range(8):
                        fw.mm(psY[nh][:, :], obT[:, kc, :], wo[:, kc, nh * 512:(nh + 1) * 512], kc == 0, kc == 7)
                postnorm_store(c, psY[0], psY[1], xl[ti], G, xos.next(), sts.next(), xout.rows(r0, r0 + 128))
        fw.barrier()


EM05 = float(np.exp(-0.5))


def block_mask(c, dst, kind, val, CH=32):
    fw = c.fw
    fw.memset("pool", dst, val)
    for cc in range(128 // CH):
        v = dst[:, cc * CH:(cc + 1) * CH]
        lo = cc * CH
        if kind == "IU":
            fw.asel(v, v, [[1, CH]], ALU.is_ge, 0.0, lo, -1)
            fw.asel(v, v, [[0, CH]], ALU.is_ge, 0.0, -lo, 1)
        elif kind == "SU":
            fw.asel(v, v, [[1, CH]], ALU.is_gt, 0.0, lo, -1)
            fw.asel(v, v, [[0, CH]], ALU.is_ge, 0.0, -lo, 1)
        else:
            fw.asel(v, v, [[-1, CH]], ALU.is_gt, 0.0, -lo, 1)
            fw.asel(v, v, [[0, CH]], ALU.is_ge, 0.0, lo + CH - 1, -1)


def chunk_ind(c, dst, val, CH=32):
    fw = c.fw
    fw.memset("pool", dst, val)
    for cc in range(128 // CH):
        v = dst[:, cc:cc + 1]
        fw.asel(v, v, [[0, 1]], ALU.is_ge, 0.0, -cc * CH, 1)
        fw.asel(v, v, [[0, 1]], ALU.is_ge, 0.0, cc * CH + CH - 1, -1)


def mixer_d(c, li, xin, xout):
    import os
    fw = c.fw
    nc = c.nc
    H = 16
    NC = 4
    names = ["At", "Rt", "Bh", "Kh", "Bt", "Kt", "Vv", "BON", "GATE"]
    dd = {n: DramT(nc.dram_tensor("d_" + n, [S, 1024], BF16, kind="Internal").ap()) for n in names}
    GLd_ap = nc.dram_tensor("d_GL", [NT * 64, 64], F32, kind="Internal").ap()
    GLd = DramT(GLd_ap, blk=64)
    Yd = DramT(nc.dram_tensor("d_Y", [S, 1024], F32, kind="Internal").ap())
    dstage = int(os.environ.get("D_STAGE", "9"))
    with ExitStack() as es:
        Wr = sb(c, es, "d_Wr", [128, 8, 1024], BF16)
        Wk = sb(c, es, "d_Wk", [128, 8, 1024], BF16)
        Wv = sb(c, es, "d_Wv", [128, 8, 1024], BF16)
        w1 = sb(c, es, "d_w1", [128, 8, 64], BF16)
        a1 = sb(c, es, "d_a1", [128, 8, 64], BF16)
        g1 = sb(c, es, "d_g1", [128, 8, 128], BF16)
        w2 = sb(c, es, "d_w2", [64, 1, 1024], BF16)
        a2 = sb(c, es, "d_a2", [64, 1, 1024], BF16)
        g2 = sb(c, es, "d_g2", [128, 1, 1024], BF16)
        mixc = sb(c, es, "d_mix", [128, 48], F32)
        bc = {}
        for n in ("d_w0", "d_a0", "d_k_k", "d_k_a"):
            bc[n] = sb(c, es, n + "b", [128, 1024], F32)
            bcast(c, bc[n][:, :], c.d[n])
        rkb = sb(c, es, "d_rkb", [128, 1024], F32)
        bcast(c, rkb[:, :], c.d["d_r_k"].rearrange("h e -> (h e)"))
        omka = sb(c, es, "d_omka", [128, 1024], F32)
        fw.ts("pool", omka[:, :], bc["d_k_a"][:, :], -1.0, 1.0, ALU.mult, ALU.add)
        triC = sb(c, es, "d_triC", [128, 128], F32)
        triD = sb(c, es, "d_triD", [128, 128], F32)
        indC = sb(c, es, "d_indC", [128, NC], F32)
        block_mask(c, triC[:, :], "IU", -EM05)
        block_mask(c, triD[:, :], "SL", -EM05)
        chunk_ind(c, indC[:, :], -EM05)
        hz = sb(c, es, "d_hz", [128, 8, 128], BF16)
        fw.memset("pool", hz[:, :, :], 0.0)
        with ExitStack() as t2:
            load_cols(c, t2, mixc[:, :], c.d["d_mix"].rearrange("i (c p) -> (i c) p", p=128), 48)
            gc = gcol(c, li, 0)
            load_w(c, t2, Wr, c.d["d_w_rkv"][0], D, 1024, gcol=gc)
            load_w(c, t2, Wk, c.d["d_w_rkv"][1], D, 1024, gcol=gc)
            load_w(c, t2, Wv, c.d["d_w_rkv"][2], D, 1024, gcol=gc)
            load_w(c, t2, w1, c.d["d_w1"], D, 64, gcol=gc)
            load_w(c, t2, a1, c.d["d_a1"], D, 64, gcol=gc)
            load_w(c, t2, g1, c.d["d_g1"], D, 128, gcol=gc)
            for dst, src, kk_ in ((w2, "d_w2", 64), (a2, "d_a2", 64), (g2, "d_g2", 128)):
                st_ = sb(c, t2, "wst", [128, 1024], F32)
                fw.dma("sp", st_[0:kk_, :], RO(c.d[src]))
                fw.cp("dve", dst[0:kk_, 0, :], st_[0:kk_, :])
            fw.barrier()
        xts = rot(c, es, "xt", 3, [128, 1024], F32)
        hns = rot(c, es, "hn", 2, [128, 1024], BF16)
        sts = rot(c, es, "st", 4, [128, 16], F32)
        hnTs = rot(c, es, "hnT", 2, [128, 8, 128], BF16)
        DTs = rot(c, es, "DT", 1, [128, 8, 128], BF16)
        Xs = rot(c, es, "X", 7, [128, 8, 128], BF16)
        smT = rot(c, es, "smT", 4, [128, 128], BF16)
        f32s = rot(c, es, "f", 9, [128, 1024], F32)
        b16s = rot(c, es, "o", 12, [128, 1024], BF16)
        s16 = rot(c, es, "s16", 4, [128, 64], F32)
        glts = rot(c, es, "glt", 2, [64, 64], F32)
        prev = hz

        def v3(buf):
            return V([buf], buf.t[:, :].rearrange("p (h e) -> p h e", h=H))

        def b3(vw):
            return V(vw.bufs, vw.ap.unsqueeze(2).to_broadcast([128, H, 64]))

        def ld_d1(ti):
            xt = xts.next()
            fw.dma("sp", xt[:, :], xin.rows(ti * 128, ti * 128 + 128))
            return xt

        for ti, xt in prefetched(NT if dstage >= 1 else 0, ld_d1):
            r0 = ti * 128
            hn = hns.next()
            norm_tile(c, xt, hn, sts.next())
            cur = hnTs.next()
            transpose_to(c, hn, cur, 0)
            DT = DTs.next()
            fw.tt("pool", DT[:, :, 0:1], prev[:, :, 127:128], cur[:, :, 0:1], ALU.subtract)
            fw.tt("pool", DT[:, :, 1:128], cur[:, :, 0:127], cur[:, :, 1:128], ALU.subtract)
            X = []
            for i in range(6):
                Xi = Xs.next()
                for kc in range(8):
                    mcol = mixc[:, i * 8 + kc:i * 8 + kc + 1]
                    if (i * 8 + kc) % 3 != 2:
                        fw.stt("dve", Xi[:, kc, :], DT[:, kc, :], mcol, cur[:, kc, :], ALU.mult, ALU.add)
                    else:
                        fw.act(Xi[:, kc, :], DT[:, kc, :], AF.Copy, scale=mcol)
                        fw.tt("pool", Xi[:, kc, :], Xi[:, kc, :], cur[:, kc, :], ALU.add)
                X.append(Xi)
            Xr, Xw, Xk, Xv, Xa, Xg = X
            prev = cur

            def proj2(Xi, W, nh):
                ps = c.ps.next()
                for kc in range(8):
                    fw.mm(ps[:, :], Xi[:, kc, :], W[:, kc, nh * 512:(nh + 1) * 512], kc == 0, kc == 7)
                return ps

            def low(Xi, Wl, M, func):
                ps = c.ps.next()
                for kc in range(8):
                    fw.mm(ps[0:M, 0:128], Wl[:, kc, :], Xi[:, kc, :], kc == 0, kc == 7)
                o = smT.next()
                fw.act(o[0:M, :], ps[0:M, 0:128], func)
                return o

            twT = low(Xw, w1, 64, AF.Tanh)
            aaT = low(Xa, a1, 64, AF.Copy)
            ggT = low(Xg, g1, 128, AF.Sigmoid)
            sig = f32s.next()
            av = f32s.next()
            for nh in range(2):
                cs_ = slice(nh * 512, (nh + 1) * 512)
                ps = c.ps.next()
                fw.mm(ps[:, :], twT[0:64, :], w2[0:64, 0, cs_])
                fw.tt("dve", sig[:, cs_], ps[:, :], bc["d_w0"][:, cs_], ALU.add)
                ps = c.ps.next()
                fw.mm(ps[:, :], aaT[0:64, :], a2[0:64, 0, cs_])
                fw.tt("dve", av[:, cs_], ps[:, :], bc["d_a0"][:, cs_], ALU.add)
            fw.act(sig[:, :], sig[:, :], AF.Sigmoid)
            fw.act(av[:, :], av[:, :], AF.Sigmoid)
            gate = b16s.next()
            for nh in range(2):
                ps = c.ps.next()
                fw.mm(ps[:, :], ggT[:, :], g2[:, 0, nh * 512:(nh + 1) * 512])
                fw.cp("act", gate[:, nh * 512:(nh + 1) * 512], ps[:, :])
            fw.dma("sp", dd["GATE"].rows(r0, r0 + 128), gate[:, :])
            rr = f32s.next()
            kx = f32s.next()
            vv = f32s.next()
            for nh in range(2):
                cs_ = slice(nh * 512, (nh + 1) * 512)
                fw.cp("act", rr[:, cs_], proj2(Xr, Wr, nh)[:, :])
                fw.cp("act", kx[:, cs_], proj2(Xk, Wk, nh)[:, :])
                fw.cp("act", vv[:, cs_], proj2(Xv, Wv, nh)[:, :])
            Vb = b16s.next()
            fw.cp("act", Vb[:, :], vv[:, :])
            fw.dma("sp", dd["Vv"].rows(r0, r0 + 128), Vb[:, :])
            kkx = f32s.next()
            tmp = f32s.next()
            fw.tt("dve", kkx[:, :], kx[:, :], bc["d_k_k"][:, :], ALU.mult)
            fw.act(tmp[:, :], kkx[:, :], AF.Square)
            sm = s16.next()
            fw.red("dve", sm[:, 0:16], v3(tmp), ALU.add)
            fw.act(sm[:, 16:32], sm[:, 0:16], AF.Sqrt)
            fw.ts("dve", sm[:, 16:32], sm[:, 16:32], 1e-12, None, ALU.max)
            fw.recip(sm[:, 32:48], sm[:, 16:32])
            fw.tt("dve", v3(kkx), v3(kkx), b3(sm[:, 32:48]), ALU.mult)
            fw.tt("pool", tmp[:, :], av[:, :], bc["d_k_a"][:, :], ALU.mult)
            fw.tt("pool", tmp[:, :], tmp[:, :], omka[:, :], ALU.add)
            fw.tt("dve", kx[:, :], kx[:, :], tmp[:, :], ALU.mult)
            fw.tt("dve", tmp[:, :], rr[:, :], rkb[:, :], ALU.mult)
            fw.tt("pool", tmp[:, :], tmp[:, :], kx[:, :], ALU.mult)
            fw.red("dve", sm[:, 48:64], v3(tmp), ALU.add)
            bon = b16s.next()
            fw.tt("dve", v3(bon), v3(vv), b3(sm[:, 48:64]), ALU.mult)
            fw.dma("sp", dd["BON"].rows(r0, r0 + 128), bon[:, :])
            fw.tt("pool", av[:, :], kkx[:, :], av[:, :], ALU.mult)
            E1 = f32s.next()
            E2 = f32s.next()
            E3 = tmp
            E4 = vv
            for nh in range(2):
                cs_ = slice(nh * 512, (nh + 1) * 512)
                ps = c.ps.next()
                fw.mm(ps[:, :], triC[:, :], sig[:, cs_])
                fw.act(E1[:, cs_], ps[:, :], AF.Exp)
                fw.act(E2[:, cs_], ps[:, :], AF.Exp, scale=-1.0)
                fw.stt("dve", E3[:, cs_], sig[:, cs_], EM05, ps[:, :], ALU.mult, ALU.add)
                ps = c.ps.next()
                fw.mm(ps[:, :], triD[:, :], sig[:, cs_])
                fw.act(E4[:, cs_], ps[:, :], AF.Exp)
            fw.act(E3[:, :], E3[:, :], AF.Exp)
            outs = {}
            for n in ("At", "Rt", "Bh", "Kh", "Bt", "Kt"):
                outs[n] = b16s.next()
            fw.stt("dve", outs["At"][:, :], kkx[:, :], -1.0, E3[:, :], ALU.mult, ALU.mult)
            fw.tt("pool", outs["Rt"][:, :], rr[:, :], E1[:, :], ALU.mult)
            fw.tt("dve", outs["Bh"][:, :], av[:, :], E2[:, :], ALU.mult)
            fw.tt("pool", outs["Kh"][:, :], kx[:, :], E2[:, :], ALU.mult)
            fw.tt("dve", outs["Bt"][:, :], av[:, :], E4[:, :], ALU.mult)
            fw.tt("pool", outs["Kt"][:, :], kx[:, :], E4[:, :], ALU.mult)
            for n in ("At", "Rt", "Bh", "Kh", "Bt", "Kt"):
                fw.dma("sp", dd[n].rows(r0, r0 + 128), outs[n][:, :])
            ps = c.ps.next()
            for h in range(H):
                fw.mm(ps[0:64, h * NC:(h + 1) * NC], sig[:, h * 64:(h + 1) * 64], indC[:, :])
            glt = glts.next()
            fw.act(glt[:, :], ps[0:64, 0:64], AF.Exp)
            fw.dma("sp", GLd.rows(ti * 64, ti * 64 + 64), glt[:, :])
        fw.barrier()
    with ExitStack() as es:
        mask1 = sb(c, es, "d_m1", [128, 384], F32)
        mask2 = sb(c, es, "d_m2", [128, 256], F32)
        II = sb(c, es, "d_II", [128, 256], BF16)
        CM = sb(c, es, "d_CM", [128, NC], F32)
        CMb = sb(c, es, "d_CMb", [128, NC], BF16)
        block_mask(c, mask1[:, 0:128], "SU", 1.0)
        block_mask(c, mask1[:, 128:256], "SL", 1.0)
        block_mask(c, mask1[:, 256:384], "SU", 1.0)
        block_mask(c, mask2[:, 0:128], "IU", 1.0)
        block_mask(c, mask2[:, 128:256], "IU", 1.0)
        chunk_ind(c, CM[:, :], 1.0)
        fw.cp("pool", CMb[:, :], CM[:, :])
        fw.cp("pool", II[:, 0:128], c.identb[:, :])
        fw.cp("pool", II[:, 128:256], c.identb[:, :])
        I64 = c.identf[0:64, 0:64]
        GS = 8
        lds = {n: rot(c, es, "l" + n, 3, [128, 1024], BF16) for n in ("At", "Rt", "Bh", "Kh", "Bt", "Kt", "Vv")}
        XTs = rot(c, es, "XT", 1, [64, H, 4, 128], BF16)
        GLs = rot(c, es, "GL", 3, [64, 64], F32)
        NAs = rot(c, es, "NA", GS, [128, 384], BF16)
        RBs = rot(c, es, "RB", GS, [128, 256], BF16)
        MMs = rot(c, es, "MM", 2 * GS, [128, 256], BF16)
        PPs = rot(c, es, "PP", 2 * GS, [128, 256], BF16)
        MFs = rot(c, es, "MF", GS, [128, 128], BF16)
        AWs = rot(c, es, "AW", GS, [128, 128], BF16)
        U0s = rot(c, es, "U0", GS, [128, 64], BF16)
        RcTs = rot(c, es, "RcT", GS, [64, 128], F32)
        Bms = rot(c, es, "Bm", GS, [128, NC, 64], BF16)
        Kms = rot(c, es, "Km", GS, [128, NC, 64], BF16)
        Gds = rot(c, es, "Gd", GS, [64, NC, 64], F32)
        Y0s = rot(c, es, "Y0", GS, [64, 128], F32)
        PhTs = rot(c, es, "PhT", GS, [64, NC, 64], F32)
        PsTs = rot(c, es, "PsT", GS, [64, NC, 64], F32)
        YTs = rot(c, es, "YT", GS, [64, 128], F32)
        Yts = rot(c, es, "Yt", 2, [128, 1024], F32)
        STs = [rot(c, es, "ST%d" % h, 3, [64, 64], F32) for h in range(H)]
        ST = []
        for h in range(H):
            b_ = STs[h].next()
            fw.memset("pool", b_[:, :], 0.0)
            ST.append(b_)
        def ld_d2(ti):
            r0 = ti * 128
            L = {}
            for n in lds:
                L[n] = lds[n].next()
                fw.dma("sp", L[n][:, :], dd[n].rows(r0, r0 + 128))
            GL = GLs.next()
            fw.dma("sp", GL[:, :], GLd.rows(ti * 64, ti * 64 + 64))
            return L, GL

        for ti, (L, GL) in prefetched(NT if dstage >= 2 else 0, ld_d2):
            r0 = ti * 128
            XT = XTs.next()
            for h2 in range(H // 2):
                ps = c.ps.next()
                psb = ps.t[:, :].bitcast(BF16)
                for hh in range(2):
                    h = h2 * 2 + hh
                    for j, n in enumerate(("At", "Rt", "Bh", "Kh")):
                        col = (hh * 4 + j) * 128
                        fw.tr(V([ps], psb[0:64, col:col + 128]), L[n][:, h * 64:(h + 1) * 64], c.identb[:, :])
                fw.cp(("act", "dve")[h2 % 2], XT[:, h2 * 2:h2 * 2 + 2, :, :],
                      V([ps], psb[0:64, :].rearrange("p (h j t) -> p h j t", h=2, j=4)))
            Yt = Yts.next()
            for g0 in range(0, H, GS):
                hs = list(range(g0, g0 + GS))
                NA, RB, MM, PP, MF, AW, U0, RcT, Bm, Km, Gd, Y0, PhT, PsT = ({} for _ in range(14))
                for h in hs:
                    AtT, RtT, BhT, KhT = (XT[:, h, j, :] for j in range(4))
                    b1, b2 = c.ps.next(), c.ps.next()
                    fw.mm(b1[:, 0:128], BhT, AtT)
                    fw.mm(b1[:, 128:256], AtT, BhT)
                    fw.mm(b1[:, 256:384], KhT, AtT)
                    fw.mm(b2[:, 0:128], BhT, RtT)
                    fw.mm(b2[:, 128:256], KhT, RtT)
                    NA[h], RB[h], MM[h] = NAs.next(), RBs.next(), MMs.next()
                    fw.tt("dve", NA[h][:, :], b1[:, 0:384], mask1[:, :], ALU.mult)
                    fw.tt("dve", RB[h][:, :], b2[:, 0:256], mask2[:, :], ALU.mult)
                    fw.tt("pool", MM[h][:, :], NA[h][:, 0:256], II[:, :], ALU.add)
                    PP[h] = NA[h]
                for lev in range(3):
                    for h in hs:
                        P_, PT_ = PP[h][:, 0:128], PP[h][:, 128:256]
                        bp = c.ps.next()
                        fw.mm(bp[:, 0:128], PT_, P_)
                        fw.mm(bp[:, 128:256], P_, PT_)
                        npp = PPs.next()
                        fw.cp("act", npp[:, :], bp[:, 0:256])
                        PP[h] = npp
                    for h in hs:
                        P_ = PP[h][:, 0:128]
                        M_, MT_ = MM[h][:, 0:128], MM[h][:, 128:256]
                        bm = c.ps.next()
                        fw.mm(bm[:, 0:128], MT_, P_)
                        fw.mm(bm[:, 128:256], P_, MT_)
                        nmm = MMs.next()
                        fw.tt("dve", nmm[:, :], bm[:, 0:256], MM[h][:, :], ALU.add)
                        MM[h] = nmm
                for h in hs:
                    bp = c.ps.next()
                    fw.mm(bp[:, 0:128], PP[h][:, 128:256], PP[h][:, 0:128])
                    npp = PPs.next()
                    fw.cp("act", npp[:, 0:128], bp[:, 0:128])
                    PP[h] = npp
                for h in hs:
                    bm = c.ps.next()
                    fw.mm(bm[:, 0:128], MM[h][:, 128:256], PP[h][:, 0:128])
                    MF[h] = MFs.next()
                    fw.tt("dve", MF[h][:, :], bm[:, 0:128], MM[h][:, 0:128], ALU.add)
                for h in hs:
                    hc = slice(h * 64, (h + 1) * 64)
                    b_ = c.ps.next()
                    fw.mm(b_[:, 0:64], MF[h][:, :], L["At"][:, hc])
                    fw.mm(b_[:, 64:128], NA[h][:, 256:384], L["Vv"][:, hc])
                    AW[h] = AWs.next()
                    fw.cp("act", AW[h][:, :], b_[:, 0:128])
                    Bm[h], Km[h], Gd[h] = Bms.next(), Kms.next(), Gds.next()
                    cmb = V([CMb], CMb.t[:, :].unsqueeze(2).to_broadcast([128, NC, 64]))
                    fw.tt("pool", Bm[h][:, :, :], V([L["Bt"]], L["Bt"].t[:, hc].unsqueeze(1).to_broadcast([128, NC, 64])),
                          cmb, ALU.mult)
                    fw.tt("pool", Km[h][:, :, :], V([L["Kt"]], L["Kt"].t[:, hc].unsqueeze(1).to_broadcast([128, NC, 64])),
                          cmb, ALU.mult)
                    fw.tt("pool", Gd[h][:, :, :],
                          V(I64.bufs, I64.ap.unsqueeze(1).to_broadcast([64, NC, 64])),
                          V([GL], GL.t[:, h * NC:(h + 1) * NC].unsqueeze(2).to_broadcast([64, NC, 64])), ALU.mult)
                for h in hs:
                    b_ = c.ps.next()
                    fw.mm(b_[:, 0:64], MF[h][:, :], AW[h][:, 64:128])
                    fw.mm(b_[0:64, 64:192], AW[h][:, 0:64], RB[h][:, 0:128])
                    U0[h], RcT[h] = U0s.next(), RcTs.next()
                    fw.cp("act", U0[h][:, :], b_[:, 0:64])
                    fw.tt("dve", RcT[h][:, :], b_[0:64, 64:192], XT[:, h, 1, :], ALU.add)
                for h in hs:
                    hc = slice(h * 64, (h + 1) * 64)
                    b1, b2 = c.ps.next(), c.ps.next()
                    fw.mm(b1[0:64, 0:128], U0[h][:, :], RB[h][:, 0:128], True, False)
                    fw.mm(b1[0:64, 0:128], L["Vv"][:, hc], RB[h][:, 128:256], False, True)
                    bmv = V([Bm[h]], Bm[h].t[:, :, :].rearrange("p c e -> p (c e)"))
                    kmv = V([Km[h]], Km[h].t[:, :, :].rearrange("p c e -> p (c e)"))
                    fw.mm(b1[0:64, 128:384], AW[h][:, 0:64], bmv)
                    fw.mm(b2[0:64, 0:256], U0[h][:, :], bmv, True, False)
                    fw.mm(b2[0:64, 0:256], L["Vv"][:, hc], kmv, False, True)
                    Y0[h], PhT[h], PsT[h] = Y0s.next(), PhTs.next(), PsTs.next()
                    fw.cp("act", Y0[h][:, :], b1[0:64, 0:128])
                    fw.tt("dve", V([PhT[h]], PhT[h].t[:, :, :].rearrange("p c e -> p (c e)")), b1[0:64, 128:384],
                          V([Gd[h]], Gd[h].t[:, :, :].rearrange("p c e -> p (c e)")), ALU.add)
                    fw.cp("act", V([PsT[h]], PsT[h].t[:, :, :].rearrange("p c e -> p (c e)")), b2[0:64, 0:256])
                yb = [c.psx.next(), c.psx.next()]
                for cc in range(NC):
                    for h in hs:
                        hl = h - g0
                        yv = yb[hl // 4][0:64, (hl % 4) * 128 + cc * 32:(hl % 4) * 128 + cc * 32 + 32]
                        fw.mm(yv, ST[h][:, :], RcT[h][:, cc * 32:(cc + 1) * 32])
                        bs = c.ps.next()
                        fw.mm(bs[0:64, 0:64], PhT[h][:, cc, :], ST[h][:, :], True, False)
                        fw.mm(bs[0:64, 0:64], PsT[h][:, cc, :], I64, False, True)
                        ns = STs[h].next()
                        fw.cp(("act", "dve")[h % 2], ns[:, :], bs[0:64, 0:64])
                        ST[h] = ns
                pt = c.ps.next()
                for h in hs:
                    hl = h - g0
                    YT = YTs.next()
                    fw.tt("dve", YT[:, :], yb[hl // 4][0:64, (hl % 4) * 128:(hl % 4) * 128 + 128], Y0[h][:, :], ALU.add)
                    fw.tr(pt[:, hl * 64:(hl + 1) * 64], YT[:, :], I64)
                fw.cp("act", Yt[:, g0 * 64:(g0 + GS) * 64], pt[:, :])
            fw.dma("sp", Yd.rows(r0, r0 + 128), Yt[:, :])
        fw.barrier()
    with ExitStack() as es:
        wo = sb(c, es, "d_wo", [128, 8, 1024], BF16)
        G = sb(c, es, "G", [128, 1024], F32)
        lw_ = sb(c, es, "d_lnw", [128, 1024], F32)
        lb_ = sb(c, es, "d_lnb", [128, 1024], F32)
        bcast(c, G[:, :], c.d["ln_gains"][li, 1])
        bcast(c, lw_[:, :], c.d["d_lnx_w"])
        bcast(c, lb_[:, :], c.d["d_lnx_b"])
        with ExitStack() as t2:
            load_w(c, t2, wo, c.d["d_w_o"], 1024, 1024)
            fw.barrier()
        xts = rot(c, es, "xt", 4, [128, 1024], F32)
        ys = rot(c, es, "y", 3, [128, 1024], F32)
        sqs = rot(c, es, "sq", 2, [128, 1024], F32)
        bons = rot(c, es, "bon", 3, [128, 1024], BF16)
        gts = rot(c, es, "gt", 3, [128, 1024], BF16)
        obs = rot(c, es, "ob", 2, [128, 1024], BF16)
        obTs = rot(c, es, "obT", 2, [128, 8, 128], BF16)
        sts = rot(c, es, "st", 4, [128, 16], F32)
        sms = rot(c, es, "sm", 2, [128, 96], F32)
        xos = rot(c, es, "xo", 2, [128, 1024], F32)
        c.pn_tmp = rot(c, es, "pnt", 2, [128, 1024], F32)

        def v3(buf):
            return V([buf], buf.t[:, :].rearrange("p (h e) -> p h e", h=H))

        def b3(vw):
            return V(vw.bufs, vw.ap.unsqueeze(2).to_broadcast([128, H, 64]))

        def ld_d3(ti):
            r0 = ti * 128
            xt, y, bon, gt = xts.next(), ys.next(), bons.next(), gts.next()
            fw.dma("sp", xt[:, :], xin.rows(r0, r0 + 128))
            fw.dma("sp", y[:, :], Yd.rows(r0, r0 + 128))
            fw.dma("sp", bon[:, :], dd["BON"].rows(r0, r0 + 128))
            fw.dma("sp", gt[:, :], dd["GATE"].rows(r0, r0 + 128))
            return xt, y, bon, gt

        for ti, (xt, y, bon, gt) in prefetched(NT if dstage >= 3 else 0, ld_d3):
            r0 = ti * 128
            sm = sms.next()
            sq = sqs.next()
            fw.red("dve", sm[:, 0:16], v3(y), ALU.add)
            fw.act(sq[:, :], y[:, :], AF.Square)
            fw.red("dve", sm[:, 16:32], v3(sq), ALU.add)
            fw.ts("dve", sm[:, 32:48], sm[:, 0:16], 1.0 / 64)
            fw.tt("dve", sm[:, 48:64], sm[:, 32:48], sm[:, 32:48], ALU.mult)
            fw.stt("dve", sm[:, 64:80], sm[:, 16:32], 1.0 / 64, sm[:, 48:64], ALU.mult, ALU.subtract)
            fw.ts("dve", sm[:, 64:80], sm[:, 64:80], 64e-5, None, ALU.add)
            fw.act(sm[:, 64:80], sm[:, 64:80], AF.Sqrt)
            fw.recip(sm[:, 80:96], sm[:, 64:80])
            fw.tt("dve", v3(y), v3(y), b3(sm[:, 32:48]), ALU.subtract)
            fw.tt("dve", v3(y), v3(y), b3(sm[:, 80:96]), ALU.mult)
            fw.tt("pool", y[:, :], y[:, :], lw_[:, :], ALU.mult)
            fw.tt("pool", y[:, :], y[:, :], lb_[:, :], ALU.add)
            fw.tt("dve", y[:, :], y[:, :], bon[:, :], ALU.add)
            ob = obs.next()
            fw.tt("pool", ob[:, :], y[:, :], gt[:, :], ALU.mult)
            obT = obTs.next()
            transpose_to(c, ob, obT, 0)
            psY = [c.ps.next(), c.ps.next()]
            for nh in range(2):
                for kc in range(8):
                    fw.mm(psY[nh][:, :], obT[:, kc, :], wo[:, kc, nh * 512:(nh + 1) * 512], kc == 0, kc == 7)
            postnorm_store(c, psY[0], psY[1], xt, G, xos.next(), sts.next(), xout.rows(r0, r0 + 128))
        fw.barrier()


MIXERS = {0: mixer_a, 1: mixer_b, 2: mixer_c, 3: mixer_d}

_CACHE = {}


def run(inputs, subs=None, cores=8):
    key = tuple(subs) if subs is not None else None
    if key not in _CACHE:
        _CACHE[key] = build(subs)
    nc, c = _CACHE[key]
    in_maps = []
    for ci in range(cores):
        b = ci % 4
        m = {}
        hc = host_consts()
        for name, shape, dt in IN_SPECS:
            a = np.asarray(hc[name] if name in hc else inputs[name])
            if name in ("x", "mem", "positions"):
                a = a[b]
            a = np.ascontiguousarray(a).reshape(shape)
            m[name] = a
        in_maps.append(m)
    res = run_bass_kernel_spmd(nc, in_maps, core_ids=list(range(cores)))
    return [r["out"] for r in res.results]


def kernel(**inputs):
    outs = run(inputs)
    return np.stack(outs[0:4], axis=0).astype(np.float32)
```

```python
import numpy as np
from contextlib import ExitStack
import concourse.bass as bass
import concourse.mybir as mybir
from concourse.bass_utils import run_bass_kernel_spmd

F32 = mybir.dt.float32
BF16 = mybir.dt.bfloat16
I32 = mybir.dt.int32
AF = mybir.ActivationFunctionType
ALU = mybir.AluOpType
AX = mybir.AxisListType

D = 1024
S = 8192
NT = S // 128
EPS = 1e-6
MEM = 256
DFF = 2816


class Buf:
    __slots__ = ("t", "w", "r", "excl")

    def __init__(self, t, excl=False):
        self.t = t
        self.w = None
        self.r = {}
        self.excl = excl

    def __getitem__(self, idx):
        return V([self], self.t[idx])

    def v(self, ap):
        return V([self], ap)


class V:
    __slots__ = ("bufs", "ap")

    def __init__(self, bufs, ap):
        self.bufs = bufs
        self.ap = ap

    def __getitem__(self, idx):
        return V(self.bufs, self.ap[idx])


class DramT:
    def __init__(self, ap, blk=128, tracked=True):
        self.ap = ap
        self.blk = blk
        n = (ap.shape[0] + blk - 1) // blk
        self.blocks = [Buf(None) for _ in range(n)] if tracked else None

    def rows(self, r0, r1, cols=None):
        ap = self.ap[r0:r1] if cols is None else self.ap[r0:r1, cols[0]:cols[1]]
        if self.blocks is None:
            return V([], ap)
        return V(self.blocks[r0 // self.blk:(r1 - 1) // self.blk + 1], ap)

    def view(self, ap, r0, r1):
        if self.blocks is None:
            return V([], ap)
        return V(self.blocks[r0 // self.blk:(r1 - 1) // self.blk + 1], ap)


def RO(ap):
    return V([], ap)


COMPUTE = ("pe", "act", "dve", "pool")


class FW:
    def __init__(self, nc, es, n_dma=32):
        self.nc = nc
        self.engs = {"pe": nc.tensor, "act": nc.scalar, "dve": nc.vector, "pool": nc.gpsimd, "sp": nc.sync}
        self.sem = {k: es.enter_context(nc.semaphore("s_" + k)) for k in COMPUTE}
        self.cnt = {k: 0 for k in COMPUTE}
        self.waited = {k: {} for k in self.engs}
        self.dsem = [es.enter_context(nc.semaphore("d%d" % i)) for i in range(n_dma)]
        self.dcnt = [0] * n_dma
        self.dnext = 0
        self.nd = n_dma
        self.ninst = 0

    def _wait(self, eng, key, val):
        if eng == "pe" and key == "pe":
            return
        w = self.waited[eng]
        if w.get(key, 0) >= val:
            return
        sem = self.sem[key] if isinstance(key, str) else self.dsem[key]
        self.engs[eng].wait_ge(sem, val)
        w[key] = val
        self.ninst += 1

    def _deps(self, eng, reads, writes):
        for v in reads:
            for b in v.bufs:
                if b.w is not None:
                    self._wait(eng, b.w[0], b.w[1])
                if b.excl:
                    for k, val in b.r.items():
                        if k != eng:
                            self._wait(eng, k, val)
        for v in writes:
            for b in v.bufs:
                if b.w is not None:
                    self._wait(eng, b.w[0], b.w[1])
                for k, val in b.r.items():
                    self._wait(eng, k, val)

    def _done(self, key, val, reads, writes):
        for v in reads:
            for b in v.bufs:
                b.r[key] = val
        for v in writes:
            for b in v.bufs:
                b.w = (key, val)
                b.r = {}

    def op(self, eng, fn, reads, writes):
        self._deps(eng, reads, writes)
        ins = fn()
        self.cnt[eng] += 1
        ins.then_inc(self.sem[eng], 1)
        self._done(eng, self.cnt[eng], reads, writes)
        self.ninst += 1

    def dma(self, q, out, in_, **kw):
        s = self.dnext
        self.dnext = (s + 1) % self.nd
        if self.dcnt[s]:
            self._wait(q, s, self.dcnt[s])
        self._deps(q, [in_], [out])
        ins = self.engs[q].dma_start(out=out.ap, in_=in_.ap, **kw)
        self.dcnt[s] += 16
        ins.then_inc(self.dsem[s], 16)
        self._done(s, self.dcnt[s], [in_], [out])
        self.ninst += 1

    def barrier(self):
        for eng in self.engs:
            for k in COMPUTE:
                if self.cnt[k]:
                    self._wait(eng, k, self.cnt[k])
            for s in range(self.nd):
                if self.dcnt[s]:
                    self._wait(eng, s, self.dcnt[s])

    def mm(self, out, lhsT, rhs, start=True, stop=True):
        self.op("pe", lambda: self.nc.tensor.matmul(out.ap, lhsT.ap, rhs.ap, start=start, stop=stop),
                [lhsT, rhs], [out])

    def tr(self, out, in_, ident):
        self.op("pe", lambda: self.nc.tensor.transpose(out.ap, in_.ap, ident.ap), [in_, ident], [out])

    def act(self, out, in_, func, bias=None, scale=None, accum=None):
        kw = {}
        reads = [in_]
        writes = [out]
        if bias is not None:
            if isinstance(bias, V):
                kw["bias"] = bias.ap
                reads.append(bias)
            else:
                kw["bias"] = bias
        if scale is not None:
            if isinstance(scale, V):
                kw["scale"] = scale.ap
                reads.append(scale)
            else:
                kw["scale"] = scale
        if accum is not None:
            kw["accum_out"] = accum.ap
            writes.append(accum)
        self.op("act", lambda: self.nc.scalar.activation(out=out.ap, in_=in_.ap, func=func, **kw), reads, writes)

    def _e(self, eng):
        return self.engs[eng]

    def tt(self, eng, out, a, b, op):
        self.op(eng, lambda: self._e(eng).tensor_tensor(out.ap, a.ap, b.ap, op), [a, b], [out])

    def ts(self, eng, out, a, s1, s2=None, op0=ALU.mult, op1=None, accum=None):
        reads = [a]
        writes = [out]
        a1 = s1
        a2 = s2
        if isinstance(s1, V):
            reads.append(s1)
            a1 = s1.ap
        if isinstance(s2, V):
            reads.append(s2)
            a2 = s2.ap
        kw = {}
        if accum is not None:
            kw["accum_out"] = accum.ap
            writes.append(accum)
        if op1 is None:
            self.op(eng, lambda: self._e(eng).tensor_scalar(out.ap, a.ap, a1, None, op0, **kw), reads, writes)
        else:
            self.op(eng, lambda: self._e(eng).tensor_scalar(out.ap, a.ap, a1, a2, op0, op1, **kw), reads, writes)

    def stt(self, eng, out, a, s, b, op0, op1):
        reads = [a, b]
        sa = s
        if isinstance(s, V):
            reads.append(s)
            sa = s.ap
        self.op(eng, lambda: self._e(eng).scalar_tensor_tensor(out.ap, a.ap, sa, b.ap, op0, op1), reads, [out])

    def cp(self, eng, out, in_):
        if eng == "act":
            self.act(out, in_, AF.Copy)
        else:
            self.op(eng, lambda: self._e(eng).tensor_copy(out.ap, in_.ap), [in_], [out])

    def recip(self, out, in_):
        self.op("dve", lambda: self.nc.vector.reciprocal(out.ap, in_.ap), [in_], [out])

    def red(self, eng, out, in_, op, axis=AX.X):
        self.op(eng, lambda: self._e(eng).tensor_reduce(out.ap, in_.ap, axis, op), [in_], [out])

    def memset(self, eng, out, val):
        self.op(eng, lambda: self._e(eng).memset(out.ap, val), [], [out])

    def asel(self, out, in_, pattern, cmp, fill, base, cm):
        self.op("pool", lambda: self.nc.gpsimd.affine_select(out=out.ap, in_=in_.ap, pattern=pattern,
                                                             compare_op=cmp, fill=fill, base=base,
                                                             channel_multiplier=cm), [in_], [out])


class Ctx:
    pass


def sb(c, es, name, shape, dt=F32):
    c.uid += 1
    return Buf(es.enter_context(c.nc.sbuf_tensor("%s_%d" % (name, c.uid), shape, dt)))


class Rot:
    def __init__(self, bufs):
        self.bufs = bufs
        self.i = 0

    def next(self):
        b = self.bufs[self.i]
        self.i = (self.i + 1) % len(self.bufs)
        return b


def rot(c, es, name, n, shape, dt=F32):
    return Rot([sb(c, es, name, shape, dt) for _ in range(n)])


def prefetched(n, load):
    nxt = load(0) if n else None
    for i in range(n):
        cur = nxt
        nxt = load(i + 1) if i + 1 < n else None
        yield i, cur


def load_w(c, es_tmp, dst, src_ap, K, N, gcol=None, n0=0, q="sp"):
    fw = c.fw
    CH = 1408 if N % 1408 == 0 else (1024 if N % 1024 == 0 else N)
    if CH > 2048:
        CH = N // ((N + 2047) // 2048)
        assert N % CH == 0
    stg = rot(c, es_tmp, "wstg", 3, [128, CH], F32)
    i = 0
    for kc in range(K // 128):
        for n1 in range(0, N, CH):
            st = stg.next()
            fw.dma(q, st[:, :], RO(src_ap[kc * 128:(kc + 1) * 128, n1:n1 + CH]))
            eng = ("dve", "pool", "act")[i % 3]
            i += 1
            o = dst[:, kc, n0 + n1:n0 + n1 + CH]
            if gcol is None:
                fw.cp(eng, o, st[:, :])
            elif eng == "act":
                fw.act(o, st[:, :], AF.Copy, scale=gcol[:, kc:kc + 1])
            else:
                fw.ts(eng, o, st[:, :], gcol[:, kc:kc + 1])


def load_cols(c, es_tmp, dst_v, src2d_ap, R):
    fw = c.fw
    st = sb(c, es_tmp, "lc", [128, 128], F32)
    fw.dma("sp", st[0:R, :], RO(src2d_ap))
    ps = c.ps.next()
    fw.tr(ps[:, 0:R], st[0:R, :], c.identf[0:R, 0:R])
    fw.cp("dve", dst_v, ps[:, 0:R])


def bcast(c, dst, vec_ap):
    c.fw.dma("sp", dst, RO(vec_ap.partition_broadcast(128)))


def rms_rstd(c, ss_v, n, out_v, tmp):
    fw = c.fw
    k = ss_v.ap.shape[1]
    fw.ts("dve", tmp[:, 0:k], ss_v, 1.0 / n, EPS, ALU.mult, ALU.add)
    fw.act(tmp[:, k:2 * k], tmp[:, 0:k], AF.Sqrt)
    fw.recip(out_v, tmp[:, k:2 * k])


def norm_tile(c, xt, hn, st):
    fw = c.fw
    fw.memset("pool", st[:, 0:1], 0.0)
    fw.act(c.junk[:, :], xt[:, :], AF.Square, accum=st[:, 0:1])
    rms_rstd(c, st[:, 0:1], D, st[:, 3:4], c.mk_tmp(st))
    fw.act(hn[:, :], xt[:, :], AF.Copy, scale=st[:, 3:4])


def transpose_to(c, hn, dstT, t0, nchunk=8):
    fw = c.fw
    ps = c.ps.next()
    psb = ps.v(ps.t[:, :].bitcast(BF16))
    for ch in range(nchunk):
        fw.tr(V([ps], psb.ap[:, ch * 128:(ch + 1) * 128]), hn[:, ch * 128:(ch + 1) * 128], c.identb[:, :])
    eng = c.evac_eng()
    fw.cp(eng, dstT[:, 0:nchunk, t0:t0 + 128],
          V([ps], psb.ap[:, 0:nchunk * 128].rearrange("p (c t) -> p c t", c=nchunk)))


def postnorm_store(c, psA, psB, xt, G, xo, st, dst_rows):
    fw = c.fw
    fw.memset("pool", st[:, 0:2], 0.0)
    fw.act(c.junk[:, 0:512], psA[:, :], AF.Square, accum=st[:, 0:1])
    fw.act(c.junk[:, 512:1024], psB[:, :], AF.Square, accum=st[:, 1:2])
    fw.tt("dve", st[:, 2:3], st[:, 0:1], st[:, 1:2], ALU.add)
    rms_rstd(c, st[:, 2:3], D, st[:, 3:4], c.mk_tmp(st))
    tmp = c.pn_tmp.next() if c.pn_tmp is not None else xo
    fw.stt("dve", tmp[:, 0:512], psA[:, :], st[:, 3:4], G[:, 0:512], ALU.mult, ALU.mult)
    fw.stt("dve", tmp[:, 512:1024], psB[:, :], st[:, 3:4], G[:, 512:1024], ALU.mult, ALU.mult)
    fw.tt("pool", xo[:, :], xt[:, :], tmp[:, :], ALU.add)
    fw.dma("sp", dst_rows, xo[:, :])


def phase_consts(c, es):
    fw = c.fw
    c.identf = sb(c, es, "identf", [128, 128], F32)
    c.identb = sb(c, es, "identb", [128, 128], BF16)
    fw.memset("pool", c.identf[:, :], 1.0)
    fw.asel(c.identf[:, :], c.identf[:, :], [[-1, 128]], ALU.is_equal, 0.0, 0, 1)
    fw.cp("pool", c.identb[:, :], c.identf[:, :])
    c.junk = sb(c, es, "junk", [128, 1024], BF16)
    c.gT = sb(c, es, "gT", [128, 192], F32)
    with ExitStack() as tmp:
        g2 = c.d["ln_gains"].rearrange("l s (c p) -> (l s c) p", p=128)
        load_cols(c, tmp, c.gT[:, 0:96], g2[0:96, :], 96)
        load_cols(c, tmp, c.gT[:, 96:192], g2[96:192, :], 96)
        fw.barrier()


def gcol(c, li, si):
    o = (li * 6 + si) * 8
    return c.gT[:, o:o + 8]


def phase_memkv(c, es):
    fw = c.fw
    c.mem_kT = sb(c, es, "memkT", [128, 4, 256], BF16)
    c.mem_v = sb(c, es, "memv", [128, 2, 512], BF16)
    with ExitStack() as tmp:
        w = sb(c, tmp, "wkv", [128, 8, 1024], BF16)
        gm = sb(c, tmp, "gm", [128, 8], F32)
        load_cols(c, tmp, gm[:, :], c.d["mem_norm"].rearrange("(c p) -> c p", p=128), 8)
        with ExitStack() as t2:
            load_w(c, t2, w, c.d["mem_w_kv"], D, 1024, gcol=gm)
            fw.barrier()
        hnT = sb(c, tmp, "mhnT", [128, 8, 256], BF16)
        for ti in range(2):
            xt = sb(c, tmp, "mx", [128, 1024], F32)
            hn = sb(c, tmp, "mhn", [128, 1024], BF16)
            st = sb(c, tmp, "mst", [128, 16], F32)
            fw.dma("sp", xt[:, :], RO(c.d["mem"][ti * 128:(ti + 1) * 128, :]))
            norm_tile(c, xt, hn, st)
            transpose_to(c, hn, hnT, ti * 128)
        for h in range(4):
            ps = c.ps.next()
            for kc in range(8):
                fw.mm(ps[:, 0:256], w[:, kc, h * 128:(h + 1) * 128], hnT[:, kc, :], kc == 0, kc == 7)
            fw.cp("act", c.mem_kT[:, h, :], ps[:, 0:256])
        for mc in range(2):
            ps = c.ps.next()
            for kc in range(8):
                fw.mm(ps[:, :], hnT[:, kc, mc * 128:(mc + 1) * 128], w[:, kc, 512:1024], kc == 0, kc == 7)
            fw.cp("act", c.mem_v[:, mc, :], ps[:, :])
        fw.barrier()


def interleave(gens):
    gens = [g for g in gens if g is not None]
    while gens:
        for g in list(gens):
            try:
                next(g)
            except StopIteration:
                gens.remove(g)


def norm_tile_g(c, xt, hn, st):
    fw = c.fw
    fw.memset("pool", st[:, 0:1], 0.0)
    fw.act(c.junk2[:, :], xt[:, :], AF.Square, accum=st[:, 0:1])
    yield
    fw.ts("dve", st[:, 8:9], st[:, 0:1], 1.0 / D, EPS, ALU.mult, ALU.add)
    yield
    fw.act(st[:, 9:10], st[:, 8:9], AF.Sqrt)
    yield
    fw.recip(st[:, 3:4], st[:, 9:10])
    yield
    fw.act(hn[:, :], xt[:, :], AF.Copy, scale=st[:, 3:4])
    yield


def postnorm_store_g(c, psA, psB, xt, G, xo, st, dst_rows):
    fw = c.fw
    fw.memset("pool", st[:, 0:2], 0.0)
    fw.act(c.junk[:, 0:512], psA[:, :], AF.Square, accum=st[:, 0:1])
    fw.act(c.junk[:, 512:1024], psB[:, :], AF.Square, accum=st[:, 1:2])
    yield
    fw.tt("dve", st[:, 2:3], st[:, 0:1], st[:, 1:2], ALU.add)
    fw.ts("dve", st[:, 8:9], st[:, 2:3], 1.0 / D, EPS, ALU.mult, ALU.add)
    yield
    fw.act(st[:, 9:10], st[:, 8:9], AF.Sqrt)
    yield
    fw.recip(st[:, 3:4], st[:, 9:10])
    fw.stt("dve", xo[:, 0:512], psA[:, :], st[:, 3:4], G[:, 0:512], ALU.mult, ALU.mult)
    fw.stt("dve", xo[:, 512:1024], psB[:, :], st[:, 3:4], G[:, 512:1024], ALU.mult, ALU.mult)
    yield
    fw.tt("pool", xo[:, :], xt[:, :], xo[:, :], ALU.add)
    fw.dma("sp", dst_rows, xo[:, :])
    yield


def phase_mem(c, li, xin, xout):
    fw = c.fw
    TB = 512
    NBK = S // TB
    rotA = Rot(c.ps.bufs[0:2])
    rotB = Rot(c.ps.bufs[2:6])
    rotC = c.psx
    with ExitStack() as es:
        wq = sb(c, es, "wq", [128, 8, 512], BF16)
        wo = sb(c, es, "wo", [128, 4, 1024], BF16)
        G = sb(c, es, "G", [128, 1024], F32)
        ones = sb(c, es, "ones", [128, 128], BF16)
        c.junk2 = sb(c, es, "junk2", [128, 1024], BF16)
        fw.memset("pool", ones[:, :], 1.0)
        bcast(c, G[:, :], c.d["ln_gains"][li, 3])
        with ExitStack() as t2:
            load_w(c, t2, wq, c.d["mem_w_q"][li], D, 512, gcol=gcol(c, li, 2))
            load_w(c, t2, wo, c.d["mem_w_o"][li], 512, 1024)
            fw.barrier()
        xts = rot(c, es, "xt", 14, [128, 1024], F32)
        hns = rot(c, es, "hn", 8, [128, 1024], BF16)
        sts = rot(c, es, "st", 16, [128, 16], F32)
        hnTs = rot(c, es, "hnT", 2, [128, 8, TB], BF16)
        qTs = rot(c, es, "qT", 3, [128, 4, TB], BF16)
        pTs = rot(c, es, "pT", 6, [128, TB], BF16)
        rds = rot(c, es, "rd", 3, [128, TB], F32)
        oTs = rot(c, es, "oT", 3, [128, 4, TB], BF16)
        xos = rot(c, es, "xo", 4, [128, 1024], F32)
        scale = 128.0 ** -0.5
        res = {}

        def gen_P(blk):
            xl, hl = [], []
            for ti in range(4):
                r0 = blk * TB + ti * 128
                xt = xts.next()
                fw.dma("sp", xt[:, :], xin.rows(r0, r0 + 128))
                xl.append(xt)
            for ti in range(4):
                hn = hns.next()
                hl.append(hn)
                yield from norm_tile_g(c, xl[ti], hn, sts.next())
            hnT = hnTs.next()
            for ti in range(4):
                ps = rotA.next()
                psb = ps.t[:, :].bitcast(BF16)
                for ch in range(8):
                    fw.tr(V([ps], psb[:, ch * 128:(ch + 1) * 128]), hl[ti][:, ch * 128:(ch + 1) * 128], c.identb[:, :])
                yield
                fw.cp(("act", "dve")[ti % 2], hnT[:, :, ti * 128:(ti + 1) * 128],
                      V([ps], psb.rearrange("p (c t) -> p c t", c=8)))
                yield
            qT = qTs.next()
            for h in range(4):
                ps = rotA.next()
                for kc in range(8):
                    fw.mm(ps[:, :], wq[:, kc, h * 128:(h + 1) * 128], hnT[:, kc, :], kc == 0, kc == 7)
                yield
                fw.cp(("act", "dve")[h % 2], qT[:, h, :], ps[:, :])
                yield
            res[("P", blk)] = (xl, qT)

        def gen_S(blk):
            xl, qT = res[("P", blk)]
            oT = oTs.next()
            for h in range(4):
                pl = []
                for mc in range(2):
                    ps = rotB.next()
                    fw.mm(ps[:, :], c.mem_kT[:, h, mc * 128:(mc + 1) * 128], qT[:, h, :])
                    yield
                    pT = pTs.next()
                    fw.act(pT[:, :], ps[:, :], AF.Exp, scale=scale)
                    pl.append(pT)
                    yield
                acc, den = rotB.next(), rotB.next()
                for mc in range(2):
                    fw.mm(acc[:, :], c.mem_v[:, mc, h * 128:(h + 1) * 128], pl[mc][:, :], mc == 0, mc == 1)
                for mc in range(2):
                    fw.mm(den[:, :], ones[:, :], pl[mc][:, :], mc == 0, mc == 1)
                yield
                rd = rds.next()
                fw.recip(rd[:, :], den[:, :])
                yield
                fw.tt("dve", oT[:, h, :], acc[:, :], rd[:, :], ALU.mult)
                yield
            res[("S", blk)] = oT

        def gen_C(blk):
            xl, qT = res.pop(("P", blk))
            oT = res.pop(("S", blk))
            for ti in range(4):
                r0 = blk * TB + ti * 128
                psY = [rotC.next(), rotC.next()]
                for nh in range(2):
                    for h in range(4):
                        fw.mm(psY[nh][:, :], oT[:, h, ti * 128:(ti + 1) * 128], wo[:, h, nh * 512:(nh + 1) * 512],
                              h == 0, h == 3)
                yield
                yield from postnorm_store_g(c, psY[0], psY[1], xl[ti], G, xos.next(), sts.next(),
                                            xout.rows(r0, r0 + 128))

        interleave([gen_P(0)])
        for step in range(NBK + 2):
            interleave([gen_P(step + 1) if step + 1 < NBK else None,
                        gen_S(step) if step < NBK else None,
                        gen_C(step - 1) if 0 <= step - 1 < NBK else None])
        fw.barrier()


def phase_ffn(c, li, xin, xout):
    fw = c.fw
    TB = 256
    NTI = TB // 128
    NBK = S // TB
    with ExitStack() as es:
        w1 = sb(c, es, "w1", [128, 8, 2 * DFF], BF16)
        w2 = sb(c, es, "w2", [128, 22, 1024], BF16)
        G = sb(c, es, "G", [128, 1024], F32)
        cw = sb(c, es, "cw", [128, 3, 44], F32)
        cb = sb(c, es, "cb", [128, 44], F32)
        bcast(c, G[:, :], c.d["ln_gains"][li, 5])
        with ExitStack() as t2:
            for j in range(3):
                load_cols(c, t2, cw[:, j, :], c.d["ffn_conv_w"][li, j].rearrange("(c p) -> c p", p=128), 44)
            load_cols(c, t2, cb[:, :], c.d["ffn_conv_b"][li].rearrange("(c p) -> c p", p=128), 44)
            load_w(c, t2, w1, c.d["ffn_w_in"][li], D, 2 * DFF, gcol=gcol(c, li, 4))
            load_w(c, t2, w2, c.d["ffn_w_out"][li], DFF, 1024)
            fw.barrier()
        xts = rot(c, es, "xt", 4, [128, 1024], F32)
        hns = rot(c, es, "hn", 2, [128, 1024], BF16)
        sts = rot(c, es, "st", 6, [128, 16], F32)
        hnTs = rot(c, es, "hnT", 2, [128, 8, TB + 2], BF16)
        tbs = rot(c, es, "tb", 8, [128, TB], F32)
        sgs = rot(c, es, "sg", 4, [128, TB], F32)
        aTs = rot(c, es, "aT", 1, [128, 22, TB], BF16)
        xos = rot(c, es, "xo", 2, [128, 1024], F32)
        c.pn_tmp = None
        for hb in hnTs.bufs:
            fw.memset("pool", hb[:, :, :], 0.0)

        def prologue(blk, prevT):
            hnT = hnTs.next()
            if prevT is not None:
                fw.cp("pool", hnT[:, :, 0:2], prevT[:, :, TB:TB + 2])
            xl = []
            for ti in range(NTI):
                r0 = blk * TB + ti * 128
                xt = xts.next()
                xl.append(xt)
                fw.dma("sp", xt[:, :], xin.rows(r0, r0 + 128))
                hn = hns.next()
                norm_tile(c, xt, hn, sts.next())
                transpose_to(c, hn, hnT, 2 + ti * 128)
            return hnT, xl

        nxt = prologue(0, None)
        for blk in range(NBK):
            hnT, xl = nxt
            aT = aTs.next()
            pend = None
            for j in range(22):
                tv = []
                for half, ch in ((0, j), (1, 22 + j)):
                    ps = c.ps.next()
                    for kc in range(8):
                        fw.mm(ps[:, 0:TB + 2], w1[:, kc, ch * 128:(ch + 1) * 128], hnT[:, kc, :], kc == 0, kc == 7)
                    t = tbs.next()
                    fw.act(t[:, :], ps[:, 2:TB + 2], AF.Identity, bias=cb[:, ch:ch + 1], scale=cw[:, 2, ch:ch + 1])
                    fw.stt("dve", t[:, :], ps[:, 0:TB], cw[:, 0, ch:ch + 1], t[:, :], ALU.mult, ALU.add)
                    fw.stt("dve", t[:, :], ps[:, 1:TB + 1], cw[:, 1, ch:ch + 1], t[:, :], ALU.mult, ALU.add)
                    tv.append(t)
                if pend is not None:
                    j0, tv0 = pend
                    sg = sgs.next()
                    fw.act(sg[:, :], tv0[0][:, :], AF.Silu)
                    fw.tt("pool", aT[:, j0, :], sg[:, :], tv0[1][:, :], ALU.mult)
                pend = (j, tv)
            j0, tv0 = pend
            sg = sgs.next()
            fw.act(sg[:, :], tv0[0][:, :], AF.Silu)
            fw.tt("pool", aT[:, j0, :], sg[:, :], tv0[1][:, :], ALU.mult)
            if blk + 1 < NBK:
                nxt = prologue(blk + 1, hnT)
            for ti in range(NTI):
                r0 = blk * TB + ti * 128
                psY = [c.ps.next(), c.ps.next()]
                for nh in range(2):
                    for kc in range(22):
                        fw.mm(psY[nh][:, :], aT[:, kc, ti * 128:(ti + 1) * 128], w2[:, kc, nh * 512:(nh + 1) * 512],
                              kc == 0, kc == 21)
                postnorm_store(c, psY[0], psY[1], xl[ti], G, xos.next(), sts.next(), xout.rows(r0, r0 + 128))
        fw.barrier()


IN_SPECS = [
    ("x", [S, D], F32), ("mem", [MEM, D], F32), ("positions", [S], I32),
    ("ln_gains", [4, 6, D], F32), ("mem_norm", [D], F32), ("mem_w_kv", [D, 1024], F32),
    ("mem_w_q", [4, D, 512], F32), ("mem_w_o", [4, 512, D], F32),
    ("ffn_w_in", [4, D, 2 * DFF], F32), ("ffn_conv_w", [4, 3, 2 * DFF], F32), ("ffn_conv_b", [4, 2 * DFF], F32),
    ("ffn_w_out", [4, DFF, D], F32),
    ("a_w_qkv", [D, 4608], F32), ("a_w_o", [512, D], F32),
    ("b_w_in", [D, 3088], F32), ("b_w_gate2", [16, 512], F32), ("b_gate_bias", [512], F32),
    ("b_o_norm", [256], F32), ("b_w_o", [D, D], F32),
    ("c_w_in", [D, 672], F32), ("c_q_norm", [384], F32), ("c_w_uq", [384, 1536], F32),
    ("c_kv_norm", [256], F32), ("c_w_ukv", [256, 2048], F32), ("c_w_o", [D, D], F32),
    ("d_mix", [6, D], F32), ("d_w_rkv", [3, D, D], F32), ("d_w0", [D], F32), ("d_w1", [D, 64], F32),
    ("d_w2", [64, D], F32), ("d_a0", [D], F32), ("d_a1", [D, 64], F32), ("d_a2", [64, D], F32),
    ("d_g1", [D, 128], F32), ("d_g2", [128, D], F32), ("d_k_k", [D], F32), ("d_k_a", [D], F32),
    ("d_r_k", [16, 64], F32), ("d_lnx_w", [D], F32), ("d_lnx_b", [D], F32), ("d_w_o", [D, D], F32),
    ("k_invf_c", [128, 1], F32),
]


def host_consts():
    invf = np.zeros((128, 1), np.float32)
    for i in range(32):
        invf[64 + i, 0] = INVF_C[i % 16]
    return {"k_invf_c": invf}

ALL_SUBS = [(k, li) for li in range(4) for k in ("mix", "mem", "ffn")]


def build(subs=None):
    if subs is None:
        subs = ALL_SUBS
    nc = bass.Bass("TRN2", target_bir_lowering=False)
    c = Ctx()
    c.nc = nc
    c.uid = 0
    c.d = {}
    for name, shape, dt in IN_SPECS:
        c.d[name] = nc.dram_tensor(name, shape, dt, kind="ExternalInput").ap()
    out = nc.dram_tensor("out", [S, D], F32, kind="ExternalOutput").ap()
    scr = [nc.dram_tensor("xs%d" % i, [S, D], F32, kind="Internal").ap() for i in range(2)]
    with ExitStack() as es:
        fw = FW(nc, es)
        c.fw = fw
        banks = [Buf(es.enter_context(nc.psum_tensor("ps%d" % i, [128, 512], F32)), excl=True) for i in range(8)]
        c.ps = Rot(banks[0:6])
        c.psx = Rot(banks[6:8])
        c._ev = 0

        def evac_eng():
            c._ev += 1
            return ("act", "dve")[c._ev % 2]
        c.evac_eng = evac_eng
        c.mk_tmp = _Tmp
        phase_consts(c, es)
        phase_memkv(c, es)
        cur = DramT(c.d["x"], tracked=False)
        for i, (kind, li) in enumerate(subs):
            last = i == len(subs) - 1
            dst = DramT(out) if last else DramT(scr[i % 2])
            if kind == "mem":
                phase_mem(c, li, cur, dst)
            elif kind == "ffn":
                phase_ffn(c, li, cur, dst)
            else:
                MIXERS[li](c, li, cur, dst)
            cur = dst
        fw.barrier()
    c.ninst = fw.ninst
    return nc, c


class _Tmp:
    def __init__(self, st):
        self.st = st

    def __getitem__(self, idx):
        p, f = idx
        return self.st[p, slice(8 + f.start, 8 + f.stop)]


INVF_A = [float(np.float32(500000.0) ** np.float32(-(np.float32(i) * np.float32(2.0 / 16)))) for i in range(8)]
INVF_C = [float(np.float32(500000.0) ** np.float32(-(np.float32(i) * np.float32(2.0 / 32)))) for i in range(16)]
TWO_PI = float(2 * np.pi)


def rope_tables(c, COS, SIN, es_tmp, pf, npair, invf):
    fw = c.fw
    nf = len(invf)
    n = npair * nf
    ang = sb(c, es_tmp, "ang", [128, npair, nf], F32)
    u = sb(c, es_tmp, "u", [128, n], F32)
    ni = sb(c, es_tmp, "ni", [128, n], I32)
    nfl = sb(c, es_tmp, "nfl", [128, n], F32)
    r = sb(c, es_tmp, "r", [128, n], F32)
    for f in range(nf):
        fw.ts("dve", ang[:, :, f], pf[:, :], invf[f])
    angf = ang.v(ang.t[:, :, :].rearrange("p a b -> p (a b)"))
    for off, dst in ((0.0, SIN), (0.25, COS)):
        fw.ts("dve", u[:, :], angf, 1.0 / TWO_PI, off, ALU.mult, ALU.add)
        fw.cp("dve", ni[:, :], u[:, :])
        fw.cp("dve", nfl[:, :], ni[:, :])
        if off:
            fw.ts("dve", u[:, :], angf, float(np.pi / 2), None, ALU.add)
            fw.stt("dve", r[:, :], nfl[:, :], -TWO_PI, u[:, :], ALU.mult, ALU.add)
        else:
            fw.stt("dve", r[:, :], nfl[:, :], -TWO_PI, angf, ALU.mult, ALU.add)
        fw.ts("dve", r[:, :], r[:, :], float(np.pi), float(-np.pi), ALU.min, ALU.max)
        fw.act(dst.v(dst.t[:, :, :].rearrange("p a b -> p (a b)")), r[:, :], AF.Sin)
    return COS, SIN


def rope_apply(c, xs, o, cosv, sinv, tmps, nh, hd, half):
    fw = c.fw
    x3 = V(xs.bufs, xs.ap.rearrange("p (h d) -> p h d", h=nh))
    o3 = V(o.bufs, o.ap.rearrange("p (h d) -> p h d", h=nh))
    cb = V(cosv.bufs, cosv.ap.unsqueeze(1).to_broadcast([128, nh, half]))
    sbv = V(sinv.bufs, sinv.ap.unsqueeze(1).to_broadcast([128, nh, half]))
    t = [tmps.next() for _ in range(4)]
    tv = [V([b], b.t[:, 0:nh * half].rearrange("p (h d) -> p h d", h=nh)) for b in t]
    x1 = x3[:, :, 0:half]
    x2 = x3[:, :, half:2 * half]
    fw.tt("dve", tv[0], x1, cb, ALU.mult)
    fw.tt("pool", tv[1], x2, sbv, ALU.mult)
    fw.tt("dve", o3[:, :, 0:half], tv[0], tv[1], ALU.subtract)
    fw.tt("dve", tv[2], x2, cb, ALU.mult)
    fw.tt("pool", tv[3], x1, sbv, ALU.mult)
    fw.tt("pool", o3[:, :, half:2 * half], tv[2], tv[3], ALU.add)
    if hd > 2 * half:
        fw.cp("act", o3[:, :, 2 * half:hd], x3[:, :, 2 * half:hd])


def load_pos(c, dst_i, src_ap_iJ, nJ):
    fw = c.fw
    step = 8
    for j0 in range(0, nJ, step):
        fw.dma("sp", dst_i[:, j0:j0 + step], RO(src_ap_iJ[:, j0:j0 + step]), allow_slow_non_contiguous=True)


def mixer_a(c, li, xin, xout):
    import os
    stage = int(os.environ.get('A_STAGE', '9'))
    fw = c.fw
    nc = c.nc
    U = 2048
    NU = S // U
    GROUPS = [(0, 1), (1, 4), (2, 16)]
    nds = [DramT(nc.dram_tensor("a_nd%d" % g, [S, 520], F32, kind="Internal").ap()) for g in range(3)]
    for g, d in GROUPS:
        nblk = 16 // d
        nJ = 64 // d
        with ExitStack() as es:
            w = sb(c, es, "wa", [128, 8, 1536], BF16)
            with ExitStack() as t2:
                for s3 in range(3):
                    col0 = (s3 * 3 + g) * 512
                    load_w(c, t2, w, c.d["a_w_qkv"][:, col0:col0 + 512], D, 512, gcol=gcol(c, li, 0), n0=s3 * 512)
                fw.barrier()
            COS = sb(c, es, "cos", [128, 64, 8], F32)
            SIN = sb(c, es, "sin", [128, 64, 8], F32)
            with ExitStack() as t2:
                pi = sb(c, t2, "pi", [128, 64], I32)
                pf = sb(c, t2, "pf", [128, 64], F32)
                if d == 1:
                    load_pos(c, pi, c.d["positions"].rearrange("(J i) -> i J", i=128), 64)
                else:
                    src = c.d["positions"].rearrange("(J i r) -> i J r", i=128, r=d)
                    step = max(1, 8 // d * 1)
                    piv = pi.t[:, :].rearrange("i (J r) -> i J r", r=d)
                    for j0 in range(0, nJ, 2):
                        fw.dma("sp", V([pi], piv[:, j0:j0 + 2, :]), RO(src[:, j0:j0 + 2, :]),
                               allow_slow_non_contiguous=True)
                fw.cp("dve", pf[:, :], pi[:, :])
                rope_tables(c, COS, SIN, t2, pf, 64, INVF_A)
                fw.barrier()
            hnT = sb(c, es, "hnTu", [128, 8, U], BF16)
            xts = rot(c, es, "xt", 2, [128, 1024], F32)
            hns = rot(c, es, "hn", 2, [128, 1024], BF16)
            sts = rot(c, es, "st", 4, [128, 16], F32)
            xss = rot(c, es, "xs", 3, [128, 512], F32)
            qrs = rot(c, es, "qr", 3, [128, 512], BF16)
            qTs = rot(c, es, "qT", 2, [128, 4, 128], BF16)
            tms = rot(c, es, "tm", 8, [128, 64], F32)
            pes = rot(c, es, "pe", 3, [128, 512], BF16)
            pms = rot(c, es, "pm", 3, [128, 512], BF16)
            stg = rot(c, es, "stg", 2, [128, 520], F32)
            nbuf = d + 3
            kfree = [sb(c, es, "kT", [128, 8, 128], BF16) for _ in range(nbuf)]
            for kb in kfree:
                fw.memset("pool", kb[:, :, :], 0.0)
            vfree = [sb(c, es, "v65", [128, 8, 80], BF16) for _ in range(nbuf)]
            for vb in vfree:
                fw.memset("pool", vb[:, :, 64:65], 1.0)
            kz = sb(c, es, "kz", [128, 8, 128], BF16)
            vz = sb(c, es, "vz", [128, 8, 80], BF16)
            fw.memset("pool", kz[:, :, :], 0.0)
            fw.memset("pool", vz[:, :, :], 0.0)
            mN = sb(c, es, "mN", [128, 512], BF16)
            mF = sb(c, es, "mF", [128, 512], BF16)
            with ExitStack() as t2:
                mt = sb(c, t2, "mt", [128, 512], F32)
                fw.memset("pool", mt[:, :], 1.0)
                for hh in range(2):
                    fw.asel(mt[:, hh * 256:hh * 256 + 128], mt[:, hh * 256:hh * 256 + 128], [[-1, 128]],
                            ALU.is_ge, 0.0, 0, 1)
                    fw.asel(mt[:, hh * 256 + 128:hh * 256 + 256], mt[:, hh * 256 + 128:hh * 256 + 256], [[1, 128]],
                            ALU.is_ge, 0.0, 0, -1)
                fw.cp("pool", mN[:, :], mt[:, :])
                for hh in range(2):
                    fw.memset("pool", mt[:, hh * 256:hh * 256 + 128], 0.0)
                fw.cp("pool", mF[:, :], mt[:, :])
                fw.barrier()
            carry = {}
            ndv = nds[g].ap.rearrange("(J i r) c -> r J i c", i=128, r=d)
            for u in range(NU if stage >= 2 else 0):
                for ti in range(U // 128):
                    r0 = u * U + ti * 128
                    xt = xts.next()
                    fw.dma("sp", xt[:, :], xin.rows(r0, r0 + 128))
                    hn = hns.next()
                    norm_tile(c, xt, hn, sts.next())
                    transpose_to(c, hn, hnT, ti * 128)
                for r in range(d if stage >= 3 else 0):
                    for jl in range(nblk):
                        J = u * nblk + jl
                        pidx = J * d + r
                        banks = []
                        for s3 in range(3):
                            ps = c.ps.next()
                            banks.append(ps)
                            for kc in range(8):
                                hv = hnT.t[:, kc, :].rearrange("p (j i r) -> p r j i", i=128, r=d)[:, r, jl, :]
                                fw.mm(ps[:, :], V([hnT], hv), w[:, kc, s3 * 512:(s3 + 1) * 512], kc == 0, kc == 7)
                        sub = int(os.environ.get('A_SUB', '9'))
                        if sub < 1:
                            continue
                        psT = c.ps.next()
                        psTb = psT.t[:, :].bitcast(BF16)
                        for s3 in range(2):
                            xs = xss.next()
                            fw.cp("act", xs[:, :], banks[s3][:, :])
                            qr = qrs.next()
                            if sub >= 2:
                                rope_apply(c, xs[:, :], qr[:, :], COS[:, pidx, :], SIN[:, pidx, :], tms, 8, 64, 8)
                            else:
                                fw.cp("dve", qr[:, :], xs[:, :])
                            for ch in range(4 if sub != 11 else 0):
                                jj = s3 * 4 + ch
                                fw.tr(V([psT], psTb[:, jj * 128:(jj + 1) * 128]), qr[:, ch * 128:(ch + 1) * 128],
                                      c.identb[:, :])
                        qT = qTs.next()
                        curK = kfree.pop()
                        curV = vfree.pop()
                        if sub not in (11, 12):
                            fw.cp("act", qT[:, :, :], V([psT], psTb[:, 0:512].rearrange("p (c t) -> p c t", c=4)))
                        if sub not in (11, 12, 13):
                            kv3 = psTb[:, 512:1024].rearrange("p (c t) -> p c t", c=4)
                            k4 = curK.t[:, :, :].rearrange("p (c e) t -> p c e t", e=2)
                            fw.cp("dve", V([curK], k4[0:64, :, 0, :]), V([psT], kv3[0:64]))
                            fw.cp("act", V([curK], k4[64:128, :, 1, :]), V([psT], kv3[64:128]))
                        if sub not in (11, 12, 13, 14):
                            fw.cp("dve", curV[:, :, 0:64], V([banks[2]], banks[2].t[:, :].rearrange("p (h e) -> p h e", h=8)))
                        if J == 0:
                            prevK, prevV, mask = kz, vz, mF
                        else:
                            prevK, prevV = carry[r]
                            mask = mN
                        pO = [c.psx.next(), c.psx.next()]
                        pend = []
                        for hp in range(5):
                            if hp < 4:
                                ps = c.ps.next()
                                for hh in range(2):
                                    fw.mm(ps[:, hh * 256:hh * 256 + 128], prevK[:, hp * 2 + hh, :], qT[:, hp, :])
                                    fw.mm(ps[:, hh * 256 + 128:hh * 256 + 256], curK[:, hp * 2 + hh, :], qT[:, hp, :])
                                pe = pes.next()
                                fw.act(pe[:, :], ps[:, :], AF.Exp, scale=0.125)
                                pm = pms.next()
                                fw.tt(("dve", "pool")[hp % 2], pm[:, :], pe[:, :], mask[:, :], ALU.mult)
                                pend.append((hp, pm))
                            if hp >= 1:
                                hp0, pm0 = pend.pop(0)
                                for hh in range(2):
                                    h = hp0 * 2 + hh
                                    ov = pO[h // 4][:, (h % 4) * 128:(h % 4) * 128 + 65]
                                    fw.mm(ov, pm0[:, hh * 256:hh * 256 + 128], prevV[:, h, 0:65], True, False)
                                    fw.mm(ov, pm0[:, hh * 256 + 128:hh * 256 + 256], curV[:, h, 0:65], False, True)
                        st = stg.next()
                        if stage < 4 or sub in (41, 42, 43, 44):
                            carry[r] = (curK, curV)
                            if J > 0:
                                kfree.append(prevK)
                                vfree.append(prevV)
                            continue
                        fw.cp("act", V([st], st.t[:, 0:260].rearrange("p (h e) -> p h e", h=4)),
                              V([pO[0]], pO[0].t[:, :].rearrange("p (h e) -> p h e", h=4)[:, :, 0:65]))
                        fw.cp("dve", V([st], st.t[:, 260:520].rearrange("p (h e) -> p h e", h=4)),
                              V([pO[1]], pO[1].t[:, :].rearrange("p (h e) -> p h e", h=4)[:, :, 0:65]))
                        if sub != 45:
                            fw.dma("sp", nds[g].view(ndv[r, J], 0, S), st[:, :])
                        if J > 0:
                            kfree.append(prevK)
                            vfree.append(prevV)
                        carry[r] = (curK, curV)
            fw.barrier()
    with ExitStack() as es:
        wo = sb(c, es, "woa", [128, 4, 1024], BF16)
        G = sb(c, es, "G", [128, 1024], F32)
        bcast(c, G[:, :], c.d["ln_gains"][li, 1])
        with ExitStack() as t2:
            load_w(c, t2, wo, c.d["a_w_o"], 512, 1024)
            fw.barrier()
        xts = rot(c, es, "xt", 4, [128, 1024], F32)
        n0s = rot(c, es, "n0", 3, [128, 520], F32)
        n1s = rot(c, es, "n1", 3, [128, 520], F32)
        n2s = rot(c, es, "n2", 3, [128, 520], F32)
        obs = rot(c, es, "ob", 2, [128, 512], BF16)
        oTs = rot(c, es, "oT", 2, [128, 4, 128], BF16)
        sts = rot(c, es, "st", 4, [128, 16], F32)
        xos = rot(c, es, "xo", 2, [128, 1024], F32)
        c.pn_tmp = rot(c, es, "pnt", 2, [128, 1024], F32)
        def ld_a2(ti):
            r0 = ti * 128
            xt = xts.next()
            fw.dma("sp", xt[:, :], xin.rows(r0, r0 + 128))
            n0, n1, n2 = n0s.next(), n1s.next(), n2s.next()
            fw.dma("sp", n0[:, :], nds[0].rows(r0, r0 + 128))
            fw.dma("sp", n1[:, :], nds[1].rows(r0, r0 + 128))
            fw.dma("sp", n2[:, :], nds[2].rows(r0, r0 + 128))
            return xt, n0, n1, n2

        for ti, (xt, n0, n1, n2) in prefetched(NT, ld_a2):
            r0 = ti * 128
            fw.tt("pool", n0[:, :], n0[:, :], n1[:, :], ALU.add)
            fw.tt("dve", n0[:, :], n0[:, :], n2[:, :], ALU.add)
            st = sts.next()
            n3 = V([n0], n0.t[:, :].rearrange("p (h e) -> p h e", h=8))
            fw.recip(st[:, 0:8], n3[:, :, 64])
            ob = obs.next()
            fw.tt("dve", V([ob], ob.t[:, :].rearrange("p (h e) -> p h e", h=8)), n3[:, :, 0:64],
                  V([st], st.t[:, 0:8].unsqueeze(2).to_broadcast([128, 8, 64])), ALU.mult)
            oT = oTs.next()
            transpose_to(c, ob, oT, 0, nchunk=4)
            psY = [c.ps.next(), c.ps.next()]
            for nh in range(2):
                for kc in range(4):
                    fw.mm(psY[nh][:, :], oT[:, kc, :], wo[:, kc, nh * 512:(nh + 1) * 512], kc == 0, kc == 3)
            postnorm_store(c, psY[0], psY[1], xt, G, xos.next(), sts.next(), xout.rows(r0, r0 + 128))
        fw.barrier()


def range_reduce_sin(c, dst, ang, off, u, ni, nfl, r):
    fw = c.fw
    fw.ts("dve", u, ang, 1.0 / TWO_PI, off, ALU.mult, ALU.add)
    fw.cp("dve", ni, u)
    fw.cp("dve", nfl, ni)
    if off:
        fw.ts("dve", u, ang, float(off * TWO_PI), None, ALU.add)
        fw.stt("dve", r, nfl, -TWO_PI, u, ALU.mult, ALU.add)
    else:
        fw.stt("dve", r, nfl, -TWO_PI, ang, ALU.mult, ALU.add)
    fw.ts("dve", r, r, float(np.pi), float(-np.pi), ALU.min, ALU.max)
    fw.act(dst, r, AF.Sin)


def mixer_c(c, li, xin, xout):
    fw = c.fw
    nc = c.nc
    TB = 512
    NB = S // TB
    H = 16
    QTd = nc.dram_tensor("c_qt", [H, 96, S], BF16, kind="Internal").ap()
    KTd = nc.dram_tensor("c_kt", [H, 96, S], BF16, kind="Internal").ap()
    Vd = nc.dram_tensor("c_v", [S, 1024], BF16, kind="Internal").ap()
    OTd = nc.dram_tensor("c_ot", [8, 128, S], BF16, kind="Internal").ap()
    QT = DramT(QTd.rearrange("h r s -> (h r) s"), blk=96)
    KT = DramT(KTd.rearrange("h r s -> (h r) s"), blk=96)
    VD = DramT(Vd)
    OT = DramT(OTd.rearrange("h r s -> (h r) s"), blk=128)
    with ExitStack() as es:
        win = sb(c, es, "c_win", [128, 8, 672], BF16)
        wkp = sb(c, es, "c_wkp", [128, 8, 96], BF16)
        wkp2 = sb(c, es, "c_wkp2", [128, 8, 96], BF16)
        wq = sb(c, es, "c_wq", [128, 3, 1536], BF16)
        wq2 = sb(c, es, "c_wq2", [128, 3, 1536], BF16)
        wkv = sb(c, es, "c_wkv", [128, 2, 2048], BF16)
        invf = sb(c, es, "c_invf", [128, 1], F32)
        gq = sb(c, es, "c_gq", [128, 3], F32)
        gkv = sb(c, es, "c_gkv", [128, 2], F32)
        fw.dma("sp", invf[:, :], RO(c.d["k_invf_c"]))
        with ExitStack() as t2:
            load_cols(c, t2, gq[:, :], c.d["c_q_norm"].rearrange("(c p) -> c p", p=128), 3)
            load_cols(c, t2, gkv[:, :], c.d["c_kv_norm"].rearrange("(c p) -> c p", p=128), 2)
            load_w(c, t2, win, c.d["c_w_in"], D, 672, gcol=gcol(c, li, 0))
            load_w(c, t2, wq, c.d["c_w_uq"], 384, 1536, gcol=gq)
            load_w(c, t2, wkv, c.d["c_w_ukv"], 256, 2048, gcol=gkv)
            fw.memset("pool", wkp[:, :, :], 0.0)
            fw.memset("pool", wkp2[:, :, :], 0.0)
            fw.memset("pool", wq2[:, :, :], 0.0)
            fw.cp("pool", wkp[:, :, 64:96], win[:, :, 640:672])
            fw.cp("pool", wkp2[:, :, 80:96], win[:, :, 640:656])
            fw.ts("pool", wkp2[:, :, 64:80], win[:, :, 656:672], -1.0)
            wq4 = wq.t[:, :, :].rearrange("p k (h e) -> p k h e", h=H)
            wq24 = wq2.t[:, :, :].rearrange("p k (h e) -> p k h e", h=H)
            for kc in range(3):
                fw.cp("pool", V([wq2], wq24[:, kc, :, 80:96]), V([wq], wq4[:, kc, :, 64:80]))
                fw.ts("pool", V([wq2], wq24[:, kc, :, 64:80]), V([wq], wq4[:, kc, :, 80:96]), -1.0)
            fw.barrier()
        xts = rot(c, es, "xt", 8, [128, 1024], F32)
        hns = rot(c, es, "hn", 2, [128, 1024], BF16)
        sts = rot(c, es, "st", 4, [128, 16], F32)
        hnTs = rot(c, es, "hnT", 2, [128, 8, TB], BF16)
        cqTs = rot(c, es, "cqT", 2, [128, 5, TB], BF16)
        cns = rot(c, es, "cn", 2, [128, 640], BF16)
        pis = rot(c, es, "pi", 1, [128, TB], I32)
        pfs = rot(c, es, "pf", 1, [128, TB], F32)
        angs = rot(c, es, "ang", 1, [128, TB], F32)
        CTs = rot(c, es, "CT", 2, [128, TB], F32)
        STs = rot(c, es, "ST", 2, [128, TB], F32)
        us = rot(c, es, "u", 1, [128, TB], F32)
        nis = rot(c, es, "ni", 1, [128, TB], I32)
        nfs = rot(c, es, "nf", 1, [128, TB], F32)
        rs = rot(c, es, "r", 1, [128, TB], F32)
        kpes = rot(c, es, "kpe", 2, [128, TB], BF16)
        t1s = rot(c, es, "t1", 3, [128, TB], F32)
        t2s = rot(c, es, "t2", 3, [128, TB], F32)
        qst = rot(c, es, "qst", 4, [128, TB], BF16)
        kst = rot(c, es, "kst", 4, [128, TB], BF16)
        vst = rot(c, es, "vst", 3, [128, 1024], BF16)
        wkv4 = wkv.t[:, :, :].rearrange("p k (h e) -> p k h e", h=H)
        for blk in range(NB):
            t0 = blk * TB
            hnT = hnTs.next()
            cqT = cqTs.next()
            pi, pf, ang = pis.next(), pfs.next(), angs.next()
            fw.dma("sp", pi[:, :], RO(c.d["positions"][t0:t0 + TB].partition_broadcast(128)))
            fw.cp("pool", pf[:, :], pi[:, :])
            fw.ts("dve", ang[:, :], pf[:, :], invf[:, 0:1])
            CT, ST = CTs.next(), STs.next()
            u, ni, nfl, r = us.next(), nis.next(), nfs.next(), rs.next()
            range_reduce_sin(c, ST[:, :], ang[:, :], 0.0, u[:, :], ni[:, :], nfl[:, :], r[:, :])
            range_reduce_sin(c, CT[:, :], ang[:, :], 0.25, u[:, :], ni[:, :], nfl[:, :], r[:, :])
            if blk == 0:
                xnext = []
                for ti in range(TB // 128):
                    xt = xts.next()
                    fw.dma("sp", xt[:, :], xin.rows(ti * 128, ti * 128 + 128))
                    xnext.append(xt)
            xcur = xnext
            xnext = []
            if blk + 1 < NB:
                for ti in range(TB // 128):
                    r1 = t0 + TB + ti * 128
                    xt = xts.next()
                    fw.dma("sp", xt[:, :], xin.rows(r1, r1 + 128))
                    xnext.append(xt)
            for ti in range(TB // 128):
                r0 = t0 + ti * 128
                xt = xcur[ti]
                hn = hns.next()
                st = sts.next()
                norm_tile(c, xt, hn, st)
                transpose_to(c, hn, hnT, ti * 128)
                psA, psB = c.ps.next(), c.ps.next()
                for kc in range(8):
                    fw.mm(psA[:, 0:384], hnT[:, kc, ti * 128:(ti + 1) * 128], win[:, kc, 0:384], kc == 0, kc == 7)
                for kc in range(8):
                    fw.mm(psB[:, 0:256], hnT[:, kc, ti * 128:(ti + 1) * 128], win[:, kc, 384:640], kc == 0, kc == 7)
                st2 = sts.next()
                fw.memset("pool", st2[:, 0:2], 0.0)
                fw.act(c.junk[:, 0:384], psA[:, 0:384], AF.Square, accum=st2[:, 0:1])
                fw.act(c.junk[:, 384:640], psB[:, 0:256], AF.Square, accum=st2[:, 1:2])
                fw.ts("dve", st2[:, 2:3], st2[:, 0:1], 1.0 / 384, EPS, ALU.mult, ALU.add)
                fw.ts("dve", st2[:, 3:4], st2[:, 1:2], 1.0 / 256, EPS, ALU.mult, ALU.add)
                fw.act(st2[:, 4:6], st2[:, 2:4], AF.Sqrt)
                fw.recip(st2[:, 6:8], st2[:, 4:6])
                cn = cns.next()
                fw.act(cn[:, 0:384], psA[:, 0:384], AF.Copy, scale=st2[:, 6:7])
                fw.act(cn[:, 384:640], psB[:, 0:256], AF.Copy, scale=st2[:, 7:8])
                transpose_to(c, cn, cqT, ti * 128, nchunk=5)
            psK, psK2 = c.ps.next(), c.ps.next()
            for kc in range(8):
                fw.mm(psK[0:96, :], wkp[:, kc, :], hnT[:, kc, :], kc == 0, kc == 7)
            for kc in range(8):
                fw.mm(psK2[0:96, :], wkp2[:, kc, :], hnT[:, kc, :], kc == 0, kc == 7)
            t1, t2 = t1s.next(), t2s.next()
            kpe = kpes.next()
            fw.tt("dve", t1[0:96, :], psK[0:96, :], CT[0:96, :], ALU.mult)
            fw.tt("dve", t2[0:96, :], psK2[0:96, :], ST[0:96, :], ALU.mult)
            fw.tt("pool", kpe[0:96, :], t1[0:96, :], t2[0:96, :], ALU.add)
            for h in range(H):
                psQ, psQ2, psKn = c.ps.next(), c.ps.next(), c.ps.next()
                for kc in range(3):
                    fw.mm(psQ[0:96, :], wq[:, kc, h * 96:(h + 1) * 96], cqT[:, kc, :], kc == 0, kc == 2)
                for kc in range(3):
                    fw.mm(psQ2[0:96, :], wq2[:, kc, h * 96:(h + 1) * 96], cqT[:, kc, :], kc == 0, kc == 2)
                for kc in range(2):
                    fw.mm(psKn[0:64, :], V([wkv], wkv4[:, kc, h, 0:64]), cqT[:, 3 + kc, :], kc == 0, kc == 1)
                t1, t2 = t1s.next(), t2s.next()
                qs = qst.next()
                fw.tt("dve", t1[0:96, :], psQ[0:96, :], CT[0:96, :], ALU.mult)
                fw.tt("dve", t2[0:96, :], psQ2[0:96, :], ST[0:96, :], ALU.mult)
                fw.tt("pool", qs[0:96, :], t1[0:96, :], t2[0:96, :], ALU.add)
                fw.dma("sp", QT.view(QTd[h, :, t0:t0 + TB], h * 96, (h + 1) * 96), qs[0:96, :])
                ks = kst.next()
                fw.cp("act", ks[0:64, :], psKn[0:64, :])
                fw.cp("pool", ks[64:96, :], kpe[64:96, :])
                fw.dma("sp", KT.view(KTd[h, :, t0:t0 + TB], h * 96, (h + 1) * 96), ks[0:96, :])
            for ti in range(TB // 128):
                r0 = t0 + ti * 128
                vs = vst.next()
                for hf in range(2):
                    ps = c.ps.next()
                    for kc in range(2):
                        fw.mm(V([ps], ps.t[:, :].rearrange("p (h e) -> p h e", h=8)),
                              cqT[:, 3 + kc, ti * 128:(ti + 1) * 128],
                              V([wkv], wkv4[:, kc, hf * 8:(hf + 1) * 8, 64:128]), kc == 0, kc == 1)
                    fw.cp(("act", "dve")[hf], vs[:, hf * 512:(hf + 1) * 512], ps[:, :])
                fw.dma("sp", VD.rows(r0, r0 + 128), vs[:, :])
        fw.barrier()
    with ExitStack() as es:
        qbufs = rot(c, es, "qh", 2, [128, S], BF16)
        kbufs = rot(c, es, "kh", 2, [128, S], BF16)
        vbufs = rot(c, es, "vh", 2, [128, NT, 80], BF16)
        for vb in vbufs.bufs:
            fw.memset("pool", vb[:, :, 64:65], 1.0)
        masks = sb(c, es, "cmask", [128, 4, TB], BF16)
        sel = sb(c, es, "csel", [128, 64], BF16)
        with ExitStack() as t2:
            mt = sb(c, t2, "mt", [128, TB], F32)
            for j in range(4):
                fw.memset("pool", mt[:, :], 1.0)
                fw.asel(mt[:, :], mt[:, :], [[1, TB]], ALU.is_ge, 0.0, -128 * j, -1)
                fw.cp("pool", masks[:, j, :], mt[:, :])
            fw.memset("pool", mt[:, 0:64], 0.0)
            fw.memset("pool", mt[64:96, 0:64], 1.0)
            fw.cp("pool", sel[:, :], mt[:, 0:64])
            fw.barrier()
        pes = rot(c, es, "pe", 5, [128, TB], BF16)
        pms = rot(c, es, "pm", 4, [128, TB], BF16)
        oas = rot(c, es, "oa", 2, [128, TB], BF16)
        ofs = rot(c, es, "of", 2, [128, TB], F32)
        rds = rot(c, es, "rd", 2, [128, TB], F32)
        ons = rot(c, es, "on", 3, [128, TB], BF16)
        scale = 96.0 ** -0.5
        def ld_c2(h):
            qh, kh, vh = qbufs.next(), kbufs.next(), vbufs.next()
            for q4 in range(4):
                cs = slice(q4 * 2048, (q4 + 1) * 2048)
                fw.dma("sp", qh[0:96, cs], QT.view(QTd[h, :, cs], h * 96, (h + 1) * 96))
                fw.dma("sp", kh[0:96, cs], KT.view(KTd[h, :, cs], h * 96, (h + 1) * 96))
            for q4 in range(4):
                fw.dma("sp", vh[:, q4 * 16:(q4 + 1) * 16, 0:64],
                       VD.view(Vd[q4 * 2048:(q4 + 1) * 2048, h * 64:(h + 1) * 64].rearrange("(t p) e -> p t e", p=128),
                               q4 * 2048, (q4 + 1) * 2048))
            return qh, kh, vh

        for h, (qh, kh, vh) in prefetched(H, ld_c2):
            for qb in range(NB):
                acc = c.psx.next()
                nk = 4 * qb + 4
                LOOK = 2
                pend = []
                for kt in range(nk + LOOK):
                    if kt < nk:
                        ps = c.ps.next()
                        fw.mm(ps[:, :], kh[0:96, kt * 128:(kt + 1) * 128], qh[0:96, qb * TB:(qb + 1) * TB])
                        pe = pes.next()
                        fw.act(pe[:, :], ps[:, :], AF.Exp, scale=scale)
                        if kt >= 4 * qb:
                            pm = pms.next()
                            fw.tt(("dve", "pool")[kt % 2], pm[:, :], pe[:, :], masks[:, kt - 4 * qb, :], ALU.mult)
                            pe = pm
                        pend.append((kt, pe))
                    if kt >= LOOK:
                        k0, pe0 = pend.pop(0)
                        fw.mm(acc[0:65, :], vh[:, k0, 0:65], pe0[:, :], k0 == 0, k0 == nk - 1)
                oa = oas.next()
                of = ofs.next()
                fw.cp("act", oa[0:65, :], acc[0:65, :])
                fw.cp("dve", of[0:64, :], acc[0:64, :])
                psd = c.ps.next()
                fw.mm(psd[0:64, :], sel[0:65, :], oa[0:65, :])
                rd = rds.next()
                fw.recip(rd[0:64, :], psd[0:64, :])
                on = ons.next()
                fw.tt("pool", on[0:64, :], of[0:64, :], rd[0:64, :], ALU.mult)
                hp, hh = h // 2, h % 2
                fw.dma("sp", OT.view(OTd[hp, hh * 64:(hh + 1) * 64, qb * TB:(qb + 1) * TB], hp * 128, (hp + 1) * 128),
                       on[0:64, :])
        fw.barrier()
    with ExitStack() as es:
        wo = sb(c, es, "c_wo", [128, 8, 1024], BF16)
        G = sb(c, es, "G", [128, 1024], F32)
        bcast(c, G[:, :], c.d["ln_gains"][li, 1])
        with ExitStack() as t2:
            load_w(c, t2, wo, c.d["c_w_o"], 1024, 1024)
            fw.barrier()
        xts = rot(c, es, "xt", 4, [128, 1024], F32)
        oTs = rot(c, es, "oT", 4, [128, 8, 128], BF16)
        sts = rot(c, es, "st", 4, [128, 16], F32)
        xos = rot(c, es, "xo", 2, [128, 1024], F32)
        c.pn_tmp = rot(c, es, "pnt", 2, [128, 1024], F32)
        def ld_c3(ti):
            r0 = ti * 128
            xt = xts.next()
            fw.dma("sp", xt[:, :], xin.rows(r0, r0 + 128))
            oT = oTs.next()
            fw.dma("sp", oT[:, :, :], OT.view(OTd[:, :, r0:r0 + 128].rearrange("h r s -> r h s"), 0, 1024))
            return xt, oT

        for ti, (xt, oT) in prefetched(NT, ld_c3):
            r0 = ti * 128
            psY = [c.ps.next(), c.ps.next()]
            for nh in range(2):
                for kc in range(8):
                    fw.mm(psY[nh][:, :], oT[:, kc, :], wo[:, kc, nh * 512:(nh + 1) * 512], kc == 0, kc == 7)
            postnorm_store(c, psY[0], psY[1], xt, G, xos.next(), sts.next(), xout.rows(r0, r0 + 128))
        fw.barrier()


def mixer_c(c, li, xin, xout):
    fw = c.fw
    nc = c.nc
    TB = 512
    NB = S // TB
    H = 16
    QTd = nc.dram_tensor("c_qt", [H, 96, S], BF16, kind="Internal").ap()
    KTd = nc.dram_tensor("c_kt", [H, 96, S], BF16, kind="Internal").ap()
    Vd = nc.dram_tensor("c_v", [S, 1024], BF16, kind="Internal").ap()
    OTd = nc.dram_tensor("c_ot", [8, 128, S], BF16, kind="Internal").ap()
    QT = DramT(QTd.rearrange("h r s -> (h r) s"), blk=96)
    KT = DramT(KTd.rearrange("h r s -> (h r) s"), blk=96)
    VD = DramT(Vd)
    OT = DramT(OTd.rearrange("h r s -> (h r) s"), blk=128)
    with ExitStack() as es:
        win = sb(c, es, "c_win", [128, 8, 672], BF16)
        wkp = sb(c, es, "c_wkp", [128, 8, 96], BF16)
        wkp2 = sb(c, es, "c_wkp2", [128, 8, 96], BF16)
        wq = sb(c, es, "c_wq", [128, 3, 1536], BF16)
        wq2 = sb(c, es, "c_wq2", [128, 3, 1536], BF16)
        wkv = sb(c, es, "c_wkv", [128, 2, 2048], BF16)
        invf = sb(c, es, "c_invf", [128, 1], F32)
        gq = sb(c, es, "c_gq", [128, 3], F32)
        gkv = sb(c, es, "c_gkv", [128, 2], F32)
        fw.dma("sp", invf[:, :], RO(c.d["k_invf_c"]))
        with ExitStack() as t2:
            load_cols(c, t2, gq[:, :], c.d["c_q_norm"].rearrange("(c p) -> c p", p=128), 3)
            load_cols(c, t2, gkv[:, :], c.d["c_kv_norm"].rearrange("(c p) -> c p", p=128), 2)
            load_w(c, t2, win, c.d["c_w_in"], D, 672, gcol=gcol(c, li, 0))
            load_w(c, t2, wq, c.d["c_w_uq"], 384, 1536, gcol=gq)
            load_w(c, t2, wkv, c.d["c_w_ukv"], 256, 2048, gcol=gkv)
            fw.memset("pool", wkp[:, :, :], 0.0)
            fw.memset("pool", wkp2[:, :, :], 0.0)
            fw.memset("pool", wq2[:, :, :], 0.0)
            fw.cp("pool", wkp[:, :, 64:96], win[:, :, 640:672])
            fw.cp("pool", wkp2[:, :, 80:96], win[:, :, 640:656])
            fw.ts("pool", wkp2[:, :, 64:80], win[:, :, 656:672], -1.0)
            wq4 = wq.t[:, :, :].rearrange("p k (h e) -> p k h e", h=H)
            wq24 = wq2.t[:, :, :].rearrange("p k (h e) -> p k h e", h=H)
            for kc in range(3):
                fw.cp("pool", V([wq2], wq24[:, kc, :, 80:96]), V([wq], wq4[:, kc, :, 64:80]))
                fw.ts("pool", V([wq2], wq24[:, kc, :, 64:80]), V([wq], wq4[:, kc, :, 80:96]), -1.0)
            fw.barrier()
        xts = rot(c, es, "xt", 8, [128, 1024], F32)
        hns = rot(c, es, "hn", 2, [128, 1024], BF16)
        sts = rot(c, es, "st", 4, [128, 16], F32)
        hnTs = rot(c, es, "hnT", 2, [128, 8, TB], BF16)
        cqTs = rot(c, es, "cqT", 2, [128, 5, TB], BF16)
        cns = rot(c, es, "cn", 2, [128, 640], BF16)
        pis = rot(c, es, "pi", 1, [128, TB], I32)
        pfs = rot(c, es, "pf", 1, [128, TB], F32)
        angs = rot(c, es, "ang", 1, [128, TB], F32)
        CTs = rot(c, es, "CT", 2, [128, TB], F32)
        STs = rot(c, es, "ST", 2, [128, TB], F32)
        us = rot(c, es, "u", 1, [128, TB], F32)
        nis = rot(c, es, "ni", 1, [128, TB], I32)
        nfs = rot(c, es, "nf", 1, [128, TB], F32)
        rs = rot(c, es, "r", 1, [128, TB], F32)
        kpes = rot(c, es, "kpe", 2, [128, TB], BF16)
        t1s = rot(c, es, "t1", 3, [128, TB], F32)
        t2s = rot(c, es, "t2", 3, [128, TB], F32)
        qst = rot(c, es, "qst", 4, [128, TB], BF16)
        kst = rot(c, es, "kst", 4, [128, TB], BF16)
        vst = rot(c, es, "vst", 3, [128, 1024], BF16)
        wkv4 = wkv.t[:, :, :].rearrange("p k (h e) -> p k h e", h=H)
        for blk in range(NB):
            t0 = blk * TB
            hnT = hnTs.next()
            cqT = cqTs.next()
            pi, pf, ang = pis.next(), pfs.next(), angs.next()
            fw.dma("sp", pi[:, :], RO(c.d["positions"][t0:t0 + TB].partition_broadcast(128)))
            fw.cp("pool", pf[:, :], pi[:, :])
            fw.ts("dve", ang[:, :], pf[:, :], invf[:, 0:1])
            CT, ST = CTs.next(), STs.next()
            u, ni, nfl, r = us.next(), nis.next(), nfs.next(), rs.next()
            range_reduce_sin(c, ST[:, :], ang[:, :], 0.0, u[:, :], ni[:, :], nfl[:, :], r[:, :])
            range_reduce_sin(c, CT[:, :], ang[:, :], 0.25, u[:, :], ni[:, :], nfl[:, :], r[:, :])
            if blk == 0:
                xnext = []
                for ti in range(TB // 128):
                    xt = xts.next()
                    fw.dma("sp", xt[:, :], xin.rows(ti * 128, ti * 128 + 128))
                    xnext.append(xt)
            xcur = xnext
            xnext = []
            if blk + 1 < NB:
                for ti in range(TB // 128):
                    r1 = t0 + TB + ti * 128
                    xt = xts.next()
                    fw.dma("sp", xt[:, :], xin.rows(r1, r1 + 128))
                    xnext.append(xt)
            for ti in range(TB // 128):
                r0 = t0 + ti * 128
                xt = xcur[ti]
                hn = hns.next()
                st = sts.next()
                norm_tile(c, xt, hn, st)
                transpose_to(c, hn, hnT, ti * 128)
                psA, psB = c.ps.next(), c.ps.next()
                for kc in range(8):
                    fw.mm(psA[:, 0:384], hnT[:, kc, ti * 128:(ti + 1) * 128], win[:, kc, 0:384], kc == 0, kc == 7)
                for kc in range(8):
                    fw.mm(psB[:, 0:256], hnT[:, kc, ti * 128:(ti + 1) * 128], win[:, kc, 384:640], kc == 0, kc == 7)
                st2 = sts.next()
                fw.memset("pool", st2[:, 0:2], 0.0)
                fw.act(c.junk[:, 0:384], psA[:, 0:384], AF.Square, accum=st2[:, 0:1])
                fw.act(c.junk[:, 384:640], psB[:, 0:256], AF.Square, accum=st2[:, 1:2])
                fw.ts("dve", st2[:, 2:3], st2[:, 0:1], 1.0 / 384, EPS, ALU.mult, ALU.add)
                fw.ts("dve", st2[:, 3:4], st2[:, 1:2], 1.0 / 256, EPS, ALU.mult, ALU.add)
                fw.act(st2[:, 4:6], st2[:, 2:4], AF.Sqrt)
                fw.recip(st2[:, 6:8], st2[:, 4:6])
                cn = cns.next()
                fw.act(cn[:, 0:384], psA[:, 0:384], AF.Copy, scale=st2[:, 6:7])
                fw.act(cn[:, 384:640], psB[:, 0:256], AF.Copy, scale=st2[:, 7:8])
                transpose_to(c, cn, cqT, ti * 128, nchunk=5)
            psK, psK2 = c.ps.next(), c.ps.next()
            for kc in range(8):
                fw.mm(psK[0:96, :], wkp[:, kc, :], hnT[:, kc, :], kc == 0, kc == 7)
            for kc in range(8):
                fw.mm(psK2[0:96, :], wkp2[:, kc, :], hnT[:, kc, :], kc == 0, kc == 7)
            t1, t2 = t1s.next(), t2s.next()
            kpe = kpes.next()
            fw.tt("dve", t1[0:96, :], psK[0:96, :], CT[0:96, :], ALU.mult)
            fw.tt("dve", t2[0:96, :], psK2[0:96, :], ST[0:96, :], ALU.mult)
            fw.tt("pool", kpe[0:96, :], t1[0:96, :], t2[0:96, :], ALU.add)
            for h in range(H):
                psQ, psQ2, psKn = c.ps.next(), c.ps.next(), c.ps.next()
                for kc in range(3):
                    fw.mm(psQ[0:96, :], wq[:, kc, h * 96:(h + 1) * 96], cqT[:, kc, :], kc == 0, kc == 2)
                for kc in range(3):
                    fw.mm(psQ2[0:96, :], wq2[:, kc, h * 96:(h + 1) * 96], cqT[:, kc, :], kc == 0, kc == 2)
                for kc in range(2):
                    fw.mm(psKn[0:64, :], V([wkv], wkv4[:, kc, h, 0:64]), cqT[:, 3 + kc, :], kc == 0, kc == 1)
                t1, t2 = t1s.next(), t2s.next()
                qs = qst.next()
                fw.tt("dve", t1[0:96, :], psQ[0:96, :], CT[0:96, :], ALU.mult)
                fw.tt("dve", t2[0:96, :], psQ2[0:96, :], ST[0:96, :], ALU.mult)
                fw.tt("pool", qs[0:96, :], t1[0:96, :], t2[0:96, :], ALU.add)
                fw.dma("sp", QT.view(QTd[h, :, t0:t0 + TB], h * 96, (h + 1) * 96), qs[0:96, :])
                ks = kst.next()
                fw.cp("act", ks[0:64, :], psKn[0:64, :])
                fw.cp("pool", ks[64:96, :], kpe[64:96, :])
                fw.dma("sp", KT.view(KTd[h, :, t0:t0 + TB], h * 96, (h + 1) * 96), ks[0:96, :])
            for ti in range(TB // 128):
                r0 = t0 + ti * 128
                vs = vst.next()
                for hf in range(2):
                    ps = c.ps.next()
                    for kc in range(2):
                        fw.mm(V([ps], ps.t[:, :].rearrange("p (h e) -> p h e", h=8)),
                              cqT[:, 3 + kc, ti * 128:(ti + 1) * 128],
                              V([wkv], wkv4[:, kc, hf * 8:(hf + 1) * 8, 64:128]), kc == 0, kc == 1)
                    fw.cp(("act", "dve")[hf], vs[:, hf * 512:(hf + 1) * 512], ps[:, :])
                fw.dma("sp", VD.rows(r0, r0 + 128), vs[:, :])
        fw.barrier()
    with ExitStack() as es:
        qbufs = rot(c, es, "qh", 2, [128, S], BF16)
        kbufs = rot(c, es, "kh", 2, [128, S], BF16)
        vbufs = rot(c, es, "vh", 2, [128, NT, 80], BF16)
        for vb in vbufs.bufs:
            fw.memset("pool", vb[:, :, 64:65], 1.0)
        masks = sb(c, es, "cmask", [128, 4, TB], BF16)
        sel = sb(c, es, "csel", [128, 64], BF16)
        with ExitStack() as t2:
            mt = sb(c, t2, "mt", [128, TB], F32)
            for j in range(4):
                fw.memset("pool", mt[:, :], 1.0)
                fw.asel(mt[:, :], mt[:, :], [[1, TB]], ALU.is_ge, 0.0, -128 * j, -1)
                fw.cp("pool", masks[:, j, :], mt[:, :])
            fw.memset("pool", mt[:, 0:64], 0.0)
            fw.memset("pool", mt[64:96, 0:64], 1.0)
            fw.cp("pool", sel[:, :], mt[:, 0:64])
            fw.barrier()
        pes = rot(c, es, "pe", 5, [128, TB], BF16)
        pms = rot(c, es, "pm", 4, [128, TB], BF16)
        oas = rot(c, es, "oa", 2, [128, TB], BF16)
        ofs = rot(c, es, "of", 2, [128, TB], F32)
        rds = rot(c, es, "rd", 2, [128, TB], F32)
        ons = rot(c, es, "on", 3, [128, TB], BF16)
        scale = 96.0 ** -0.5
        def ld_c2(h):
            qh, kh, vh = qbufs.next(), kbufs.next(), vbufs.next()
            for q4 in range(4):
                cs = slice(q4 * 2048, (q4 + 1) * 2048)
                fw.dma("sp", qh[0:96, cs], QT.view(QTd[h, :, cs], h * 96, (h + 1) * 96))
                fw.dma("sp", kh[0:96, cs], KT.view(KTd[h, :, cs], h * 96, (h + 1) * 96))
            for q4 in range(4):
                fw.dma("sp", vh[:, q4 * 16:(q4 + 1) * 16, 0:64],
                       VD.view(Vd[q4 * 2048:(q4 + 1) * 2048, h * 64:(h + 1) * 64].rearrange("(t p) e -> p t e", p=128),
                               q4 * 2048, (q4 + 1) * 2048))
            return qh, kh, vh

        for h, (qh, kh, vh) in prefetched(H, ld_c2):
            for qb in range(NB):
                acc = c.psx.next()
                nk = 4 * qb + 4
                LOOK = 2
                pend = []
                for kt in range(nk + LOOK):
                    if kt < nk:
                        ps = c.ps.next()
                        fw.mm(ps[:, :], kh[0:96, kt * 128:(kt + 1) * 128], qh[0:96, qb * TB:(qb + 1) * TB])
                        pe = pes.next()
                        fw.act(pe[:, :], ps[:, :], AF.Exp, scale=scale)
                        if kt >= 4 * qb:
                            pm = pms.next()
                            fw.tt(("dve", "pool")[kt % 2], pm[:, :], pe[:, :], masks[:, kt - 4 * qb, :], ALU.mult)
                            pe = pm
                        pend.append((kt, pe))
                    if kt >= LOOK:
                        k0, pe0 = pend.pop(0)
                        fw.mm(acc[0:65, :], vh[:, k0, 0:65], pe0[:, :], k0 == 0, k0 == nk - 1)
                oa = oas.next()
                of = ofs.next()
                fw.cp("act", oa[0:65, :], acc[0:65, :])
                fw.cp("dve", of[0:64, :], acc[0:64, :])
                psd = c.ps.next()
                fw.mm(psd[0:64, :], sel[0:65, :], oa[0:65, :])
                rd = rds.next()
                fw.recip(rd[0:64, :], psd[0:64, :])
                on = ons.next()
                fw.tt("pool", on[0:64, :], of[0:64, :], rd[0:64, :], ALU.mult)
                hp, hh = h // 2, h % 2
                fw.dma("sp", OT.view(OTd[hp, hh * 64:(hh + 1) * 64, qb * TB:(qb + 1) * TB], hp * 128, (hp + 1) * 128),
                       on[0:64, :])
        fw.barrier()
    with ExitStack() as es:
        wo = sb(c, es, "c_wo", [128, 8, 1024], BF16)
        G = sb(c, es, "G", [128, 1024], F32)
        bcast(c, G[:, :], c.d["ln_gains"][li, 1])
        with ExitStack() as t2:
            load_w(c, t2, wo, c.d["c_w_o"], 1024, 1024)
            fw.barrier()
        xts = rot(c, es, "xt", 4, [128, 1024], F32)
        oTs = rot(c, es, "oT", 4, [128, 8, 128], BF16)
        sts = rot(c, es, "st", 4, [128, 16], F32)
        xos = rot(c, es, "xo", 2, [128, 1024], F32)
        c.pn_tmp = rot(c, es, "pnt", 2, [128, 1024], F32)
        def ld_c3(ti):
            r0 = ti * 128
            xt = xts.next()
            fw.dma("sp", xt[:, :], xin.rows(r0, r0 + 128))
            oT = oTs.next()
            fw.dma("sp", oT[:, :, :], OT.view(OTd[:, :, r0:r0 + 128].rearrange("h r s -> r h s"), 0, 1024))
            return xt, oT

        for ti, (xt, oT) in prefetched(NT, ld_c3):
            r0 = ti * 128
            psY = [c.ps.next(), c.ps.next()]
            for nh in range(2):
                for kc in range(8):
                    fw.mm(psY[nh][:, :], oT[:, kc, :], wo[:, kc, nh * 512:(nh + 1) * 512], kc == 0, kc == 7)
            postnorm_store(c, psY[0], psY[1], xt, G, xos.next(), sts.next(), xout.rows(r0, r0 + 128))
        fw.barrier()


def mixer_b(c, li, xin, xout):
    fw = c.fw
    TB = 512
    NB = S // TB
    with ExitStack() as es:
        w = sb(c, es, "b_w", [128, 8, 3088], BF16)
        wo = sb(c, es, "b_wo", [128, 8, 1024], BF16)
        wg2 = sb(c, es, "b_wg2", [16, 512], F32)
        bb = sb(c, es, "b_bias", [128, 512], F32)
        onb = sb(c, es, "b_onb", [128, 256], F32)
        G = sb(c, es, "G", [128, 1024], F32)
        triU = sb(c, es, "b_triU", [128, 128], F32)
        triS = sb(c, es, "b_triS", [128, 128], F32)
        maskU = sb(c, es, "b_maskU", [128, 4, 128], F32)
        S32 = sb(c, es, "b_S32", [128, 4, 256], F32)
        Sbf = sb(c, es, "b_Sbf", [128, 4, 256], BF16)
        bcast(c, G[:, :], c.d["ln_gains"][li, 1])
        bcast(c, bb[:, :], c.d["b_gate_bias"])
        bcast(c, onb[:, :], c.d["b_o_norm"])
        fw.dma("sp", wg2[:, :], RO(c.d["b_w_gate2"]))
        fw.memset("pool", S32[:, :, :], 0.0)
        fw.memset("pool", Sbf[:, :, :], 0.0)
        fw.memset("pool", triU[:, :], -1.0 / 16)
        fw.asel(triU[:, :], triU[:, :], [[1, 128]], ALU.is_ge, 0.0, 0, -1)
        fw.memset("pool", triS[:, :], -1.0 / 16)
        fw.asel(triS[:, :], triS[:, :], [[-1, 128]], ALU.is_gt, 0.0, 0, 1)
        fw.memset("pool", maskU[:, :, :], 1.0)
        for h in range(4):
            fw.asel(maskU[:, h, :], maskU[:, h, :], [[1, 128]], ALU.is_ge, 0.0, 0, -1)
        with ExitStack() as t2:
            load_w(c, t2, w, c.d["b_w_in"], D, 3088, gcol=gcol(c, li, 0))
            load_w(c, t2, wo, c.d["b_w_o"], 1024, 1024)
            fw.barrier()
        xts = rot(c, es, "xt", 8, [128, 1024], F32)
        hns = rot(c, es, "hn", 2, [128, 1024], BF16)
        sts = rot(c, es, "st", 6, [128, 16], F32)
        hnTs = rot(c, es, "hnT", 1, [128, 8, TB], BF16)
        qkTs = rot(c, es, "qkT", 1, [128, 8, TB], BF16)
        glTs = rot(c, es, "glT", 2, [16, TB], F32)
        zs = rot(c, es, "z", 2, [128, 512], F32)
        Ls = rot(c, es, "L", 2, [128, 512], F32)
        EGs = rot(c, es, "EG", 2, [128, 4, 128], F32)
        EnGs = rot(c, es, "EnG", 2, [128, 4, 128], F32)
        EGcs = rot(c, es, "EGc", 2, [128, 512], F32)
        qds = rot(c, es, "qd", 2, [128, 4, 128], BF16)
        kis = rot(c, es, "ki", 2, [128, 4, 128], BF16)
        kes = rot(c, es, "ke", 2, [128, 512], BF16)
        vs_ = rot(c, es, "v", 2, [128, 1024], BF16)
        ats = rot(c, es, "at", 2, [128, 4, 128], BF16)
        gss = rot(c, es, "gs", 1, [128, 1024], F32)
        ons = rot(c, es, "on", 1, [128, 1024], F32)
        obs = rot(c, es, "ob", 1, [128, 1024], BF16)
        obTs = rot(c, es, "obT", 2, [128, 8, 128], BF16)
        xos = rot(c, es, "xo", 2, [128, 1024], F32)
        c.pn_tmp = None
        for blk in range(NB):
            t0 = blk * TB
            hnT = hnTs.next()
            if blk == 0:
                xnext = []
                for ti in range(4):
                    xt = xts.next()
                    fw.dma("sp", xt[:, :], xin.rows(ti * 128, ti * 128 + 128))
                    xnext.append(xt)
            xl = xnext
            xnext = []
            if blk + 1 < NB:
                for ti in range(4):
                    r1 = t0 + TB + ti * 128
                    xt = xts.next()
                    fw.dma("sp", xt[:, :], xin.rows(r1, r1 + 128))
                    xnext.append(xt)
            for ti in range(4):
                hn = hns.next()
                norm_tile(c, xl[ti], hn, sts.next())
                transpose_to(c, hn, hnT, ti * 128)
            qkT = qkTs.next()
            for j in range(8):
                ps = c.ps.next()
                for kc in range(8):
                    fw.mm(ps[:, :], w[:, kc, j * 128:(j + 1) * 128], hnT[:, kc, :], kc == 0, kc == 7)
                fw.cp(("act", "dve")[j % 2], qkT[:, j, :], ps[:, :])
            glT = glTs.next()
            ps = c.ps.next()
            for kc in range(8):
                fw.mm(ps[0:16, :], w[:, kc, 3072:3088], hnT[:, kc, :], kc == 0, kc == 7)
            fw.cp("act", glT[:, :], ps[0:16, :])
            for ti in range(4):
                r0 = t0 + ti * 128
                tsl = slice(ti * 128, (ti + 1) * 128)
                psZ = c.ps.next()
                fw.mm(psZ[:, :], glT[:, tsl], wg2[:, :])
                z = zs.next()
                fw.tt("dve", z[:, :], psZ[:, :], bb[:, :], ALU.add)
                L = Ls.next()
                fw.act(z[:, :], z[:, :], AF.Exp, scale=-1.0)
                fw.act(L[:, :], z[:, :], AF.Ln, bias=1.0)
                psG = c.ps.next()
                for h in range(4):
                    fw.mm(psG[:, h * 128:(h + 1) * 128], L[:, h * 128:(h + 1) * 128], triU[:, :])
                EG, EnG = EGs.next(), EnGs.next()
                pg3 = V([psG], psG.t[:, :].rearrange("p (h t) -> p h t", h=4))
                fw.act(EG[:, :, :], pg3, AF.Exp)
                fw.act(EnG[:, :, :], pg3, AF.Exp, scale=-1.0)
                psGc = c.ps.next()
                fw.mm(psGc[:, :], triS[:, :], L[:, :])
                EGc = EGcs.next()
                fw.act(EGc[:, :], psGc[:, :], AF.Exp)
                qd, ki = qds.next(), kis.next()
                fw.stt("dve", qd[:, :, :], qkT[:, 0:4, tsl], 128.0 ** -0.5, EG[:, :, :], ALU.mult, ALU.mult)
                fw.tt("pool", ki[:, :, :], qkT[:, 4:8, tsl], EnG[:, :, :], ALU.mult)
                psK = c.ps.next()
                for kc in range(8):
                    fw.mm(psK[:, :], hnT[:, kc, tsl], w[:, kc, 512:1024], kc == 0, kc == 7)
                ke = kes.next()
                fw.tt("dve", ke[:, :], psK[:, :], EGc[:, :], ALU.mult)
                v = vs_.next()
                for hf in range(2):
                    ps = c.ps.next()
                    for kc in range(8):
                        fw.mm(ps[:, :], hnT[:, kc, tsl], w[:, kc, 1024 + hf * 512:1536 + hf * 512], kc == 0, kc == 7)
                    fw.cp(("act", "dve")[hf], v[:, hf * 512:(hf + 1) * 512], ps[:, :])
                psA = c.ps.next()
                for h in range(4):
                    fw.mm(psA[:, h * 128:(h + 1) * 128], ki[:, h, :], qd[:, h, :])
                at = ats.next()
                fw.tt("dve", at[:, :, :], V([psA], psA.t[:, :].rearrange("p (h t) -> p h t", h=4)), maskU[:, :, :],
                      ALU.mult)
                pO = [c.psx.next(), c.psx.next()]
                for h in range(4):
                    ov = pO[h // 2][:, (h % 2) * 256:(h % 2 + 1) * 256]
                    fw.mm(ov, at[:, h, :], v[:, h * 256:(h + 1) * 256], True, False)
                    fw.mm(ov, qd[:, h, :], Sbf[:, h, :], False, True)
                gs = gss.next()
                for hf in range(2):
                    ps = c.ps.next()
                    for kc in range(8):
                        fw.mm(ps[:, :], hnT[:, kc, tsl], w[:, kc, 2048 + hf * 512:2560 + hf * 512], kc == 0, kc == 7)
                    fw.act(gs[:, hf * 512:(hf + 1) * 512], ps[:, :], AF.Silu)
                for hf in range(2):
                    ps = c.ps.next()
                    for hh in range(2):
                        h = hf * 2 + hh
                        fw.mm(ps[:, hh * 256:(hh + 1) * 256], ke[:, h * 128:(h + 1) * 128], v[:, h * 256:(h + 1) * 256])
                    for hh in range(2):
                        h = hf * 2 + hh
                        fw.stt("dve", S32[:, h, :], S32[:, h, :], EG[:, h, 127:128], ps[:, hh * 256:(hh + 1) * 256],
                               ALU.mult, ALU.add)
                        fw.cp("act", Sbf[:, h, :], S32[:, h, :])
                st = sts.next()
                fw.memset("pool", st[:, 0:4], 0.0)
                for h in range(4):
                    fw.act(c.junk[:, h * 256:(h + 1) * 256], pO[h // 2][:, (h % 2) * 256:(h % 2 + 1) * 256], AF.Square,
                           accum=st[:, h:h + 1])
                fw.ts("dve", st[:, 4:8], st[:, 0:4], 1.0 / 256, EPS, ALU.mult, ALU.add)
                fw.act(st[:, 8:12], st[:, 4:8], AF.Sqrt)
                fw.recip(st[:, 12:16], st[:, 8:12])
                on = ons.next()
                for h in range(4):
                    fw.stt("dve", on[:, h * 256:(h + 1) * 256], pO[h // 2][:, (h % 2) * 256:(h % 2 + 1) * 256],
                           st[:, 12 + h:13 + h], onb[:, :], ALU.mult, ALU.mult)
                ob = obs.next()
                fw.tt("pool", ob[:, :], on[:, :], gs[:, :], ALU.mult)
                obT = obTs.next()
                transpose_to(c, ob, obT, 0)
                psY = [c.ps.next(), c.ps.next()]
                for nh in range(2):
                    for kc in range(8):
                        fw.mm(psY[nh][:, :], obT[:, kc, :], wo[:, kc, nh * 512:(nh + 1) * 512], kc == 0, kc == 7)
                postnorm_store(c, psY[0], psY[1], xl[ti], G, xos.next(), sts.next(), xout.rows(r0, r0 + 128))
        fw.barrier()


EM05 = float(np.exp(-0.5))


def block_mask(c, dst, kind, val, CH=32):
    fw = c.fw
    fw.memset("pool", dst, val)
    for cc in range(128 // CH):
        v = dst[:, cc * CH:(cc + 1) * CH]
        lo = cc * CH
        if kind == "IU":
            fw.asel(v, v, [[1, CH]], ALU.is_ge, 0.0, lo, -1)
            fw.asel(v, v, [[0, CH]], ALU.is_ge, 0.0, -lo, 1)
        elif kind == "SU":
            fw.asel(v, v, [[1, CH]], ALU.is_gt, 0.0, lo, -1)
            fw.asel(v, v, [[0, CH]], ALU.is_ge, 0.0, -lo, 1)
        else:
            fw.asel(v, v, [[-1, CH]], ALU.is_gt, 0.0, -lo, 1)
            fw.asel(v, v, [[0, CH]], ALU.is_ge, 0.0, lo + CH - 1, -1)


def chunk_ind(c, dst, val, CH=32):
    fw = c.fw
    fw.memset("pool", dst, val)
    for cc in range(128 // CH):
        v = dst[:, cc:cc + 1]
        fw.asel(v, v, [[0, 1]], ALU.is_ge, 0.0, -cc * CH, 1)
        fw.asel(v, v, [[0, 1]], ALU.is_ge, 0.0, cc * CH + CH - 1, -1)


def mixer_d(c, li, xin, xout):
    import os
    fw = c.fw
    nc = c.nc
    H = 16
    NC = 4
    names = ["At", "Rt", "Bh", "Kh", "Bt", "Kt", "Vv", "BON", "GATE"]
    dd = {n: DramT(nc.dram_tensor("d_" + n, [S, 1024], BF16, kind="Internal").ap()) for n in names}
    GLd_ap = nc.dram_tensor("d_GL", [NT * 64, 64], F32, kind="Internal").ap()
    GLd = DramT(GLd_ap, blk=64)
    Yd = DramT(nc.dram_tensor("d_Y", [S, 1024], F32, kind="Internal").ap())
    dstage = int(os.environ.get("D_STAGE", "9"))
    with ExitStack() as es:
        Wr = sb(c, es, "d_Wr", [128, 8, 1024], BF16)
        Wk = sb(c, es, "d_Wk", [128, 8, 1024], BF16)
        Wv = sb(c, es, "d_Wv", [128, 8, 1024], BF16)
        w1 = sb(c, es, "d_w1", [128, 8, 64], BF16)
        a1 = sb(c, es, "d_a1", [128, 8, 64], BF16)
        g1 = sb(c, es, "d_g1", [128, 8, 128], BF16)
        w2 = sb(c, es, "d_w2", [64, 1, 1024], BF16)
        a2 = sb(c, es, "d_a2", [64, 1, 1024], BF16)
        g2 = sb(c, es, "d_g2", [128, 1, 1024], BF16)
        mixc = sb(c, es, "d_mix", [128, 48], F32)
        bc = {}
        for n in ("d_w0", "d_a0", "d_k_k", "d_k_a"):
            bc[n] = sb(c, es, n + "b", [128, 1024], F32)
            bcast(c, bc[n][:, :], c.d[n])
        rkb = sb(c, es, "d_rkb", [128, 1024], F32)
        bcast(c, rkb[:, :], c.d["d_r_k"].rearrange("h e -> (h e)"))
        omka = sb(c, es, "d_omka", [128, 1024], F32)
        fw.ts("pool", omka[:, :], bc["d_k_a"][:, :], -1.0, 1.0, ALU.mult, ALU.add)
        triC = sb(c, es, "d_triC", [128, 128], F32)
        triD = sb(c, es, "d_triD", [128, 128], F32)
        indC = sb(c, es, "d_indC", [128, NC], F32)
        block_mask(c, triC[:, :], "IU", -EM05)
        block_mask(c, triD[:, :], "SL", -EM05)
        chunk_ind(c, indC[:, :], -EM05)
        hz = sb(c, es, "d_hz", [128, 8, 128], BF16)
        fw.memset("pool", hz[:, :, :], 0.0)
        with ExitStack() as t2:
            load_cols(c, t2, mixc[:, :], c.d["d_mix"].rearrange("i (c p) -> (i c) p", p=128), 48)
            gc = gcol(c, li, 0)
            load_w(c, t2, Wr, c.d["d_w_rkv"][0], D, 1024, gcol=gc)
            load_w(c, t2, Wk, c.d["d_w_rkv"][1], D, 1024, gcol=gc)
            load_w(c, t2, Wv, c.d["d_w_rkv"][2], D, 1024, gcol=gc)
            load_w(c, t2, w1, c.d["d_w1"], D, 64, gcol=gc)
            load_w(c, t2, a1, c.d["d_a1"], D, 64, gcol=gc)
            load_w(c, t2, g1, c.d["d_g1"], D, 128, gcol=gc)
            for dst, src, kk_ in ((w2, "d_w2", 64), (a2, "d_a2", 64), (g2, "d_g2", 128)):
                st_ = sb(c, t2, "wst", [128, 1024], F32)
                fw.dma("sp", st_[0:kk_, :], RO(c.d[src]))
                fw.cp("dve", dst[0:kk_, 0, :], st_[0:kk_, :])
            fw.barrier()
        xts = rot(c, es, "xt", 3, [128, 1024], F32)
        hns = rot(c, es, "hn", 2, [128, 1024], BF16)
        sts = rot(c, es, "st", 4, [128, 16], F32)
        hnTs = rot(c, es, "hnT", 2, [128, 8, 128], BF16)
        DTs = rot(c, es, "DT", 1, [128, 8, 128], BF16)
        Xs = rot(c, es, "X", 7, [128, 8, 128], BF16)
        smT = rot(c, es, "smT", 4, [128, 128], BF16)
        f32s = rot(c, es, "f", 9, [128, 1024], F32)
        b16s = rot(c, es, "o", 12, [128, 1024], BF16)
        s16 = rot(c, es, "s16", 4, [128, 64], F32)
        glts = rot(c, es, "glt", 2, [64, 64], F32)
        prev = hz

        def v3(buf):
            return V([buf], buf.t[:, :].rearrange("p (h e) -> p h e", h=H))

        def b3(vw):
            return V(vw.bufs, vw.ap.unsqueeze(2).to_broadcast([128, H, 64]))

        def ld_d1(ti):
            xt = xts.next()
            fw.dma("sp", xt[:, :], xin.rows(ti * 128, ti * 128 + 128))
            return xt

        for ti, xt in prefetched(NT if dstage >= 1 else 0, ld_d1):
            r0 = ti * 128
            hn = hns.next()
            norm_tile(c, xt, hn, sts.next())
            cur = hnTs.next()
            transpose_to(c, hn, cur, 0)
            DT = DTs.next()
            fw.tt("pool", DT[:, :, 0:1], prev[:, :, 127:128], cur[:, :, 0:1], ALU.subtract)
            fw.tt("pool", DT[:, :, 1:128], cur[:, :, 0:127], cur[:, :, 1:128], ALU.subtract)
            X = []
            for i in range(6):
                Xi = Xs.next()
                for kc in range(8):
                    mcol = mixc[:, i * 8 + kc:i * 8 + kc + 1]
                    if (i * 8 + kc) % 3 != 2:
                        fw.stt("dve", Xi[:, kc, :], DT[:, kc, :], mcol, cur[:, kc, :], ALU.mult, ALU.add)
                    else:
                        fw.act(Xi[:, kc, :], DT[:, kc, :], AF.Copy, scale=mcol)
                        fw.tt("pool", Xi[:, kc, :], Xi[:, kc, :], cur[:, kc, :], ALU.add)
                X.append(Xi)
            Xr, Xw, Xk, Xv, Xa, Xg = X
            prev = cur

            def proj2(Xi, W, nh):
                ps = c.ps.next()
                for kc in range(8):
                    fw.mm(ps[:, :], Xi[:, kc, :], W[:, kc, nh * 512:(nh + 1) * 512], kc == 0, kc == 7)
                return ps

            def low(Xi, Wl, M, func):
                ps = c.ps.next()
                for kc in range(8):
                    fw.mm(ps[0:M, 0:128], Wl[:, kc, :], Xi[:, kc, :], kc == 0, kc == 7)
                o = smT.next()
                fw.act(o[0:M, :], ps[0:M, 0:128], func)
                return o

            twT = low(Xw, w1, 64, AF.Tanh)
            aaT = low(Xa, a1, 64, AF.Copy)
            ggT = low(Xg, g1, 128, AF.Sigmoid)
            sig = f32s.next()
            av = f32s.next()
            for nh in range(2):
                cs_ = slice(nh * 512, (nh + 1) * 512)
                ps = c.ps.next()
                fw.mm(ps[:, :], twT[0:64, :], w2[0:64, 0, cs_])
                fw.tt("dve", sig[:, cs_], ps[:, :], bc["d_w0"][:, cs_], ALU.add)
                ps = c.ps.next()
                fw.mm(ps[:, :], aaT[0:64, :], a2[0:64, 0, cs_])
                fw.tt("dve", av[:, cs_], ps[:, :], bc["d_a0"][:, cs_], ALU.add)
            fw.act(sig[:, :], sig[:, :], AF.Sigmoid)
            fw.act(av[:, :], av[:, :], AF.Sigmoid)
            gate = b16s.next()
            for nh in range(2):
                ps = c.ps.next()
                fw.mm(ps[:, :], ggT[:, :], g2[:, 0, nh * 512:(nh + 1) * 512])
                fw.cp("act", gate[:, nh * 512:(nh + 1) * 512], ps[:, :])
            fw.dma("sp", dd["GATE"].rows(r0, r0 + 128), gate[:, :])
            rr = f32s.next()
            kx = f32s.next()
            vv = f32s.next()
            for nh in range(2):
                cs_ = slice(nh * 512, (nh + 1) * 512)
                fw.cp("act", rr[:, cs_], proj2(Xr, Wr, nh)[:, :])
                fw.cp("act", kx[:, cs_], proj2(Xk, Wk, nh)[:, :])
                fw.cp("act", vv[:, cs_], proj2(Xv, Wv, nh)[:, :])
            Vb = b16s.next()
            fw.cp("act", Vb[:, :], vv[:, :])
            fw.dma("sp", dd["Vv"].rows(r0, r0 + 128), Vb[:, :])
            kkx = f32s.next()
            tmp = f32s.next()
            fw.tt("dve", kkx[:, :], kx[:, :], bc["d_k_k"][:, :], ALU.mult)
            fw.act(tmp[:, :], kkx[:, :], AF.Square)
            sm = s16.next()
            fw.red("dve", sm[:, 0:16], v3(tmp), ALU.add)
            fw.act(sm[:, 16:32], sm[:, 0:16], AF.Sqrt)
            fw.ts("dve", sm[:, 16:32], sm[:, 16:32], 1e-12, None, ALU.max)
            fw.recip(sm[:, 32:48], sm[:, 16:32])
            fw.tt("dve", v3(kkx), v3(kkx), b3(sm[:, 32:48]), ALU.mult)
            fw.tt("pool", tmp[:, :], av[:, :], bc["d_k_a"][:, :], ALU.mult)
            fw.tt("pool", tmp[:, :], tmp[:, :], omka[:, :], ALU.add)
            fw.tt("dve", kx[:, :], kx[:, :], tmp[:, :], ALU.mult)
            fw.tt("dve", tmp[:, :], rr[:, :], rkb[:, :], ALU.mult)
            fw.tt("pool", tmp[:, :], tmp[:, :], kx[:, :], ALU.mult)
            fw.red("dve", sm[:, 48:64], v3(tmp), ALU.add)
            bon = b16s.next()
            fw.tt("dve", v3(bon), v3(vv), b3(sm[:, 48:64]), ALU.mult)
            fw.dma("sp", dd["BON"].rows(r0, r0 + 128), bon[:, :])
            fw.tt("pool", av[:, :], kkx[:, :], av[:, :], ALU.mult)
            E1 = f32s.next()
            E2 = f32s.next()
            E3 = tmp
            E4 = vv
            for nh in range(2):
                cs_ = slice(nh * 512, (nh + 1) * 512)
                ps = c.ps.next()
                fw.mm(ps[:, :], triC[:, :], sig[:, cs_])
                fw.act(E1[:, cs_], ps[:, :], AF.Exp)
                fw.act(E2[:, cs_], ps[:, :], AF.Exp, scale=-1.0)
                fw.stt("dve", E3[:, cs_], sig[:, cs_], EM05, ps[:, :], ALU.mult, ALU.add)
                ps = c.ps.next()
                fw.mm(ps[:, :], triD[:, :], sig[:, cs_])
                fw.act(E4[:, cs_], ps[:, :], AF.Exp)
            fw.act(E3[:, :], E3[:, :], AF.Exp)
            outs = {}
            for n in ("At", "Rt", "Bh", "Kh", "Bt", "Kt"):
                outs[n] = b16s.next()
            fw.stt("dve", outs["At"][:, :], kkx[:, :], -1.0, E3[:, :], ALU.mult, ALU.mult)
            fw.tt("pool", outs["Rt"][:, :], rr[:, :], E1[:, :], ALU.mult)
            fw.tt("dve", outs["Bh"][:, :], av[:, :], E2[:, :], ALU.mult)
            fw.tt("pool", outs["Kh"][:, :], kx[:, :], E2[:, :], ALU.mult)
            fw.tt("dve", outs["Bt"][:, :], av[:, :], E4[:, :], ALU.mult)
            fw.tt("pool", outs["Kt"][:, :], kx[:, :], E4[:, :], ALU.mult)
            for n in ("At", "Rt", "Bh", "Kh", "Bt", "Kt"):
                fw.dma("sp", dd[n].rows(r0, r0 + 128), outs[n][:, :])
            ps = c.ps.next()
            for h in range(H):
                fw.mm(ps[0:64, h * NC:(h + 1) * NC], sig[:, h * 64:(h + 1) * 64], indC[:, :])
            glt = glts.next()
            fw.act(glt[:, :], ps[0:64, 0:64], AF.Exp)
            fw.dma("sp", GLd.rows(ti * 64, ti * 64 + 64), glt[:, :])
        fw.barrier()
    with ExitStack() as es:
        mask1 = sb(c, es, "d_m1", [128, 384], F32)
        mask2 = sb(c, es, "d_m2", [128, 256], F32)
        II = sb(c, es, "d_II", [128, 256], BF16)
        CM = sb(c, es, "d_CM", [128, NC], F32)
        CMb = sb(c, es, "d_CMb", [128, NC], BF16)
        block_mask(c, mask1[:, 0:128], "SU", 1.0)
        block_mask(c, mask1[:, 128:256], "SL", 1.0)
        block_mask(c, mask1[:, 256:384], "SU", 1.0)
        block_mask(c, mask2[:, 0:128], "IU", 1.0)
        block_mask(c, mask2[:, 128:256], "IU", 1.0)
        chunk_ind(c, CM[:, :], 1.0)
        fw.cp("pool", CMb[:, :], CM[:, :])
        fw.cp("pool", II[:, 0:128], c.identb[:, :])
        fw.cp("pool", II[:, 128:256], c.identb[:, :])
        I64 = c.identf[0:64, 0:64]
        GS = 8
        lds = {n: rot(c, es, "l" + n, 3, [128, 1024], BF16) for n in ("At", "Rt", "Bh", "Kh", "Bt", "Kt", "Vv")}
        XTs = rot(c, es, "XT", 1, [64, H, 4, 128], BF16)
        GLs = rot(c, es, "GL", 3, [64, 64], F32)
        NAs = rot(c, es, "NA", GS, [128, 384], BF16)
        RBs = rot(c, es, "RB", GS, [128, 256], BF16)
        MMs = rot(c, es, "MM", 2 * GS, [128, 256], BF16)
        PPs = rot(c, es, "PP", 2 * GS, [128, 256], BF16)
        MFs = rot(c, es, "MF", GS, [128, 128], BF16)
        AWs = rot(c, es, "AW", GS, [128, 128], BF16)
        U0s = rot(c, es, "U0", GS, [128, 64], BF16)
        RcTs = rot(c, es, "RcT", GS, [64, 128], F32)
        Bms = rot(c, es, "Bm", GS, [128, NC, 64], BF16)
        Kms = rot(c, es, "Km", GS, [128, NC, 64], BF16)
        Gds = rot(c, es, "Gd", GS, [64, NC, 64], F32)
        Y0s = rot(c, es, "Y0", GS, [64, 128], F32)
        PhTs = rot(c, es, "PhT", GS, [64, NC, 64], F32)
        PsTs = rot(c, es, "PsT", GS, [64, NC, 64], F32)
        YTs = rot(c, es, "YT", GS, [64, 128], F32)
        Yts = rot(c, es, "Yt", 2, [128, 1024], F32)
        STs = [rot(c, es, "ST%d" % h, 3, [64, 64], F32) for h in range(H)]
        ST = []
        for h in range(H):
            b_ = STs[h].next()
            fw.memset("pool", b_[:, :], 0.0)
            ST.append(b_)
        def ld_d2(ti):
            r0 = ti * 128
            L = {}
            for n in lds:
                L[n] = lds[n].next()
                fw.dma("sp", L[n][:, :], dd[n].rows(r0, r0 + 128))
            GL = GLs.next()
            fw.dma("sp", GL[:, :], GLd.rows(ti * 64, ti * 64 + 64))
            return L, GL

        for ti, (L, GL) in prefetched(NT if dstage >= 2 else 0, ld_d2):
            r0 = ti * 128
            XT = XTs.next()
            for h2 in range(H // 2):
                ps = c.ps.next()
                psb = ps.t[:, :].bitcast(BF16)
                for hh in range(2):
                    h = h2 * 2 + hh
                    for j, n in enumerate(("At", "Rt", "Bh", "Kh")):
                        col = (hh * 4 + j) * 128
                        fw.tr(V([ps], psb[0:64, col:col + 128]), L[n][:, h * 64:(h + 1) * 64], c.identb[:, :])
                fw.cp(("act", "dve")[h2 % 2], XT[:, h2 * 2:h2 * 2 + 2, :, :],
                      V([ps], psb[0:64, :].rearrange("p (h j t) -> p h j t", h=2, j=4)))
            Yt = Yts.next()
            for g0 in range(0, H, GS):
                hs = list(range(g0, g0 + GS))
                NA, RB, MM, PP, MF, AW, U0, RcT, Bm, Km, Gd, Y0, PhT, PsT = ({} for _ in range(14))
                for h in hs:
                    AtT, RtT, BhT, KhT = (XT[:, h, j, :] for j in range(4))
                    b1, b2 = c.ps.next(), c.ps.next()
                    fw.mm(b1[:, 0:128], BhT, AtT)
                    fw.mm(b1[:, 128:256], AtT, BhT)
                    fw.mm(b1[:, 256:384], KhT, AtT)
                    fw.mm(b2[:, 0:128], BhT, RtT)
                    fw.mm(b2[:, 128:256], KhT, RtT)
                    NA[h], RB[h], MM[h] = NAs.next(), RBs.next(), MMs.next()
                    fw.tt("dve", NA[h][:, :], b1[:, 0:384], mask1[:, :], ALU.mult)
                    fw.tt("dve", RB[h][:, :], b2[:, 0:256], mask2[:, :], ALU.mult)
                    fw.tt("pool", MM[h][:, :], NA[h][:, 0:256], II[:, :], ALU.add)
                    PP[h] = NA[h]
                for lev in range(3):
                    for h in hs:
                        P_, PT_ = PP[h][:, 0:128], PP[h][:, 128:256]
                        bp = c.ps.next()
                        fw.mm(bp[:, 0:128], PT_, P_)
                        fw.mm(bp[:, 128:256], P_, PT_)
                        npp = PPs.next()
                        fw.cp("act", npp[:, :], bp[:, 0:256])
                        PP[h] = npp
                    for h in hs:
                        P_ = PP[h][:, 0:128]
                        M_, MT_ = MM[h][:, 0:128], MM[h][:, 128:256]
                        bm = c.ps.next()
                        fw.mm(bm[:, 0:128], MT_, P_)
                        fw.mm(bm[:, 128:256], P_, MT_)
                        nmm = MMs.next()
                        fw.tt("dve", nmm[:, :], bm[:, 0:256], MM[h][:, :], ALU.add)
                        MM[h] = nmm
                for h in hs:
                    bp = c.ps.next()
                    fw.mm(bp[:, 0:128], PP[h][:, 128:256], PP[h][:, 0:128])
                    npp = PPs.next()
                    fw.cp("act", npp[:, 0:128], bp[:, 0:128])
                    PP[h] = npp
                for h in hs:
                    bm = c.ps.next()
                    fw.mm(bm[:, 0:128], MM[h][:, 128:256], PP[h][:, 0:128])
                    MF[h] = MFs.next()
                    fw.tt("dve", MF[h][:, :], bm[:, 0:128], MM[h][:, 0:128], ALU.add)
                for h in hs:
                    hc = slice(h * 64, (h + 1) * 64)
                    b_ = c.ps.next()
                    fw.mm(b_[:, 0:64], MF[h][:, :], L["At"][:, hc])
                    fw.mm(b_[:, 64:128], NA[h][:, 256:384], L["Vv"][:, hc])
                    AW[h] = AWs.next()
                    fw.cp("act", AW[h][:, :], b_[:, 0:128])
                    Bm[h], Km[h], Gd[h] = Bms.next(), Kms.next(), Gds.next()
                    cmb = V([CMb], CMb.t[:, :].unsqueeze(2).to_broadcast([128, NC, 64]))
                    fw.tt("pool", Bm[h][:, :, :], V([L["Bt"]], L["Bt"].t[:, hc].unsqueeze(1).to_broadcast([128, NC, 64])),
                          cmb, ALU.mult)
                    fw.tt("pool", Km[h][:, :, :], V([L["Kt"]], L["Kt"].t[:, hc].unsqueeze(1).to_broadcast([128, NC, 64])),
                          cmb, ALU.mult)
                    fw.tt("pool", Gd[h][:, :, :],
                          V(I64.bufs, I64.ap.unsqueeze(1).to_broadcast([64, NC, 64])),
                          V([GL], GL.t[:, h * NC:(h + 1) * NC].unsqueeze(2).to_broadcast([64, NC, 64])), ALU.mult)
                for h in hs:
                    b_ = c.ps.next()
                    fw.mm(b_[:, 0:64], MF[h][:, :], AW[h][:, 64:128])
                    fw.mm(b_[0:64, 64:192], AW[h][:, 0:64], RB[h][:, 0:128])
                    U0[h], RcT[h] = U0s.next(), RcTs.next()
                    fw.cp("act", U0[h][:, :], b_[:, 0:64])
                    fw.tt("dve", RcT[h][:, :], b_[0:64, 64:192], XT[:, h, 1, :], ALU.add)
                for h in hs:
                    hc = slice(h * 64, (h + 1) * 64)
                    b1, b2 = c.ps.next(), c.ps.next()
                    fw.mm(b1[0:64, 0:128], U0[h][:, :], RB[h][:, 0:128], True, False)
                    fw.mm(b1[0:64, 0:128], L["Vv"][:, hc], RB[h][:, 128:256], False, True)
                    bmv = V([Bm[h]], Bm[h].t[:, :, :].rearrange("p c e -> p (c e)"))
                    kmv = V([Km[h]], Km[h].t[:, :, :].rearrange("p c e -> p (c e)"))
                    fw.mm(b1[0:64, 128:384], AW[h][:, 0:64], bmv)
                    fw.mm(b2[0:64, 0:256], U0[h][:, :], bmv, True, False)
                    fw.mm(b2[0:64, 0:256], L["Vv"][:, hc], kmv, False, True)
                    Y0[h], PhT[h], PsT[h] = Y0s.next(), PhTs.next(), PsTs.next()
                    fw.cp("act", Y0[h][:, :], b1[0:64, 0:128])
                    fw.tt("dve", V([PhT[h]], PhT[h].t[:, :, :].rearrange("p c e -> p (c e)")), b1[0:64, 128:384],
                          V([Gd[h]], Gd[h].t[:, :, :].rearrange("p c e -> p (c e)")), ALU.add)
                    fw.cp("act", V([PsT[h]], PsT[h].t[:, :, :].rearrange("p c e -> p (c e)")), b2[0:64, 0:256])
                yb = [c.psx.next(), c.psx.next()]
                for cc in range(NC):
                    for h in hs:
                        hl = h - g0
                        yv = yb[hl // 4][0:64, (hl % 4) * 128 + cc * 32:(hl % 4) * 128 + cc * 32 + 32]
                        fw.mm(yv, ST[h][:, :], RcT[h][:, cc * 32:(cc + 1) * 32])
                        bs = c.ps.next()
                        fw.mm(bs[0:64, 0:64], PhT[h][:, cc, :], ST[h][:, :], True, False)
                        fw.mm(bs[0:64, 0:64], PsT[h][:, cc, :], I64, False, True)
                        ns = STs[h].next()
                        fw.cp(("act", "dve")[h % 2], ns[:, :], bs[0:64, 0:64])
                        ST[h] = ns
                pt = c.ps.next()
                for h in hs:
                    hl = h - g0
                    YT = YTs.next()
                    fw.tt("dve", YT[:, :], yb[hl // 4][0:64, (hl % 4) * 128:(hl % 4) * 128 + 128], Y0[h][:, :], ALU.add)
                    fw.tr(pt[:, hl * 64:(hl + 1) * 64], YT[:, :], I64)
                fw.cp("act", Yt[:, g0 * 64:(g0 + GS) * 64], pt[:, :])
            fw.dma("sp", Yd.rows(r0, r0 + 128), Yt[:, :])
        fw.barrier()
    with ExitStack() as es:
        wo = sb(c, es, "d_wo", [128, 8, 1024], BF16)
        G = sb(c, es, "G", [128, 1024], F32)
        lw_ = sb(c, es, "d_lnw", [128, 1024], F32)
        lb_ = sb(c, es, "d_lnb", [128, 1024], F32)
        bcast(c, G[:, :], c.d["ln_gains"][li, 1])
        bcast(c, lw_[:, :], c.d["d_lnx_w"])
        bcast(c, lb_[:, :], c.d["d_lnx_b"])
        with ExitStack() as t2:
            load_w(c, t2, wo, c.d["d_w_o"], 1024, 1024)
            fw.barrier()
        xts = rot(c, es, "xt", 4, [128, 1024], F32)
        ys = rot(c, es, "y", 3, [128, 1024], F32)
        sqs = rot(c, es, "sq", 2, [128, 1024], F32)
        bons = rot(c, es, "bon", 3, [128, 1024], BF16)
        gts = rot(c, es, "gt", 3, [128, 1024], BF16)
        obs = rot(c, es, "ob", 2, [128, 1024], BF16)
        obTs = rot(c, es, "obT", 2, [128, 8, 128], BF16)
        sts = rot(c, es, "st", 4, [128, 16], F32)
        sms = rot(c, es, "sm", 2, [128, 96], F32)
        xos = rot(c, es, "xo", 2, [128, 1024], F32)
        c.pn_tmp = rot(c, es, "pnt", 2, [128, 1024], F32)

        def v3(buf):
            return V([buf], buf.t[:, :].rearrange("p (h e) -> p h e", h=H))

        def b3(vw):
            return V(vw.bufs, vw.ap.unsqueeze(2).to_broadcast([128, H, 64]))

        def ld_d3(ti):
            r0 = ti * 128
            xt, y, bon, gt = xts.next(), ys.next(), bons.next(), gts.next()
            fw.dma("sp", xt[:, :], xin.rows(r0, r0 + 128))
            fw.dma("sp", y[:, :], Yd.rows(r0, r0 + 128))
            fw.dma("sp", bon[:, :], dd["BON"].rows(r0, r0 + 128))
            fw.dma("sp", gt[:, :], dd["GATE"].rows(r0, r0 + 128))
            return xt, y, bon, gt

        for ti, (xt, y, bon, gt) in prefetched(NT if dstage >= 3 else 0, ld_d3):
            r0 = ti * 128
            sm = sms.next()
            sq = sqs.next()
            fw.red("dve", sm[:, 0:16], v3(y), ALU.add)
            fw.act(sq[:, :], y[:, :], AF.Square)
            fw.red("dve", sm[:, 16:32], v3(sq), ALU.add)
            fw.ts("dve", sm[:, 32:48], sm[:, 0:16], 1.0 / 64)
            fw.tt("dve", sm[:, 48:64], sm[:, 32:48], sm[:, 32:48], ALU.mult)
            fw.stt("dve", sm[:, 64:80], sm[:, 16:32], 1.0 / 64, sm[:, 48:64], ALU.mult, ALU.subtract)
            fw.ts("dve", sm[:, 64:80], sm[:, 64:80], 64e-5, None, ALU.add)
            fw.act(sm[:, 64:80], sm[:, 64:80], AF.Sqrt)
            fw.recip(sm[:, 80:96], sm[:, 64:80])
            fw.tt("dve", v3(y), v3(y), b3(sm[:, 32:48]), ALU.subtract)
            fw.tt("dve", v3(y), v3(y), b3(sm[:, 80:96]), ALU.mult)
            fw.tt("pool", y[:, :], y[:, :], lw_[:, :], ALU.mult)
            fw.tt("pool", y[:, :], y[:, :], lb_[:, :], ALU.add)
            fw.tt("dve", y[:, :], y[:, :], bon[:, :], ALU.add)
            ob = obs.next()
            fw.tt("pool", ob[:, :], y[:, :], gt[:, :], ALU.mult)
            obT = obTs.next()
            transpose_to(c, ob, obT, 0)
            psY = [c.ps.next(), c.ps.next()]
            for nh in range(2):
                for kc in range(8):
                    fw.mm(psY[nh][:, :], obT[:, kc, :], wo[:, kc, nh * 512:(nh + 1) * 512], kc == 0, kc == 7)
            postnorm_store(c, psY[0], psY[1], xt, G, xos.next(), sts.next(), xout.rows(r0, r0 + 128))
        fw.barrier()


MIXERS = {0: mixer_a, 1: mixer_b, 2: mixer_c, 3: mixer_d}

_CACHE = {}


def run(inputs, subs=None, cores=8):
    key = tuple(subs) if subs is not None else None
    if key not in _CACHE:
        _CACHE[key] = build(subs)
    nc, c = _CACHE[key]
    in_maps = []
    for ci in range(cores):
        b = ci % 4
        m = {}
        hc = host_consts()
        for name, shape, dt in IN_SPECS:
            a = np.asarray(hc[name] if name in hc else inputs[name])
            if name in ("x", "mem", "positions"):
                a = a[b]
            a = np.ascontiguousarray(a).reshape(shape)
            m[name] = a
        in_maps.append(m)
    res = run_bass_kernel_spmd(nc, in_maps, core_ids=list(range(cores)))
    return [r["out"] for r in res.results]


def kernel(**inputs):
    outs = run(inputs)
    return np.stack(outs[0:4], axis=0).astype(np.float32)
```
